# Optimizing a Trainium2 kernel written in Bass

```python
import math
import jax
import jax.numpy as jnp
from jax import lax
import numpy as np

D_MODEL = 1024
BATCH = 8
SEQ = 8192
DEPTH = 1
DEC_BATCH = 32
DEC_SEQ = 64
PAST_LEN = 4096

CHUNK = 64
Q_BLOCK = 128
D_MIX = D_MODEL
D_ATTN = D_MIX // 2
D_SSM = D_MIX - D_ATTN
N_HEADS = 8
QK_NOPE = 64
QK_ROPE = 32
V_DIM = D_ATTN // N_HEADS
KV_LORA = 256
Q_LORA = 768
ROPE_THETA = 10000.0
SSM_GROUP = 16
N_SSM_GROUPS = D_SSM // SSM_GROUP
SSM_STATE = 64
D_FF = 4 * D_MODEL
D_IN = Q_LORA + KV_LORA + QK_ROPE + D_SSM
SPLITS = (Q_LORA, Q_LORA + KV_LORA, Q_LORA + KV_LORA + QK_ROPE)
SOFTMAX_SCALE = (QK_NOPE + QK_ROPE) ** -0.5
EPS = 1e-6
NEG_INF = -1e30

kernel_name = "hybrid_mla_s5_streaming_step"


def rmsnorm(x, g):
    xf = x.astype(jnp.float32)
    y = xf * lax.rsqrt(jnp.mean(xf * xf, axis=-1, keepdims=True) + EPS)
    return (y * g.astype(jnp.float32)).astype(x.dtype)


def apply_rope(x, pos):
    rdim = x.shape[-1]
    half = rdim // 2
    inv_freq = ROPE_THETA ** (-(jnp.arange(half, dtype=jnp.float32) * 2.0) / rdim)
    ang = pos.astype(jnp.float32)[:, None] * inv_freq[None, :]
    cos = jnp.cos(ang)[None, :, None, :]
    sin = jnp.sin(ang)[None, :, None, :]
    xf = x.astype(jnp.float32)
    x1, x2 = xf[..., :half], xf[..., half:]
    return jnp.concatenate([x1 * cos - x2 * sin, x1 * sin + x2 * cos], axis=-1).astype(x.dtype)


def chunk_attention(q, k, v, q_pos, k_pos):
    s = jnp.einsum("bqhd,bkhd->bhqk", q, k, preferred_element_type=jnp.float32) * SOFTMAX_SCALE
    visible = (k_pos[None, :] // CHUNK) <= (q_pos[:, None] // CHUNK)
    s = jnp.where(visible[None, None], s, NEG_INF)
    p = jax.nn.softmax(s, axis=-1)
    return jnp.einsum("bhqk,bkhd->bqhd", p.astype(v.dtype), v)


def mla_attention(q, k, v, q_pos, k_pos):
    bsz, t = q.shape[0], q.shape[1]
    if t <= Q_BLOCK:
        return chunk_attention(q, k, v, q_pos, k_pos)
    nb = t // Q_BLOCK
    qb = q.reshape(bsz, nb, Q_BLOCK, N_HEADS, q.shape[-1]).transpose(1, 0, 2, 3, 4)
    pb = q_pos.reshape(nb, Q_BLOCK)
    ob = lax.map(lambda args: chunk_attention(args[0], k, v, args[1], k_pos), (qb, pb))
    return ob.transpose(1, 0, 2, 3, 4).reshape(bsz, t, N_HEADS, V_DIM)


def s5_discretize(a_re, a_im, log_step):
    dt = jnp.exp(log_step.astype(jnp.float32))[:, None]
    lr, li = a_re.astype(jnp.float32), a_im.astype(jnp.float32)
    mag = jnp.exp(lr * dt)
    lb_re, lb_im = mag * jnp.cos(li * dt), mag * jnp.sin(li * dt)
    nr, ni = lb_re - 1.0, lb_im
    den = lr * lr + li * li
    coef_re = (nr * lr + ni * li) / den
    coef_im = (ni * lr - nr * li) / den
    return lb_re, lb_im, coef_re, coef_im


def _ssm_combine(e1, e2):
    a1r, a1i, b1r, b1i = e1
    a2r, a2i, b2r, b2i = e2
    return (a1r * a2r - a1i * a2i,
            a1r * a2i + a1i * a2r,
            a2r * b1r - a2i * b1i + b2r,
            a2r * b1i + a2i * b1r + b2i)


def s5_mixer(u, h0_re, h0_im, a_re, a_im, log_step, b_re, b_im, c_re, c_im, d_skip, w_glu):
    bsz, t, _ = u.shape
    f32 = jnp.float32
    lb_re, lb_im, coef_re, coef_im = s5_discretize(a_re, a_im, log_step)
    uf = u.astype(f32).reshape(bsz, t, N_SSM_GROUPS, SSM_GROUP)
    bu_re = jnp.einsum("btgp,gnp->btgn", uf, b_re.astype(f32))
    bu_im = jnp.einsum("btgp,gnp->btgn", uf, b_im.astype(f32))
    x_re = coef_re * bu_re - coef_im * bu_im
    x_im = coef_re * bu_im + coef_im * bu_re
    h0r, h0i = h0_re.astype(f32), h0_im.astype(f32)
    x_re = x_re.at[:, 0].add(lb_re * h0r - lb_im * h0i)
    x_im = x_im.at[:, 0].add(lb_re * h0i + lb_im * h0r)
    shape = (1, t, N_SSM_GROUPS, SSM_STATE)
    a_r = jnp.broadcast_to(lb_re, shape)
    a_i = jnp.broadcast_to(lb_im, shape)
    _, _, h_re, h_im = lax.associative_scan(_ssm_combine, (a_r, a_i, x_re, x_im), axis=1)
    y = (jnp.einsum("btgn,gpn->btgp", h_re, c_re.astype(f32))
         - jnp.einsum("btgn,gpn->btgp", h_im, c_im.astype(f32))
         + d_skip.astype(f32).reshape(N_SSM_GROUPS, SSM_GROUP) * uf)
    y = jax.nn.gelu(y.reshape(bsz, t, D_SSM))
    y = y * jax.nn.sigmoid(y @ w_glu.astype(f32))
    return y.astype(u.dtype), h_re[:, -1].astype(h0_re.dtype), h_im[:, -1].astype(h0_im.dtype)


def hybrid_layer(x, pos, past_latent, past_k_rope, past_pos, h0_re, h0_im,
                 g_mix, w_in, g_q_a, w_q_up, g_kv_a, w_kv_up, a_re, a_im, log_step,
                 b_re, b_im, c_re, c_im, d_skip, w_glu, g_attn_out, g_ssm_out, w_out,
                 g_mlp, w_up, w_down):
    bsz, t, _ = x.shape
    xn = rmsnorm(x, g_mix)
    c_q, c_kv, k_pe, u = jnp.split(xn @ w_in, SPLITS, axis=-1)
    q = (rmsnorm(c_q, g_q_a) @ w_q_up).reshape(bsz, t, N_HEADS, QK_NOPE + QK_ROPE)
    q = jnp.concatenate([q[..., :QK_NOPE], apply_rope(q[..., QK_NOPE:], pos)], axis=-1)
    latent = rmsnorm(c_kv, g_kv_a)
    k_rope = apply_rope(k_pe[:, :, None, :], pos)[:, :, 0, :]
    if past_latent is None:
        all_latent, all_k_rope, k_pos = latent, k_rope, pos
    else:
        all_latent = jnp.concatenate([past_latent, latent], axis=1)
        all_k_rope = jnp.concatenate([past_k_rope, k_rope], axis=1)
        k_pos = jnp.concatenate([past_pos, pos], axis=0)
    tk = all_latent.shape[1]
    kv = (all_latent @ w_kv_up).reshape(bsz, tk, N_HEADS, QK_NOPE + V_DIM)
    k = jnp.concatenate(
        [kv[..., :QK_NOPE], jnp.broadcast_to(all_k_rope[:, :, None, :], (bsz, tk, N_HEADS, QK_ROPE))],
        axis=-1)
    v = kv[..., QK_NOPE:]
    attn = mla_attention(q, k, v, pos, k_pos).reshape(bsz, t, D_ATTN)
    ssm, h_re, h_im = s5_mixer(u, h0_re, h0_im, a_re, a_im, log_step,
                               b_re, b_im, c_re, c_im, d_skip, w_glu)
    mixed = jnp.concatenate([rmsnorm(attn, g_attn_out), rmsnorm(ssm, g_ssm_out)], axis=-1) @ w_out
    h = x + mixed
    hn = rmsnorm(h, g_mlp)
    h = h + jnp.square(jax.nn.relu(hn @ w_up)) @ w_down
    return h, latent, k_rope, h_re, h_im


def setup_inputs(seed: int = 0) -> dict:
    key = jax.random.key(seed)
    ks = jax.random.split(key, 32)
    f32 = jnp.float32
    nrm = lambda k, shape, scale: jax.random.normal(k, shape, f32) * scale
    gain = lambda k, shape: 1.0 + 0.01 * jax.random.normal(k, shape, f32)
    L, G, N, P = DEPTH, N_SSM_GROUPS, SSM_STATE, SSM_GROUP
    n_idx = jnp.arange(N, dtype=f32)
    a_re = -0.5 + 0.01 * jax.random.normal(ks[6], (L, G, N), f32)
    a_im = math.pi * n_idx[None, None, :] + 0.01 * jax.random.normal(ks[7], (L, G, N), f32)
    log_step = jax.random.uniform(ks[8], (L, G), f32, math.log(1e-3), math.log(1e-1))
    return {
        "x_prompt": nrm(ks[0], (BATCH, SEQ, D_MODEL), 1.0),
        "x_sample": nrm(ks[1], (DEC_BATCH, DEC_SEQ, D_MODEL), 1.0),
        "cache_kv_latent": nrm(ks[2], (L, DEC_BATCH, PAST_LEN, KV_LORA), 1.0),
        "cache_k_rope": nrm(ks[3], (L, DEC_BATCH, PAST_LEN, QK_ROPE), 1.0),
        "state_ssm_re": nrm(ks[4], (L, DEC_BATCH, G, N), 0.5),
        "state_ssm_im": nrm(ks[5], (L, DEC_BATCH, G, N), 0.5),
        "g_mix": gain(ks[9], (L, D_MODEL)),
        "w_in": nrm(ks[10], (L, D_MODEL, D_IN), D_MODEL ** -0.5),
        "g_q_a": gain(ks[11], (L, Q_LORA)),
        "w_q_up": nrm(ks[12], (L, Q_LORA, N_HEADS * (QK_NOPE + QK_ROPE)), Q_LORA ** -0.5),
        "g_kv_a": gain(ks[13], (L, KV_LORA)),
        "w_kv_up": nrm(ks[14], (L, KV_LORA, N_HEADS * (QK_NOPE + V_DIM)), KV_LORA ** -0.5),
        "a_re": a_re,
        "a_im": a_im,
        "log_step": log_step,
        "b_re": nrm(ks[15], (L, G, N, P), (2 * P) ** -0.5),
        "b_im": nrm(ks[16], (L, G, N, P), (2 * P) ** -0.5),
        "c_re": nrm(ks[17], (L, G, P, N), N ** -0.5),
        "c_im": nrm(ks[18], (L, G, P, N), N ** -0.5),
        "d_skip": nrm(ks[19], (L, D_SSM), 1.0),
        "w_glu": nrm(ks[20], (L, D_SSM, D_SSM), D_SSM ** -0.5),
        "g_attn_out": gain(ks[21], (L, D_ATTN)),
        "g_ssm_out": gain(ks[22], (L, D_SSM)),
        "w_out": nrm(ks[23], (L, D_MIX, D_MODEL), D_MIX ** -0.5),
        "g_mlp": gain(ks[24], (L, D_MODEL)),
        "w_up": nrm(ks[25], (L, D_MODEL, D_FF), D_MODEL ** -0.5),
        "w_down": nrm(ks[26], (L, D_FF, D_MODEL), D_FF ** -0.5),
        "g_final": gain(ks[27], (D_MODEL,)),
    }


def reference(x_prompt, x_sample, cache_kv_latent, cache_k_rope, state_ssm_re, state_ssm_im,
              g_mix, w_in, g_q_a, w_q_up, g_kv_a, w_kv_up, a_re, a_im, log_step,
              b_re, b_im, c_re, c_im, d_skip, w_glu, g_attn_out, g_ssm_out, w_out,
              g_mlp, w_up, w_down, g_final):
    layer_weights = (g_mix, w_in, g_q_a, w_q_up, g_kv_a, w_kv_up, a_re, a_im, log_step,
                     b_re, b_im, c_re, c_im, d_skip, w_glu, g_attn_out, g_ssm_out, w_out,
                     g_mlp, w_up, w_down)

    def trunk(x, pos, past_latent, past_k_rope, past_pos, h0_re, h0_im):
        lat_out, kr_out, hr_out, hi_out = [], [], [], []
        for layer in range(DEPTH):
            w_l = [w[layer] for w in layer_weights]
            x, lat, kr, hr, hi = hybrid_layer(
                x, pos,
                None if past_latent is None else past_latent[layer],
                None if past_k_rope is None else past_k_rope[layer],
                past_pos, h0_re[layer], h0_im[layer], *w_l)
            lat_out.append(lat)
            kr_out.append(kr)
            hr_out.append(hr)
            hi_out.append(hi)
        return (rmsnorm(x, g_final), jnp.stack(lat_out), jnp.stack(kr_out),
                jnp.stack(hr_out), jnp.stack(hi_out))

    bp, tp = x_prompt.shape[0], x_prompt.shape[1]
    pos_p = jnp.arange(tp, dtype=jnp.int32)
    h0p = jnp.zeros((DEPTH, bp, N_SSM_GROUPS, SSM_STATE), state_ssm_re.dtype)
    y_prompt, lat_p, kr_p, hr_p, hi_p = trunk(x_prompt, pos_p, None, None, None, h0p, h0p)

    past = cache_kv_latent.shape[2]
    ts = x_sample.shape[1]
    past_pos = jnp.arange(past, dtype=jnp.int32)
    pos_s = past + jnp.arange(ts, dtype=jnp.int32)
    y_sample, lat_s, kr_s, hr_s, hi_s = trunk(x_sample, pos_s, cache_kv_latent, cache_k_rope,
                                              past_pos, state_ssm_re, state_ssm_im)
    return (y_prompt, y_sample, lat_p, kr_p, hr_p, hi_p, lat_s, kr_s, hr_s, hi_s)
```

```python
import math
import numpy as np
import ml_dtypes
import concourse.bass as bass
import concourse.mybir as mybir
from concourse.bass_utils import run_bass_kernel_spmd

F32 = mybir.dt.float32
BF16 = mybir.dt.bfloat16
I32 = mybir.dt.int32
AF = mybir.ActivationFunctionType
ALU = mybir.AluOpType
AX = mybir.AxisListType

D = 1024
DIN = 1568
QL = 768
KVL = 256
NH = 8
DFF = 4096
EPS = 1e-6
SCALE = 96 ** -0.5
NSEQ_S = 4
DSEQ = 64
TWO_PI = 2.0 * math.pi


class T:
    def __init__(self, ap, name="", share=None):
        self.ap = ap
        self.name = name
        self.d = share.d if share is not None else {"w": None, "r": {}}

    @property
    def w(self):
        return self.d["w"]

    @w.setter
    def w(self, v):
        self.d["w"] = v

    @property
    def r(self):
        return self.d["r"]

    @r.setter
    def r(self, v):
        self.d["r"] = v


class Chan:
    def __init__(self, nc, name):
        self.sem = nc.alloc_semaphore(name)
        self.total = 0
        self.key = name
        self.holder = [0]


class Trk:
    def __init__(self, nc):
        self.nc = nc
        self.E = {"pe": nc.tensor, "act": nc.scalar, "dve": nc.vector, "pool": nc.gpsimd, "sp": nc.sync}
        self.sem = {}
        self.cnt = {}
        self.gen = {}
        for e in ("pe", "act", "dve", "pool"):
            self.gen[e] = 0
            self._newsem(e)
        self.seen = {}
        self.chans = []
        self.ninst = {e: 0 for e in self.E}

    def _newsem(self, e):
        self.sem[e] = self.nc.alloc_semaphore("c_%s_%d" % (e, self.gen[e]))
        self.cnt[e] = 0
        self.gen[e] += 1

    def chan(self, name):
        c = Chan(self.nc, name)
        self.chans.append(c)
        return c

    def _wait(self, eng, ev):
        if ev is None:
            return
        sem, key, val = ev
        if isinstance(val, list):
            val = val[0]
        k = (eng, key)
        if self.seen.get(k, 0) >= val:
            return
        self.E[eng].wait_ge(sem, val)
        self.ninst[eng] += 1
        self.seen[k] = val

    def _deps(self, eng, reads, writes, skip_same=False):
        for t in reads:
            if t.w is not None and not (skip_same and t.w[1][0] == eng):
                self._wait(eng, t.w)
        for t in writes:
            if t.w is not None and not (skip_same and t.w[1][0] == eng):
                self._wait(eng, t.w)
            for ev in t.r.values():
                if not (skip_same and ev[1][0] == eng):
                    self._wait(eng, ev)

    def _done(self, ev, reads, writes):
        for t in reads:
            t.r[ev[1]] = ev
        for t in writes:
            t.w = ev
            t.r = {}

    def op(self, eng, fn, reads=(), writes=()):
        self._deps(eng, reads, writes)
        inst = fn()
        if self.cnt[eng] >= 60000:
            self._newsem(eng)
        self.cnt[eng] += 1
        inst.then_inc(self.sem[eng], 1)
        self.ninst[eng] += 1
        ev = (self.sem[eng], (eng, self.gen[eng]), self.cnt[eng])
        self._done(ev, reads, writes)

    def pe(self, fns, reads=(), writes=()):
        self._deps("pe", reads, writes, skip_same=True)
        inst = None
        for f in fns:
            inst = f()
            self.ninst["pe"] += 1
        if self.cnt["pe"] >= 60000:
            self._newsem("pe")
        self.cnt["pe"] += 1
        inst.then_inc(self.sem["pe"], 1)
        ev = (self.sem["pe"], ("pe", self.gen["pe"]), self.cnt["pe"])
        self._done(ev, reads, writes)

    def dma(self, q, ch, out_ap, in_ap, reads=(), writes=(), batch=False, **kw):
        self._deps(q, reads, writes)
        if not batch:
            if ch.total > 0:
                self._wait(q, (ch.sem, ch.key, ch.total))
            ch.holder = [ch.total]
        inst = self.E[q].dma_start(out=out_ap, in_=in_ap, allow_slow_non_contiguous=True, **kw)
        ch.total += 16
        ch.holder[0] = ch.total
        inst.then_inc(ch.sem, 16)
        self.ninst[q] += 1
        ev = (ch.sem, ch.key, ch.holder)
        self._done(ev, reads, writes)

    def barrier(self):
        for eng in ("pe", "act", "dve", "pool", "sp"):
            self.finish(eng)

    def finish(self, eng="pool"):
        for c in self.chans:
            if c.total > 0:
                self._wait(eng, (c.sem, c.key, c.total))
        for e in ("pe", "act", "dve", "pool"):
            if self.cnt[e] > 0 and e != eng:
                self._wait(eng, (self.sem[e], (e, self.gen[e]), self.cnt[e]))


class Rot:
    def __init__(self, items):
        self.items = items
        self.i = 0

    def next(self):
        t = self.items[self.i % len(self.items)]
        self.i += 1
        return t


def bc(ap, shape):
    return ap.to_broadcast(shape)


def build(SEQ, PAST, stage=99):
    nc = bass.Bass("TRN2", target_bir_lowering=False)
    K = Trk(nc)
    NT_P = SEQ // 512
    NKT_P = SEQ // 128
    NK_S = PAST + 128
    NKT_S = NK_S // 128
    NPT_S = PAST // 512

    def din(name, shape, dt=F32):
        return nc.dram_tensor(name, list(shape), dt, kind="ExternalInput").ap()

    def dout(name, shape):
        return nc.dram_tensor(name, list(shape), F32, kind="ExternalOutput").ap()

    def dscr(name, shape, dt=BF16):
        return nc.dram_tensor(name, list(shape), dt, kind="Internal").ap()

    def sb(name, shape, dt=F32):
        return nc.alloc_sbuf_tensor(name, list(shape), dt).ap()

    xp = din("xp", [SEQ, D]); xs = din("xs", [NSEQ_S * DSEQ, D])
    ckl = din("ckl", [NSEQ_S, PAST, KVL]); ckr = din("ckr", [NSEQ_S, PAST, 32])
    sre = din("sre", [NSEQ_S, 32, 64]); sim = din("sim", [NSEQ_S, 32, 64])
    g_mix = din("g_mix", [1, D]); w_in = din("w_in", [D, DIN]); g_q = din("g_q", [1, QL])
    w_q = din("w_q", [QL, QL]); g_kv = din("g_kv", [1, KVL]); w_kv = din("w_kv", [KVL, 1024])
    a_re = din("a_re", [32, 64]); a_im = din("a_im", [32, 64]); lstep = din("lstep", [1, 32])
    b_re = din("b_re", [32, 64, 16]); b_im = din("b_im", [32, 64, 16])
    c_re = din("c_re", [32, 16, 64]); c_im = din("c_im", [32, 16, 64])
    d_skip = din("d_skip", [1, 512]); w_glu = din("w_glu", [512, 512])
    g_attn = din("g_attn", [1, 512]); g_ssm = din("g_ssm", [1, 512]); w_out = din("w_out", [D, D])
    g_mlp = din("g_mlp", [1, D]); w_up = din("w_up", [D, DFF]); w_down = din("w_down", [DFF, D])
    g_fin = din("g_fin", [1, D])
    cs_p = din("cs_p", [SEQ, 32]); cs_s = din("cs_s", [NSEQ_S * DSEQ, 32])
    ident_in = din("ident", [128, 128])

    yp = dout("yp", [SEQ, D]); ys = dout("ys", [NSEQ_S * DSEQ, D])
    latp = dout("latp", [SEQ, KVL]); krp = dout("krp", [SEQ, 32])
    hrp = dout("hrp", [32, 64]); hip = dout("hip", [32, 64])
    lats = dout("lats", [NSEQ_S * DSEQ, KVL]); krs = dout("krs", [NSEQ_S * DSEQ, 32])
    hrs = dout("hrs", [NSEQ_S, 32, 64]); his = dout("his", [NSEQ_S, 32, 64])

    s_winc = dscr("s_winc", [128, 8, 1056]); s_winu = dscr("s_winu", [128, 8, 512])
    s_wq = dscr("s_wq", [128, 6, 768]); s_wkv = dscr("s_wkv", [128, 2, 1024])
    s_wglu = dscr("s_wglu", [128, 4, 512]); s_wout = dscr("s_wout", [128, 8, 1024])
    s_wup = dscr("s_wup", [8, 128, 8, 512]); s_wdn = dscr("s_wdn", [8, 128, 4, 1024])
    ktc = [dscr("ktc0", [NH, 96, SEQ])] + [dscr("ktc%d" % (s + 1), [NH, 96, NK_S]) for s in range(NSEQ_S)]
    vc = [dscr("vc0", [NH, 128, NKT_P, 65])] + [dscr("vc%d" % (s + 1), [NH, 128, NKT_S, 65]) for s in range(NSEQ_S)]
    kvreg = [[T(None, "kvreg0_%d" % i) for i in range(NT_P)]] + \
            [[T(None, "kvreg%d_%d" % (s + 1, i)) for i in range(NPT_S + 1)] for s in range(NSEQ_S)]
    wscr = T(None, "wscr")

    ps = nc.alloc_psum_tensor("ps", [128, 4096], F32).ap()
    PB = [T(ps[:, 512 * b:512 * (b + 1)], "pb%d" % b) for b in range(8)]

    def pbf(b):
        return ps[:, 512 * b:512 * (b + 1)].bitcast(BF16)

    from contextlib import ExitStack

    def prod(sh):
        n = 1
        for v_ in sh:
            n *= v_
        return n

    def vw(t, dt, shape, off=0):
        a_ = t.ap if dt == F32 else t.ap.bitcast(dt)
        esz = 4 if dt in (F32, I32) else 2
        n = prod(shape)
        a_ = a_[:, off // esz:off // esz + n]
        if len(shape) == 2:
            a_ = a_.rearrange("p (a b) -> p a b", a=shape[0])
        elif len(shape) == 3:
            a_ = a_.rearrange("p (a b c) -> p a b c", a=shape[0], b=shape[1])
        return a_

    def raw(name, nbytes):
        return T(sb(name, [128, nbytes // 4]), name)

    ident_f = T(sb("ident_f", [128, 128]))
    ident_b = T(sb("ident_b", [128, 128], BF16))
    ones_b = T(sb("ones_b", [128, 128], BF16))
    gkv_bc = T(sb("gkv_bc", [128, KVL])); gfin_bc = T(sb("gfin_bc", [128, D]))
    gcols = T(sb("gcols", [128, 40]))
    W1 = T(sb("W1", [128, 4, 2, 8, 128], BF16))
    W2 = T(sb("W2", [128, 16, 2, 8, 32], BF16))
    BD = T(sb("BD", [128, 4, 8, 128], BF16))
    Dd = T(sb("Dd", [128, 4, 128], BF16))
    A8 = T(sb("A8", [128, 16, 2])); B8 = T(sb("B8", [128, 16, 2]))
    Hc = T(sb("Hc", [128, 16, 2])); H0s = T(sb("H0s", [128, NSEQ_S, 16, 2])); h0ch = K.chan("h0")
    cch = K.chan("const")
    cchA = K.chan("ssmA")
    cchB = K.chan("ssmB")
    castch = K.chan("cast")

    def alloc_runtime():
        g = {}
        NSLOT = 5
        g["NSLOT"] = NSLOT
        g["ring"] = [T(sb("ring%d" % i, [128, 4096], BF16)) for i in range(NSLOT)]
        g["ringch"] = [K.chan("ring%d" % i) for i in range(NSLOT)]
        g["xt"] = T(sb("xt", [128, 4, D])); g["xch"] = K.chan("xt"); g["ych"] = K.chan("yst")
        g["cst"] = T(sb("cst", [128, 4, 32])); g["csch"] = K.chan("cst"); g["kvlch"] = K.chan("kvl")
        g["actT"] = T(sb("actT", [128, 8, 512], BF16))
        rx = [raw("rx%d" % i, 2048) for i in range(2)]
        g["xnb"] = Rot([T(vw(r_, BF16, (D,)), share=r_) for r_ in rx])
        g["trl"] = Rot([T(vw(r_, F32, (512,)), share=r_) for r_ in rx])
        g["junk"] = T(sb("junk", [128, D], BF16))
        g["cqn"] = Rot([T(sb("cqn%d" % i, [128, QL], BF16)) for i in range(2)])
        rq = [raw("rq%d" % i, 4096) for i in range(2)]
        g["aT"] = [T(vw(r_, BF16, (4, 512)), share=r_) for r_ in rq]
        g["qTh"] = [T(vw(r_, BF16, (4, 512)), share=r_) for r_ in rq]
        rD = raw("rD", 8192)
        g["cqnT"] = T(vw(rD, BF16, (6, 512)), share=rD)
        g["Ssb"] = T(vw(rD, F32, (16, 2, 64)), share=rD)
        g["kvtok"] = Rot([T(sb("kvtok%d" % i, [128, 352], BF16)) for i in range(2)])
        g["kvout"] = [T(sb("kvout%d" % i, [128, 288])) for i in range(2)]
        g["kvoch"] = [K.chan("kvout%d" % i) for i in range(2)]
        g["latT"] = T(sb("latT", [128, 2, 512], BF16))
        g["qtok"] = Rot([T(sb("qtok%d" % i, [128, 8, 96], BF16)) for i in range(2)])
        g["uT"] = T(sb("uT", [128, 4, 512], BF16))
        g["KTs"] = T(sb("KTs", [128, 8, 512], BF16)); g["ktsch"] = K.chan("kts")
        g["Vs"] = T(sb("Vs", [128, 8, 4, 65], BF16)); g["vsch"] = K.chan("vs")
        rA = raw("rA", 8192)
        g["attn"] = T(vw(rA, F32, (4, 512)), share=rA)
        g["y2"] = T(vw(rA, F32, (4, 512)), share=rA)
        rB = [raw("rB%d" % i, 2048) for i in range(2)]
        g["OTs"] = Rot([T(vw(r_, F32, (512,)), share=r_) for r_ in rB])
        g["tA"] = T(vw(rB[0], F32, (512,)), share=rB[0]); g["tB"] = T(vw(rB[1], F32, (512,)), share=rB[1])
        rC = [raw("rC%d" % i, 4096) for i in range(2)]
        g["Kblk"] = [T(vw(r_, BF16, (2048,)), share=r_) for r_ in rC]
        g["y2b"] = T(vw(rC[0], BF16, (4, 512)), share=rC[0]); g["sqb"] = T(vw(rC[1], BF16, (4, 512)), share=rC[1])
        g["Vblk"] = [T(sb("Vblk%d" % i, [128, 16, 65], BF16)) for i in range(2)]
        g["kbch"] = [K.chan("kb%d" % i) for i in range(2)]
        g["vbch"] = [K.chan("vb%d" % i) for i in range(2)]
        g["PT"] = Rot([T(sb("PT%d" % i, [128, 512], BF16)) for i in range(3)])
        g["stat"] = T(sb("stat", [128, 64]))
        g["rtmp"] = T(sb("rtmp", [128, 8, 64]))
        g["Hbf"] = T(sb("Hbf", [128, 16, 2, 64], BF16))
        g["st1"] = T(sb("st1", [128, 16, 2])); g["st2"] = T(sb("st2", [128, 16, 2]))
        g["hout"] = T(sb("hout", [128, 16, 2])); g["hoch"] = K.chan("hout")
        g["rbc"] = T(sb("rbc", [128, 512]))
        g["qsb"] = T(sb("qsb", [128, 768]))
        return g


    K.dma("sp", cch, ident_f.ap, ident_in, writes=[ident_f], batch=True)
    K.dma("sp", cch, gkv_bc.ap, g_kv.partition_broadcast(128), writes=[gkv_bc], batch=True)
    K.dma("sp", cch, gfin_bc.ap, g_fin.partition_broadcast(128), writes=[gfin_bc], batch=True)
    for (src, off, n) in ((g_mix, 0, 8), (g_mlp, 8, 8), (g_q, 16, 6), (g_attn, 22, 4), (g_ssm, 26, 4), (d_skip, 30, 4)):
        K.dma("sp", cch, gcols.ap[:, off:off + n], src.rearrange("o (k c) -> c (o k)", c=128), writes=[gcols], batch=True)
    K.op("dve", lambda: nc.vector.tensor_copy(out=ident_b.ap, in_=ident_f.ap), [ident_f], [ident_b])
    K.op("pool", lambda: nc.gpsimd.memset(ones_b.ap, 1.0), [], [ones_b])

    wv = w_in.rearrange("(k p) n -> p k n", p=128)
    casts = [
        (s_winc, wv[:, :, 0:1056]), (s_winu, wv[:, :, 1056:1568]),
        (s_wq, w_q.rearrange("(k p) n -> p k n", p=128)),
    ]
    wkvv = w_kv.rearrange("(k p) (h c) -> p k h c", p=128, c=128)
    for k in range(2):
        casts.append((s_wkv[:, k, 0:512].rearrange("p (h c) -> p h c", c=64), wkvv[:, k, :, 0:64]))
        casts.append((s_wkv[:, k, 512:1024].rearrange("p (h c) -> p h c", c=64), wkvv[:, k, :, 64:128]))
    casts.append((s_wglu, w_glu.rearrange("(k p) n -> p k n", p=128)))
    casts.append((s_wout, w_out.rearrange("(k p) n -> p k n", p=128)))
    wupv = w_up.rearrange("(k p) (fc n) -> fc p k n", p=128, n=512)
    wdnv = w_down.rearrange("(fc ft p) n -> fc p ft n", ft=4, p=128)
    for fc in range(8):
        for k in range(0, 8, 4):
            casts.append((s_wup[fc, :, k:k + 4, :], wupv[fc, :, k:k + 4, :]))
        for ft in range(0, 4, 2):
            casts.append((s_wdn[fc, :, ft:ft + 2, :], wdnv[fc, :, ft:ft + 2, :]))
    for (o, i) in casts:
        K.dma("pool", castch, o, i, writes=[wscr], batch=True)

    if stage == 0:
        K.finish("pool")
        return nc, K
    def ssm_setup(es):
        def sb(name, shape, dt=F32):
            return es.enter_context(nc.sbuf_tensor(name, list(shape), dt)).ap()
        Are = T(sb("Are", [128, 16])); Aim = T(sb("Aim", [128, 16])); LS = T(sb("LS", [128, 16]))
        for g2 in range(2):
            K.dma("sp", cchA, Are.ap[64 * g2:64 * g2 + 64, :], a_re.rearrange("(k two) n -> two n k", two=2)[g2], writes=[Are], batch=True)
            K.dma("sp", cchA, Aim.ap[64 * g2:64 * g2 + 64, :], a_im.rearrange("(k two) n -> two n k", two=2)[g2], writes=[Aim], batch=True)
            K.dma("sp", cchA, LS.ap[64 * g2:64 * g2 + 64, :], lstep.rearrange("o (k two) -> two o k", two=2)[g2].partition_broadcast(64), writes=[LS], batch=True)
        BreD = T(sb("BreD", [128, 16, 32])); BimD = T(sb("BimD", [128, 16, 32]))
        CreD = T(sb("CreD", [32, 16, 128])); CimD = T(sb("CimD", [32, 16, 128]))
        for t in (BreD, BimD, CreD, CimD):
            K.op("pool", lambda t=t: nc.gpsimd.memset(t.ap, 0.0), [], [t])
        for g2 in range(2):
            K.dma("sp", cchB, BreD.ap[64 * g2:64 * g2 + 64, :, 16 * g2:16 * g2 + 16], b_re.rearrange("(k two) n q -> two n k q", two=2)[g2], writes=[BreD], batch=True)
            K.dma("sp", cchB, BimD.ap[64 * g2:64 * g2 + 64, :, 16 * g2:16 * g2 + 16], b_im.rearrange("(k two) n q -> two n k q", two=2)[g2], writes=[BimD], batch=True)
            K.dma("sp", cchB, CreD.ap[16 * g2:16 * g2 + 16, :, 64 * g2:64 * g2 + 64], c_re.rearrange("(k two) p n -> two p k n", two=2)[g2], writes=[CreD], batch=True)
            K.dma("sp", cchB, CimD.ap[16 * g2:16 * g2 + 16, :, 64 * g2:64 * g2 + 64], c_im.rearrange("(k two) p n -> two p k n", two=2)[g2], writes=[CimD], batch=True)
        for s in range(NSEQ_S):
            for g2 in range(2):
                K.dma("sp", h0ch, H0s.ap[64 * g2:64 * g2 + 64, s, :, 0], sre[s].rearrange("(k two) n -> two n k", two=2)[g2], writes=[H0s], batch=True)
                K.dma("sp", h0ch, H0s.ap[64 * g2:64 * g2 + 64, s, :, 1], sim[s].rearrange("(k two) n -> two n k", two=2)[g2], writes=[H0s], batch=True)

        def V(name):
            return T(sb(name, [128, 16]))
        dt_ = V("dt_"); mag = V("mag"); th = V("th"); cs = V("cs"); sn = V("sn")
        w1 = V("w1"); w2 = V("w2"); w3 = V("w3"); wi = T(sb("wi", [128, 16], I32))
        K.op("act", lambda: nc.scalar.activation(out=dt_.ap, in_=LS.ap, func=AF.Exp), [LS], [dt_])
        K.op("dve", lambda: nc.vector.tensor_tensor(out=w1.ap, in0=Are.ap, in1=dt_.ap, op=ALU.mult), [Are, dt_], [w1])
        K.op("act", lambda: nc.scalar.activation(out=mag.ap, in_=w1.ap, func=AF.Exp), [w1], [mag])
        K.op("dve", lambda: nc.vector.tensor_tensor(out=th.ap, in0=Aim.ap, in1=dt_.ap, op=ALU.mult), [Aim, dt_], [th])

        def sin_of(dst, shift):
            K.op("dve", lambda: nc.vector.tensor_scalar(out=w1.ap, in0=th.ap, scalar1=shift, scalar2=1.0 / TWO_PI, op0=ALU.add, op1=ALU.mult), [th], [w1])
            K.op("dve", lambda: nc.vector.tensor_copy(out=wi.ap, in_=w1.ap), [w1], [wi])
            K.op("dve", lambda: nc.vector.tensor_copy(out=w2.ap, in_=wi.ap), [wi], [w2])
            K.op("dve", lambda: nc.vector.tensor_scalar(out=w1.ap, in0=th.ap, scalar1=shift, scalar2=None, op0=ALU.add), [th], [w1])
            K.op("dve", lambda: nc.vector.scalar_tensor_tensor(out=w1.ap, in0=w2.ap, scalar=-TWO_PI, in1=w1.ap, op0=ALU.mult, op1=ALU.add), [w2, w1], [w1])
            K.op("dve", lambda: nc.vector.tensor_scalar(out=w2.ap, in0=w1.ap, scalar1=math.pi, scalar2=-TWO_PI, op0=ALU.is_gt, op1=ALU.mult), [w1], [w2])
            K.op("dve", lambda: nc.vector.tensor_tensor(out=w1.ap, in0=w1.ap, in1=w2.ap, op=ALU.add), [w1, w2], [w1])
            K.op("dve", lambda: nc.vector.tensor_scalar(out=w2.ap, in0=w1.ap, scalar1=-math.pi, scalar2=TWO_PI, op0=ALU.is_lt, op1=ALU.mult), [w1], [w2])
            K.op("dve", lambda: nc.vector.tensor_tensor(out=w1.ap, in0=w1.ap, in1=w2.ap, op=ALU.add), [w1, w2], [w1])
            K.op("dve", lambda: nc.vector.tensor_scalar(out=w1.ap, in0=w1.ap, scalar1=3.1415925, scalar2=-3.1415925, op0=ALU.min, op1=ALU.max), [w1], [w1])
            K.op("act", lambda: nc.scalar.activation(out=dst.ap, in_=w1.ap, func=AF.Sin), [w1], [dst])
        sin_of(sn, 0.0)
        sin_of(cs, math.pi / 2)
        LP = T(sb("LP", [128, 9, 2, 16]))
        K.op("pool", lambda: nc.gpsimd.memset(LP.ap[:, 0, 0, :], 1.0), [], [LP])
        K.op("pool", lambda: nc.gpsimd.memset(LP.ap[:, 0, 1, :], 0.0), [], [LP])
        K.op("dve", lambda: nc.vector.tensor_tensor(out=LP.ap[:, 1, 0, :], in0=mag.ap, in1=cs.ap, op=ALU.mult), [mag, cs], [LP])
        K.op("dve", lambda: nc.vector.tensor_tensor(out=LP.ap[:, 1, 1, :], in0=mag.ap, in1=sn.ap, op=ALU.mult), [mag, sn], [LP])
        for k in range(2, 9):
            pr, pi_ = LP.ap[:, k - 1, 0, :], LP.ap[:, k - 1, 1, :]
            lr, li = LP.ap[:, 1, 0, :], LP.ap[:, 1, 1, :]
            K.op("dve", lambda: nc.vector.tensor_tensor(out=w1.ap, in0=pr, in1=lr, op=ALU.mult), [LP], [w1])
            K.op("dve", lambda: nc.vector.tensor_tensor(out=w2.ap, in0=pi_, in1=li, op=ALU.mult), [LP], [w2])
            K.op("dve", lambda: nc.vector.tensor_tensor(out=LP.ap[:, k, 0, :], in0=w1.ap, in1=w2.ap, op=ALU.subtract), [w1, w2], [LP])
            K.op("dve", lambda: nc.vector.tensor_tensor(out=w1.ap, in0=pr, in1=li, op=ALU.mult), [LP], [w1])
            K.op("dve", lambda: nc.vector.tensor_tensor(out=w2.ap, in0=pi_, in1=lr, op=ALU.mult), [LP], [w2])
            K.op("dve", lambda: nc.vector.tensor_tensor(out=LP.ap[:, k, 1, :], in0=w1.ap, in1=w2.ap, op=ALU.add), [w1, w2], [LP])
        K.op("dve", lambda: nc.vector.tensor_copy(out=A8.ap[:, :, 0], in_=LP.ap[:, 8, 0, :]), [LP], [A8])
        K.op("dve", lambda: nc.vector.tensor_copy(out=A8.ap[:, :, 1], in_=LP.ap[:, 8, 1, :]), [LP], [A8])
        K.op("dve", lambda: nc.vector.tensor_scalar(out=B8.ap[:, :, 0], in0=LP.ap[:, 8, 1, :], scalar1=-1.0, scalar2=None, op0=ALU.mult), [LP], [B8])
        K.op("dve", lambda: nc.vector.tensor_copy(out=B8.ap[:, :, 1], in_=LP.ap[:, 8, 0, :]), [LP], [B8])
        cre = V("cre"); cim = V("cim"); den = V("den"); nr = V("nr")
        K.op("dve", lambda: nc.vector.tensor_scalar(out=nr.ap, in0=LP.ap[:, 1, 0, :], scalar1=-1.0, scalar2=None, op0=ALU.add), [LP], [nr])
        K.op("dve", lambda: nc.vector.tensor_tensor(out=w1.ap, in0=Are.ap, in1=Are.ap, op=ALU.mult), [Are], [w1])
        K.op("dve", lambda: nc.vector.tensor_tensor(out=w2.ap, in0=Aim.ap, in1=Aim.ap, op=ALU.mult), [Aim], [w2])
        K.op("dve", lambda: nc.vector.tensor_tensor(out=den.ap, in0=w1.ap, in1=w2.ap, op=ALU.add), [w1, w2], [den])
        K.op("dve", lambda: nc.vector.reciprocal(out=den.ap, in_=den.ap), [den], [den])
        K.op("dve", lambda: nc.vector.tensor_tensor(out=w1.ap, in0=nr.ap, in1=Are.ap, op=ALU.mult), [nr, Are], [w1])
        K.op("dve", lambda: nc.vector.tensor_tensor(out=w2.ap, in0=LP.ap[:, 1, 1, :], in1=Aim.ap, op=ALU.mult), [LP, Aim], [w2])
        K.op("dve", lambda: nc.vector.tensor_tensor(out=w1.ap, in0=w1.ap, in1=w2.ap, op=ALU.add), [w1, w2], [w1])
        K.op("dve", lambda: nc.vector.tensor_tensor(out=cre.ap, in0=w1.ap, in1=den.ap, op=ALU.mult), [w1, den], [cre])
        K.op("dve", lambda: nc.vector.tensor_tensor(out=w1.ap, in0=LP.ap[:, 1, 1, :], in1=Are.ap, op=ALU.mult), [LP, Are], [w1])
        K.op("dve", lambda: nc.vector.tensor_tensor(out=w2.ap, in0=nr.ap, in1=Aim.ap, op=ALU.mult), [nr, Aim], [w2])
        K.op("dve", lambda: nc.vector.tensor_tensor(out=w1.ap, in0=w1.ap, in1=w2.ap, op=ALU.subtract), [w1, w2], [w1])
        K.op("dve", lambda: nc.vector.tensor_tensor(out=cim.ap, in0=w1.ap, in1=den.ap, op=ALU.mult), [w1, den], [cim])

        def B3(t):
            return t.unsqueeze(2).to_broadcast([128, 16, 32])
        m1 = T(sb("m1", [128, 16, 32])); m2 = T(sb("m2", [128, 16, 32]))
        BbR = T(sb("BbR", [128, 16, 32])); BbI = T(sb("BbI", [128, 16, 32]))

        def cmul(dre, dim, are, aim, bre, bim, rd, wr):
            if dre is not None:
                K.op("dve", lambda: nc.vector.tensor_tensor(out=m1.ap, in0=bre, in1=B3(are), op=ALU.mult), rd, [m1])
                K.op("dve", lambda: nc.vector.tensor_tensor(out=m2.ap, in0=bim, in1=B3(aim), op=ALU.mult), rd, [m2])
                K.op("dve", lambda: nc.vector.tensor_tensor(out=dre, in0=m1.ap, in1=m2.ap, op=ALU.subtract), [m1, m2], wr)
            if dim is not None:
                K.op("dve", lambda: nc.vector.tensor_tensor(out=m1.ap, in0=bim, in1=B3(are), op=ALU.mult), rd, [m1])
                K.op("dve", lambda: nc.vector.tensor_tensor(out=m2.ap, in0=bre, in1=B3(aim), op=ALU.mult), rd, [m2])
                K.op("dve", lambda: nc.vector.tensor_tensor(out=dim, in0=m1.ap, in1=m2.ap, op=ALU.add), [m1, m2], wr)
        cmul(BbR.ap, BbI.ap, cre.ap, cim.ap, BreD.ap, BimD.ap, [cre, cim, BreD, BimD], [BbR, BbI])
        GR = T(sb("GR", [128, 16, 32])); GI = T(sb("GI", [128, 16, 32]))
        for i in range(8):
            cmul(GR.ap, GI.ap, LP.ap[:, 7 - i, 0, :], LP.ap[:, 7 - i, 1, :], BbR.ap, BbI.ap, [LP, BbR, BbI], [GR, GI])
            for reim, G in ((0, GR), (1, GI)):
                for r in range(4):
                    bank = (reim * 4 + r)
                    fns = []
                    for kk in range(4):
                        kp = 4 * kk + r
                        fns.append(lambda kk=kk, kp=kp, G=G, bank=bank: nc.tensor.transpose(
                            out=PB[bank].ap[0:32, 128 * kk:128 * kk + 128], in_=G.ap[:, kp, :], identity=ident_f.ap))
                    K.pe(fns, [G, ident_f], [PB[bank]])
                    K.op("act", lambda r=r, bank=bank, reim=reim, i=i: nc.scalar.activation(
                        out=W1.ap[32 * r:32 * r + 32, :, reim, i, :],
                        in_=PB[bank].ap[0:32, :].rearrange("p (a b) -> p a b", a=4), func=AF.Copy), [PB[bank]], [W1])
        CTR = T(sb("CTR", [128, 16, 32])); CTI = T(sb("CTI", [128, 16, 32]))
        for (src, dst, bank) in ((CreD, CTR, 0), (CimD, CTI, 1)):
            fns = [lambda kp=kp, src=src, bank=bank: nc.tensor.transpose(out=PB[bank].ap[:, 32 * kp:32 * kp + 32], in_=src.ap[:, kp, :], identity=ident_f.ap[0:32, 0:32]) for kp in range(16)]
            K.pe(fns, [src, ident_f], [PB[bank]])
            K.op("act", lambda dst=dst, bank=bank: nc.scalar.activation(out=dst.ap, in_=PB[bank].ap.rearrange("p (a b) -> p a b", a=16), func=AF.Copy), [PB[bank]], [dst])
        K.op("pool", lambda: nc.gpsimd.memset(BD.ap, 0.0), [], [BD])
        CLR = T(sb("CLR", [128, 16, 32])); CLI = T(sb("CLI", [128, 16, 32])); NCLI = T(sb("NCLI", [128, 16, 32]))
        for kpow in range(9):
            cmul(CLR.ap, CLI.ap, LP.ap[:, kpow, 0, :], LP.ap[:, kpow, 1, :], CTR.ap, CTI.ap, [LP, CTR, CTI], [CLR, CLI])
            K.op("dve", lambda: nc.vector.tensor_scalar(out=NCLI.ap, in0=CLI.ap, scalar1=-1.0, scalar2=None, op0=ALU.mult), [CLI], [NCLI])
            if kpow >= 1:
                j = kpow - 1
                K.op("act", lambda j=j: nc.scalar.activation(out=W2.ap[:, :, 0, j, :], in_=CLR.ap, func=AF.Copy), [CLR], [W2])
                K.op("act", lambda j=j: nc.scalar.activation(out=W2.ap[:, :, 1, j, :], in_=NCLI.ap, func=AF.Copy), [NCLI], [W2])
            if kpow <= 7:
                tau = kpow
                for r in range(4):
                    for kk in range(4):
                        kp = 4 * kk + r
                        col = (kk * 8 + tau) * 32
                        bank = 2 * r + col // 512
                        c0 = col % 512
                        fns = [
                            lambda kp=kp, bank=bank, c0=c0: nc.tensor.matmul(PB[bank].ap[0:32, c0:c0 + 32], BbR.ap[:, kp, :], CLR.ap[:, kp, :], start=True, stop=False),
                            lambda kp=kp, bank=bank, c0=c0: nc.tensor.matmul(PB[bank].ap[0:32, c0:c0 + 32], BbI.ap[:, kp, :], NCLI.ap[:, kp, :], start=False, stop=True),
                        ]
                        K.pe(fns, [BbR, BbI, CLR, NCLI], [PB[bank]])
        for r in range(4):
            for hb in range(2):
                bank = 2 * r + hb
                K.op("act", lambda r=r, hb=hb, bank=bank: nc.scalar.activation(
                    out=BD.ap[32 * r:32 * r + 32, 2 * hb:2 * hb + 2, :, 32 * r:32 * r + 32],
                    in_=PB[bank].ap[0:32, :].rearrange("p (a t c) -> p a t c", a=2, t=8), func=AF.Copy), [PB[bank]], [BD])
        for kk in range(4):
            K.op("dve", lambda kk=kk: nc.vector.tensor_scalar(out=Dd.ap[:, kk, :], in0=ident_f.ap, scalar1=gcols.ap[:, 30 + kk:31 + kk], scalar2=None, op0=ALU.mult), [ident_f, gcols], [Dd])

    with ExitStack() as es_:
        ssm_setup(es_)
        K.barrier()
    if stage == 1:
        K.finish("pool")
        return nc, K
    G = alloc_runtime()
    NSLOT = G["NSLOT"]; ring = G["ring"]; ringch = G["ringch"]; xt = G["xt"]; xch = G["xch"]; ych = G["ych"]; cst = G["cst"]; csch = G["csch"]; kvlch = G["kvlch"]
    actT = G["actT"]; xnb = G["xnb"]; trl = G["trl"]; junk = G["junk"]; cqn = G["cqn"]; aT = G["aT"]; qTh = G["qTh"]; cqnT = G["cqnT"]; Ssb = G["Ssb"]
    kvtok = G["kvtok"]; kvout = G["kvout"]; kvoch = G["kvoch"]; latT = G["latT"]; qtok = G["qtok"]; uT = G["uT"]; KTs = G["KTs"]; ktsch = G["ktsch"]
    Vs = G["Vs"]; vsch = G["vsch"]; attn = G["attn"]; y2 = G["y2"]; OTs = G["OTs"]; tA = G["tA"]; tB = G["tB"]; Kblk = G["Kblk"]; y2b = G["y2b"]; sqb = G["sqb"]
    Vblk = G["Vblk"]; kbch = G["kbch"]; vbch = G["vbch"]; PT = G["PT"]; stat = G["stat"]; rtmp = G["rtmp"]; Hbf = G["Hbf"]; st1 = G["st1"]; st2 = G["st2"]
    hout = G["hout"]; hoch = G["hoch"]; rbc = G["rbc"]; qsb = G["qsb"]
    NKB = 2
    aTr = Rot(aT)
    kvo_i = [0]
    K.op("pool", lambda: nc.gpsimd.memset(Vs.ap, 1.0), [], [Vs])

    class Ring:
        def __init__(self):
            self.plan = []
            self.issued = 0
            self.got = 0

        def add(self, items):
            self.plan.extend(items)

        def _issue(self, n):
            name, src, shape = self.plan[n]
            slot = n % NSLOT
            dst = ring[slot].ap
            ne = 1
            for s_ in shape:
                ne *= s_
            d = dst[:, 0:ne]
            if len(shape) == 2:
                d = d.rearrange("p (a b) -> p a b", a=shape[0])
            K.dma("sp", ringch[slot], d, src, reads=[wscr], writes=[ring[slot]])

        def get(self, *names):
            n0 = self.got
            assert len(names) <= NSLOT - 1
            for i_, nm in enumerate(names):
                assert self.plan[n0 + i_][0] == nm, (self.plan[n0 + i_][0], nm)
            while self.issued < min(len(self.plan), n0 + NSLOT):
                self._issue(self.issued)
                self.issued += 1
            self.got += len(names)
            outs = []
            for i_ in range(len(names)):
                n = n0 + i_
                shape = self.plan[n][2]
                ne = prod(shape)
                v = ring[n % NSLOT].ap[:, 0:ne]
                if len(shape) == 2:
                    v = v.rearrange("p (a b) -> p a b", a=shape[0])
                outs.append((ring[n % NSLOT], v))
            return outs

    R = Ring()
    plan_tok = [("winc0", s_winc[:, 0:3, :], (3, 1056)), ("winc1", s_winc[:, 3:6, :], (3, 1056)), ("winc2", s_winc[:, 6:8, :], (2, 1056)),
                ("wq0", s_wq[:, 0:3, :], (3, 768)), ("wq1", s_wq[:, 3:6, :], (3, 768)),
                ("wkv", s_wkv, (2, 1024)), ("winu", s_winu, (8, 512)), ("wglu", s_wglu, (4, 512)),
                ("wout0", s_wout[:, 0:4, :], (4, 1024)), ("wout1", s_wout[:, 4:8, :], (4, 1024))]
    for fc in range(8):
        plan_tok.append(("wup%d" % fc, s_wup[fc], (8, 512)))
        plan_tok.append(("wdn%d" % fc, s_wdn[fc], (4, 1024)))
    plan_kv = [("wkv", s_wkv, (2, 1024))]

    def rms_stats(src_ap, reads, n, col):
        ss = stat.ap[:, col:col + 1]
        K.op("act", lambda: nc.scalar.activation(out=junk.ap[:, 0:n], in_=src_ap, func=AF.Square, accum_out=ss), reads, [junk, stat])
        K.op("act", lambda: nc.scalar.activation(out=ss, in_=ss, func=AF.Sqrt, scale=1.0 / n, bias=EPS), [stat], [stat])
        K.op("dve", lambda: nc.vector.reciprocal(out=ss, in_=ss), [stat], [stat])
        return ss

    def transposes_to(dstT, dst_kslice, src_tile, nk, sub, gcol0):
        fns = [lambda k=k: nc.tensor.transpose(out=pbf(3)[:, 128 * k:128 * k + 128], in_=src_tile.ap[:, 128 * k:128 * k + 128], identity=ident_b.ap) for k in range(nk)]
        K.pe(fns, [src_tile, ident_b], [PB[3]])
        for k in range(nk):
            K.op("dve", lambda k=k: nc.vector.tensor_scalar(out=dstT.ap[:, dst_kslice + k, 128 * sub:128 * sub + 128], in0=pbf(3)[:, 128 * k:128 * k + 128],
                                                           scalar1=gcols.ap[:, gcol0 + k:gcol0 + k + 1], scalar2=None, op0=ALU.mult), [PB[3], gcols], [dstT])

    def kv_from_tok(kvt, sub, ncols_total):
        fns = [lambda k=k: nc.tensor.transpose(out=pbf(3)[:, 128 * k:128 * k + 128], in_=kvt.ap[:, 128 * k:128 * k + 128], identity=ident_b.ap) for k in range(2)]
        fns.append(lambda: nc.tensor.transpose(out=pbf(3)[0:96, 256:384], in_=kvt.ap[:, 256:352], identity=ident_b.ap))
        K.pe(fns, [kvt, ident_b], [PB[3]])
        K.op("dve", lambda: nc.vector.tensor_copy(out=latT.ap[:, :, 128 * sub:128 * sub + 128], in_=pbf(3)[:, 0:256].rearrange("p (a b) -> p a b", a=2)), [PB[3]], [latT])
        K.op("dve", lambda: nc.vector.tensor_copy(out=KTs.ap[64:96, :, 128 * sub:128 * sub + 128],
                                                 in_=pbf(3)[64:96, 256:384].unsqueeze(1).to_broadcast([32, 8, 128])), [PB[3]], [KTs])

    def kv_build(wkv_t, wkv_v, nsub):
        N = 128 * nsub
        for hp in range(4):
            b = 4 + hp % 2
            fns = [lambda k=k, hp=hp, b=b: nc.tensor.matmul(PB[b].ap[:, 0:N], wkv_v[:, k, 128 * hp:128 * hp + 128], latT.ap[:, k, 0:N], start=(k == 0), stop=(k == 1)) for k in range(2)]
            K.pe(fns, [wkv_t, latT], [PB[b]])
            K.op("dve", lambda hp=hp, b=b: nc.vector.tensor_copy(out=KTs.ap[0:64, 2 * hp, 0:N], in_=PB[b].ap[0:64, 0:N]), [PB[b]], [KTs])
            K.op("act", lambda hp=hp, b=b: nc.scalar.activation(out=KTs.ap[0:64, 2 * hp + 1, 0:N], in_=PB[b].ap[64:128, 0:N], func=AF.Copy), [PB[b]], [KTs])
        for sub in range(nsub):
            b = 6 + sub % 2
            fns = [lambda k=k, sub=sub, b=b: nc.tensor.matmul(PB[b].ap[:, :], latT.ap[:, k, 128 * sub:128 * sub + 128], wkv_v[:, k, 512:1024], start=(k == 0), stop=(k == 1)) for k in range(2)]
            K.pe(fns, [wkv_t, latT], [PB[b]])
            K.op("dve", lambda sub=sub, b=b: nc.vector.tensor_copy(out=Vs.ap[:, :, sub, 0:64], in_=PB[b].ap.rearrange("p (h c) -> p h c", c=64)), [PB[b]], [Vs])

    def attention(ci, n_full_kt, qc0, nq, diag_i, half_last, dsts):
        nkt = n_full_kt + (1 if half_last else 0)
        nblk = (nkt + 15) // 16
        tot_keys = n_full_kt * 128 + (64 if half_last else 0)
        for h in range(NH):
            ob = 6 + h % 2
            qh = qTh[h // 4]
            hl = h % 4
            first = True
            for blk in range(nblk):
                kt0 = blk * 16
                nkb = min(16, nkt - kt0)
                slot = actr[0] % NKB
                actr[0] += 1
                nkeys = min(128 * nkb, tot_keys - 128 * kt0)
                regs = kvreg[ci][(kt0 * 128) // 512:(kt0 * 128 + nkeys + 511) // 512]
                K.dma("pool", kbch[slot], Kblk[slot].ap[0:96, 0:nkeys], ktc[ci][h, :, 128 * kt0:128 * kt0 + nkeys], reads=regs, writes=[Kblk[slot]])
                nvp = 128 if nkeys >= 128 else 64
                K.dma("pool", vbch[slot], Vblk[slot].ap[0:nvp, 0:nkb, :], vc[ci][h, 0:nvp, kt0:kt0 + nkb, :], reads=regs, writes=[Vblk[slot]])
                for kl in range(nkb):
                    kt = kt0 + kl
                    kp = 64 if (half_last and kt == nkt - 1) else 128
                    n0 = 0
                    isdiag = diag_i is not None and kt >= 4 * diag_i
                    if isdiag:
                        n0 = 128 * (kt - 4 * diag_i)
                    N = nq - n0
                    sbk = 4 + actr[1] % 2
                    actr[1] += 1
                    K.pe([lambda slot=slot, kl=kl, kp=kp, n0=n0, N=N, sbk=sbk, hl=hl, qh=qh: nc.tensor.matmul(
                        PB[sbk].ap[0:kp, 0:N], Kblk[slot].ap[0:96, 128 * kl:128 * kl + kp], qh.ap[0:96, hl, qc0 + n0:qc0 + n0 + N], start=True, stop=True)],
                        [Kblk[slot], qh], [PB[sbk]])
                    pt = PT.next()
                    K.op("act", lambda pt=pt, kp=kp, N=N, sbk=sbk: nc.scalar.activation(out=pt.ap[0:kp, 0:N], in_=PB[sbk].ap[0:kp, 0:N], func=AF.Exp, scale=SCALE), [PB[sbk]], [pt])
                    if isdiag:
                        K.op("pool", lambda pt=pt: nc.gpsimd.memset(pt.ap[64:128, 0:64], 0.0), [], [pt])
                    last = (kt == nkt - 1)
                    K.pe([lambda slot=slot, kl=kl, kp=kp, n0=n0, N=N, pt=pt, ob=ob, first=first, last=last: nc.tensor.matmul(
                        PB[ob].ap[0:65, n0:n0 + N], Vblk[slot].ap[0:kp, kl, :], pt.ap[0:kp, 0:N], start=first, stop=True, skip_group_check=(not first))],
                        [Vblk[slot], pt], [PB[ob]])
                    first = False
            ot = OTs.next()
            K.op("dve", lambda ot=ot, ob=ob: nc.vector.tensor_copy(out=ot.ap[0:64, 0:nq], in_=PB[ob].ap[0:64, 0:nq]), [PB[ob]], [ot])
            K.op("dve", lambda ot=ot, ob=ob: nc.vector.tensor_copy(out=ot.ap[64:65, 0:nq], in_=PB[ob].ap[64:65, 0:nq]), [PB[ob]], [ot])
            c0 = 0
            for (po, nqq, sub) in dsts:
                K.pe([lambda ot=ot, c0=c0, nqq=nqq: nc.tensor.transpose(out=PB[3].ap[0:nqq, 0:65], in_=ot.ap[0:65, c0:c0 + nqq], identity=ident_f.ap[0:65, 0:65])], [ot, ident_f], [PB[3]])
                rc = stat.ap[0:nqq, 32 + (actr[2] % 16):33 + (actr[2] % 16)]
                actr[2] += 1
                K.op("dve", lambda rc=rc, nqq=nqq: nc.vector.reciprocal(out=rc, in_=PB[3].ap[0:nqq, 64:65]), [PB[3]], [stat])
                K.op("dve", lambda rc=rc, po=po, nqq=nqq, sub=sub, h=h: nc.vector.tensor_scalar(out=attn.ap[po:po + nqq, sub, 64 * h:64 * h + 64], in0=PB[3].ap[0:nqq, 0:64], scalar1=rc, scalar2=None, op0=ALU.mult), [PB[3], stat], [attn])
                c0 += nqq
    actr = [0, 0, 0]

    class Stop(Exception):
        pass

    def chk(n):
        if stage == n:
            raise Stop()

    def token_tile(kind, ti):
        if kind == "p":
            nsub, x_src, cs_src = 4, xp[512 * ti:512 * ti + 512, :], cs_p[512 * ti:512 * ti + 512, :]
            y_dst, lat_dst, kr_dst = yp[512 * ti:512 * ti + 512, :], latp[512 * ti:512 * ti + 512, :], krp[512 * ti:512 * ti + 512, :]
        else:
            nsub, x_src, cs_src = 2, xs, cs_s
            y_dst, lat_dst, kr_dst = ys, lats, krs
        TT = 128 * nsub
        NC = TT // 8
        K.dma("sp", xch, xt.ap[:, 0:nsub, :], x_src.rearrange("(s p) d -> p s d", p=128), writes=[xt])
        K.dma("sp", csch, cst.ap[:, 0:nsub, :], cs_src.rearrange("(s p) d -> p s d", p=128), writes=[cst])
        for sub in range(nsub):
            r = rms_stats(xt.ap[:, sub, :], [xt], D, sub)
            xb = xnb.next()
            K.op("dve", lambda sub=sub, xb=xb, r=r: nc.vector.tensor_scalar(out=xb.ap, in0=xt.ap[:, sub, :], scalar1=r, scalar2=None, op0=ALU.mult), [xt, stat], [xb])
            transposes_to(actT, 0, xb, 8, sub, 0)
        chk(2)
        wc = R.get("winc0", "winc1", "winc2")
        for sub in range(nsub):
            fns = []
            for k in range(8):
                wvw = wc[k // 3][1]
                kl = k % 3
                for (c0, c1) in ((0, 512), (512, 1024), (1024, 1056)):
                    fns.append(lambda k=k, kl=kl, wvw=wvw, c0=c0, c1=c1, sub=sub: nc.tensor.matmul(
                        ps[:, c0:c1], actT.ap[:, k, 128 * sub:128 * sub + 128], wvw[:, kl, c0:c1], start=(k == 0), stop=(k == 7)))
            K.pe(fns, [actT, wc[0][0], wc[1][0], wc[2][0]], [PB[0], PB[1], PB[2]])
            r = rms_stats(ps[:, 0:QL], [PB[0], PB[1]], QL, 8 + sub)
            cq = cqn.next()
            K.op("dve", lambda cq=cq, r=r: nc.vector.tensor_scalar(out=cq.ap, in0=ps[:, 0:QL], scalar1=r, scalar2=None, op0=ALU.mult), [PB[0], PB[1], stat], [cq])
            r2 = rms_stats(ps[:, QL:QL + KVL], [PB[1]], KVL, 12 + sub)
            ko = kvout[kvo_i[0] % 2]; koc = kvoch[kvo_i[0] % 2]; kvo_i[0] += 1
            kvt = kvtok.next()
            K.op("dve", lambda ko=ko, r2=r2: nc.vector.scalar_tensor_tensor(out=ko.ap[:, 0:KVL], in0=ps[:, QL:QL + KVL], scalar=r2, in1=gkv_bc.ap, op0=ALU.mult, op1=ALU.mult), [PB[1], stat, gkv_bc], [ko])
            x1, x2 = ps[:, 1024:1040], ps[:, 1040:1056]
            cs_, sn_ = cst.ap[:, sub, 0:16], cst.ap[:, sub, 16:32]
            rt = rtmp.ap[:, 0, :]
            K.op("dve", lambda: nc.vector.tensor_tensor(out=rt[:, 0:16], in0=x1, in1=cs_, op=ALU.mult), [PB[2], cst], [rtmp])
            K.op("dve", lambda: nc.vector.tensor_tensor(out=rt[:, 16:32], in0=x2, in1=sn_, op=ALU.mult), [PB[2], cst], [rtmp])
            K.op("dve", lambda: nc.vector.tensor_tensor(out=rt[:, 32:48], in0=x1, in1=sn_, op=ALU.mult), [PB[2], cst], [rtmp])
            K.op("dve", lambda: nc.vector.tensor_tensor(out=rt[:, 48:64], in0=x2, in1=cs_, op=ALU.mult), [PB[2], cst], [rtmp])
            K.op("dve", lambda ko=ko: nc.vector.tensor_tensor(out=ko.ap[:, 256:272], in0=rt[:, 0:16], in1=rt[:, 16:32], op=ALU.subtract), [rtmp], [ko])
            K.op("dve", lambda ko=ko: nc.vector.tensor_tensor(out=ko.ap[:, 272:288], in0=rt[:, 32:48], in1=rt[:, 48:64], op=ALU.add), [rtmp], [ko])
            K.op("pool", lambda kvt=kvt, ko=ko: nc.gpsimd.tensor_copy(out=kvt.ap[:, 0:256], in_=ko.ap[:, 0:256]), [ko], [kvt])
            K.op("pool", lambda kvt=kvt, ko=ko: nc.gpsimd.tensor_copy(out=kvt.ap[:, 320:352], in_=ko.ap[:, 256:288]), [ko], [kvt])
            K.op("pool", lambda kvt=kvt: nc.gpsimd.memset(kvt.ap[:, 256:320], 0.0), [], [kvt])
            K.dma("pool", koc, lat_dst[128 * sub:128 * sub + 128, :], ko.ap[:, 0:256], reads=[ko])
            K.dma("pool", koc, kr_dst[128 * sub:128 * sub + 128, :], ko.ap[:, 256:288], reads=[ko], batch=True)
            kv_from_tok(kvt, sub, TT)
            transposes_to(cqnT, 0, cq, 6, sub, 16)
        chk(3)
        wqs = R.get("wq0", "wq1")
        for sub in range(nsub):
            fns = []
            for k in range(6):
                wvw = wqs[k // 3][1]
                kl = k % 3
                for (c0, c1) in ((0, 512), (512, 768)):
                    fns.append(lambda k=k, kl=kl, wvw=wvw, c0=c0, c1=c1, sub=sub: nc.tensor.matmul(
                        ps[:, c0:c1], cqnT.ap[:, k, 128 * sub:128 * sub + 128], wvw[:, kl, c0:c1], start=(k == 0), stop=(k == 5)))
            K.pe(fns, [cqnT, wqs[0][0], wqs[1][0]], [PB[0], PB[1]])
            K.op("act", lambda: nc.scalar.activation(out=qsb.ap[:, 0:512], in_=ps[:, 0:512], func=AF.Copy), [PB[0]], [qsb])
            K.op("dve", lambda: nc.vector.tensor_copy(out=qsb.ap[:, 512:768], in_=ps[:, 512:768]), [PB[1]], [qsb])
            qv = qsb.ap.rearrange("p (h c) -> p h c", c=96)
            qk = qtok.next()
            K.op("pool", lambda qk=qk: nc.gpsimd.tensor_copy(out=qk.ap[:, :, 0:64], in_=qv[:, :, 0:64]), [qsb], [qk])
            cs8 = cst.ap[:, sub, 0:16].unsqueeze(1).to_broadcast([128, 8, 16])
            sn8 = cst.ap[:, sub, 16:32].unsqueeze(1).to_broadcast([128, 8, 16])
            q1, q2 = qv[:, :, 64:80], qv[:, :, 80:96]
            rt4 = rtmp.ap.rearrange("p h (a c) -> p h a c", a=4)
            K.op("dve", lambda: nc.vector.tensor_tensor(out=rt4[:, :, 0, :], in0=q1, in1=cs8, op=ALU.mult), [qsb, cst], [rtmp])
            K.op("dve", lambda: nc.vector.tensor_tensor(out=rt4[:, :, 1, :], in0=q2, in1=sn8, op=ALU.mult), [qsb, cst], [rtmp])
            K.op("dve", lambda: nc.vector.tensor_tensor(out=rt4[:, :, 2, :], in0=q1, in1=sn8, op=ALU.mult), [qsb, cst], [rtmp])
            K.op("dve", lambda: nc.vector.tensor_tensor(out=rt4[:, :, 3, :], in0=q2, in1=cs8, op=ALU.mult), [qsb, cst], [rtmp])
            K.op("dve", lambda qk=qk: nc.vector.tensor_tensor(out=qk.ap[:, :, 64:80], in0=rt4[:, :, 0, :], in1=rt4[:, :, 1, :], op=ALU.subtract), [rtmp], [qk])
            K.op("dve", lambda qk=qk: nc.vector.tensor_tensor(out=qk.ap[:, :, 80:96], in0=rt4[:, :, 2, :], in1=rt4[:, :, 3, :], op=ALU.add), [rtmp], [qk])
            fns = [lambda h=h, qk=qk: nc.tensor.transpose(out=pbf(3)[0:96, 128 * h:128 * h + 128], in_=qk.ap[:, h, :], identity=ident_b.ap) for h in range(NH)]
            K.pe(fns, [qk, ident_b], [PB[3]])
            for hh in range(2):
                for (p0, p1) in ((0, 64), (64, 96)):
                    K.op("act", lambda sub=sub, hh=hh, p0=p0, p1=p1: nc.scalar.activation(out=qTh[hh].ap[p0:p1, :, 128 * sub:128 * sub + 128],
                                                                                           in_=pbf(3)[p0:p1, 512 * hh:512 * hh + 512].rearrange("p (h c) -> p h c", h=4), func=AF.Copy), [PB[3]], [qTh[hh]])
        chk(4)
        (wkv_t, wkv_v), = R.get("wkv")
        kv_build(wkv_t, wkv_v, nsub)
        if kind == "p":
            reg = kvreg[0][ti]
            K.dma("pool", ktsch, ktc[0][:, :, 512 * ti:512 * ti + 512].rearrange("h r n -> r h n"), KTs.ap[0:96, :, :], reads=[KTs], writes=[reg])
            K.dma("pool", vsch, vc[0][:, :, 4 * ti:4 * ti + 4, :].rearrange("h p s c -> p h s c"), Vs.ap[:, :, :, :], reads=[Vs], writes=[reg])
        else:
            for s in range(NSEQ_S):
                reg = kvreg[s + 1][NPT_S]
                sub, po = s // 2, 64 * (s % 2)
                K.dma("pool", ktsch, ktc[s + 1][:, :, PAST:PAST + 64].rearrange("h r n -> r h n"), KTs.ap[0:96, :, 64 * s:64 * s + 64], reads=[KTs], writes=[reg], batch=(s > 0))
                K.dma("pool", vsch, vc[s + 1][:, 0:64, NKT_S - 1, :].rearrange("h p c -> p h c"), Vs.ap[po:po + 64, :, sub, :], reads=[Vs], writes=[reg], batch=(s > 0))
                K.dma("pool", vsch, vc[s + 1][:, 64:128, NKT_S - 1, :].rearrange("h p c -> p h c"), Vs.ap[64 - po:128 - po, :, sub, :], reads=[Vs], writes=[reg], batch=True)
        (wu_t, wu_v), = R.get("winu")
        for m in range(4):
            b = 4 + m % 2
            fns = [lambda k=k, m=m, b=b: nc.tensor.matmul(PB[b].ap[:, 0:TT], wu_v[:, k, 128 * m:128 * m + 128], actT.ap[:, k, 0:TT], start=(k == 0), stop=(k == 7)) for k in range(8)]
            K.pe(fns, [wu_t, actT], [PB[b]])
            K.op("act", lambda m=m, b=b: nc.scalar.activation(out=uT.ap[:, m, 0:TT], in_=PB[b].ap[:, 0:TT], func=AF.Copy), [PB[b]], [uT])
        chk(5)
        if kind == "p":
            attention(0, 4 * ti + 4, 0, 512, ti, False, [(0, 128, s_) for s_ in range(4)])
        else:
            for s in range(NSEQ_S):
                attention(s + 1, PAST // 128, 64 * s, 64, None, True, [(64 * (s % 2), 64, s // 2)])
        chk(6)
        for sub in range(nsub):
            r = rms_stats(attn.ap[:, sub, :], [attn], 512, 16 + sub)
            xb = xnb.next()
            K.op("dve", lambda sub=sub, xb=xb, r=r: nc.vector.tensor_scalar(out=xb.ap[:, 0:512], in0=attn.ap[:, sub, :], scalar1=r, scalar2=None, op0=ALU.mult), [attn, stat], [xb])
            transposes_to(actT, 0, xb, 4, sub, 22)
        ssm(kind, ti, nsub, TT, NC)
        chk(7)
        wo = R.get("wout0", "wout1")
        for sub in range(nsub):
            fns = []
            for k in range(8):
                wvw = wo[k // 4][1]
                kl = k % 4
                for (c0, c1) in ((0, 512), (512, 1024)):
                    fns.append(lambda k=k, kl=kl, wvw=wvw, c0=c0, c1=c1, sub=sub: nc.tensor.matmul(
                        ps[:, c0:c1], actT.ap[:, k, 128 * sub:128 * sub + 128], wvw[:, kl, c0:c1], start=(k == 0), stop=(k == 7)))
            K.pe(fns, [actT, wo[0][0], wo[1][0]], [PB[0], PB[1]])
            K.op("dve", lambda sub=sub: nc.vector.tensor_tensor(out=xt.ap[:, sub, :], in0=ps[:, 0:1024], in1=xt.ap[:, sub, :], op=ALU.add), [PB[0], PB[1], xt], [xt])
        chk(8)
        for sub in range(nsub):
            r = rms_stats(xt.ap[:, sub, :], [xt], D, 20 + sub)
            xb = xnb.next()
            K.op("dve", lambda sub=sub, xb=xb, r=r: nc.vector.tensor_scalar(out=xb.ap, in0=xt.ap[:, sub, :], scalar1=r, scalar2=None, op0=ALU.mult), [xt, stat], [xb])
            transposes_to(actT, 0, xb, 8, sub, 8)
        for fc in range(8):
            (wu_t, wu_v), (wd_t, wd_v) = R.get("wup%d" % fc, "wdn%d" % fc)
            a = aTr.next()
            for ft in range(4):
                b = 4 + ft % 2
                fns = [lambda k=k, ft=ft, b=b: nc.tensor.matmul(PB[b].ap[:, 0:TT], wu_v[:, k, 128 * ft:128 * ft + 128], actT.ap[:, k, 0:TT], start=(k == 0), stop=(k == 7)) for k in range(8)]
                K.pe(fns, [wu_t, actT], [PB[b]])
                tr = trl.next()
                K.op("dve", lambda b=b, tr=tr: nc.vector.tensor_scalar(out=tr.ap[:, 0:TT], in0=PB[b].ap[:, 0:TT], scalar1=0.0, scalar2=None, op0=ALU.max), [PB[b]], [tr])
                K.op("pool", lambda ft=ft, tr=tr, a=a: nc.gpsimd.tensor_tensor(out=a.ap[:, ft, 0:TT], in0=tr.ap[:, 0:TT], in1=tr.ap[:, 0:TT], op=ALU.mult), [tr], [a])
            for sub in range(nsub):
                bb = (0, 1) if sub % 2 == 0 else (6, 7)
                base = 512 * bb[0]
                fns = []
                for ft in range(4):
                    for hh in range(2):
                        fns.append(lambda ft=ft, hh=hh, sub=sub, base=base, a=a: nc.tensor.matmul(
                            ps[:, base + 512 * hh:base + 512 * hh + 512], a.ap[:, ft, 128 * sub:128 * sub + 128], wd_v[:, ft, 512 * hh:512 * hh + 512], start=(ft == 0), stop=(ft == 3)))
                K.pe(fns, [a, wd_t], [PB[bb[0]], PB[bb[1]]])
                K.op("dve", lambda sub=sub, base=base: nc.vector.tensor_tensor(out=xt.ap[:, sub, :], in0=ps[:, base:base + 1024], in1=xt.ap[:, sub, :], op=ALU.add), [PB[bb[0]], PB[bb[1]], xt], [xt])
        for sub in range(nsub):
            r = rms_stats(xt.ap[:, sub, :], [xt], D, 24 + sub)
            K.op("dve", lambda sub=sub, r=r: nc.vector.scalar_tensor_tensor(out=xt.ap[:, sub, :], in0=xt.ap[:, sub, :], scalar=r, in1=gfin_bc.ap, op0=ALU.mult, op1=ALU.mult), [xt, stat, gfin_bc], [xt])
        K.dma("pool", ych, y_dst.rearrange("(s p) d -> p s d", p=128), xt.ap[:, 0:nsub, :], reads=[xt])

    def ssm(kind, ti, nsub, TT, NC):
        uTc = uT.ap[:, :, 0:TT].rearrange("p m (c i) -> p m i c", i=8)
        Sv = Ssb.ap.rearrange("p (kk r) t c -> p r kk t c", r=4)
        for r in range(4):
            bank = 4 + r
            for kk in range(4):
                for reim in range(2):
                    c0 = (kk * 2 + reim) * NC
                    fns = [lambda i=i, kk=kk, r=r, reim=reim, bank=bank, c0=c0: nc.tensor.matmul(
                        PB[bank].ap[:, c0:c0 + NC], W1.ap[32 * r:32 * r + 32, kk, reim, i, :], uTc[32 * r:32 * r + 32, kk, i, :],
                        start=(i == 0), stop=(i == 7), tile_position=(32 * r, 0)) for i in range(8)]
                    K.pe(fns, [W1, uT], [PB[bank]])
            K.op("act", lambda r=r, bank=bank: nc.scalar.activation(
                out=Sv[:, r, :, :, 0:NC], in_=PB[bank].ap[:, 0:8 * NC].rearrange("p (a t c) -> p a t c", a=4, t=2), func=AF.Copy), [PB[bank]], [Ssb])
        for c in range(NC):
            if kind == "p":
                prev_t, prev = (Hc, Hc.ap) if c == 0 else (Ssb, Ssb.ap[:, :, :, c - 1])
            else:
                if c % 8 == 0:
                    prev_t, prev = H0s, H0s.ap[:, c // 8, :, :]
                else:
                    prev_t, prev = Ssb, Ssb.ap[:, :, :, c - 1]
            pre = prev[:, :, 0:1].to_broadcast([128, 16, 2])
            pim = prev[:, :, 1:2].to_broadcast([128, 16, 2])
            if c % 8 == 0 or kind != "p" or True:
                pass
            K.op("pool", lambda c=c, prev=prev: nc.gpsimd.tensor_copy(out=Hbf.ap[:, :, :, c], in_=prev), [prev_t], [Hbf])
            K.op("dve", lambda pre=pre: nc.vector.tensor_tensor(out=st1.ap, in0=A8.ap, in1=pre, op=ALU.mult), [A8, prev_t], [st1])
            K.op("dve", lambda pim=pim: nc.vector.tensor_tensor(out=st2.ap, in0=B8.ap, in1=pim, op=ALU.mult), [B8, prev_t], [st2])
            K.op("dve", lambda: nc.vector.tensor_tensor(out=st1.ap, in0=st1.ap, in1=st2.ap, op=ALU.add), [st1, st2], [st1])
            K.op("dve", lambda c=c: nc.vector.tensor_tensor(out=Ssb.ap[:, :, :, c], in0=Ssb.ap[:, :, :, c], in1=st1.ap, op=ALU.add), [Ssb, st1], [Ssb])
        if kind == "p":
            K.op("dve", lambda: nc.vector.tensor_copy(out=Hc.ap, in_=Ssb.ap[:, :, :, NC - 1]), [Ssb], [Hc])
            if ti == NT_P - 1:
                K.op("dve", lambda: nc.vector.tensor_copy(out=hout.ap, in_=Ssb.ap[:, :, :, NC - 1]), [Ssb], [hout])
                for g2 in range(2):
                    K.dma("pool", hoch, hrp.rearrange("(k two) n -> two n k", two=2)[g2], hout.ap[64 * g2:64 * g2 + 64, :, 0], reads=[hout], batch=(g2 > 0))
                    K.dma("pool", hoch, hip.rearrange("(k two) n -> two n k", two=2)[g2], hout.ap[64 * g2:64 * g2 + 64, :, 1], reads=[hout], batch=True)
        else:
            for s in range(NSEQ_S):
                for g2 in range(2):
                    K.dma("pool", hoch, hrs[s].rearrange("(k two) n -> two n k", two=2)[g2], Ssb.ap[64 * g2:64 * g2 + 64, :, 0, 8 * s + 7], reads=[Ssb], batch=(s + g2 > 0))
                    K.dma("pool", hoch, his[s].rearrange("(k two) n -> two n k", two=2)[g2], Ssb.ap[64 * g2:64 * g2 + 64, :, 1, 8 * s + 7], reads=[Ssb], batch=True)
        (wg_t, wg_v), = R.get("wglu")
        for kk in range(4):
            b = 6 + kk % 2
            yv = PB[b].ap[:, 0:TT].rearrange("p (c i) -> p i c", i=8)
            fns = [lambda kk=kk, b=b: nc.tensor.matmul(PB[b].ap[:, 0:TT], Dd.ap[:, kk, :], uT.ap[:, kk, 0:TT], start=True, stop=True)]
            for j in range(8):
                for i in range(j + 1):
                    fns.append(lambda kk=kk, j=j, i=i, yv=yv: nc.tensor.matmul(yv[:, j, :], BD.ap[:, kk, j - i, :], uTc[:, kk, i, :], start=False, stop=True, skip_group_check=True))
            for r in range(4):
                kp = 4 * kk + r
                for j in range(8):
                    for reim in range(2):
                        lastone = (r == 3 and j == 7 and reim == 1)
                        fns.append(lambda kp=kp, r=r, j=j, reim=reim, yv=yv, lastone=lastone: nc.tensor.matmul(
                            yv[32 * r:32 * r + 32, j, :], W2.ap[:, kp, reim, j, :], Hbf.ap[:, kp, reim, 0:NC], start=False, stop=True, skip_group_check=True, tile_position=(0, 32 * r)))
            K.pe(fns, [Dd, uT, BD, W2, Hbf], [PB[b]])
            yp_ = PB[b].ap[:, 0:TT]
            K.op("act", lambda yp_=yp_: nc.scalar.activation(out=tA.ap[:, 0:TT], in_=yp_, func=AF.Square), [PB[b]], [tA])
            K.op("dve", lambda: nc.vector.tensor_scalar(out=tA.ap[:, 0:TT], in0=tA.ap[:, 0:TT], scalar1=0.044715, scalar2=1.0, op0=ALU.mult, op1=ALU.add), [tA], [tA])
            K.op("dve", lambda yp_=yp_: nc.vector.tensor_tensor(out=tA.ap[:, 0:TT], in0=yp_, in1=tA.ap[:, 0:TT], op=ALU.mult), [PB[b], tA], [tA])
            K.op("act", lambda: nc.scalar.activation(out=tA.ap[:, 0:TT], in_=tA.ap[:, 0:TT], func=AF.Sigmoid, scale=1.5957691216), [tA], [tA])
            K.op("dve", lambda kk=kk, yp_=yp_: nc.vector.tensor_tensor(out=y2.ap[:, kk, 0:TT], in0=yp_, in1=tA.ap[:, 0:TT], op=ALU.mult), [PB[b], tA], [y2])
            K.op("pool", lambda kk=kk: nc.gpsimd.tensor_copy(out=y2b.ap[:, kk, 0:TT], in_=y2.ap[:, kk, 0:TT]), [y2], [y2b])
        for m in range(4):
            b = 4 + m % 2
            fns = [lambda k=k, m=m, b=b: nc.tensor.matmul(PB[b].ap[:, 0:TT], wg_v[:, k, 128 * m:128 * m + 128], y2b.ap[:, k, 0:TT], start=(k == 0), stop=(k == 3)) for k in range(4)]
            K.pe(fns, [wg_t, y2b], [PB[b]])
            K.op("act", lambda b=b: nc.scalar.activation(out=tB.ap[:, 0:TT], in_=PB[b].ap[:, 0:TT], func=AF.Sigmoid), [PB[b]], [tB])
            K.op("dve", lambda m=m: nc.vector.tensor_tensor(out=y2.ap[:, m, 0:TT], in0=y2.ap[:, m, 0:TT], in1=tB.ap[:, 0:TT], op=ALU.mult), [y2, tB], [y2])
            K.op("pool", lambda m=m: nc.gpsimd.tensor_tensor(out=sqb.ap[:, m, 0:TT], in0=y2.ap[:, m, 0:TT], in1=y2.ap[:, m, 0:TT], op=ALU.mult), [y2], [sqb])
        fns = [lambda m=m: nc.tensor.matmul(PB[6].ap[:, 0:TT], ones_b.ap, sqb.ap[:, m, 0:TT], start=(m == 0), stop=(m == 3)) for m in range(4)]
        K.pe(fns, [ones_b, sqb], [PB[6]])
        K.op("act", lambda: nc.scalar.activation(out=rbc.ap[:, 0:TT], in_=PB[6].ap[:, 0:TT], func=AF.Sqrt, scale=1.0 / 512, bias=EPS), [PB[6]], [rbc])
        K.op("dve", lambda: nc.vector.reciprocal(out=rbc.ap[:, 0:TT], in_=rbc.ap[:, 0:TT]), [rbc], [rbc])
        for m in range(4):
            K.op("dve", lambda m=m: nc.vector.scalar_tensor_tensor(out=actT.ap[:, 4 + m, 0:TT], in0=y2.ap[:, m, 0:TT], scalar=gcols.ap[:, 26 + m:27 + m], in1=rbc.ap[:, 0:TT], op0=ALU.mult, op1=ALU.mult), [y2, gcols, rbc], [actT])

    def kv_from_cache(s, pt_i):
        for sub in range(4):
            kvt = kvtok.next()
            r0 = 512 * pt_i + 128 * sub
            K.op("pool", lambda kvt=kvt: nc.gpsimd.memset(kvt.ap[:, 256:320], 0.0), [], [kvt])
            K.dma("pool", kvlch, kvt.ap[:, 0:256], ckl[s, r0:r0 + 128, :], writes=[kvt])
            K.dma("pool", kvlch, kvt.ap[:, 320:352], ckr[s, r0:r0 + 128, :], writes=[kvt], batch=True)
            kv_from_tok(kvt, sub, 512)
        (wkv_t, wkv_v), = R.get("wkv")
        kv_build(wkv_t, wkv_v, 4)
        reg = kvreg[s + 1][pt_i]
        K.dma("pool", ktsch, ktc[s + 1][:, :, 512 * pt_i:512 * pt_i + 512].rearrange("h r n -> r h n"), KTs.ap[0:96, :, :], reads=[KTs], writes=[reg])
        K.dma("pool", vsch, vc[s + 1][:, :, 4 * pt_i:4 * pt_i + 4, :].rearrange("h p s c -> p h s c"), Vs.ap[:, :, :, :], reads=[Vs], writes=[reg])

    K.op("pool", lambda: nc.gpsimd.memset(Hc.ap, 0.0), [], [Hc])
    for ti in range(NT_P):
        R.add(plan_tok)
    for s in range(NSEQ_S * NPT_S):
        R.add(plan_kv)
    R.add(plan_tok)
    try:
        for ti in range(NT_P):
            token_tile("p", ti)
        chk(9)
        for s in range(NSEQ_S):
            for pt_i in range(NPT_S):
                kv_from_cache(s, pt_i)
        chk(10)
        token_tile("s", 0)
    except Stop:
        pass
    K.finish("pool")
    return nc, K


_CACHE = {}


def _rope_table(pos):
    half = 16
    inv_freq = (10000.0 ** (-(np.arange(half, dtype=np.float32) * 2.0) / 32)).astype(np.float32)
    ang = pos.astype(np.float32)[:, None] * inv_freq[None, :]
    return np.concatenate([np.cos(ang), np.sin(ang)], axis=1).astype(np.float32)


def run(inputs, SEQ, PAST, ncores=8):
    key = (SEQ, PAST)
    if key not in _CACHE:
        _CACHE[key] = build(SEQ, PAST)
    nc, K = _CACHE[key]
    f = lambda a: np.ascontiguousarray(np.asarray(a, dtype=np.float32))
    x_prompt = f(inputs["x_prompt"]); x_sample = f(inputs["x_sample"])
    ckl = f(inputs["cache_kv_latent"])[0]; ckr = f(inputs["cache_k_rope"])[0]
    sre = f(inputs["state_ssm_re"])[0]; sim = f(inputs["state_ssm_im"])[0]
    cs_p = _rope_table(np.arange(SEQ))
    cs_s = np.tile(_rope_table(PAST + np.arange(DSEQ)), (NSEQ_S, 1))
    ident = np.eye(128, dtype=np.float32)
    shared = {
        "g_mix": f(inputs["g_mix"]), "w_in": f(inputs["w_in"])[0], "g_q": f(inputs["g_q_a"]), "w_q": f(inputs["w_q_up"])[0],
        "g_kv": f(inputs["g_kv_a"]), "w_kv": f(inputs["w_kv_up"])[0], "a_re": f(inputs["a_re"])[0], "a_im": f(inputs["a_im"])[0],
        "lstep": f(inputs["log_step"]), "b_re": f(inputs["b_re"])[0], "b_im": f(inputs["b_im"])[0],
        "c_re": f(inputs["c_re"])[0], "c_im": f(inputs["c_im"])[0], "d_skip": f(inputs["d_skip"]), "w_glu": f(inputs["w_glu"])[0],
        "g_attn": f(inputs["g_attn_out"]), "g_ssm": f(inputs["g_ssm_out"]), "w_out": f(inputs["w_out"])[0],
        "g_mlp": f(inputs["g_mlp"]), "w_up": f(inputs["w_up"])[0], "w_down": f(inputs["w_down"])[0],
        "g_fin": f(inputs["g_final"]).reshape(1, D), "cs_p": cs_p, "cs_s": cs_s, "ident": ident,
    }
    in_maps = []
    for c in range(ncores):
        m = dict(shared)
        m["xp"] = x_prompt[c]
        sl = slice(NSEQ_S * c, NSEQ_S * (c + 1))
        m["xs"] = np.ascontiguousarray(x_sample[sl].reshape(NSEQ_S * DSEQ, D))
        m["ckl"] = np.ascontiguousarray(ckl[sl]); m["ckr"] = np.ascontiguousarray(ckr[sl])
        m["sre"] = np.ascontiguousarray(sre[sl]); m["sim"] = np.ascontiguousarray(sim[sl])
        in_maps.append(m)
    res = run_bass_kernel_spmd(nc, in_maps, core_ids=list(range(ncores)))
    rs = res.results
    cat = lambda k: np.stack([np.asarray(r[k], dtype=np.float32) for r in rs])
    y_prompt = cat("yp")
    y_sample = cat("ys").reshape(ncores * NSEQ_S, DSEQ, D)
    lat_p = cat("latp")[None]
    kr_p = cat("krp")[None]
    hr_p = cat("hrp")[None]
    hi_p = cat("hip")[None]
    lat_s = cat("lats").reshape(ncores * NSEQ_S, DSEQ, KVL)[None]
    kr_s = cat("krs").reshape(ncores * NSEQ_S, DSEQ, 32)[None]
    hr_s = cat("hrs").reshape(ncores * NSEQ_S, 32, 64)[None]
    hi_s = cat("his").reshape(ncores * NSEQ_S, 32, 64)[None]
    return (y_prompt, y_sample, lat_p, kr_p, hr_p, hi_p, lat_s, kr_s, hr_s, hi_s)


def kernel(**inputs):
    return run(inputs, 8192, 4096, 8)
```

```python
import math
import numpy as np
import ml_dtypes
import concourse.bass as bass
import concourse.mybir as mybir
from concourse.bass_utils import run_bass_kernel_spmd

F32 = mybir.dt.float32
BF16 = mybir.dt.bfloat16
I32 = mybir.dt.int32
AF = mybir.ActivationFunctionType
ALU = mybir.AluOpType
AX = mybir.AxisListType

D = 1024
DIN = 1568
QL = 768
KVL = 256
NH = 8
DFF = 4096
EPS = 1e-6
SCALE = 96 ** -0.5
NSEQ_S = 4
DSEQ = 64
TWO_PI = 2.0 * math.pi


class T:
    def __init__(self, ap, name="", share=None):
        self.ap = ap
        self.name = name
        self.d = share.d if share is not None else {"w": None, "r": {}}

    @property
    def w(self):
        return self.d["w"]

    @w.setter
    def w(self, v):
        self.d["w"] = v

    @property
    def r(self):
        return self.d["r"]

    @r.setter
    def r(self, v):
        self.d["r"] = v


class Chan:
    def __init__(self, nc, name):
        self.sem = nc.alloc_semaphore(name)
        self.total = 0
        self.key = name
        self.holder = [0]


class Trk:
    def __init__(self, nc):
        self.nc = nc
        self.E = {"pe": nc.tensor, "act": nc.scalar, "dve": nc.vector, "pool": nc.gpsimd, "sp": nc.sync}
        self.sem = {}
        self.cnt = {}
        self.gen = {}
        for e in ("pe", "act", "dve", "pool"):
            self.gen[e] = 0
            self._newsem(e)
        self.seen = {}
        self.chans = []
        self.ninst = {e: 0 for e in self.E}

    def _newsem(self, e):
        self.sem[e] = self.nc.alloc_semaphore("c_%s_%d" % (e, self.gen[e]))
        self.cnt[e] = 0
        self.gen[e] += 1

    def chan(self, name):
        c = Chan(self.nc, name)
        self.chans.append(c)
        return c

    def _wait(self, eng, ev):
        if ev is None:
            return
        sem, key, val = ev
        if isinstance(val, list):
            val = val[0]
        k = (eng, key)
        if self.seen.get(k, 0) >= val:
            return
        self.E[eng].wait_ge(sem, val)
        self.ninst[eng] += 1
        self.seen[k] = val

    def _deps(self, eng, reads, writes, skip_same=False):
        for t in reads:
            if t.w is not None and not (skip_same and t.w[1][0] == eng):
                self._wait(eng, t.w)
        for t in writes:
            if t.w is not None and not (skip_same and t.w[1][0] == eng):
                self._wait(eng, t.w)
            for ev in t.r.values():
                if not (skip_same and ev[1][0] == eng):
                    self._wait(eng, ev)

    def _done(self, ev, reads, writes):
        for t in reads:
            t.r[ev[1]] = ev
        for t in writes:
            t.w = ev
            t.r = {}

    def op(self, eng, fn, reads=(), writes=()):
        self._deps(eng, reads, writes)
        inst = fn()
        if self.cnt[eng] >= 60000:
            self._newsem(eng)
        self.cnt[eng] += 1
        inst.then_inc(self.sem[eng], 1)
        self.ninst[eng] += 1
        ev = (self.sem[eng], (eng, self.gen[eng]), self.cnt[eng])
        self._done(ev, reads, writes)

    def pe(self, fns, reads=(), writes=()):
        self._deps("pe", reads, writes, skip_same=True)
        inst = None
        for f in fns:
            inst = f()
            self.ninst["pe"] += 1
        if self.cnt["pe"] >= 60000:
            self._newsem("pe")
        self.cnt["pe"] += 1
        inst.then_inc(self.sem["pe"], 1)
        ev = (self.sem["pe"], ("pe", self.gen["pe"]), self.cnt["pe"])
        self._done(ev, reads, writes)

    def dma(self, q, ch, out_ap, in_ap, reads=(), writes=(), batch=False, **kw):
        self._deps(q, reads, writes)
        if not batch:
            if ch.total > 0:
                self._wait(q, (ch.sem, ch.key, ch.total))
            ch.holder = [ch.total]
        inst = self.E[q].dma_start(out=out_ap, in_=in_ap, allow_slow_non_contiguous=True, **kw)
        ch.total += 16
        ch.holder[0] = ch.total
        inst.then_inc(ch.sem, 16)
        self.ninst[q] += 1
        ev = (ch.sem, ch.key, ch.holder)
        self._done(ev, reads, writes)

    def barrier(self):
        for eng in ("pe", "act", "dve", "pool", "sp"):
            self.finish(eng)

    def finish(self, eng="pool"):
        for c in self.chans:
            if c.total > 0:
                self._wait(eng, (c.sem, c.key, c.total))
        for e in ("pe", "act", "dve", "pool"):
            if self.cnt[e] > 0 and e != eng:
                self._wait(eng, (self.sem[e], (e, self.gen[e]), self.cnt[e]))


class Rot:
    def __init__(self, items):
        self.items = items
        self.i = 0

    def next(self):
        t = self.items[self.i % len(self.items)]
        self.i += 1
        return t


def bc(ap, shape):
    return ap.to_broadcast(shape)


def build(SEQ, PAST, stage=99):
    nc = bass.Bass("TRN2", target_bir_lowering=False)
    K = Trk(nc)
    NT_P = SEQ // 512
    NKT_P = SEQ // 128
    NK_S = PAST + 128
    NKT_S = NK_S // 128
    NPT_S = PAST // 512

    def din(name, shape, dt=F32):
        return nc.dram_tensor(name, list(shape), dt, kind="ExternalInput").ap()

    def dout(name, shape):
        return nc.dram_tensor(name, list(shape), F32, kind="ExternalOutput").ap()

    def dscr(name, shape, dt=BF16):
        return nc.dram_tensor(name, list(shape), dt, kind="Internal").ap()

    def sb(name, shape, dt=F32):
        return nc.alloc_sbuf_tensor(name, list(shape), dt).ap()

    xp = din("xp", [SEQ, D]); xs = din("xs", [NSEQ_S * DSEQ, D])
    ckl = din("ckl", [NSEQ_S, PAST, KVL]); ckr = din("ckr", [NSEQ_S, PAST, 32])
    sre = din("sre", [NSEQ_S, 32, 64]); sim = din("sim", [NSEQ_S, 32, 64])
    g_mix = din("g_mix", [1, D]); w_in = din("w_in", [D, DIN]); g_q = din("g_q", [1, QL])
    w_q = din("w_q", [QL, QL]); g_kv = din("g_kv", [1, KVL]); w_kv = din("w_kv", [KVL, 1024])
    a_re = din("a_re", [32, 64]); a_im = din("a_im", [32, 64]); lstep = din("lstep", [1, 32])
    b_re = din("b_re", [32, 64, 16]); b_im = din("b_im", [32, 64, 16])
    c_re = din("c_re", [32, 16, 64]); c_im = din("c_im", [32, 16, 64])
    d_skip = din("d_skip", [1, 512]); w_glu = din("w_glu", [512, 512])
    g_attn = din("g_attn", [1, 512]); g_ssm = din("g_ssm", [1, 512]); w_out = din("w_out", [D, D])
    g_mlp = din("g_mlp", [1, D]); w_up = din("w_up", [D, DFF]); w_down = din("w_down", [DFF, D])
    g_fin = din("g_fin", [1, D])
    cs_p = din("cs_p", [SEQ, 32]); cs_s = din("cs_s", [NSEQ_S * DSEQ, 32])
    ident_in = din("ident", [128, 128])

    yp = dout("yp", [SEQ, D]); ys = dout("ys", [NSEQ_S * DSEQ, D])
    latp = dout("latp", [SEQ, KVL]); krp = dout("krp", [SEQ, 32])
    hrp = dout("hrp", [32, 64]); hip = dout("hip", [32, 64])
    lats = dout("lats", [NSEQ_S * DSEQ, KVL]); krs = dout("krs", [NSEQ_S * DSEQ, 32])
    hrs = dout("hrs", [NSEQ_S, 32, 64]); his = dout("his", [NSEQ_S, 32, 64])

    s_winc = dscr("s_winc", [128, 8, 1056]); s_winu = dscr("s_winu", [128, 8, 512])
    s_wq = dscr("s_wq", [128, 6, 768]); s_wkv = dscr("s_wkv", [128, 2, 1024])
    s_wglu = dscr("s_wglu", [128, 4, 512]); s_wout = dscr("s_wout", [128, 8, 1024])
    s_wup = dscr("s_wup", [8, 128, 8, 512]); s_wdn = dscr("s_wdn", [8, 128, 4, 1024])
    ktc = [dscr("ktc0", [NH, 96, SEQ])] + [dscr("ktc%d" % (s + 1), [NH, 96, NK_S]) for s in range(NSEQ_S)]
    vc = [dscr("vc0", [NH, 128, NKT_P, 65])] + [dscr("vc%d" % (s + 1), [NH, 128, NKT_S, 65]) for s in range(NSEQ_S)]
    kvreg = [[T(None, "kvreg0_%d" % i) for i in range(NT_P)]] + \
            [[T(None, "kvreg%d_%d" % (s + 1, i)) for i in range(NPT_S + 1)] for s in range(NSEQ_S)]
    wscr = T(None, "wscr")

    ps = nc.alloc_psum_tensor("ps", [128, 4096], F32).ap()
    PB = [T(ps[:, 512 * b:512 * (b + 1)], "pb%d" % b) for b in range(8)]

    def pbf(b):
        return ps[:, 512 * b:512 * (b + 1)].bitcast(BF16)

    from contextlib import ExitStack

    def prod(sh):
        n = 1
        for v_ in sh:
            n *= v_
        return n

    def vw(t, dt, shape, off=0):
        a_ = t.ap if dt == F32 else t.ap.bitcast(dt)
        esz = 4 if dt in (F32, I32) else 2
        n = prod(shape)
        a_ = a_[:, off // esz:off // esz + n]
        if len(shape) == 2:
            a_ = a_.rearrange("p (a b) -> p a b", a=shape[0])
        elif len(shape) == 3:
            a_ = a_.rearrange("p (a b c) -> p a b c", a=shape[0], b=shape[1])
        return a_

    def raw(name, nbytes):
        return T(sb(name, [128, nbytes // 4]), name)

    ident_f = T(sb("ident_f", [128, 128]))
    ident_b = T(sb("ident_b", [128, 128], BF16))
    ones_b = T(sb("ones_b", [128, 128], BF16))
    gkv_bc = T(sb("gkv_bc", [128, KVL])); gfin_bc = T(sb("gfin_bc", [128, D]))
    gcols = T(sb("gcols", [128, 40]))
    W1 = T(sb("W1", [128, 4, 2, 8, 128], BF16))
    W2 = T(sb("W2", [128, 16, 2, 8, 32], BF16))
    BD = T(sb("BD", [128, 4, 8, 128], BF16))
    Dd = T(sb("Dd", [128, 4, 128], BF16))
    A8 = T(sb("A8", [128, 16, 2])); B8 = T(sb("B8", [128, 16, 2]))
    Hc = T(sb("Hc", [128, 16, 2])); H0s = T(sb("H0s", [128, NSEQ_S, 16, 2])); h0ch = K.chan("h0")
    cch = K.chan("const")
    cchA = K.chan("ssmA")
    cchB = K.chan("ssmB")
    castch = K.chan("cast")

    def alloc_runtime():
        g = {}
        NSLOT = 5
        g["NSLOT"] = NSLOT
        g["ring"] = [T(sb("ring%d" % i, [128, 4096], BF16)) for i in range(NSLOT)]
        g["ringch"] = [K.chan("ring%d" % i) for i in range(NSLOT)]
        g["xt"] = T(sb("xt", [128, 4, D])); g["xch"] = K.chan("xt"); g["ych"] = K.chan("yst")
        g["cst"] = T(sb("cst", [128, 4, 32])); g["csch"] = K.chan("cst"); g["kvlch"] = K.chan("kvl")
        g["actT"] = T(sb("actT", [128, 8, 512], BF16))
        rx = [raw("rx%d" % i, 2048) for i in range(2)]
        g["xnb"] = Rot([T(vw(r_, BF16, (D,)), share=r_) for r_ in rx])
        g["trl"] = Rot([T(vw(r_, F32, (512,)), share=r_) for r_ in rx])
        g["junk"] = T(sb("junk", [128, D], BF16))
        g["cqn"] = Rot([T(sb("cqn%d" % i, [128, QL], BF16)) for i in range(2)])
        rq = [raw("rq%d" % i, 4096) for i in range(2)]
        g["aT"] = [T(vw(r_, BF16, (4, 512)), share=r_) for r_ in rq]
        g["qTh"] = [T(vw(r_, BF16, (4, 512)), share=r_) for r_ in rq]
        rD = raw("rD", 8192)
        g["cqnT"] = T(vw(rD, BF16, (6, 512)), share=rD)
        g["Ssb"] = T(vw(rD, F32, (16, 2, 64)), share=rD)
        g["kvtok"] = Rot([T(sb("kvtok%d" % i, [128, 352], BF16)) for i in range(2)])
        g["kvout"] = [T(sb("kvout%d" % i, [128, 288])) for i in range(2)]
        g["kvoch"] = [K.chan("kvout%d" % i) for i in range(2)]
        g["latT"] = T(sb("latT", [128, 2, 512], BF16))
        g["qtok"] = Rot([T(sb("qtok%d" % i, [128, 8, 96], BF16)) for i in range(2)])
        g["uT"] = T(sb("uT", [128, 4, 512], BF16))
        g["KTs"] = T(sb("KTs", [128, 8, 512], BF16)); g["ktsch"] = K.chan("kts")
        g["Vs"] = T(sb("Vs", [128, 8, 4, 65], BF16)); g["vsch"] = K.chan("vs")
        rA = raw("rA", 8192)
        g["attn"] = T(vw(rA, F32, (4, 512)), share=rA)
        g["y2"] = T(vw(rA, F32, (4, 512)), share=rA)
        rB = [raw("rB%d" % i, 2048) for i in range(2)]
        g["OTs"] = Rot([T(vw(r_, F32, (512,)), share=r_) for r_ in rB])
        g["tA"] = T(vw(rB[0], F32, (512,)), share=rB[0]); g["tB"] = T(vw(rB[1], F32, (512,)), share=rB[1])
        rC = [raw("rC%d" % i, 4096) for i in range(2)]
        g["Kblk"] = [T(vw(r_, BF16, (2048,)), share=r_) for r_ in rC]
        g["y2b"] = T(vw(rC[0], BF16, (4, 512)), share=rC[0]); g["sqb"] = T(vw(rC[1], BF16, (4, 512)), share=rC[1])
        g["Vblk"] = [T(sb("Vblk%d" % i, [128, 16, 65], BF16)) for i in range(2)]
        g["kbch"] = [K.chan("kb%d" % i) for i in range(2)]
        g["vbch"] = [K.chan("vb%d" % i) for i in range(2)]
        g["PT"] = Rot([T(sb("PT%d" % i, [128, 512], BF16)) for i in range(3)])
        g["stat"] = T(sb("stat", [128, 64]))
        g["rtmp"] = T(sb("rtmp", [128, 8, 64]))
        g["Hbf"] = T(sb("Hbf", [128, 16, 2, 64], BF16))
        g["st1"] = T(sb("st1", [128, 16, 2])); g["st2"] = T(sb("st2", [128, 16, 2]))
        g["hout"] = T(sb("hout", [128, 16, 2])); g["hoch"] = K.chan("hout")
        g["rbc"] = T(sb("rbc", [128, 512]))
        g["qsb"] = T(sb("qsb", [128, 768]))
        return g


    K.dma("sp", cch, ident_f.ap, ident_in, writes=[ident_f], batch=True)
    K.dma("sp", cch, gkv_bc.ap, g_kv.partition_broadcast(128), writes=[gkv_bc], batch=True)
    K.dma("sp", cch, gfin_bc.ap, g_fin.partition_broadcast(128), writes=[gfin_bc], batch=True)
    for (src, off, n) in ((g_mix, 0, 8), (g_mlp, 8, 8), (g_q, 16, 6), (g_attn, 22, 4), (g_ssm, 26, 4), (d_skip, 30, 4)):
        K.dma("sp", cch, gcols.ap[:, off:off + n], src.rearrange("o (k c) -> c (o k)", c=128), writes=[gcols], batch=True)
    K.op("dve", lambda: nc.vector.tensor_copy(out=ident_b.ap, in_=ident_f.ap), [ident_f], [ident_b])
    K.op("pool", lambda: nc.gpsimd.memset(ones_b.ap, 1.0), [], [ones_b])

    wv = w_in.rearrange("(k p) n -> p k n", p=128)
    casts = [
        (s_winc, wv[:, :, 0:1056]), (s_winu, wv[:, :, 1056:1568]),
        (s_wq, w_q.rearrange("(k p) n -> p k n", p=128)),
    ]
    wkvv = w_kv.rearrange("(k p) (h c) -> p k h c", p=128, c=128)
    for k in range(2):
        casts.append((s_wkv[:, k, 0:512].rearrange("p (h c) -> p h c", c=64), wkvv[:, k, :, 0:64]))
        casts.append((s_wkv[:, k, 512:1024].rearrange("p (h c) -> p h c", c=64), wkvv[:, k, :, 64:128]))
    casts.append((s_wglu, w_glu.rearrange("(k p) n -> p k n", p=128)))
    casts.append((s_wout, w_out.rearrange("(k p) n -> p k n", p=128)))
    wupv = w_up.rearrange("(k p) (fc n) -> fc p k n", p=128, n=512)
    wdnv = w_down.rearrange("(fc ft p) n -> fc p ft n", ft=4, p=128)
    for fc in range(8):
        for k in range(0, 8, 4):
            casts.append((s_wup[fc, :, k:k + 4, :], wupv[fc, :, k:k + 4, :]))
        for ft in range(0, 4, 2):
            casts.append((s_wdn[fc, :, ft:ft + 2, :], wdnv[fc, :, ft:ft + 2, :]))
    for (o, i) in casts:
        K.dma("pool", castch, o, i, writes=[wscr], batch=True)

    if stage == 0:
        K.finish("pool")
        return nc, K
    def ssm_setup(es):
        def sb(name, shape, dt=F32):
            return es.enter_context(nc.sbuf_tensor(name, list(shape), dt)).ap()
        Are = T(sb("Are", [128, 16])); Aim = T(sb("Aim", [128, 16])); LS = T(sb("LS", [128, 16]))
        for g2 in range(2):
            K.dma("sp", cchA, Are.ap[64 * g2:64 * g2 + 64, :], a_re.rearrange("(k two) n -> two n k", two=2)[g2], writes=[Are], batch=True)
            K.dma("sp", cchA, Aim.ap[64 * g2:64 * g2 + 64, :], a_im.rearrange("(k two) n -> two n k", two=2)[g2], writes=[Aim], batch=True)
            K.dma("sp", cchA, LS.ap[64 * g2:64 * g2 + 64, :], lstep.rearrange("o (k two) -> two o k", two=2)[g2].partition_broadcast(64), writes=[LS], batch=True)
        BreD = T(sb("BreD", [128, 16, 32])); BimD = T(sb("BimD", [128, 16, 32]))
        CreD = T(sb("CreD", [32, 16, 128])); CimD = T(sb("CimD", [32, 16, 128]))
        for t in (BreD, BimD, CreD, CimD):
            K.op("pool", lambda t=t: nc.gpsimd.memset(t.ap, 0.0), [], [t])
        for g2 in range(2):
            K.dma("sp", cchB, BreD.ap[64 * g2:64 * g2 + 64, :, 16 * g2:16 * g2 + 16], b_re.rearrange("(k two) n q -> two n k q", two=2)[g2], writes=[BreD], batch=True)
            K.dma("sp", cchB, BimD.ap[64 * g2:64 * g2 + 64, :, 16 * g2:16 * g2 + 16], b_im.rearrange("(k two) n q -> two n k q", two=2)[g2], writes=[BimD], batch=True)
            K.dma("sp", cchB, CreD.ap[16 * g2:16 * g2 + 16, :, 64 * g2:64 * g2 + 64], c_re.rearrange("(k two) p n -> two p k n", two=2)[g2], writes=[CreD], batch=True)
            K.dma("sp", cchB, CimD.ap[16 * g2:16 * g2 + 16, :, 64 * g2:64 * g2 + 64], c_im.rearrange("(k two) p n -> two p k n", two=2)[g2], writes=[CimD], batch=True)
        for s in range(NSEQ_S):
            for g2 in range(2):
                K.dma("sp", h0ch, H0s.ap[64 * g2:64 * g2 + 64, s, :, 0], sre[s].rearrange("(k two) n -> two n k", two=2)[g2], writes=[H0s], batch=True)
                K.dma("sp", h0ch, H0s.ap[64 * g2:64 * g2 + 64, s, :, 1], sim[s].rearrange("(k two) n -> two n k", two=2)[g2], writes=[H0s], batch=True)

        def V(name):
            return T(sb(name, [128, 16]))
        dt_ = V("dt_"); mag = V("mag"); th = V("th"); cs = V("cs"); sn = V("sn")
        w1 = V("w1"); w2 = V("w2"); w3 = V("w3"); wi = T(sb("wi", [128, 16], I32))
        K.op("act", lambda: nc.scalar.activation(out=dt_.ap, in_=LS.ap, func=AF.Exp), [LS], [dt_])
        K.op("dve", lambda: nc.vector.tensor_tensor(out=w1.ap, in0=Are.ap, in1=dt_.ap, op=ALU.mult), [Are, dt_], [w1])
        K.op("act", lambda: nc.scalar.activation(out=mag.ap, in_=w1.ap, func=AF.Exp), [w1], [mag])
        K.op("dve", lambda: nc.vector.tensor_tensor(out=th.ap, in0=Aim.ap, in1=dt_.ap, op=ALU.mult), [Aim, dt_], [th])

        def sin_of(dst, shift):
            K.op("dve", lambda: nc.vector.tensor_scalar(out=w1.ap, in0=th.ap, scalar1=shift, scalar2=1.0 / TWO_PI, op0=ALU.add, op1=ALU.mult), [th], [w1])
            K.op("dve", lambda: nc.vector.tensor_copy(out=wi.ap, in_=w1.ap), [w1], [wi])
            K.op("dve", lambda: nc.vector.tensor_copy(out=w2.ap, in_=wi.ap), [wi], [w2])
            K.op("dve", lambda: nc.vector.tensor_scalar(out=w1.ap, in0=th.ap, scalar1=shift, scalar2=None, op0=ALU.add), [th], [w1])
            K.op("dve", lambda: nc.vector.scalar_tensor_tensor(out=w1.ap, in0=w2.ap, scalar=-TWO_PI, in1=w1.ap, op0=ALU.mult, op1=ALU.add), [w2, w1], [w1])
            K.op("dve", lambda: nc.vector.tensor_scalar(out=w2.ap, in0=w1.ap, scalar1=math.pi, scalar2=-TWO_PI, op0=ALU.is_gt, op1=ALU.mult), [w1], [w2])
            K.op("dve", lambda: nc.vector.tensor_tensor(out=w1.ap, in0=w1.ap, in1=w2.ap, op=ALU.add), [w1, w2], [w1])
            K.op("dve", lambda: nc.vector.tensor_scalar(out=w2.ap, in0=w1.ap, scalar1=-math.pi, scalar2=TWO_PI, op0=ALU.is_lt, op1=ALU.mult), [w1], [w2])
            K.op("dve", lambda: nc.vector.tensor_tensor(out=w1.ap, in0=w1.ap, in1=w2.ap, op=ALU.add), [w1, w2], [w1])
            K.op("dve", lambda: nc.vector.tensor_scalar(out=w1.ap, in0=w1.ap, scalar1=3.1415925, scalar2=-3.1415925, op0=ALU.min, op1=ALU.max), [w1], [w1])
            K.op("act", lambda: nc.scalar.activation(out=dst.ap, in_=w1.ap, func=AF.Sin), [w1], [dst])
        sin_of(sn, 0.0)
        sin_of(cs, math.pi / 2)
        LP = T(sb("LP", [128, 9, 2, 16]))
        K.op("pool", lambda: nc.gpsimd.memset(LP.ap[:, 0, 0, :], 1.0), [], [LP])
        K.op("pool", lambda: nc.gpsimd.memset(LP.ap[:, 0, 1, :], 0.0), [], [LP])
        K.op("dve", lambda: nc.vector.tensor_tensor(out=LP.ap[:, 1, 0, :], in0=mag.ap, in1=cs.ap, op=ALU.mult), [mag, cs], [LP])
        K.op("dve", lambda: nc.vector.tensor_tensor(out=LP.ap[:, 1, 1, :], in0=mag.ap, in1=sn.ap, op=ALU.mult), [mag, sn], [LP])
        for k in range(2, 9):
            pr, pi_ = LP.ap[:, k - 1, 0, :], LP.ap[:, k - 1, 1, :]
            lr, li = LP.ap[:, 1, 0, :], LP.ap[:, 1, 1, :]
            K.op("dve", lambda: nc.vector.tensor_tensor(out=w1.ap, in0=pr, in1=lr, op=ALU.mult), [LP], [w1])
            K.op("dve", lambda: nc.vector.tensor_tensor(out=w2.ap, in0=pi_, in1=li, op=ALU.mult), [LP], [w2])
            K.op("dve", lambda: nc.vector.tensor_tensor(out=LP.ap[:, k, 0, :], in0=w1.ap, in1=w2.ap, op=ALU.subtract), [w1, w2], [LP])
            K.op("dve", lambda: nc.vector.tensor_tensor(out=w1.ap, in0=pr, in1=li, op=ALU.mult), [LP], [w1])
            K.op("dve", lambda: nc.vector.tensor_tensor(out=w2.ap, in0=pi_, in1=lr, op=ALU.mult), [LP], [w2])
            K.op("dve", lambda: nc.vector.tensor_tensor(out=LP.ap[:, k, 1, :], in0=w1.ap, in1=w2.ap, op=ALU.add), [w1, w2], [LP])
        K.op("dve", lambda: nc.vector.tensor_copy(out=A8.ap[:, :, 0], in_=LP.ap[:, 8, 0, :]), [LP], [A8])
        K.op("dve", lambda: nc.vector.tensor_copy(out=A8.ap[:, :, 1], in_=LP.ap[:, 8, 1, :]), [LP], [A8])
        K.op("dve", lambda: nc.vector.tensor_scalar(out=B8.ap[:, :, 0], in0=LP.ap[:, 8, 1, :], scalar1=-1.0, scalar2=None, op0=ALU.mult), [LP], [B8])
        K.op("dve", lambda: nc.vector.tensor_copy(out=B8.ap[:, :, 1], in_=LP.ap[:, 8, 0, :]), [LP], [B8])
        cre = V("cre"); cim = V("cim"); den = V("den"); nr = V("nr")
        K.op("dve", lambda: nc.vector.tensor_scalar(out=nr.ap, in0=LP.ap[:, 1, 0, :], scalar1=-1.0, scalar2=None, op0=ALU.add), [LP], [nr])
        K.op("dve", lambda: nc.vector.tensor_tensor(out=w1.ap, in0=Are.ap, in1=Are.ap, op=ALU.mult), [Are], [w1])
        K.op("dve", lambda: nc.vector.tensor_tensor(out=w2.ap, in0=Aim.ap, in1=Aim.ap, op=ALU.mult), [Aim], [w2])
        K.op("dve", lambda: nc.vector.tensor_tensor(out=den.ap, in0=w1.ap, in1=w2.ap, op=ALU.add), [w1, w2], [den])
        K.op("dve", lambda: nc.vector.reciprocal(out=den.ap, in_=den.ap), [den], [den])
        K.op("dve", lambda: nc.vector.tensor_tensor(out=w1.ap, in0=nr.ap, in1=Are.ap, op=ALU.mult), [nr, Are], [w1])
        K.op("dve", lambda: nc.vector.tensor_tensor(out=w2.ap, in0=LP.ap[:, 1, 1, :], in1=Aim.ap, op=ALU.mult), [LP, Aim], [w2])
        K.op("dve", lambda: nc.vector.tensor_tensor(out=w1.ap, in0=w1.ap, in1=w2.ap, op=ALU.add), [w1, w2], [w1])
        K.op("dve", lambda: nc.vector.tensor_tensor(out=cre.ap, in0=w1.ap, in1=den.ap, op=ALU.mult), [w1, den], [cre])
        K.op("dve", lambda: nc.vector.tensor_tensor(out=w1.ap, in0=LP.ap[:, 1, 1, :], in1=Are.ap, op=ALU.mult), [LP, Are], [w1])
        K.op("dve", lambda: nc.vector.tensor_tensor(out=w2.ap, in0=nr.ap, in1=Aim.ap, op=ALU.mult), [nr, Aim], [w2])
        K.op("dve", lambda: nc.vector.tensor_tensor(out=w1.ap, in0=w1.ap, in1=w2.ap, op=ALU.subtract), [w1, w2], [w1])
        K.op("dve", lambda: nc.vector.tensor_tensor(out=cim.ap, in0=w1.ap, in1=den.ap, op=ALU.mult), [w1, den], [cim])

        def B3(t):
            return t.unsqueeze(2).to_broadcast([128, 16, 32])
        m1 = T(sb("m1", [128, 16, 32])); m2 = T(sb("m2", [128, 16, 32]))
        BbR = T(sb("BbR", [128, 16, 32])); BbI = T(sb("BbI", [128, 16, 32]))

        def cmul(dre, dim, are, aim, bre, bim, rd, wr):
            if dre is not None:
                K.op("dve", lambda: nc.vector.tensor_tensor(out=m1.ap, in0=bre, in1=B3(are), op=ALU.mult), rd, [m1])
                K.op("dve", lambda: nc.vector.tensor_tensor(out=m2.ap, in0=bim, in1=B3(aim), op=ALU.mult), rd, [m2])
                K.op("dve", lambda: nc.vector.tensor_tensor(out=dre, in0=m1.ap, in1=m2.ap, op=ALU.subtract), [m1, m2], wr)
            if dim is not None:
                K.op("dve", lambda: nc.vector.tensor_tensor(out=m1.ap, in0=bim, in1=B3(are), op=ALU.mult), rd, [m1])
                K.op("dve", lambda: nc.vector.tensor_tensor(out=m2.ap, in0=bre, in1=B3(aim), op=ALU.mult), rd, [m2])
                K.op("dve", lambda: nc.vector.tensor_tensor(out=dim, in0=m1.ap, in1=m2.ap, op=ALU.add), [m1, m2], wr)
        cmul(BbR.ap, BbI.ap, cre.ap, cim.ap, BreD.ap, BimD.ap, [cre, cim, BreD, BimD], [BbR, BbI])
        GR = T(sb("GR", [128, 16, 32])); GI = T(sb("GI", [128, 16, 32]))
        for i in range(8):
            cmul(GR.ap, GI.ap, LP.ap[:, 7 - i, 0, :], LP.ap[:, 7 - i, 1, :], BbR.ap, BbI.ap, [LP, BbR, BbI], [GR, GI])
            for reim, G in ((0, GR), (1, GI)):
                for r in range(4):
                    bank = (reim * 4 + r)
                    fns = []
                    for kk in range(4):
                        kp = 4 * kk + r
                        fns.append(lambda kk=kk, kp=kp, G=G, bank=bank: nc.tensor.transpose(
                            out=PB[bank].ap[0:32, 128 * kk:128 * kk + 128], in_=G.ap[:, kp, :], identity=ident_f.ap))
                    K.pe(fns, [G, ident_f], [PB[bank]])
                    K.op("act", lambda r=r, bank=bank, reim=reim, i=i: nc.scalar.activation(
                        out=W1.ap[32 * r:32 * r + 32, :, reim, i, :],
                        in_=PB[bank].ap[0:32, :].rearrange("p (a b) -> p a b", a=4), func=AF.Copy), [PB[bank]], [W1])
        CTR = T(sb("CTR", [128, 16, 32])); CTI = T(sb("CTI", [128, 16, 32]))
        for (src, dst, bank) in ((CreD, CTR, 0), (CimD, CTI, 1)):
            fns = [lambda kp=kp, src=src, bank=bank: nc.tensor.transpose(out=PB[bank].ap[:, 32 * kp:32 * kp + 32], in_=src.ap[:, kp, :], identity=ident_f.ap[0:32, 0:32]) for kp in range(16)]
            K.pe(fns, [src, ident_f], [PB[bank]])
            K.op("act", lambda dst=dst, bank=bank: nc.scalar.activation(out=dst.ap, in_=PB[bank].ap.rearrange("p (a b) -> p a b", a=16), func=AF.Copy), [PB[bank]], [dst])
        K.op("pool", lambda: nc.gpsimd.memset(BD.ap, 0.0), [], [BD])
        CLR = T(sb("CLR", [128, 16, 32])); CLI = T(sb("CLI", [128, 16, 32])); NCLI = T(sb("NCLI", [128, 16, 32]))
        for kpow in range(9):
            cmul(CLR.ap, CLI.ap, LP.ap[:, kpow, 0, :], LP.ap[:, kpow, 1, :], CTR.ap, CTI.ap, [LP, CTR, CTI], [CLR, CLI])
            K.op("dve", lambda: nc.vector.tensor_scalar(out=NCLI.ap, in0=CLI.ap, scalar1=-1.0, scalar2=None, op0=ALU.mult), [CLI], [NCLI])
            if kpow >= 1:
                j = kpow - 1
                K.op("act", lambda j=j: nc.scalar.activation(out=W2.ap[:, :, 0, j, :], in_=CLR.ap, func=AF.Copy), [CLR], [W2])
                K.op("act", lambda j=j: nc.scalar.activation(out=W2.ap[:, :, 1, j, :], in_=NCLI.ap, func=AF.Copy), [NCLI], [W2])
            if kpow <= 7:
                tau = kpow
                for r in range(4):
                    for kk in range(4):
                        kp = 4 * kk + r
                        col = (kk * 8 + tau) * 32
                        bank = 2 * r + col // 512
                        c0 = col % 512
                        fns = [
                            lambda kp=kp, bank=bank, c0=c0: nc.tensor.matmul(PB[bank].ap[0:32, c0:c0 + 32], BbR.ap[:, kp, :], CLR.ap[:, kp, :], start=True, stop=False),
                            lambda kp=kp, bank=bank, c0=c0: nc.tensor.matmul(PB[bank].ap[0:32, c0:c0 + 32], BbI.ap[:, kp, :], NCLI.ap[:, kp, :], start=False, stop=True),
                        ]
                        K.pe(fns, [BbR, BbI, CLR, NCLI], [PB[bank]])
        for r in range(4):
            for hb in range(2):
                bank = 2 * r + hb
                K.op("act", lambda r=r, hb=hb, bank=bank: nc.scalar.activation(
                    out=BD.ap[32 * r:32 * r + 32, 2 * hb:2 * hb + 2, :, 32 * r:32 * r + 32],
                    in_=PB[bank].ap[0:32, :].rearrange("p (a t c) -> p a t c", a=2, t=8), func=AF.Copy), [PB[bank]], [BD])
        for kk in range(4):
            K.op("dve", lambda kk=kk: nc.vector.tensor_scalar(out=Dd.ap[:, kk, :], in0=ident_f.ap, scalar1=gcols.ap[:, 30 + kk:31 + kk], scalar2=None, op0=ALU.mult), [ident_f, gcols], [Dd])

    with ExitStack() as es_:
        ssm_setup(es_)
        K.barrier()
    if stage == 1:
        K.finish("pool")
        return nc, K
    G = alloc_runtime()
    NSLOT = G["NSLOT"]; ring = G["ring"]; ringch = G["ringch"]; xt = G["xt"]; xch = G["xch"]; ych = G["ych"]; cst = G["cst"]; csch = G["csch"]; kvlch = G["kvlch"]
    actT = G["actT"]; xnb = G["xnb"]; trl = G["trl"]; junk = G["junk"]; cqn = G["cqn"]; aT = G["aT"]; qTh = G["qTh"]; cqnT = G["cqnT"]; Ssb = G["Ssb"]
    kvtok = G["kvtok"]; kvout = G["kvout"]; kvoch = G["kvoch"]; latT = G["latT"]; qtok = G["qtok"]; uT = G["uT"]; KTs = G["KTs"]; ktsch = G["ktsch"]
    Vs = G["Vs"]; vsch = G["vsch"]; attn = G["attn"]; y2 = G["y2"]; OTs = G["OTs"]; tA = G["tA"]; tB = G["tB"]; Kblk = G["Kblk"]; y2b = G["y2b"]; sqb = G["sqb"]
    Vblk = G["Vblk"]; kbch = G["kbch"]; vbch = G["vbch"]; PT = G["PT"]; stat = G["stat"]; rtmp = G["rtmp"]; Hbf = G["Hbf"]; st1 = G["st1"]; st2 = G["st2"]
    hout = G["hout"]; hoch = G["hoch"]; rbc = G["rbc"]; qsb = G["qsb"]
    NKB = 2
    aTr = Rot(aT)
    kvo_i = [0]
    K.op("pool", lambda: nc.gpsimd.memset(Vs.ap, 1.0), [], [Vs])

    class Ring:
        def __init__(self):
            self.plan = []
            self.issued = 0
            self.got = 0

        def add(self, items):
            self.plan.extend(items)

        def _issue(self, n):
            name, src, shape = self.plan[n]
            slot = n % NSLOT
            dst = ring[slot].ap
            ne = 1
            for s_ in shape:
                ne *= s_
            d = dst[:, 0:ne]
            if len(shape) == 2:
                d = d.rearrange("p (a b) -> p a b", a=shape[0])
            K.dma("sp", ringch[slot], d, src, reads=[wscr], writes=[ring[slot]])

        def get(self, *names, hold=0):
            n0 = self.got
            assert len(names) + hold <= NSLOT
            for i_, nm in enumerate(names):
                assert self.plan[n0 + i_][0] == nm, (self.plan[n0 + i_][0], nm)
            while self.issued < min(len(self.plan), n0 + NSLOT - hold):
                self._issue(self.issued)
                self.issued += 1
            self.got += len(names)
            outs = []
            for i_ in range(len(names)):
                n = n0 + i_
                shape = self.plan[n][2]
                ne = prod(shape)
                v = ring[n % NSLOT].ap[:, 0:ne]
                if len(shape) == 2:
                    v = v.rearrange("p (a b) -> p a b", a=shape[0])
                outs.append((ring[n % NSLOT], v))
            return outs

    R = Ring()
    plan_tok = [("winc0", s_winc[:, 0:3, :], (3, 1056)), ("winc1", s_winc[:, 3:6, :], (3, 1056)), ("winc2", s_winc[:, 6:8, :], (2, 1056)),
                ("wq0", s_wq[:, 0:3, :], (3, 768)), ("wq1", s_wq[:, 3:6, :], (3, 768)),
                ("wkv", s_wkv, (2, 1024)), ("winu", s_winu, (8, 512)), ("wglu", s_wglu, (4, 512)),
                ("wout0", s_wout[:, 0:4, :], (4, 1024)), ("wout1", s_wout[:, 4:8, :], (4, 1024))]
    for fc in range(8):
        plan_tok.append(("wup%d" % fc, s_wup[fc], (8, 512)))
        plan_tok.append(("wdn%d" % fc, s_wdn[fc], (4, 1024)))
    plan_kv = [("wkv", s_wkv, (2, 1024))]

    def rms_stats(src_ap, reads, n, col):
        ss = stat.ap[:, col:col + 1]
        K.op("act", lambda: nc.scalar.activation(out=junk.ap[:, 0:n], in_=src_ap, func=AF.Square, accum_out=ss), reads, [junk, stat])
        K.op("act", lambda: nc.scalar.activation(out=ss, in_=ss, func=AF.Sqrt, scale=1.0 / n, bias=EPS), [stat], [stat])
        K.op("dve", lambda: nc.vector.reciprocal(out=ss, in_=ss), [stat], [stat])
        return ss

    def transposes_to(dstT, dst_kslice, src_tile, nk, sub, gcol0):
        fns = [lambda k=k: nc.tensor.transpose(out=pbf(3)[:, 128 * k:128 * k + 128], in_=src_tile.ap[:, 128 * k:128 * k + 128], identity=ident_b.ap) for k in range(nk)]
        K.pe(fns, [src_tile, ident_b], [PB[3]])
        for k in range(nk):
            K.op("dve", lambda k=k: nc.vector.tensor_scalar(out=dstT.ap[:, dst_kslice + k, 128 * sub:128 * sub + 128], in0=pbf(3)[:, 128 * k:128 * k + 128],
                                                           scalar1=gcols.ap[:, gcol0 + k:gcol0 + k + 1], scalar2=None, op0=ALU.mult), [PB[3], gcols], [dstT])

    def kv_from_tok(kvt, sub, ncols_total):
        fns = [lambda k=k: nc.tensor.transpose(out=pbf(3)[:, 128 * k:128 * k + 128], in_=kvt.ap[:, 128 * k:128 * k + 128], identity=ident_b.ap) for k in range(2)]
        fns.append(lambda: nc.tensor.transpose(out=pbf(3)[0:96, 256:384], in_=kvt.ap[:, 256:352], identity=ident_b.ap))
        K.pe(fns, [kvt, ident_b], [PB[3]])
        K.op("dve", lambda: nc.vector.tensor_copy(out=latT.ap[:, :, 128 * sub:128 * sub + 128], in_=pbf(3)[:, 0:256].rearrange("p (a b) -> p a b", a=2)), [PB[3]], [latT])
        K.op("dve", lambda: nc.vector.tensor_copy(out=KTs.ap[64:96, :, 128 * sub:128 * sub + 128],
                                                 in_=pbf(3)[64:96, 256:384].unsqueeze(1).to_broadcast([32, 8, 128])), [PB[3]], [KTs])

    def kv_build(wkv_t, wkv_v, nsub):
        N = 128 * nsub
        for hp in range(4):
            b = 4 + hp % 2
            fns = [lambda k=k, hp=hp, b=b: nc.tensor.matmul(PB[b].ap[:, 0:N], wkv_v[:, k, 128 * hp:128 * hp + 128], latT.ap[:, k, 0:N], start=(k == 0), stop=(k == 1)) for k in range(2)]
            K.pe(fns, [wkv_t, latT], [PB[b]])
            K.op("dve", lambda hp=hp, b=b: nc.vector.tensor_copy(out=KTs.ap[0:64, 2 * hp, 0:N], in_=PB[b].ap[0:64, 0:N]), [PB[b]], [KTs])
            K.op("act", lambda hp=hp, b=b: nc.scalar.activation(out=KTs.ap[0:64, 2 * hp + 1, 0:N], in_=PB[b].ap[64:128, 0:N], func=AF.Copy), [PB[b]], [KTs])
        for sub in range(nsub):
            b = 6 + sub % 2
            fns = [lambda k=k, sub=sub, b=b: nc.tensor.matmul(PB[b].ap[:, :], latT.ap[:, k, 128 * sub:128 * sub + 128], wkv_v[:, k, 512:1024], start=(k == 0), stop=(k == 1)) for k in range(2)]
            K.pe(fns, [wkv_t, latT], [PB[b]])
            K.op("dve", lambda sub=sub, b=b: nc.vector.tensor_copy(out=Vs.ap[:, :, sub, 0:64], in_=PB[b].ap.rearrange("p (h c) -> p h c", c=64)), [PB[b]], [Vs])

    def attention(ci, n_full_kt, qc0, nq, diag_i, half_last, dsts, between=None):
        nkt = n_full_kt + (1 if half_last else 0)
        nblk = (nkt + 15) // 16
        tot_keys = n_full_kt * 128 + (64 if half_last else 0)
        for h in range(NH):
            ob = 6 + h % 2
            qh = qTh[h // 4]
            hl = h % 4
            steps = []
            for blk in range(nblk):
                kt0 = blk * 16
                nkb = min(16, nkt - kt0)
                for kl in range(nkb):
                    kt = kt0 + kl
                    kp = 64 if (half_last and kt == nkt - 1) else 128
                    isdiag = diag_i is not None and kt >= 4 * diag_i
                    n0 = 128 * (kt - 4 * diag_i) if isdiag else 0
                    steps.append(dict(blk=blk, kt0=kt0, nkb=nkb, kl=kl, kt=kt, kp=kp, isdiag=isdiag, n0=n0, N=nq - n0))
            blkslot = {}

            def emit_S(st):
                blk = st["blk"]
                if blk not in blkslot:
                    slot = actr[0] % NKB
                    actr[0] += 1
                    blkslot[blk] = slot
                    kt0, nkb = st["kt0"], st["nkb"]
                    nkeys = min(128 * nkb, tot_keys - 128 * kt0)
                    regs = kvreg[ci][(kt0 * 128) // 512:(kt0 * 128 + nkeys + 511) // 512]
                    K.dma("sp", kbch[slot], Kblk[slot].ap[0:96, 0:nkeys], ktc[ci][h, :, 128 * kt0:128 * kt0 + nkeys], reads=regs, writes=[Kblk[slot]])
                    nvp = 128 if nkeys >= 128 else 64
                    K.dma("sp", vbch[slot], Vblk[slot].ap[0:nvp, 0:nkb, :], vc[ci][h, 0:nvp, kt0:kt0 + nkb, :], reads=regs, writes=[Vblk[slot]])
                slot = blkslot[blk]
                st["slot"] = slot
                sbk = 4 + actr[1] % 2
                actr[1] += 1
                st["sbk"] = sbk
                kl, kp, n0, N = st["kl"], st["kp"], st["n0"], st["N"]
                K.pe([lambda: nc.tensor.matmul(PB[sbk].ap[0:kp, 0:N], Kblk[slot].ap[0:96, 128 * kl:128 * kl + kp], qh.ap[0:96, hl, qc0 + n0:qc0 + n0 + N], start=True, stop=True)],
                     [Kblk[slot], qh], [PB[sbk]])

            def emit_PV(st, first):
                slot, sbk, kl, kp, n0, N = st["slot"], st["sbk"], st["kl"], st["kp"], st["n0"], st["N"]
                pt = PT.next()
                K.op("act", lambda: nc.scalar.activation(out=pt.ap[0:kp, 0:N], in_=PB[sbk].ap[0:kp, 0:N], func=AF.Exp, scale=SCALE), [PB[sbk]], [pt])
                if st["isdiag"]:
                    K.op("pool", lambda: nc.gpsimd.memset(pt.ap[64:128, 0:64], 0.0), [], [pt])
                K.pe([lambda: nc.tensor.matmul(PB[ob].ap[0:65, n0:n0 + N], Vblk[slot].ap[0:kp, kl, :], pt.ap[0:kp, 0:N], start=first, stop=True, skip_group_check=(not first))],
                     [Vblk[slot], pt], [PB[ob]])

            emit_S(steps[0])
            for j in range(len(steps)):
                if j + 1 < len(steps):
                    emit_S(steps[j + 1])
                emit_PV(steps[j], j == 0)
            if between is not None:
                between(h)
            ot = OTs.next()
            K.op("dve", lambda ot=ot, ob=ob: nc.vector.tensor_copy(out=ot.ap[0:64, 0:nq], in_=PB[ob].ap[0:64, 0:nq]), [PB[ob]], [ot])
            K.op("dve", lambda ot=ot, ob=ob: nc.vector.tensor_copy(out=ot.ap[64:65, 0:nq], in_=PB[ob].ap[64:65, 0:nq]), [PB[ob]], [ot])
            c0 = 0
            for (po, nqq, sub) in dsts:
                K.pe([lambda ot=ot, c0=c0, nqq=nqq: nc.tensor.transpose(out=PB[3].ap[0:nqq, 0:65], in_=ot.ap[0:65, c0:c0 + nqq], identity=ident_f.ap[0:65, 0:65])], [ot, ident_f], [PB[3]])
                rc = stat.ap[0:nqq, 32 + (actr[2] % 16):33 + (actr[2] % 16)]
                actr[2] += 1
                K.op("dve", lambda rc=rc, nqq=nqq: nc.vector.reciprocal(out=rc, in_=PB[3].ap[0:nqq, 64:65]), [PB[3]], [stat])
                K.op("dve", lambda rc=rc, po=po, nqq=nqq, sub=sub, h=h: nc.vector.tensor_scalar(out=attn.ap[po:po + nqq, sub, 64 * h:64 * h + 64], in0=PB[3].ap[0:nqq, 0:64], scalar1=rc, scalar2=None, op0=ALU.mult), [PB[3], stat], [attn])
                c0 += nqq
    actr = [0, 0, 0]

    class Stop(Exception):
        pass

    def chk(n):
        if stage == n:
            raise Stop()

    def token_tile(kind, ti):
        if kind == "p":
            nsub, x_src, cs_src = 4, xp[512 * ti:512 * ti + 512, :], cs_p[512 * ti:512 * ti + 512, :]
            y_dst, lat_dst, kr_dst = yp[512 * ti:512 * ti + 512, :], latp[512 * ti:512 * ti + 512, :], krp[512 * ti:512 * ti + 512, :]
        else:
            nsub, x_src, cs_src = 2, xs, cs_s
            y_dst, lat_dst, kr_dst = ys, lats, krs
        TT = 128 * nsub
        NC = TT // 8
        K.dma("sp", xch, xt.ap[:, 0:nsub, :], x_src.rearrange("(s p) d -> p s d", p=128), writes=[xt])
        K.dma("sp", csch, cst.ap[:, 0:nsub, :], cs_src.rearrange("(s p) d -> p s d", p=128), writes=[cst])
        for sub in range(nsub):
            r = rms_stats(xt.ap[:, sub, :], [xt], D, sub)
            xb = xnb.next()
            K.op("dve", lambda sub=sub, xb=xb, r=r: nc.vector.tensor_scalar(out=xb.ap, in0=xt.ap[:, sub, :], scalar1=r, scalar2=None, op0=ALU.mult), [xt, stat], [xb])
            transposes_to(actT, 0, xb, 8, sub, 0)
        chk(2)
        wc = R.get("winc0", "winc1", "winc2")
        for sub in range(nsub):
            fns = []
            for k in range(8):
                wvw = wc[k // 3][1]
                kl = k % 3
                for (c0, c1) in ((0, 512), (512, 1024), (1024, 1056)):
                    fns.append(lambda k=k, kl=kl, wvw=wvw, c0=c0, c1=c1, sub=sub: nc.tensor.matmul(
                        ps[:, c0:c1], actT.ap[:, k, 128 * sub:128 * sub + 128], wvw[:, kl, c0:c1], start=(k == 0), stop=(k == 7)))
            K.pe(fns, [actT, wc[0][0], wc[1][0], wc[2][0]], [PB[0], PB[1], PB[2]])
            r = rms_stats(ps[:, 0:QL], [PB[0], PB[1]], QL, 8 + sub)
            cq = cqn.next()
            K.op("dve", lambda cq=cq, r=r: nc.vector.tensor_scalar(out=cq.ap, in0=ps[:, 0:QL], scalar1=r, scalar2=None, op0=ALU.mult), [PB[0], PB[1], stat], [cq])
            r2 = rms_stats(ps[:, QL:QL + KVL], [PB[1]], KVL, 12 + sub)
            ko = kvout[kvo_i[0] % 2]; koc = kvoch[kvo_i[0] % 2]; kvo_i[0] += 1
            kvt = kvtok.next()
            K.op("dve", lambda ko=ko, r2=r2: nc.vector.scalar_tensor_tensor(out=ko.ap[:, 0:KVL], in0=ps[:, QL:QL + KVL], scalar=r2, in1=gkv_bc.ap, op0=ALU.mult, op1=ALU.mult), [PB[1], stat, gkv_bc], [ko])
            x1, x2 = ps[:, 1024:1040], ps[:, 1040:1056]
            cs_, sn_ = cst.ap[:, sub, 0:16], cst.ap[:, sub, 16:32]
            rt = rtmp.ap[:, 0, :]
            K.op("dve", lambda: nc.vector.tensor_tensor(out=rt[:, 0:16], in0=x1, in1=cs_, op=ALU.mult), [PB[2], cst], [rtmp])
            K.op("dve", lambda: nc.vector.tensor_tensor(out=rt[:, 16:32], in0=x2, in1=sn_, op=ALU.mult), [PB[2], cst], [rtmp])
            K.op("dve", lambda: nc.vector.tensor_tensor(out=rt[:, 32:48], in0=x1, in1=sn_, op=ALU.mult), [PB[2], cst], [rtmp])
            K.op("dve", lambda: nc.vector.tensor_tensor(out=rt[:, 48:64], in0=x2, in1=cs_, op=ALU.mult), [PB[2], cst], [rtmp])
            K.op("dve", lambda ko=ko: nc.vector.tensor_tensor(out=ko.ap[:, 256:272], in0=rt[:, 0:16], in1=rt[:, 16:32], op=ALU.subtract), [rtmp], [ko])
            K.op("dve", lambda ko=ko: nc.vector.tensor_tensor(out=ko.ap[:, 272:288], in0=rt[:, 32:48], in1=rt[:, 48:64], op=ALU.add), [rtmp], [ko])
            K.op("pool", lambda kvt=kvt, ko=ko: nc.gpsimd.tensor_copy(out=kvt.ap[:, 0:256], in_=ko.ap[:, 0:256]), [ko], [kvt])
            K.op("pool", lambda kvt=kvt, ko=ko: nc.gpsimd.tensor_copy(out=kvt.ap[:, 320:352], in_=ko.ap[:, 256:288]), [ko], [kvt])
            K.op("pool", lambda kvt=kvt: nc.gpsimd.memset(kvt.ap[:, 256:320], 0.0), [], [kvt])
            K.dma("pool", koc, lat_dst[128 * sub:128 * sub + 128, :], ko.ap[:, 0:256], reads=[ko])
            K.dma("pool", koc, kr_dst[128 * sub:128 * sub + 128, :], ko.ap[:, 256:288], reads=[ko], batch=True)
            kv_from_tok(kvt, sub, TT)
            transposes_to(cqnT, 0, cq, 6, sub, 16)
        chk(3)
        wqs = R.get("wq0", "wq1")
        for sub in range(nsub):
            fns = []
            for k in range(6):
                wvw = wqs[k // 3][1]
                kl = k % 3
                for (c0, c1) in ((0, 512), (512, 768)):
                    fns.append(lambda k=k, kl=kl, wvw=wvw, c0=c0, c1=c1, sub=sub: nc.tensor.matmul(
                        ps[:, c0:c1], cqnT.ap[:, k, 128 * sub:128 * sub + 128], wvw[:, kl, c0:c1], start=(k == 0), stop=(k == 5)))
            K.pe(fns, [cqnT, wqs[0][0], wqs[1][0]], [PB[0], PB[1]])
            K.op("act", lambda: nc.scalar.activation(out=qsb.ap[:, 0:512], in_=ps[:, 0:512], func=AF.Copy), [PB[0]], [qsb])
            K.op("dve", lambda: nc.vector.tensor_copy(out=qsb.ap[:, 512:768], in_=ps[:, 512:768]), [PB[1]], [qsb])
            qv = qsb.ap.rearrange("p (h c) -> p h c", c=96)
            qk = qtok.next()
            K.op("pool", lambda qk=qk: nc.gpsimd.tensor_copy(out=qk.ap[:, :, 0:64], in_=qv[:, :, 0:64]), [qsb], [qk])
            cs8 = cst.ap[:, sub, 0:16].unsqueeze(1).to_broadcast([128, 8, 16])
            sn8 = cst.ap[:, sub, 16:32].unsqueeze(1).to_broadcast([128, 8, 16])
            q1, q2 = qv[:, :, 64:80], qv[:, :, 80:96]
            rt4 = rtmp.ap.rearrange("p h (a c) -> p h a c", a=4)
            K.op("dve", lambda: nc.vector.tensor_tensor(out=rt4[:, :, 0, :], in0=q1, in1=cs8, op=ALU.mult), [qsb, cst], [rtmp])
            K.op("dve", lambda: nc.vector.tensor_tensor(out=rt4[:, :, 1, :], in0=q2, in1=sn8, op=ALU.mult), [qsb, cst], [rtmp])
            K.op("dve", lambda: nc.vector.tensor_tensor(out=rt4[:, :, 2, :], in0=q1, in1=sn8, op=ALU.mult), [qsb, cst], [rtmp])
            K.op("dve", lambda: nc.vector.tensor_tensor(out=rt4[:, :, 3, :], in0=q2, in1=cs8, op=ALU.mult), [qsb, cst], [rtmp])
            K.op("dve", lambda qk=qk: nc.vector.tensor_tensor(out=qk.ap[:, :, 64:80], in0=rt4[:, :, 0, :], in1=rt4[:, :, 1, :], op=ALU.subtract), [rtmp], [qk])
            K.op("dve", lambda qk=qk: nc.vector.tensor_tensor(out=qk.ap[:, :, 80:96], in0=rt4[:, :, 2, :], in1=rt4[:, :, 3, :], op=ALU.add), [rtmp], [qk])
            fns = [lambda h=h, qk=qk: nc.tensor.transpose(out=pbf(3)[0:96, 128 * h:128 * h + 128], in_=qk.ap[:, h, :], identity=ident_b.ap) for h in range(NH)]
            K.pe(fns, [qk, ident_b], [PB[3]])
            for hh in range(2):
                for (p0, p1) in ((0, 64), (64, 96)):
                    K.op("act", lambda sub=sub, hh=hh, p0=p0, p1=p1: nc.scalar.activation(out=qTh[hh].ap[p0:p1, :, 128 * sub:128 * sub + 128],
                                                                                           in_=pbf(3)[p0:p1, 512 * hh:512 * hh + 512].rearrange("p (h c) -> p h c", h=4), func=AF.Copy), [PB[3]], [qTh[hh]])
        chk(4)
        (wkv_t, wkv_v), = R.get("wkv")
        kv_build(wkv_t, wkv_v, nsub)
        if kind == "p":
            reg = kvreg[0][ti]
            K.dma("pool", ktsch, ktc[0][:, :, 512 * ti:512 * ti + 512].rearrange("h r n -> r h n"), KTs.ap[0:96, :, :], reads=[KTs], writes=[reg])
            K.dma("pool", vsch, vc[0][:, :, 4 * ti:4 * ti + 4, :].rearrange("h p s c -> p h s c"), Vs.ap[:, :, :, :], reads=[Vs], writes=[reg])
        else:
            for s in range(NSEQ_S):
                reg = kvreg[s + 1][NPT_S]
                sub, po = s // 2, 64 * (s % 2)
                K.dma("pool", ktsch, ktc[s + 1][:, :, PAST:PAST + 64].rearrange("h r n -> r h n"), KTs.ap[0:96, :, 64 * s:64 * s + 64], reads=[KTs], writes=[reg], batch=(s > 0))
                K.dma("pool", vsch, vc[s + 1][:, 0:64, NKT_S - 1, :].rearrange("h p c -> p h c"), Vs.ap[po:po + 64, :, sub, :], reads=[Vs], writes=[reg], batch=(s > 0))
                K.dma("pool", vsch, vc[s + 1][:, 64:128, NKT_S - 1, :].rearrange("h p c -> p h c"), Vs.ap[64 - po:128 - po, :, sub, :], reads=[Vs], writes=[reg], batch=True)
        (wu_t, wu_v), = R.get("winu")
        for m in range(4):
            b = 4 + m % 2
            fns = [lambda k=k, m=m, b=b: nc.tensor.matmul(PB[b].ap[:, 0:TT], wu_v[:, k, 128 * m:128 * m + 128], actT.ap[:, k, 0:TT], start=(k == 0), stop=(k == 7)) for k in range(8)]
            K.pe(fns, [wu_t, actT], [PB[b]])
            K.op("act", lambda m=m, b=b: nc.scalar.activation(out=uT.ap[:, m, 0:TT], in_=PB[b].ap[:, 0:TT], func=AF.Copy), [PB[b]], [uT])
        chk(5)
        if kind == "p":
            attention(0, 4 * ti + 4, 0, 512, ti, False, [(0, 128, s_) for s_ in range(4)])
        else:
            for s in range(NSEQ_S):
                attention(s + 1, PAST // 128, 64 * s, 64, None, True, [(64 * (s % 2), 64, s // 2)])
        chk(6)
        for sub in range(nsub):
            r = rms_stats(attn.ap[:, sub, :], [attn], 512, 16 + sub)
            xb = xnb.next()
            K.op("dve", lambda sub=sub, xb=xb, r=r: nc.vector.tensor_scalar(out=xb.ap[:, 0:512], in0=attn.ap[:, sub, :], scalar1=r, scalar2=None, op0=ALU.mult), [attn, stat], [xb])
            transposes_to(actT, 0, xb, 4, sub, 22)
        ssm(kind, ti, nsub, TT, NC)
        chk(7)
        wo = R.get("wout0", "wout1")
        for sub in range(nsub):
            fns = []
            for k in range(8):
                wvw = wo[k // 4][1]
                kl = k % 4
                for (c0, c1) in ((0, 512), (512, 1024)):
                    fns.append(lambda k=k, kl=kl, wvw=wvw, c0=c0, c1=c1, sub=sub: nc.tensor.matmul(
                        ps[:, c0:c1], actT.ap[:, k, 128 * sub:128 * sub + 128], wvw[:, kl, c0:c1], start=(k == 0), stop=(k == 7)))
            K.pe(fns, [actT, wo[0][0], wo[1][0]], [PB[0], PB[1]])
            K.op("dve", lambda sub=sub: nc.vector.tensor_tensor(out=xt.ap[:, sub, :], in0=ps[:, 0:1024], in1=xt.ap[:, sub, :], op=ALU.add), [PB[0], PB[1], xt], [xt])
        chk(8)
        for sub in range(nsub):
            r = rms_stats(xt.ap[:, sub, :], [xt], D, 20 + sub)
            xb = xnb.next()
            K.op("dve", lambda sub=sub, xb=xb, r=r: nc.vector.tensor_scalar(out=xb.ap, in0=xt.ap[:, sub, :], scalar1=r, scalar2=None, op0=ALU.mult), [xt, stat], [xb])
            transposes_to(actT, 0, xb, 8, sub, 8)
        def mlp_up(fc):
            (wu_t, wu_v), (wd_t, wd_v) = R.get("wup%d" % fc, "wdn%d" % fc, hold=(2 if fc > 0 else 0))
            a = aTr.next()
            for ft in range(4):
                b = 4 + ft % 2
                fns = [lambda k=k, ft=ft, b=b: nc.tensor.matmul(PB[b].ap[:, 0:TT], wu_v[:, k, 128 * ft:128 * ft + 128], actT.ap[:, k, 0:TT], start=(k == 0), stop=(k == 7)) for k in range(8)]
                K.pe(fns, [wu_t, actT], [PB[b]])
                tr = trl.next()
                K.op("dve", lambda b=b, tr=tr: nc.vector.tensor_scalar(out=tr.ap[:, 0:TT], in0=PB[b].ap[:, 0:TT], scalar1=0.0, scalar2=None, op0=ALU.max), [PB[b]], [tr])
                K.op("pool", lambda ft=ft, tr=tr, a=a: nc.gpsimd.tensor_tensor(out=a.ap[:, ft, 0:TT], in0=tr.ap[:, 0:TT], in1=tr.ap[:, 0:TT], op=ALU.mult), [tr], [a])
            return a, wd_t, wd_v

        def mlp_down(a, wd_t, wd_v):
            for sub in range(nsub):
                bb = (0, 1) if sub % 2 == 0 else (6, 7)
                base = 512 * bb[0]
                fns = []
                for ft in range(4):
                    for hh in range(2):
                        fns.append(lambda ft=ft, hh=hh, sub=sub, base=base: nc.tensor.matmul(
                            ps[:, base + 512 * hh:base + 512 * hh + 512], a.ap[:, ft, 128 * sub:128 * sub + 128], wd_v[:, ft, 512 * hh:512 * hh + 512], start=(ft == 0), stop=(ft == 3)))
                K.pe(fns, [a, wd_t], [PB[bb[0]], PB[bb[1]]])
                K.op("dve", lambda sub=sub, base=base: nc.vector.tensor_tensor(out=xt.ap[:, sub, :], in0=ps[:, base:base + 1024], in1=xt.ap[:, sub, :], op=ALU.add), [PB[bb[0]], PB[bb[1]], xt], [xt])

        cur = mlp_up(0)
        for fc in range(8):
            nxt = mlp_up(fc + 1) if fc + 1 < 8 else None
            mlp_down(*cur)
            cur = nxt
        for sub in range(nsub):
            r = rms_stats(xt.ap[:, sub, :], [xt], D, 24 + sub)
            K.op("dve", lambda sub=sub, r=r: nc.vector.scalar_tensor_tensor(out=xt.ap[:, sub, :], in0=xt.ap[:, sub, :], scalar=r, in1=gfin_bc.ap, op0=ALU.mult, op1=ALU.mult), [xt, stat, gfin_bc], [xt])
        K.dma("pool", ych, y_dst.rearrange("(s p) d -> p s d", p=128), xt.ap[:, 0:nsub, :], reads=[xt])

    def ssm(kind, ti, nsub, TT, NC):
        uTc = uT.ap[:, :, 0:TT].rearrange("p m (c i) -> p m i c", i=8)
        Sv = Ssb.ap.rearrange("p (kk r) t c -> p r kk t c", r=4)
        for r in range(4):
            bank = 4 + r
            for kk in range(4):
                for reim in range(2):
                    c0 = (kk * 2 + reim) * NC
                    fns = [lambda i=i, kk=kk, r=r, reim=reim, bank=bank, c0=c0: nc.tensor.matmul(
                        PB[bank].ap[:, c0:c0 + NC], W1.ap[32 * r:32 * r + 32, kk, reim, i, :], uTc[32 * r:32 * r + 32, kk, i, :],
                        start=(i == 0), stop=(i == 7), tile_position=(32 * r, 0)) for i in range(8)]
                    K.pe(fns, [W1, uT], [PB[bank]])
            K.op("act", lambda r=r, bank=bank: nc.scalar.activation(
                out=Sv[:, r, :, :, 0:NC], in_=PB[bank].ap[:, 0:8 * NC].rearrange("p (a t c) -> p a t c", a=4, t=2), func=AF.Copy), [PB[bank]], [Ssb])
        for c in range(NC):
            if kind == "p":
                prev_t, prev = (Hc, Hc.ap) if c == 0 else (Ssb, Ssb.ap[:, :, :, c - 1])
            else:
                if c % 8 == 0:
                    prev_t, prev = H0s, H0s.ap[:, c // 8, :, :]
                else:
                    prev_t, prev = Ssb, Ssb.ap[:, :, :, c - 1]
            pre = prev[:, :, 0:1].to_broadcast([128, 16, 2])
            pim = prev[:, :, 1:2].to_broadcast([128, 16, 2])
            if c % 8 == 0 or kind != "p" or True:
                pass
            K.op("pool", lambda c=c, prev=prev: nc.gpsimd.tensor_copy(out=Hbf.ap[:, :, :, c], in_=prev), [prev_t], [Hbf])
            K.op("dve", lambda pre=pre: nc.vector.tensor_tensor(out=st1.ap, in0=A8.ap, in1=pre, op=ALU.mult), [A8, prev_t], [st1])
            K.op("dve", lambda pim=pim: nc.vector.tensor_tensor(out=st2.ap, in0=B8.ap, in1=pim, op=ALU.mult), [B8, prev_t], [st2])
            K.op("dve", lambda: nc.vector.tensor_tensor(out=st1.ap, in0=st1.ap, in1=st2.ap, op=ALU.add), [st1, st2], [st1])
            K.op("dve", lambda c=c: nc.vector.tensor_tensor(out=Ssb.ap[:, :, :, c], in0=Ssb.ap[:, :, :, c], in1=st1.ap, op=ALU.add), [Ssb, st1], [Ssb])
        if kind == "p":
            K.op("dve", lambda: nc.vector.tensor_copy(out=Hc.ap, in_=Ssb.ap[:, :, :, NC - 1]), [Ssb], [Hc])
            if ti == NT_P - 1:
                K.op("dve", lambda: nc.vector.tensor_copy(out=hout.ap, in_=Ssb.ap[:, :, :, NC - 1]), [Ssb], [hout])
                for g2 in range(2):
                    K.dma("pool", hoch, hrp.rearrange("(k two) n -> two n k", two=2)[g2], hout.ap[64 * g2:64 * g2 + 64, :, 0], reads=[hout], batch=(g2 > 0))
                    K.dma("pool", hoch, hip.rearrange("(k two) n -> two n k", two=2)[g2], hout.ap[64 * g2:64 * g2 + 64, :, 1], reads=[hout], batch=True)
        else:
            for s in range(NSEQ_S):
                for g2 in range(2):
                    K.dma("pool", hoch, hrs[s].rearrange("(k two) n -> two n k", two=2)[g2], Ssb.ap[64 * g2:64 * g2 + 64, :, 0, 8 * s + 7], reads=[Ssb], batch=(s + g2 > 0))
                    K.dma("pool", hoch, his[s].rearrange("(k two) n -> two n k", two=2)[g2], Ssb.ap[64 * g2:64 * g2 + 64, :, 1, 8 * s + 7], reads=[Ssb], batch=True)
        (wg_t, wg_v), = R.get("wglu")
        for kk in range(4):
            b = 6 + kk % 2
            yv = PB[b].ap[:, 0:TT].rearrange("p (c i) -> p i c", i=8)
            fns = [lambda kk=kk, b=b: nc.tensor.matmul(PB[b].ap[:, 0:TT], Dd.ap[:, kk, :], uT.ap[:, kk, 0:TT], start=True, stop=True)]
            for j in range(8):
                for i in range(j + 1):
                    fns.append(lambda kk=kk, j=j, i=i, yv=yv: nc.tensor.matmul(yv[:, j, :], BD.ap[:, kk, j - i, :], uTc[:, kk, i, :], start=False, stop=True, skip_group_check=True))
            for r in range(4):
                kp = 4 * kk + r
                for j in range(8):
                    for reim in range(2):
                        lastone = (r == 3 and j == 7 and reim == 1)
                        fns.append(lambda kp=kp, r=r, j=j, reim=reim, yv=yv, lastone=lastone: nc.tensor.matmul(
                            yv[32 * r:32 * r + 32, j, :], W2.ap[:, kp, reim, j, :], Hbf.ap[:, kp, reim, 0:NC], start=False, stop=True, skip_group_check=True, tile_position=(0, 32 * r)))
            K.pe(fns, [Dd, uT, BD, W2, Hbf], [PB[b]])
            yp_ = PB[b].ap[:, 0:TT]
            K.op("act", lambda yp_=yp_: nc.scalar.activation(out=tA.ap[:, 0:TT], in_=yp_, func=AF.Square), [PB[b]], [tA])
            K.op("dve", lambda: nc.vector.tensor_scalar(out=tA.ap[:, 0:TT], in0=tA.ap[:, 0:TT], scalar1=0.044715, scalar2=1.0, op0=ALU.mult, op1=ALU.add), [tA], [tA])
            K.op("dve", lambda yp_=yp_: nc.vector.tensor_tensor(out=tA.ap[:, 0:TT], in0=yp_, in1=tA.ap[:, 0:TT], op=ALU.mult), [PB[b], tA], [tA])
            K.op("act", lambda: nc.scalar.activation(out=tA.ap[:, 0:TT], in_=tA.ap[:, 0:TT], func=AF.Sigmoid, scale=1.5957691216), [tA], [tA])
            K.op("dve", lambda kk=kk, yp_=yp_: nc.vector.tensor_tensor(out=y2.ap[:, kk, 0:TT], in0=yp_, in1=tA.ap[:, 0:TT], op=ALU.mult), [PB[b], tA], [y2])
            K.op("pool", lambda kk=kk: nc.gpsimd.tensor_copy(out=y2b.ap[:, kk, 0:TT], in_=y2.ap[:, kk, 0:TT]), [y2], [y2b])
        for m in range(4):
            b = 4 + m % 2
            fns = [lambda k=k, m=m, b=b: nc.tensor.matmul(PB[b].ap[:, 0:TT], wg_v[:, k, 128 * m:128 * m + 128], y2b.ap[:, k, 0:TT], start=(k == 0), stop=(k == 3)) for k in range(4)]
            K.pe(fns, [wg_t, y2b], [PB[b]])
            K.op("act", lambda b=b: nc.scalar.activation(out=tB.ap[:, 0:TT], in_=PB[b].ap[:, 0:TT], func=AF.Sigmoid), [PB[b]], [tB])
            K.op("dve", lambda m=m: nc.vector.tensor_tensor(out=y2.ap[:, m, 0:TT], in0=y2.ap[:, m, 0:TT], in1=tB.ap[:, 0:TT], op=ALU.mult), [y2, tB], [y2])
            K.op("pool", lambda m=m: nc.gpsimd.tensor_tensor(out=sqb.ap[:, m, 0:TT], in0=y2.ap[:, m, 0:TT], in1=y2.ap[:, m, 0:TT], op=ALU.mult), [y2], [sqb])
        fns = [lambda m=m: nc.tensor.matmul(PB[6].ap[:, 0:TT], ones_b.ap, sqb.ap[:, m, 0:TT], start=(m == 0), stop=(m == 3)) for m in range(4)]
        K.pe(fns, [ones_b, sqb], [PB[6]])
        K.op("act", lambda: nc.scalar.activation(out=rbc.ap[:, 0:TT], in_=PB[6].ap[:, 0:TT], func=AF.Sqrt, scale=1.0 / 512, bias=EPS), [PB[6]], [rbc])
        K.op("dve", lambda: nc.vector.reciprocal(out=rbc.ap[:, 0:TT], in_=rbc.ap[:, 0:TT]), [rbc], [rbc])
        for m in range(4):
            K.op("dve", lambda m=m: nc.vector.scalar_tensor_tensor(out=actT.ap[:, 4 + m, 0:TT], in0=y2.ap[:, m, 0:TT], scalar=gcols.ap[:, 26 + m:27 + m], in1=rbc.ap[:, 0:TT], op0=ALU.mult, op1=ALU.mult), [y2, gcols, rbc], [actT])

    kvbufs = list(kvtok.items)
    for t_ in cqn.items:
        kvbufs.append(T(t_.ap[:, 0:352], share=t_))
    for t_ in qtok.items:
        kvbufs.append(T(t_.ap.rearrange("p h c -> p (h c)")[:, 0:352], share=t_))
    kvl_items = [(s, pt_i, sub) for s in range(NSEQ_S) for pt_i in range(NPT_S) for sub in range(4)]
    kvl_state = {"loaded": 0}
    kvl_ch = [K.chan("kvl%d" % i) for i in range(len(kvbufs))]

    def kvl_prefetch(upto):
        while kvl_state["loaded"] < min(len(kvl_items), upto):
            i_ = kvl_state["loaded"]
            s, pt_i, sub = kvl_items[i_]
            kvt = kvbufs[i_ % len(kvbufs)]
            ch = kvl_ch[i_ % len(kvbufs)]
            r0 = 512 * pt_i + 128 * sub
            K.op("pool", lambda kvt=kvt: nc.gpsimd.memset(kvt.ap[:, 256:320], 0.0), [], [kvt])
            K.dma("pool", ch, kvt.ap[:, 0:256], ckl[s, r0:r0 + 128, :], writes=[kvt])
            K.dma("pool", ch, kvt.ap[:, 320:352], ckr[s, r0:r0 + 128, :], writes=[kvt], batch=True)
            kvl_state["loaded"] += 1

    def kv_from_cache(s, pt_i):
        base = (s * NPT_S + pt_i) * 4
        for sub in range(4):
            kvl_prefetch(base + sub + 5)
            kvt = kvbufs[(base + sub) % len(kvbufs)]
            kv_from_tok(kvt, sub, 512)
        (wkv_t, wkv_v), = R.get("wkv")
        kv_build(wkv_t, wkv_v, 4)
        reg = kvreg[s + 1][pt_i]
        K.dma("pool", ktsch, ktc[s + 1][:, :, 512 * pt_i:512 * pt_i + 512].rearrange("h r n -> r h n"), KTs.ap[0:96, :, :], reads=[KTs], writes=[reg])
        K.dma("pool", vsch, vc[s + 1][:, :, 4 * pt_i:4 * pt_i + 4, :].rearrange("h p s c -> p h s c"), Vs.ap[:, :, :, :], reads=[Vs], writes=[reg])

    K.op("pool", lambda: nc.gpsimd.memset(Hc.ap, 0.0), [], [Hc])
    for ti in range(NT_P):
        R.add(plan_tok)
    for s in range(NSEQ_S * NPT_S):
        R.add(plan_kv)
    R.add(plan_tok)
    try:
        for ti in range(NT_P):
            token_tile("p", ti)
        chk(9)
        for s in range(NSEQ_S):
            for pt_i in range(NPT_S):
                kv_from_cache(s, pt_i)
        chk(10)
        token_tile("s", 0)
    except Stop:
        pass
    K.finish("pool")
    return nc, K


_CACHE = {}


def _rope_table(pos):
    half = 16
    inv_freq = (10000.0 ** (-(np.arange(half, dtype=np.float32) * 2.0) / 32)).astype(np.float32)
    ang = pos.astype(np.float32)[:, None] * inv_freq[None, :]
    return np.concatenate([np.cos(ang), np.sin(ang)], axis=1).astype(np.float32)


def run(inputs, SEQ, PAST, ncores=8):
    key = (SEQ, PAST)
    if key not in _CACHE:
        _CACHE[key] = build(SEQ, PAST)
    nc, K = _CACHE[key]
    f = lambda a: np.ascontiguousarray(np.asarray(a, dtype=np.float32))
    x_prompt = f(inputs["x_prompt"]); x_sample = f(inputs["x_sample"])
    ckl = f(inputs["cache_kv_latent"])[0]; ckr = f(inputs["cache_k_rope"])[0]
    sre = f(inputs["state_ssm_re"])[0]; sim = f(inputs["state_ssm_im"])[0]
    cs_p = _rope_table(np.arange(SEQ))
    cs_s = np.tile(_rope_table(PAST + np.arange(DSEQ)), (NSEQ_S, 1))
    ident = np.eye(128, dtype=np.float32)
    shared = {
        "g_mix": f(inputs["g_mix"]), "w_in": f(inputs["w_in"])[0], "g_q": f(inputs["g_q_a"]), "w_q": f(inputs["w_q_up"])[0],
        "g_kv": f(inputs["g_kv_a"]), "w_kv": f(inputs["w_kv_up"])[0], "a_re": f(inputs["a_re"])[0], "a_im": f(inputs["a_im"])[0],
        "lstep": f(inputs["log_step"]), "b_re": f(inputs["b_re"])[0], "b_im": f(inputs["b_im"])[0],
        "c_re": f(inputs["c_re"])[0], "c_im": f(inputs["c_im"])[0], "d_skip": f(inputs["d_skip"]), "w_glu": f(inputs["w_glu"])[0],
        "g_attn": f(inputs["g_attn_out"]), "g_ssm": f(inputs["g_ssm_out"]), "w_out": f(inputs["w_out"])[0],
        "g_mlp": f(inputs["g_mlp"]), "w_up": f(inputs["w_up"])[0], "w_down": f(inputs["w_down"])[0],
        "g_fin": f(inputs["g_final"]).reshape(1, D), "cs_p": cs_p, "cs_s": cs_s, "ident": ident,
    }
    in_maps = []
    for c in range(ncores):
        m = dict(shared)
        m["xp"] = x_prompt[c]
        sl = slice(NSEQ_S * c, NSEQ_S * (c + 1))
        m["xs"] = np.ascontiguousarray(x_sample[sl].reshape(NSEQ_S * DSEQ, D))
        m["ckl"] = np.ascontiguousarray(ckl[sl]); m["ckr"] = np.ascontiguousarray(ckr[sl])
        m["sre"] = np.ascontiguousarray(sre[sl]); m["sim"] = np.ascontiguousarray(sim[sl])
        in_maps.append(m)
    res = run_bass_kernel_spmd(nc, in_maps, core_ids=list(range(ncores)))
    rs = res.results
    cat = lambda k: np.stack([np.asarray(r[k], dtype=np.float32) for r in rs])
    y_prompt = cat("yp")
    y_sample = cat("ys").reshape(ncores * NSEQ_S, DSEQ, D)
    lat_p = cat("latp")[None]
    kr_p = cat("krp")[None]
    hr_p = cat("hrp")[None]
    hi_p = cat("hip")[None]
    lat_s = cat("lats").reshape(ncores * NSEQ_S, DSEQ, KVL)[None]
    kr_s = cat("krs").reshape(ncores * NSEQ_S, DSEQ, 32)[None]
    hr_s = cat("hrs").reshape(ncores * NSEQ_S, 32, 64)[None]
    hi_s = cat("his").reshape(ncores * NSEQ_S, 32, 64)[None]
    return (y_prompt, y_sample, lat_p, kr_p, hr_p, hi_p, lat_s, kr_s, hr_s, hi_s)


def kernel(**inputs):
    return run(inputs, 8192, 4096, 8)
```

```python
import math
import numpy as np
import ml_dtypes
import concourse.bass as bass
import concourse.mybir as mybir
from concourse.bass_utils import run_bass_kernel_spmd

F32 = mybir.dt.float32
BF16 = mybir.dt.bfloat16
I32 = mybir.dt.int32
AF = mybir.ActivationFunctionType
ALU = mybir.AluOpType
AX = mybir.AxisListType

D = 1024
DIN = 1568
QL = 768
KVL = 256
NH = 8
DFF = 4096
EPS = 1e-6
SCALE = 96 ** -0.5
NSEQ_S = 4
DSEQ = 64
TWO_PI = 2.0 * math.pi


class T:
    def __init__(self, ap, name="", share=None):
        self.ap = ap
        self.name = name
        self.d = share.d if share is not None else {"w": None, "r": {}}

    @property
    def w(self):
        return self.d["w"]

    @w.setter
    def w(self, v):
        self.d["w"] = v

    @property
    def r(self):
        return self.d["r"]

    @r.setter
    def r(self, v):
        self.d["r"] = v


class Chan:
    def __init__(self, nc, name):
        self.sem = nc.alloc_semaphore(name)
        self.total = 0
        self.key = name
        self.holder = [0]


class Trk:
    def __init__(self, nc):
        self.nc = nc
        self.E = {"pe": nc.tensor, "act": nc.scalar, "dve": nc.vector, "pool": nc.gpsimd, "sp": nc.sync}
        self.sem = {}
        self.cnt = {}
        self.gen = {}
        for e in ("pe", "act", "dve", "pool"):
            self.gen[e] = 0
            self._newsem(e)
        self.seen = {}
        self.chans = []
        self.ninst = {e: 0 for e in self.E}

    def _newsem(self, e):
        self.sem[e] = self.nc.alloc_semaphore("c_%s_%d" % (e, self.gen[e]))
        self.cnt[e] = 0
        self.gen[e] += 1

    def chan(self, name):
        c = Chan(self.nc, name)
        self.chans.append(c)
        return c

    def _wait(self, eng, ev):
        if ev is None:
            return
        sem, key, val = ev
        if isinstance(val, list):
            val = val[0]
        k = (eng, key)
        if self.seen.get(k, 0) >= val:
            return
        self.E[eng].wait_ge(sem, val)
        self.ninst[eng] += 1
        self.seen[k] = val

    def _deps(self, eng, reads, writes, skip_same=False):
        for t in reads:
            if t.w is not None and not (skip_same and t.w[1][0] == eng):
                self._wait(eng, t.w)
        for t in writes:
            if t.w is not None and not (skip_same and t.w[1][0] == eng):
                self._wait(eng, t.w)
            for ev in t.r.values():
                if not (skip_same and ev[1][0] == eng):
                    self._wait(eng, ev)

    def _done(self, ev, reads, writes):
        for t in reads:
            t.r[ev[1]] = ev
        for t in writes:
            t.w = ev
            t.r = {}

    def op(self, eng, fn, reads=(), writes=()):
        self._deps(eng, reads, writes)
        inst = fn()
        if self.cnt[eng] >= 60000:
            self._newsem(eng)
        self.cnt[eng] += 1
        inst.then_inc(self.sem[eng], 1)
        self.ninst[eng] += 1
        ev = (self.sem[eng], (eng, self.gen[eng]), self.cnt[eng])
        self._done(ev, reads, writes)

    def pe(self, fns, reads=(), writes=()):
        self._deps("pe", reads, writes, skip_same=True)
        inst = None
        for f in fns:
            inst = f()
            self.ninst["pe"] += 1
        if self.cnt["pe"] >= 60000:
            self._newsem("pe")
        self.cnt["pe"] += 1
        inst.then_inc(self.sem["pe"], 1)
        ev = (self.sem["pe"], ("pe", self.gen["pe"]), self.cnt["pe"])
        self._done(ev, reads, writes)

    def dma(self, q, ch, out_ap, in_ap, reads=(), writes=(), batch=False, **kw):
        self._deps(q, reads, writes)
        if not batch:
            if ch.total > 0:
                self._wait(q, (ch.sem, ch.key, ch.total))
            ch.holder = [ch.total]
        inst = self.E[q].dma_start(out=out_ap, in_=in_ap, allow_slow_non_contiguous=True, **kw)
        ch.total += 16
        ch.holder[0] = ch.total
        inst.then_inc(ch.sem, 16)
        self.ninst[q] += 1
        ev = (ch.sem, ch.key, ch.holder)
        self._done(ev, reads, writes)

    def barrier(self):
        for eng in ("pe", "act", "dve", "pool", "sp"):
            self.finish(eng)

    def finish(self, eng="pool"):
        for c in self.chans:
            if c.total > 0:
                self._wait(eng, (c.sem, c.key, c.total))
        for e in ("pe", "act", "dve", "pool"):
            if self.cnt[e] > 0 and e != eng:
                self._wait(eng, (self.sem[e], (e, self.gen[e]), self.cnt[e]))


class Rot:
    def __init__(self, items):
        self.items = items
        self.i = 0

    def next(self):
        t = self.items[self.i % len(self.items)]
        self.i += 1
        return t


def bc(ap, shape):
    return ap.to_broadcast(shape)


def build(SEQ, PAST, stage=99):
    nc = bass.Bass("TRN2", target_bir_lowering=False)
    K = Trk(nc)
    NT_P = SEQ // 512
    NKT_P = SEQ // 128
    NK_S = PAST + 128
    NKT_S = NK_S // 128
    NPT_S = PAST // 512

    def din(name, shape, dt=F32):
        return nc.dram_tensor(name, list(shape), dt, kind="ExternalInput").ap()

    def dout(name, shape):
        return nc.dram_tensor(name, list(shape), F32, kind="ExternalOutput").ap()

    def dscr(name, shape, dt=BF16):
        return nc.dram_tensor(name, list(shape), dt, kind="Internal").ap()

    def sb(name, shape, dt=F32):
        return nc.alloc_sbuf_tensor(name, list(shape), dt).ap()

    xp = din("xp", [SEQ, D]); xs = din("xs", [NSEQ_S * DSEQ, D])
    ckl = din("ckl", [NSEQ_S, PAST, KVL]); ckr = din("ckr", [NSEQ_S, PAST, 32])
    sre = din("sre", [NSEQ_S, 32, 64]); sim = din("sim", [NSEQ_S, 32, 64])
    g_mix = din("g_mix", [1, D]); w_in = din("w_in", [D, DIN]); g_q = din("g_q", [1, QL])
    w_q = din("w_q", [QL, QL]); g_kv = din("g_kv", [1, KVL]); w_kv = din("w_kv", [KVL, 1024])
    a_re = din("a_re", [32, 64]); a_im = din("a_im", [32, 64]); lstep = din("lstep", [1, 32])
    b_re = din("b_re", [32, 64, 16]); b_im = din("b_im", [32, 64, 16])
    c_re = din("c_re", [32, 16, 64]); c_im = din("c_im", [32, 16, 64])
    d_skip = din("d_skip", [1, 512]); w_glu = din("w_glu", [512, 512])
    g_attn = din("g_attn", [1, 512]); g_ssm = din("g_ssm", [1, 512]); w_out = din("w_out", [D, D])
    g_mlp = din("g_mlp", [1, D]); w_up = din("w_up", [D, DFF]); w_down = din("w_down", [DFF, D])
    g_fin = din("g_fin", [1, D])
    cs_p = din("cs_p", [SEQ, 32]); cs_s = din("cs_s", [NSEQ_S * DSEQ, 32])
    ident_in = din("ident", [128, 128])

    yp = dout("yp", [SEQ, D]); ys = dout("ys", [NSEQ_S * DSEQ, D])
    latp = dout("latp", [SEQ, KVL]); krp = dout("krp", [SEQ, 32])
    hrp = dout("hrp", [32, 64]); hip = dout("hip", [32, 64])
    lats = dout("lats", [NSEQ_S * DSEQ, KVL]); krs = dout("krs", [NSEQ_S * DSEQ, 32])
    hrs = dout("hrs", [NSEQ_S, 32, 64]); his = dout("his", [NSEQ_S, 32, 64])

    s_winc = dscr("s_winc", [128, 8, 1056]); s_winu = dscr("s_winu", [128, 8, 512])
    s_wq = dscr("s_wq", [128, 6, 768]); s_wkv = dscr("s_wkv", [128, 2, 1024])
    s_wglu = dscr("s_wglu", [128, 4, 512]); s_wout = dscr("s_wout", [128, 8, 1024])
    s_wup = dscr("s_wup", [8, 128, 8, 512]); s_wdn = dscr("s_wdn", [8, 128, 4, 1024])
    ktc = [dscr("ktc0", [NH, 96, SEQ])] + [dscr("ktc%d" % (s + 1), [NH, 96, NK_S]) for s in range(NSEQ_S)]
    vc = [dscr("vc0", [NH, 128, NKT_P, 65])] + [dscr("vc%d" % (s + 1), [NH, 128, NKT_S, 65]) for s in range(NSEQ_S)]
    kvreg = [[T(None, "kvreg0_%d" % i) for i in range(NT_P)]] + \
            [[T(None, "kvreg%d_%d" % (s + 1, i)) for i in range(NPT_S + 1)] for s in range(NSEQ_S)]
    wscr = T(None, "wscr")

    ps = nc.alloc_psum_tensor("ps", [128, 4096], F32).ap()
    PB = [T(ps[:, 512 * b:512 * (b + 1)], "pb%d" % b) for b in range(8)]

    def pbf(b):
        return ps[:, 512 * b:512 * (b + 1)].bitcast(BF16)

    from contextlib import ExitStack

    def prod(sh):
        n = 1
        for v_ in sh:
            n *= v_
        return n

    def vw(t, dt, shape, off=0):
        a_ = t.ap if dt == F32 else t.ap.bitcast(dt)
        esz = 4 if dt in (F32, I32) else 2
        n = prod(shape)
        a_ = a_[:, off // esz:off // esz + n]
        if len(shape) == 2:
            a_ = a_.rearrange("p (a b) -> p a b", a=shape[0])
        elif len(shape) == 3:
            a_ = a_.rearrange("p (a b c) -> p a b c", a=shape[0], b=shape[1])
        return a_

    def raw(name, nbytes):
        return T(sb(name, [128, nbytes // 4]), name)

    ident_f = T(sb("ident_f", [128, 128]))
    ident_b = T(sb("ident_b", [128, 128], BF16))
    ones_b = T(sb("ones_b", [128, 128], BF16))
    gkv_bc = T(sb("gkv_bc", [128, KVL])); gfin_bc = T(sb("gfin_bc", [128, D]))
    gcols = T(sb("gcols", [128, 40]))
    W1 = T(sb("W1", [128, 4, 2, 8, 128], BF16))
    W2 = T(sb("W2", [128, 16, 2, 8, 32], BF16))
    BD = T(sb("BD", [128, 4, 8, 128], BF16))
    Dd = T(sb("Dd", [128, 4, 128], BF16))
    A8 = T(sb("A8", [128, 16, 2])); B8 = T(sb("B8", [128, 16, 2]))
    Hc = T(sb("Hc", [128, 16, 2])); H0s = T(sb("H0s", [128, NSEQ_S, 16, 2])); h0ch = K.chan("h0")
    cch = K.chan("const")
    cchA = K.chan("ssmA")
    cchB = K.chan("ssmB")
    castch = K.chan("cast")

    def alloc_runtime():
        g = {}
        NSLOT = 5
        g["NSLOT"] = NSLOT
        g["ring"] = [T(sb("ring%d" % i, [128, 4096], BF16)) for i in range(NSLOT)]
        g["ringch"] = [K.chan("ring%d" % i) for i in range(NSLOT)]
        g["xt"] = T(sb("xt", [128, 4, D])); g["xch"] = K.chan("xt"); g["ych"] = K.chan("yst")
        g["cst"] = T(sb("cst", [128, 4, 32])); g["csch"] = K.chan("cst"); g["kvlch"] = K.chan("kvl")
        g["actT"] = T(sb("actT", [128, 8, 512], BF16))
        rx = [raw("rx%d" % i, 2048) for i in range(2)]
        g["xnb"] = Rot([T(vw(r_, BF16, (D,)), share=r_) for r_ in rx])
        g["trl"] = Rot([T(vw(r_, F32, (512,)), share=r_) for r_ in rx])
        g["junk"] = T(sb("junk", [128, D], BF16))
        g["cqn"] = Rot([T(sb("cqn%d" % i, [128, QL], BF16)) for i in range(2)])
        rq = [raw("rq%d" % i, 4096) for i in range(2)]
        g["aT"] = [T(vw(r_, BF16, (4, 512)), share=r_) for r_ in rq]
        g["qTh"] = [T(vw(r_, BF16, (4, 512)), share=r_) for r_ in rq]
        rD = raw("rD", 8192)
        g["cqnT"] = T(vw(rD, BF16, (6, 512)), share=rD)
        g["Ssb"] = T(vw(rD, F32, (16, 2, 64)), share=rD)
        g["kvtok"] = Rot([T(sb("kvtok%d" % i, [128, 352], BF16)) for i in range(2)])
        g["kvout"] = [T(sb("kvout%d" % i, [128, 288])) for i in range(2)]
        g["kvoch"] = [K.chan("kvout%d" % i) for i in range(2)]
        g["latT"] = T(sb("latT", [128, 2, 512], BF16))
        g["qtok"] = Rot([T(sb("qtok%d" % i, [128, 8, 96], BF16)) for i in range(2)])
        g["uT"] = T(sb("uT", [128, 4, 512], BF16))
        g["KTs"] = T(sb("KTs", [128, 8, 512], BF16)); g["ktsch"] = K.chan("kts")
        g["Vs"] = T(sb("Vs", [128, 8, 4, 65], BF16)); g["vsch"] = K.chan("vs")
        rA = raw("rA", 8192)
        g["attn"] = T(vw(rA, F32, (4, 512)), share=rA)
        g["y2"] = T(vw(rA, F32, (4, 512)), share=rA)
        rB = [raw("rB%d" % i, 2048) for i in range(2)]
        g["OTs"] = Rot([T(vw(r_, F32, (512,)), share=r_) for r_ in rB])
        g["tA"] = T(vw(rB[0], F32, (512,)), share=rB[0]); g["tB"] = T(vw(rB[1], F32, (512,)), share=rB[1])
        rC = [raw("rC%d" % i, 4096) for i in range(2)]
        g["Kblk"] = [T(vw(r_, BF16, (2048,)), share=r_) for r_ in rC]
        g["y2b"] = T(vw(rC[0], BF16, (4, 512)), share=rC[0]); g["sqb"] = T(vw(rC[1], BF16, (4, 512)), share=rC[1])
        g["Vblk"] = [T(sb("Vblk%d" % i, [128, 16, 65], BF16)) for i in range(2)]
        g["kbch"] = [K.chan("kb%d" % i) for i in range(2)]
        g["vbch"] = [K.chan("vb%d" % i) for i in range(2)]
        g["PT"] = Rot([T(sb("PT%d" % i, [128, 512], BF16)) for i in range(3)])
        g["stat"] = T(sb("stat", [128, 64]))
        g["rtmp"] = T(sb("rtmp", [128, 8, 64]))
        g["Hbf"] = T(sb("Hbf", [128, 16, 2, 64], BF16))
        g["st1"] = T(sb("st1", [128, 16, 2])); g["st2"] = T(sb("st2", [128, 16, 2]))
        g["hout"] = T(sb("hout", [128, 16, 2])); g["hoch"] = K.chan("hout")
        g["rbc"] = T(sb("rbc", [128, 512]))
        g["qsb"] = T(sb("qsb", [128, 768]))
        return g


    K.dma("sp", cch, ident_f.ap, ident_in, writes=[ident_f], batch=True)
    K.dma("sp", cch, gkv_bc.ap, g_kv.partition_broadcast(128), writes=[gkv_bc], batch=True)
    K.dma("sp", cch, gfin_bc.ap, g_fin.partition_broadcast(128), writes=[gfin_bc], batch=True)
    for (src, off, n) in ((g_mix, 0, 8), (g_mlp, 8, 8), (g_q, 16, 6), (g_attn, 22, 4), (g_ssm, 26, 4), (d_skip, 30, 4)):
        K.dma("sp", cch, gcols.ap[:, off:off + n], src.rearrange("o (k c) -> c (o k)", c=128), writes=[gcols], batch=True)
    K.op("dve", lambda: nc.vector.tensor_copy(out=ident_b.ap, in_=ident_f.ap), [ident_f], [ident_b])
    K.op("pool", lambda: nc.gpsimd.memset(ones_b.ap, 1.0), [], [ones_b])

    wv = w_in.rearrange("(k p) n -> p k n", p=128)
    casts = [
        (s_winc, wv[:, :, 0:1056]), (s_winu, wv[:, :, 1056:1568]),
        (s_wq, w_q.rearrange("(k p) n -> p k n", p=128)),
    ]
    wkvv = w_kv.rearrange("(k p) (h c) -> p k h c", p=128, c=128)
    for k in range(2):
        casts.append((s_wkv[:, k, 0:512].rearrange("p (h c) -> p h c", c=64), wkvv[:, k, :, 0:64]))
        casts.append((s_wkv[:, k, 512:1024].rearrange("p (h c) -> p h c", c=64), wkvv[:, k, :, 64:128]))
    casts.append((s_wglu, w_glu.rearrange("(k p) n -> p k n", p=128)))
    casts.append((s_wout, w_out.rearrange("(k p) n -> p k n", p=128)))
    wupv = w_up.rearrange("(k p) (fc n) -> fc p k n", p=128, n=512)
    wdnv = w_down.rearrange("(fc ft p) n -> fc p ft n", ft=4, p=128)
    for fc in range(8):
        for k in range(0, 8, 4):
            casts.append((s_wup[fc, :, k:k + 4, :], wupv[fc, :, k:k + 4, :]))
        for ft in range(0, 4, 2):
            casts.append((s_wdn[fc, :, ft:ft + 2, :], wdnv[fc, :, ft:ft + 2, :]))
    for (o, i) in casts:
        K.dma("pool", castch, o, i, writes=[wscr], batch=True)

    if stage == 0:
        K.finish("pool")
        return nc, K
    def ssm_setup(es):
        def sb(name, shape, dt=F32):
            return es.enter_context(nc.sbuf_tensor(name, list(shape), dt)).ap()
        Are = T(sb("Are", [128, 16])); Aim = T(sb("Aim", [128, 16])); LS = T(sb("LS", [128, 16]))
        for g2 in range(2):
            K.dma("sp", cchA, Are.ap[64 * g2:64 * g2 + 64, :], a_re.rearrange("(k two) n -> two n k", two=2)[g2], writes=[Are], batch=True)
            K.dma("sp", cchA, Aim.ap[64 * g2:64 * g2 + 64, :], a_im.rearrange("(k two) n -> two n k", two=2)[g2], writes=[Aim], batch=True)
            K.dma("sp", cchA, LS.ap[64 * g2:64 * g2 + 64, :], lstep.rearrange("o (k two) -> two o k", two=2)[g2].partition_broadcast(64), writes=[LS], batch=True)
        BreD = T(sb("BreD", [128, 16, 32])); BimD = T(sb("BimD", [128, 16, 32]))
        CreD = T(sb("CreD", [32, 16, 128])); CimD = T(sb("CimD", [32, 16, 128]))
        for t in (BreD, BimD, CreD, CimD):
            K.op("pool", lambda t=t: nc.gpsimd.memset(t.ap, 0.0), [], [t])
        for g2 in range(2):
            K.dma("sp", cchB, BreD.ap[64 * g2:64 * g2 + 64, :, 16 * g2:16 * g2 + 16], b_re.rearrange("(k two) n q -> two n k q", two=2)[g2], writes=[BreD], batch=True)
            K.dma("sp", cchB, BimD.ap[64 * g2:64 * g2 + 64, :, 16 * g2:16 * g2 + 16], b_im.rearrange("(k two) n q -> two n k q", two=2)[g2], writes=[BimD], batch=True)
            K.dma("sp", cchB, CreD.ap[16 * g2:16 * g2 + 16, :, 64 * g2:64 * g2 + 64], c_re.rearrange("(k two) p n -> two p k n", two=2)[g2], writes=[CreD], batch=True)
            K.dma("sp", cchB, CimD.ap[16 * g2:16 * g2 + 16, :, 64 * g2:64 * g2 + 64], c_im.rearrange("(k two) p n -> two p k n", two=2)[g2], writes=[CimD], batch=True)
        for s in range(NSEQ_S):
            for g2 in range(2):
                K.dma("sp", h0ch, H0s.ap[64 * g2:64 * g2 + 64, s, :, 0], sre[s].rearrange("(k two) n -> two n k", two=2)[g2], writes=[H0s], batch=True)
                K.dma("sp", h0ch, H0s.ap[64 * g2:64 * g2 + 64, s, :, 1], sim[s].rearrange("(k two) n -> two n k", two=2)[g2], writes=[H0s], batch=True)

        def V(name):
            return T(sb(name, [128, 16]))
        dt_ = V("dt_"); mag = V("mag"); th = V("th"); cs = V("cs"); sn = V("sn")
        w1 = V("w1"); w2 = V("w2"); w3 = V("w3"); wi = T(sb("wi", [128, 16], I32))
        K.op("act", lambda: nc.scalar.activation(out=dt_.ap, in_=LS.ap, func=AF.Exp), [LS], [dt_])
        K.op("dve", lambda: nc.vector.tensor_tensor(out=w1.ap, in0=Are.ap, in1=dt_.ap, op=ALU.mult), [Are, dt_], [w1])
        K.op("act", lambda: nc.scalar.activation(out=mag.ap, in_=w1.ap, func=AF.Exp), [w1], [mag])
        K.op("dve", lambda: nc.vector.tensor_tensor(out=th.ap, in0=Aim.ap, in1=dt_.ap, op=ALU.mult), [Aim, dt_], [th])

        def sin_of(dst, shift):
            K.op("dve", lambda: nc.vector.tensor_scalar(out=w1.ap, in0=th.ap, scalar1=shift, scalar2=1.0 / TWO_PI, op0=ALU.add, op1=ALU.mult), [th], [w1])
            K.op("dve", lambda: nc.vector.tensor_copy(out=wi.ap, in_=w1.ap), [w1], [wi])
            K.op("dve", lambda: nc.vector.tensor_copy(out=w2.ap, in_=wi.ap), [wi], [w2])
            K.op("dve", lambda: nc.vector.tensor_scalar(out=w1.ap, in0=th.ap, scalar1=shift, scalar2=None, op0=ALU.add), [th], [w1])
            K.op("dve", lambda: nc.vector.scalar_tensor_tensor(out=w1.ap, in0=w2.ap, scalar=-TWO_PI, in1=w1.ap, op0=ALU.mult, op1=ALU.add), [w2, w1], [w1])
            K.op("dve", lambda: nc.vector.tensor_scalar(out=w2.ap, in0=w1.ap, scalar1=math.pi, scalar2=-TWO_PI, op0=ALU.is_gt, op1=ALU.mult), [w1], [w2])
            K.op("dve", lambda: nc.vector.tensor_tensor(out=w1.ap, in0=w1.ap, in1=w2.ap, op=ALU.add), [w1, w2], [w1])
            K.op("dve", lambda: nc.vector.tensor_scalar(out=w2.ap, in0=w1.ap, scalar1=-math.pi, scalar2=TWO_PI, op0=ALU.is_lt, op1=ALU.mult), [w1], [w2])
            K.op("dve", lambda: nc.vector.tensor_tensor(out=w1.ap, in0=w1.ap, in1=w2.ap, op=ALU.add), [w1, w2], [w1])
            K.op("dve", lambda: nc.vector.tensor_scalar(out=w1.ap, in0=w1.ap, scalar1=3.1415925, scalar2=-3.1415925, op0=ALU.min, op1=ALU.max), [w1], [w1])
            K.op("act", lambda: nc.scalar.activation(out=dst.ap, in_=w1.ap, func=AF.Sin), [w1], [dst])
        sin_of(sn, 0.0)
        sin_of(cs, math.pi / 2)
        LP = T(sb("LP", [128, 9, 2, 16]))
        K.op("pool", lambda: nc.gpsimd.memset(LP.ap[:, 0, 0, :], 1.0), [], [LP])
        K.op("pool", lambda: nc.gpsimd.memset(LP.ap[:, 0, 1, :], 0.0), [], [LP])
        K.op("dve", lambda: nc.vector.tensor_tensor(out=LP.ap[:, 1, 0, :], in0=mag.ap, in1=cs.ap, op=ALU.mult), [mag, cs], [LP])
        K.op("dve", lambda: nc.vector.tensor_tensor(out=LP.ap[:, 1, 1, :], in0=mag.ap, in1=sn.ap, op=ALU.mult), [mag, sn], [LP])
        for k in range(2, 9):
            pr, pi_ = LP.ap[:, k - 1, 0, :], LP.ap[:, k - 1, 1, :]
            lr, li = LP.ap[:, 1, 0, :], LP.ap[:, 1, 1, :]
            K.op("dve", lambda: nc.vector.tensor_tensor(out=w1.ap, in0=pr, in1=lr, op=ALU.mult), [LP], [w1])
            K.op("dve", lambda: nc.vector.tensor_tensor(out=w2.ap, in0=pi_, in1=li, op=ALU.mult), [LP], [w2])
            K.op("dve", lambda: nc.vector.tensor_tensor(out=LP.ap[:, k, 0, :], in0=w1.ap, in1=w2.ap, op=ALU.subtract), [w1, w2], [LP])
            K.op("dve", lambda: nc.vector.tensor_tensor(out=w1.ap, in0=pr, in1=li, op=ALU.mult), [LP], [w1])
            K.op("dve", lambda: nc.vector.tensor_tensor(out=w2.ap, in0=pi_, in1=lr, op=ALU.mult), [LP], [w2])
            K.op("dve", lambda: nc.vector.tensor_tensor(out=LP.ap[:, k, 1, :], in0=w1.ap, in1=w2.ap, op=ALU.add), [w1, w2], [LP])
        K.op("dve", lambda: nc.vector.tensor_copy(out=A8.ap[:, :, 0], in_=LP.ap[:, 8, 0, :]), [LP], [A8])
        K.op("dve", lambda: nc.vector.tensor_copy(out=A8.ap[:, :, 1], in_=LP.ap[:, 8, 1, :]), [LP], [A8])
        K.op("dve", lambda: nc.vector.tensor_scalar(out=B8.ap[:, :, 0], in0=LP.ap[:, 8, 1, :], scalar1=-1.0, scalar2=None, op0=ALU.mult), [LP], [B8])
        K.op("dve", lambda: nc.vector.tensor_copy(out=B8.ap[:, :, 1], in_=LP.ap[:, 8, 0, :]), [LP], [B8])
        cre = V("cre"); cim = V("cim"); den = V("den"); nr = V("nr")
        K.op("dve", lambda: nc.vector.tensor_scalar(out=nr.ap, in0=LP.ap[:, 1, 0, :], scalar1=-1.0, scalar2=None, op0=ALU.add), [LP], [nr])
        K.op("dve", lambda: nc.vector.tensor_tensor(out=w1.ap, in0=Are.ap, in1=Are.ap, op=ALU.mult), [Are], [w1])
        K.op("dve", lambda: nc.vector.tensor_tensor(out=w2.ap, in0=Aim.ap, in1=Aim.ap, op=ALU.mult), [Aim], [w2])
        K.op("dve", lambda: nc.vector.tensor_tensor(out=den.ap, in0=w1.ap, in1=w2.ap, op=ALU.add), [w1, w2], [den])
        K.op("dve", lambda: nc.vector.reciprocal(out=den.ap, in_=den.ap), [den], [den])
        K.op("dve", lambda: nc.vector.tensor_tensor(out=w1.ap, in0=nr.ap, in1=Are.ap, op=ALU.mult), [nr, Are], [w1])
        K.op("dve", lambda: nc.vector.tensor_tensor(out=w2.ap, in0=LP.ap[:, 1, 1, :], in1=Aim.ap, op=ALU.mult), [LP, Aim], [w2])
        K.op("dve", lambda: nc.vector.tensor_tensor(out=w1.ap, in0=w1.ap, in1=w2.ap, op=ALU.add), [w1, w2], [w1])
        K.op("dve", lambda: nc.vector.tensor_tensor(out=cre.ap, in0=w1.ap, in1=den.ap, op=ALU.mult), [w1, den], [cre])
        K.op("dve", lambda: nc.vector.tensor_tensor(out=w1.ap, in0=LP.ap[:, 1, 1, :], in1=Are.ap, op=ALU.mult), [LP, Are], [w1])
        K.op("dve", lambda: nc.vector.tensor_tensor(out=w2.ap, in0=nr.ap, in1=Aim.ap, op=ALU.mult), [nr, Aim], [w2])
        K.op("dve", lambda: nc.vector.tensor_tensor(out=w1.ap, in0=w1.ap, in1=w2.ap, op=ALU.subtract), [w1, w2], [w1])
        K.op("dve", lambda: nc.vector.tensor_tensor(out=cim.ap, in0=w1.ap, in1=den.ap, op=ALU.mult), [w1, den], [cim])

        def B3(t):
            return t.unsqueeze(2).to_broadcast([128, 16, 32])
        m1 = T(sb("m1", [128, 16, 32])); m2 = T(sb("m2", [128, 16, 32]))
        BbR = T(sb("BbR", [128, 16, 32])); BbI = T(sb("BbI", [128, 16, 32]))

        def cmul(dre, dim, are, aim, bre, bim, rd, wr):
            if dre is not None:
                K.op("dve", lambda: nc.vector.tensor_tensor(out=m1.ap, in0=bre, in1=B3(are), op=ALU.mult), rd, [m1])
                K.op("dve", lambda: nc.vector.tensor_tensor(out=m2.ap, in0=bim, in1=B3(aim), op=ALU.mult), rd, [m2])
                K.op("dve", lambda: nc.vector.tensor_tensor(out=dre, in0=m1.ap, in1=m2.ap, op=ALU.subtract), [m1, m2], wr)
            if dim is not None:
                K.op("dve", lambda: nc.vector.tensor_tensor(out=m1.ap, in0=bim, in1=B3(are), op=ALU.mult), rd, [m1])
                K.op("dve", lambda: nc.vector.tensor_tensor(out=m2.ap, in0=bre, in1=B3(aim), op=ALU.mult), rd, [m2])
                K.op("dve", lambda: nc.vector.tensor_tensor(out=dim, in0=m1.ap, in1=m2.ap, op=ALU.add), [m1, m2], wr)
        cmul(BbR.ap, BbI.ap, cre.ap, cim.ap, BreD.ap, BimD.ap, [cre, cim, BreD, BimD], [BbR, BbI])
        GR = T(sb("GR", [128, 16, 32])); GI = T(sb("GI", [128, 16, 32]))
        for i in range(8):
            cmul(GR.ap, GI.ap, LP.ap[:, 7 - i, 0, :], LP.ap[:, 7 - i, 1, :], BbR.ap, BbI.ap, [LP, BbR, BbI], [GR, GI])
            for reim, G in ((0, GR), (1, GI)):
                for r in range(4):
                    bank = (reim * 4 + r)
                    fns = []
                    for kk in range(4):
                        kp = 4 * kk + r
                        fns.append(lambda kk=kk, kp=kp, G=G, bank=bank: nc.tensor.transpose(
                            out=PB[bank].ap[0:32, 128 * kk:128 * kk + 128], in_=G.ap[:, kp, :], identity=ident_f.ap))
                    K.pe(fns, [G, ident_f], [PB[bank]])
                    K.op("act", lambda r=r, bank=bank, reim=reim, i=i: nc.scalar.activation(
                        out=W1.ap[32 * r:32 * r + 32, :, reim, i, :],
                        in_=PB[bank].ap[0:32, :].rearrange("p (a b) -> p a b", a=4), func=AF.Copy), [PB[bank]], [W1])
        CTR = T(sb("CTR", [128, 16, 32])); CTI = T(sb("CTI", [128, 16, 32]))
        for (src, dst, bank) in ((CreD, CTR, 0), (CimD, CTI, 1)):
            fns = [lambda kp=kp, src=src, bank=bank: nc.tensor.transpose(out=PB[bank].ap[:, 32 * kp:32 * kp + 32], in_=src.ap[:, kp, :], identity=ident_f.ap[0:32, 0:32]) for kp in range(16)]
            K.pe(fns, [src, ident_f], [PB[bank]])
            K.op("act", lambda dst=dst, bank=bank: nc.scalar.activation(out=dst.ap, in_=PB[bank].ap.rearrange("p (a b) -> p a b", a=16), func=AF.Copy), [PB[bank]], [dst])
        K.op("pool", lambda: nc.gpsimd.memset(BD.ap, 0.0), [], [BD])
        CLR = T(sb("CLR", [128, 16, 32])); CLI = T(sb("CLI", [128, 16, 32])); NCLI = T(sb("NCLI", [128, 16, 32]))
        for kpow in range(9):
            cmul(CLR.ap, CLI.ap, LP.ap[:, kpow, 0, :], LP.ap[:, kpow, 1, :], CTR.ap, CTI.ap, [LP, CTR, CTI], [CLR, CLI])
            K.op("dve", lambda: nc.vector.tensor_scalar(out=NCLI.ap, in0=CLI.ap, scalar1=-1.0, scalar2=None, op0=ALU.mult), [CLI], [NCLI])
            if kpow >= 1:
                j = kpow - 1
                K.op("act", lambda j=j: nc.scalar.activation(out=W2.ap[:, :, 0, j, :], in_=CLR.ap, func=AF.Copy), [CLR], [W2])
                K.op("act", lambda j=j: nc.scalar.activation(out=W2.ap[:, :, 1, j, :], in_=NCLI.ap, func=AF.Copy), [NCLI], [W2])
            if kpow <= 7:
                tau = kpow
                for r in range(4):
                    for kk in range(4):
                        kp = 4 * kk + r
                        col = (kk * 8 + tau) * 32
                        bank = 2 * r + col // 512
                        c0 = col % 512
                        fns = [
                            lambda kp=kp, bank=bank, c0=c0: nc.tensor.matmul(PB[bank].ap[0:32, c0:c0 + 32], BbR.ap[:, kp, :], CLR.ap[:, kp, :], start=True, stop=False),
                            lambda kp=kp, bank=bank, c0=c0: nc.tensor.matmul(PB[bank].ap[0:32, c0:c0 + 32], BbI.ap[:, kp, :], NCLI.ap[:, kp, :], start=False, stop=True),
                        ]
                        K.pe(fns, [BbR, BbI, CLR, NCLI], [PB[bank]])
        for r in range(4):
            for hb in range(2):
                bank = 2 * r + hb
                K.op("act", lambda r=r, hb=hb, bank=bank: nc.scalar.activation(
                    out=BD.ap[32 * r:32 * r + 32, 2 * hb:2 * hb + 2, :, 32 * r:32 * r + 32],
                    in_=PB[bank].ap[0:32, :].rearrange("p (a t c) -> p a t c", a=2, t=8), func=AF.Copy), [PB[bank]], [BD])
        for kk in range(4):
            K.op("dve", lambda kk=kk: nc.vector.tensor_scalar(out=Dd.ap[:, kk, :], in0=ident_f.ap, scalar1=gcols.ap[:, 30 + kk:31 + kk], scalar2=None, op0=ALU.mult), [ident_f, gcols], [Dd])

    with ExitStack() as es_:
        ssm_setup(es_)
        K.barrier()
    if stage == 1:
        K.finish("pool")
        return nc, K
    G = alloc_runtime()
    NSLOT = G["NSLOT"]; ring = G["ring"]; ringch = G["ringch"]; xt = G["xt"]; xch = G["xch"]; ych = G["ych"]; cst = G["cst"]; csch = G["csch"]; kvlch = G["kvlch"]
    actT = G["actT"]; xnb = G["xnb"]; trl = G["trl"]; junk = G["junk"]; cqn = G["cqn"]; aT = G["aT"]; qTh = G["qTh"]; cqnT = G["cqnT"]; Ssb = G["Ssb"]
    kvtok = G["kvtok"]; kvout = G["kvout"]; kvoch = G["kvoch"]; latT = G["latT"]; qtok = G["qtok"]; uT = G["uT"]; KTs = G["KTs"]; ktsch = G["ktsch"]
    Vs = G["Vs"]; vsch = G["vsch"]; attn = G["attn"]; y2 = G["y2"]; OTs = G["OTs"]; tA = G["tA"]; tB = G["tB"]; Kblk = G["Kblk"]; y2b = G["y2b"]; sqb = G["sqb"]
    Vblk = G["Vblk"]; kbch = G["kbch"]; vbch = G["vbch"]; PT = G["PT"]; stat = G["stat"]; rtmp = G["rtmp"]; Hbf = G["Hbf"]; st1 = G["st1"]; st2 = G["st2"]
    hout = G["hout"]; hoch = G["hoch"]; rbc = G["rbc"]; qsb = G["qsb"]
    NKB = 2
    statc = [T(stat.ap[:, i:i + 1], "stat%d" % i) for i in range(64)]
    aTr = Rot(aT)
    kvo_i = [0]
    K.op("pool", lambda: nc.gpsimd.memset(Vs.ap, 1.0), [], [Vs])

    class Ring:
        def __init__(self):
            self.plan = []
            self.issued = 0
            self.got = 0

        def add(self, items):
            self.plan.extend(items)

        def _issue(self, n):
            name, src, shape = self.plan[n]
            slot = n % NSLOT
            dst = ring[slot].ap
            ne = 1
            for s_ in shape:
                ne *= s_
            d = dst[:, 0:ne]
            if len(shape) == 2:
                d = d.rearrange("p (a b) -> p a b", a=shape[0])
            K.dma("sp", ringch[slot], d, src, reads=[wscr], writes=[ring[slot]])

        def get(self, *names, hold=0):
            n0 = self.got
            assert len(names) + hold <= NSLOT
            for i_, nm in enumerate(names):
                assert self.plan[n0 + i_][0] == nm, (self.plan[n0 + i_][0], nm)
            while self.issued < min(len(self.plan), n0 + NSLOT - hold):
                self._issue(self.issued)
                self.issued += 1
            self.got += len(names)
            outs = []
            for i_ in range(len(names)):
                n = n0 + i_
                shape = self.plan[n][2]
                ne = prod(shape)
                v = ring[n % NSLOT].ap[:, 0:ne]
                if len(shape) == 2:
                    v = v.rearrange("p (a b) -> p a b", a=shape[0])
                outs.append((ring[n % NSLOT], v))
            return outs

    R = Ring()
    plan_tok = [("winc0", s_winc[:, 0:3, :], (3, 1056)), ("winc1", s_winc[:, 3:6, :], (3, 1056)), ("winc2", s_winc[:, 6:8, :], (2, 1056)),
                ("wq0", s_wq[:, 0:3, :], (3, 768)), ("wq1", s_wq[:, 3:6, :], (3, 768)),
                ("wkv", s_wkv, (2, 1024)), ("winu", s_winu, (8, 512)), ("wglu", s_wglu, (4, 512)),
                ("wout0", s_wout[:, 0:4, :], (4, 1024)), ("wout1", s_wout[:, 4:8, :], (4, 1024))]
    for fc in range(8):
        plan_tok.append(("wup%d" % fc, s_wup[fc], (8, 512)))
        plan_tok.append(("wdn%d" % fc, s_wdn[fc], (4, 1024)))
    plan_kv = [("wkv", s_wkv, (2, 1024))]

    def rms_stats(src_ap, reads, n, col):
        st = statc[col]
        ss = st.ap
        K.op("act", lambda: nc.scalar.activation(out=junk.ap[:, 0:n], in_=src_ap, func=AF.Square, accum_out=ss), reads, [junk, st])
        K.op("act", lambda: nc.scalar.activation(out=ss, in_=ss, func=AF.Sqrt, scale=1.0 / n, bias=EPS), [st], [st])
        K.op("dve", lambda: nc.vector.reciprocal(out=ss, in_=ss), [st], [st])
        return st

    def transposes_to(dstT, dst_kslice, src_tile, nk, sub, gcol0):
        fns = [lambda k=k: nc.tensor.transpose(out=pbf(3)[:, 128 * k:128 * k + 128], in_=src_tile.ap[:, 128 * k:128 * k + 128], identity=ident_b.ap) for k in range(nk)]
        K.pe(fns, [src_tile, ident_b], [PB[3]])
        for k in range(nk):
            K.op("dve", lambda k=k: nc.vector.tensor_scalar(out=dstT.ap[:, dst_kslice + k, 128 * sub:128 * sub + 128], in0=pbf(3)[:, 128 * k:128 * k + 128],
                                                           scalar1=gcols.ap[:, gcol0 + k:gcol0 + k + 1], scalar2=None, op0=ALU.mult), [PB[3], gcols], [dstT])

    def kv_from_tok(kvt, sub, ncols_total):
        fns = [lambda k=k: nc.tensor.transpose(out=pbf(3)[:, 128 * k:128 * k + 128], in_=kvt.ap[:, 128 * k:128 * k + 128], identity=ident_b.ap) for k in range(2)]
        fns.append(lambda: nc.tensor.transpose(out=pbf(3)[0:96, 256:384], in_=kvt.ap[:, 256:352], identity=ident_b.ap))
        K.pe(fns, [kvt, ident_b], [PB[3]])
        K.op("dve", lambda: nc.vector.tensor_copy(out=latT.ap[:, :, 128 * sub:128 * sub + 128], in_=pbf(3)[:, 0:256].rearrange("p (a b) -> p a b", a=2)), [PB[3]], [latT])
        K.op("dve", lambda: nc.vector.tensor_copy(out=KTs.ap[64:96, :, 128 * sub:128 * sub + 128],
                                                 in_=pbf(3)[64:96, 256:384].unsqueeze(1).to_broadcast([32, 8, 128])), [PB[3]], [KTs])

    def kv_build(wkv_t, wkv_v, nsub):
        N = 128 * nsub
        for hp in range(4):
            b = 4 + hp % 2
            fns = [lambda k=k, hp=hp, b=b: nc.tensor.matmul(PB[b].ap[:, 0:N], wkv_v[:, k, 128 * hp:128 * hp + 128], latT.ap[:, k, 0:N], start=(k == 0), stop=(k == 1)) for k in range(2)]
            K.pe(fns, [wkv_t, latT], [PB[b]])
            K.op("dve", lambda hp=hp, b=b: nc.vector.tensor_copy(out=KTs.ap[0:64, 2 * hp, 0:N], in_=PB[b].ap[0:64, 0:N]), [PB[b]], [KTs])
            K.op("act", lambda hp=hp, b=b: nc.scalar.activation(out=KTs.ap[0:64, 2 * hp + 1, 0:N], in_=PB[b].ap[64:128, 0:N], func=AF.Copy), [PB[b]], [KTs])
        for sub in range(nsub):
            b = 6 + sub % 2
            fns = [lambda k=k, sub=sub, b=b: nc.tensor.matmul(PB[b].ap[:, :], latT.ap[:, k, 128 * sub:128 * sub + 128], wkv_v[:, k, 512:1024], start=(k == 0), stop=(k == 1)) for k in range(2)]
            K.pe(fns, [wkv_t, latT], [PB[b]])
            K.op("dve", lambda sub=sub, b=b: nc.vector.tensor_copy(out=Vs.ap[:, :, sub, 0:64], in_=PB[b].ap.rearrange("p (h c) -> p h c", c=64)), [PB[b]], [Vs])

    def attention(ci, n_full_kt, qc0, nq, diag_i, half_last, dsts, between=None):
        nkt = n_full_kt + (1 if half_last else 0)
        nblk = (nkt + 15) // 16
        tot_keys = n_full_kt * 128 + (64 if half_last else 0)
        for h in range(NH):
            ob = 6 + h % 2
            qh = qTh[h // 4]
            hl = h % 4
            steps = []
            for blk in range(nblk):
                kt0 = blk * 16
                nkb = min(16, nkt - kt0)
                for kl in range(nkb):
                    kt = kt0 + kl
                    kp = 64 if (half_last and kt == nkt - 1) else 128
                    isdiag = diag_i is not None and kt >= 4 * diag_i
                    n0 = 128 * (kt - 4 * diag_i) if isdiag else 0
                    steps.append(dict(blk=blk, kt0=kt0, nkb=nkb, kl=kl, kt=kt, kp=kp, isdiag=isdiag, n0=n0, N=nq - n0))
            blkslot = {}

            def emit_S(st):
                blk = st["blk"]
                if blk not in blkslot:
                    slot = actr[0] % NKB
                    actr[0] += 1
                    blkslot[blk] = slot
                    kt0, nkb = st["kt0"], st["nkb"]
                    nkeys = min(128 * nkb, tot_keys - 128 * kt0)
                    regs = kvreg[ci][(kt0 * 128) // 512:(kt0 * 128 + nkeys + 511) // 512]
                    K.dma("sp", kbch[slot], Kblk[slot].ap[0:96, 0:nkeys], ktc[ci][h, :, 128 * kt0:128 * kt0 + nkeys], reads=regs, writes=[Kblk[slot]])
                    nvp = 128 if nkeys >= 128 else 64
                    K.dma("sp", vbch[slot], Vblk[slot].ap[0:nvp, 0:nkb, :], vc[ci][h, 0:nvp, kt0:kt0 + nkb, :], reads=regs, writes=[Vblk[slot]])
                slot = blkslot[blk]
                st["slot"] = slot
                sbk = 4 + actr[1] % 2
                actr[1] += 1
                st["sbk"] = sbk
                kl, kp, n0, N = st["kl"], st["kp"], st["n0"], st["N"]
                K.pe([lambda: nc.tensor.matmul(PB[sbk].ap[0:kp, 0:N], Kblk[slot].ap[0:96, 128 * kl:128 * kl + kp], qh.ap[0:96, hl, qc0 + n0:qc0 + n0 + N], start=True, stop=True)],
                     [Kblk[slot], qh], [PB[sbk]])

            def emit_PV(st, first):
                slot, sbk, kl, kp, n0, N = st["slot"], st["sbk"], st["kl"], st["kp"], st["n0"], st["N"]
                pt = PT.next()
                K.op("act", lambda: nc.scalar.activation(out=pt.ap[0:kp, 0:N], in_=PB[sbk].ap[0:kp, 0:N], func=AF.Exp, scale=SCALE), [PB[sbk]], [pt])
                if st["isdiag"]:
                    K.op("pool", lambda: nc.gpsimd.memset(pt.ap[64:128, 0:64], 0.0), [], [pt])
                K.pe([lambda: nc.tensor.matmul(PB[ob].ap[0:65, n0:n0 + N], Vblk[slot].ap[0:kp, kl, :], pt.ap[0:kp, 0:N], start=first, stop=True, skip_group_check=(not first))],
                     [Vblk[slot], pt], [PB[ob]])

            emit_S(steps[0])
            for j in range(len(steps)):
                if j + 1 < len(steps):
                    emit_S(steps[j + 1])
                emit_PV(steps[j], j == 0)
            if between is not None:
                between(h)
            ot = OTs.next()
            K.op("dve", lambda ot=ot, ob=ob: nc.vector.tensor_copy(out=ot.ap[0:64, 0:nq], in_=PB[ob].ap[0:64, 0:nq]), [PB[ob]], [ot])
            K.op("dve", lambda ot=ot, ob=ob: nc.vector.tensor_copy(out=ot.ap[64:65, 0:nq], in_=PB[ob].ap[64:65, 0:nq]), [PB[ob]], [ot])
            c0 = 0
            for (po, nqq, sub) in dsts:
                K.pe([lambda ot=ot, c0=c0, nqq=nqq: nc.tensor.transpose(out=PB[3].ap[0:nqq, 0:65], in_=ot.ap[0:65, c0:c0 + nqq], identity=ident_f.ap[0:65, 0:65])], [ot, ident_f], [PB[3]])
                rct = statc[32 + (actr[2] % 16)]
                rc = rct.ap[0:nqq, :]
                actr[2] += 1
                K.op("dve", lambda rc=rc, nqq=nqq: nc.vector.reciprocal(out=rc, in_=PB[3].ap[0:nqq, 64:65]), [PB[3]], [rct])
                K.op("dve", lambda rc=rc, po=po, nqq=nqq, sub=sub, h=h: nc.vector.tensor_scalar(out=attn.ap[po:po + nqq, sub, 64 * h:64 * h + 64], in0=PB[3].ap[0:nqq, 0:64], scalar1=rc, scalar2=None, op0=ALU.mult), [PB[3], rct], [attn])
                c0 += nqq
    actr = [0, 0, 0]

    class Stop(Exception):
        pass

    def chk(n):
        if stage == n:
            raise Stop()

    def token_tile(kind, ti):
        if kind == "p":
            nsub, x_src, cs_src = 4, xp[512 * ti:512 * ti + 512, :], cs_p[512 * ti:512 * ti + 512, :]
            y_dst, lat_dst, kr_dst = yp[512 * ti:512 * ti + 512, :], latp[512 * ti:512 * ti + 512, :], krp[512 * ti:512 * ti + 512, :]
        else:
            nsub, x_src, cs_src = 2, xs, cs_s
            y_dst, lat_dst, kr_dst = ys, lats, krs
        TT = 128 * nsub
        NC = TT // 8
        K.dma("sp", xch, xt.ap[:, 0:nsub, :], x_src.rearrange("(s p) d -> p s d", p=128), writes=[xt])
        K.dma("sp", csch, cst.ap[:, 0:nsub, :], cs_src.rearrange("(s p) d -> p s d", p=128), writes=[cst])
        rs = [rms_stats(xt.ap[:, sub, :], [xt], D, sub) for sub in range(nsub)]
        xbs = []

        def s1_scale(sub):
            xb = xnb.next()
            xbs.append(xb)
            K.op("dve", lambda: nc.vector.tensor_scalar(out=xb.ap, in0=xt.ap[:, sub, :], scalar1=rs[sub].ap, scalar2=None, op0=ALU.mult), [xt, rs[sub]], [xb])
        s1_scale(0)
        for sub in range(nsub):
            if sub + 1 < nsub:
                s1_scale(sub + 1)
            transposes_to(actT, 0, xbs[sub], 8, sub, 0)
        chk(2)
        wc = R.get("winc0", "winc1", "winc2")

        def c_mm(sub):
            pb0 = 0 if sub % 2 == 0 else 5
            base = 512 * pb0
            fns = []
            for k in range(8):
                wvw = wc[k // 3][1]
                kl = k % 3
                for (c0, c1) in ((0, 512), (512, 1024), (1024, 1056)):
                    fns.append(lambda k=k, kl=kl, wvw=wvw, c0=c0, c1=c1: nc.tensor.matmul(
                        ps[:, base + c0:base + c1], actT.ap[:, k, 128 * sub:128 * sub + 128], wvw[:, kl, c0:c1], start=(k == 0), stop=(k == 7)))
            K.pe(fns, [actT, wc[0][0], wc[1][0], wc[2][0]], [PB[pb0], PB[pb0 + 1], PB[pb0 + 2]])

        def c_post(sub):
            pb0 = 0 if sub % 2 == 0 else 5
            base = 512 * pb0
            P0, P1, P2 = PB[pb0], PB[pb0 + 1], PB[pb0 + 2]
            r = rms_stats(ps[:, base:base + QL], [P0, P1], QL, 8 + sub)
            cq = cqn.next()
            K.op("dve", lambda: nc.vector.tensor_scalar(out=cq.ap, in0=ps[:, base:base + QL], scalar1=r.ap, scalar2=None, op0=ALU.mult), [P0, P1, r], [cq])
            r2 = rms_stats(ps[:, base + QL:base + QL + KVL], [P1], KVL, 12 + sub)
            ko = kvout[kvo_i[0] % 2]; koc = kvoch[kvo_i[0] % 2]; kvo_i[0] += 1
            kvt = kvtok.next()
            K.op("dve", lambda: nc.vector.scalar_tensor_tensor(out=ko.ap[:, 0:KVL], in0=ps[:, base + QL:base + QL + KVL], scalar=r2.ap, in1=gkv_bc.ap, op0=ALU.mult, op1=ALU.mult), [P1, r2, gkv_bc], [ko])
            x1, x2 = ps[:, base + 1024:base + 1040], ps[:, base + 1040:base + 1056]
            cs_, sn_ = cst.ap[:, sub, 0:16], cst.ap[:, sub, 16:32]
            rt = rtmp.ap[:, 0, :]
            K.op("dve", lambda: nc.vector.tensor_tensor(out=rt[:, 0:16], in0=x1, in1=cs_, op=ALU.mult), [P2, cst], [rtmp])
            K.op("dve", lambda: nc.vector.tensor_tensor(out=rt[:, 16:32], in0=x2, in1=sn_, op=ALU.mult), [P2, cst], [rtmp])
            K.op("dve", lambda: nc.vector.tensor_tensor(out=rt[:, 32:48], in0=x1, in1=sn_, op=ALU.mult), [P2, cst], [rtmp])
            K.op("dve", lambda: nc.vector.tensor_tensor(out=rt[:, 48:64], in0=x2, in1=cs_, op=ALU.mult), [P2, cst], [rtmp])
            K.op("dve", lambda: nc.vector.tensor_tensor(out=ko.ap[:, 256:272], in0=rt[:, 0:16], in1=rt[:, 16:32], op=ALU.subtract), [rtmp], [ko])
            K.op("dve", lambda: nc.vector.tensor_tensor(out=ko.ap[:, 272:288], in0=rt[:, 32:48], in1=rt[:, 48:64], op=ALU.add), [rtmp], [ko])
            K.op("pool", lambda: nc.gpsimd.tensor_copy(out=kvt.ap[:, 0:256], in_=ko.ap[:, 0:256]), [ko], [kvt])
            K.op("pool", lambda: nc.gpsimd.tensor_copy(out=kvt.ap[:, 320:352], in_=ko.ap[:, 256:288]), [ko], [kvt])
            K.op("pool", lambda: nc.gpsimd.memset(kvt.ap[:, 256:320], 0.0), [], [kvt])
            K.dma("pool", koc, lat_dst[128 * sub:128 * sub + 128, :], ko.ap[:, 0:256], reads=[ko])
            K.dma("pool", koc, kr_dst[128 * sub:128 * sub + 128, :], ko.ap[:, 256:288], reads=[ko], batch=True)
            kv_from_tok(kvt, sub, TT)
            transposes_to(cqnT, 0, cq, 6, sub, 16)
        c_mm(0)
        for sub in range(nsub):
            if sub + 1 < nsub:
                c_mm(sub + 1)
            c_post(sub)
        chk(3)
        wqs = R.get("wq0", "wq1")

        def q_mm(sub):
            pb0 = 0 if sub % 2 == 0 else 5
            base = 512 * pb0
            fns = []
            for k in range(6):
                wvw = wqs[k // 3][1]
                kl = k % 3
                for (c0, c1) in ((0, 512), (512, 768)):
                    fns.append(lambda k=k, kl=kl, wvw=wvw, c0=c0, c1=c1: nc.tensor.matmul(
                        ps[:, base + c0:base + c1], cqnT.ap[:, k, 128 * sub:128 * sub + 128], wvw[:, kl, c0:c1], start=(k == 0), stop=(k == 5)))
            K.pe(fns, [cqnT, wqs[0][0], wqs[1][0]], [PB[pb0], PB[pb0 + 1]])

        def q_post(sub):
            pb0 = 0 if sub % 2 == 0 else 5
            base = 512 * pb0
            K.op("act", lambda: nc.scalar.activation(out=qsb.ap[:, 0:512], in_=ps[:, base:base + 512], func=AF.Copy), [PB[pb0]], [qsb])
            K.op("dve", lambda: nc.vector.tensor_copy(out=qsb.ap[:, 512:768], in_=ps[:, base + 512:base + 768]), [PB[pb0 + 1]], [qsb])
            qv = qsb.ap.rearrange("p (h c) -> p h c", c=96)
            qk = qtok.next()
            K.op("pool", lambda: nc.gpsimd.tensor_copy(out=qk.ap[:, :, 0:64], in_=qv[:, :, 0:64]), [qsb], [qk])
            cs8 = cst.ap[:, sub, 0:16].unsqueeze(1).to_broadcast([128, 8, 16])
            sn8 = cst.ap[:, sub, 16:32].unsqueeze(1).to_broadcast([128, 8, 16])
            q1, q2 = qv[:, :, 64:80], qv[:, :, 80:96]
            rt4 = rtmp.ap.rearrange("p h (a c) -> p h a c", a=4)
            K.op("dve", lambda: nc.vector.tensor_tensor(out=rt4[:, :, 0, :], in0=q1, in1=cs8, op=ALU.mult), [qsb, cst], [rtmp])
            K.op("dve", lambda: nc.vector.tensor_tensor(out=rt4[:, :, 1, :], in0=q2, in1=sn8, op=ALU.mult), [qsb, cst], [rtmp])
            K.op("dve", lambda: nc.vector.tensor_tensor(out=rt4[:, :, 2, :], in0=q1, in1=sn8, op=ALU.mult), [qsb, cst], [rtmp])
            K.op("dve", lambda: nc.vector.tensor_tensor(out=rt4[:, :, 3, :], in0=q2, in1=cs8, op=ALU.mult), [qsb, cst], [rtmp])
            K.op("dve", lambda: nc.vector.tensor_tensor(out=qk.ap[:, :, 64:80], in0=rt4[:, :, 0, :], in1=rt4[:, :, 1, :], op=ALU.subtract), [rtmp], [qk])
            K.op("dve", lambda: nc.vector.tensor_tensor(out=qk.ap[:, :, 80:96], in0=rt4[:, :, 2, :], in1=rt4[:, :, 3, :], op=ALU.add), [rtmp], [qk])
            fns = [lambda h=h: nc.tensor.transpose(out=pbf(3)[0:96, 128 * h:128 * h + 128], in_=qk.ap[:, h, :], identity=ident_b.ap) for h in range(NH)]
            K.pe(fns, [qk, ident_b], [PB[3]])
            for hh in range(2):
                for (p0, p1) in ((0, 64), (64, 96)):
                    K.op("act", lambda hh=hh, p0=p0, p1=p1: nc.scalar.activation(out=qTh[hh].ap[p0:p1, :, 128 * sub:128 * sub + 128],
                                                                               in_=pbf(3)[p0:p1, 512 * hh:512 * hh + 512].rearrange("p (h c) -> p h c", h=4), func=AF.Copy), [PB[3]], [qTh[hh]])
        q_mm(0)
        for sub in range(nsub):
            if sub + 1 < nsub:
                q_mm(sub + 1)
            q_post(sub)
        chk(4)
        (wkv_t, wkv_v), = R.get("wkv")
        kv_build(wkv_t, wkv_v, nsub)
        if kind == "p":
            reg = kvreg[0][ti]
            K.dma("pool", ktsch, ktc[0][:, :, 512 * ti:512 * ti + 512].rearrange("h r n -> r h n"), KTs.ap[0:96, :, :], reads=[KTs], writes=[reg])
            K.dma("pool", vsch, vc[0][:, :, 4 * ti:4 * ti + 4, :].rearrange("h p s c -> p h s c"), Vs.ap[:, :, :, :], reads=[Vs], writes=[reg])
        else:
            for s in range(NSEQ_S):
                reg = kvreg[s + 1][NPT_S]
                sub, po = s // 2, 64 * (s % 2)
                K.dma("pool", ktsch, ktc[s + 1][:, :, PAST:PAST + 64].rearrange("h r n -> r h n"), KTs.ap[0:96, :, 64 * s:64 * s + 64], reads=[KTs], writes=[reg], batch=(s > 0))
                K.dma("pool", vsch, vc[s + 1][:, 0:64, NKT_S - 1, :].rearrange("h p c -> p h c"), Vs.ap[po:po + 64, :, sub, :], reads=[Vs], writes=[reg], batch=(s > 0))
                K.dma("pool", vsch, vc[s + 1][:, 64:128, NKT_S - 1, :].rearrange("h p c -> p h c"), Vs.ap[64 - po:128 - po, :, sub, :], reads=[Vs], writes=[reg], batch=True)
        (wu_t, wu_v), = R.get("winu")
        for m in range(4):
            b = 4 + m % 2
            fns = [lambda k=k, m=m, b=b: nc.tensor.matmul(PB[b].ap[:, 0:TT], wu_v[:, k, 128 * m:128 * m + 128], actT.ap[:, k, 0:TT], start=(k == 0), stop=(k == 7)) for k in range(8)]
            K.pe(fns, [wu_t, actT], [PB[b]])
            K.op("act", lambda m=m, b=b: nc.scalar.activation(out=uT.ap[:, m, 0:TT], in_=PB[b].ap[:, 0:TT], func=AF.Copy), [PB[b]], [uT])
        chk(5)
        ssm_a(kind, ti, nsub, TT, NC)
        scan_q = scan_step_fns(kind, NC)
        ncalls = [NH if kind == "p" else NH * NSEQ_S]

        def between(h):
            n_ = (len(scan_q) + ncalls[0] - 1) // ncalls[0]
            ncalls[0] -= 1
            for _ in range(n_):
                scan_q.pop(0)()
        if kind == "p":
            attention(0, 4 * ti + 4, 0, 512, ti, False, [(0, 128, s_) for s_ in range(4)], between)
        else:
            for s in range(NSEQ_S):
                attention(s + 1, PAST // 128, 64 * s, 64, None, True, [(64 * (s % 2), 64, s // 2)], between)
        while scan_q:
            scan_q.pop(0)()
        chk(6)
        for sub in range(nsub):
            r = rms_stats(attn.ap[:, sub, :], [attn], 512, 16 + sub)
            xb = xnb.next()
            K.op("dve", lambda sub=sub, xb=xb, r=r: nc.vector.tensor_scalar(out=xb.ap[:, 0:512], in0=attn.ap[:, sub, :], scalar1=r.ap, scalar2=None, op0=ALU.mult), [attn, r], [xb])
            transposes_to(actT, 0, xb, 4, sub, 22)
        ssm_c(kind, ti, nsub, TT, NC)
        chk(7)
        wo = R.get("wout0", "wout1")
        for sub in range(nsub):
            fns = []
            for k in range(8):
                wvw = wo[k // 4][1]
                kl = k % 4
                for (c0, c1) in ((0, 512), (512, 1024)):
                    fns.append(lambda k=k, kl=kl, wvw=wvw, c0=c0, c1=c1, sub=sub: nc.tensor.matmul(
                        ps[:, c0:c1], actT.ap[:, k, 128 * sub:128 * sub + 128], wvw[:, kl, c0:c1], start=(k == 0), stop=(k == 7)))
            K.pe(fns, [actT, wo[0][0], wo[1][0]], [PB[0], PB[1]])
            K.op("dve", lambda sub=sub: nc.vector.tensor_tensor(out=xt.ap[:, sub, :], in0=ps[:, 0:1024], in1=xt.ap[:, sub, :], op=ALU.add), [PB[0], PB[1], xt], [xt])
        chk(8)
        for sub in range(nsub):
            r = rms_stats(xt.ap[:, sub, :], [xt], D, 20 + sub)
            xb = xnb.next()
            K.op("dve", lambda sub=sub, xb=xb, r=r: nc.vector.tensor_scalar(out=xb.ap, in0=xt.ap[:, sub, :], scalar1=r.ap, scalar2=None, op0=ALU.mult), [xt, r], [xb])
            transposes_to(actT, 0, xb, 8, sub, 8)
        def mlp_up(fc):
            (wu_t, wu_v), (wd_t, wd_v) = R.get("wup%d" % fc, "wdn%d" % fc, hold=(2 if fc > 0 else 0))
            a = aTr.next()
            for ft in range(4):
                b = 4 + ft % 2
                fns = [lambda k=k, ft=ft, b=b: nc.tensor.matmul(PB[b].ap[:, 0:TT], wu_v[:, k, 128 * ft:128 * ft + 128], actT.ap[:, k, 0:TT], start=(k == 0), stop=(k == 7)) for k in range(8)]
                K.pe(fns, [wu_t, actT], [PB[b]])
                tr = trl.next()
                K.op("dve", lambda b=b, tr=tr: nc.vector.tensor_scalar(out=tr.ap[:, 0:TT], in0=PB[b].ap[:, 0:TT], scalar1=0.0, scalar2=None, op0=ALU.max), [PB[b]], [tr])
                K.op("pool", lambda ft=ft, tr=tr, a=a: nc.gpsimd.tensor_tensor(out=a.ap[:, ft, 0:TT], in0=tr.ap[:, 0:TT], in1=tr.ap[:, 0:TT], op=ALU.mult), [tr], [a])
            return a, wd_t, wd_v

        def mlp_down(a, wd_t, wd_v):
            for sub in range(nsub):
                bb = (0, 1) if sub % 2 == 0 else (6, 7)
                base = 512 * bb[0]
                fns = []
                for ft in range(4):
                    for hh in range(2):
                        fns.append(lambda ft=ft, hh=hh, sub=sub, base=base: nc.tensor.matmul(
                            ps[:, base + 512 * hh:base + 512 * hh + 512], a.ap[:, ft, 128 * sub:128 * sub + 128], wd_v[:, ft, 512 * hh:512 * hh + 512], start=(ft == 0), stop=(ft == 3)))
                K.pe(fns, [a, wd_t], [PB[bb[0]], PB[bb[1]]])
                K.op("dve", lambda sub=sub, base=base: nc.vector.tensor_tensor(out=xt.ap[:, sub, :], in0=ps[:, base:base + 1024], in1=xt.ap[:, sub, :], op=ALU.add), [PB[bb[0]], PB[bb[1]], xt], [xt])

        cur = mlp_up(0)
        for fc in range(8):
            nxt = mlp_up(fc + 1) if fc + 1 < 8 else None
            mlp_down(*cur)
            cur = nxt
        for sub in range(nsub):
            r = rms_stats(xt.ap[:, sub, :], [xt], D, 24 + sub)
            K.op("dve", lambda sub=sub, r=r: nc.vector.scalar_tensor_tensor(out=xt.ap[:, sub, :], in0=xt.ap[:, sub, :], scalar=r.ap, in1=gfin_bc.ap, op0=ALU.mult, op1=ALU.mult), [xt, r, gfin_bc], [xt])
        K.dma("pool", ych, y_dst.rearrange("(s p) d -> p s d", p=128), xt.ap[:, 0:nsub, :], reads=[xt])

    def ssm_a(kind, ti, nsub, TT, NC):
        uTc = uT.ap[:, :, 0:TT].rearrange("p m (c i) -> p m i c", i=8)
        Sv = Ssb.ap.rearrange("p (kk r) t c -> p r kk t c", r=4)
        for r in range(4):
            bank = 4 + r
            for kk in range(4):
                for reim in range(2):
                    c0 = (kk * 2 + reim) * NC
                    fns = [lambda i=i, kk=kk, r=r, reim=reim, bank=bank, c0=c0: nc.tensor.matmul(
                        PB[bank].ap[:, c0:c0 + NC], W1.ap[32 * r:32 * r + 32, kk, reim, i, :], uTc[32 * r:32 * r + 32, kk, i, :],
                        start=(i == 0), stop=(i == 7), tile_position=(32 * r, 0)) for i in range(8)]
                    K.pe(fns, [W1, uT], [PB[bank]])
            K.op("act", lambda r=r, bank=bank: nc.scalar.activation(
                out=Sv[:, r, :, :, 0:NC], in_=PB[bank].ap[:, 0:8 * NC].rearrange("p (a t c) -> p a t c", a=4, t=2), func=AF.Copy), [PB[bank]], [Ssb])
    def scan_step_fns(kind, NC):
        return [(lambda c=c: scan_step(kind, c)) for c in range(NC)]

    def scan_step(kind, c):
        if True:
            if kind == "p":
                prev_t, prev = (Hc, Hc.ap) if c == 0 else (Ssb, Ssb.ap[:, :, :, c - 1])
            else:
                if c % 8 == 0:
                    prev_t, prev = H0s, H0s.ap[:, c // 8, :, :]
                else:
                    prev_t, prev = Ssb, Ssb.ap[:, :, :, c - 1]
            pre = prev[:, :, 0:1].to_broadcast([128, 16, 2])
            pim = prev[:, :, 1:2].to_broadcast([128, 16, 2])
            if c % 8 == 0 or kind != "p" or True:
                pass
            K.op("pool", lambda c=c, prev=prev: nc.gpsimd.tensor_copy(out=Hbf.ap[:, :, :, c], in_=prev), [prev_t], [Hbf])
            K.op("dve", lambda pre=pre: nc.vector.tensor_tensor(out=st1.ap, in0=A8.ap, in1=pre, op=ALU.mult), [A8, prev_t], [st1])
            K.op("dve", lambda pim=pim: nc.vector.tensor_tensor(out=st2.ap, in0=B8.ap, in1=pim, op=ALU.mult), [B8, prev_t], [st2])
            K.op("dve", lambda: nc.vector.tensor_tensor(out=st1.ap, in0=st1.ap, in1=st2.ap, op=ALU.add), [st1, st2], [st1])
            K.op("dve", lambda c=c: nc.vector.tensor_tensor(out=Ssb.ap[:, :, :, c], in0=Ssb.ap[:, :, :, c], in1=st1.ap, op=ALU.add), [Ssb, st1], [Ssb])
    def ssm_c(kind, ti, nsub, TT, NC):
        uTc = uT.ap[:, :, 0:TT].rearrange("p m (c i) -> p m i c", i=8)
        if kind == "p":
            K.op("dve", lambda: nc.vector.tensor_copy(out=Hc.ap, in_=Ssb.ap[:, :, :, NC - 1]), [Ssb], [Hc])
            if ti == NT_P - 1:
                K.op("dve", lambda: nc.vector.tensor_copy(out=hout.ap, in_=Ssb.ap[:, :, :, NC - 1]), [Ssb], [hout])
                for g2 in range(2):
                    K.dma("pool", hoch, hrp.rearrange("(k two) n -> two n k", two=2)[g2], hout.ap[64 * g2:64 * g2 + 64, :, 0], reads=[hout], batch=(g2 > 0))
                    K.dma("pool", hoch, hip.rearrange("(k two) n -> two n k", two=2)[g2], hout.ap[64 * g2:64 * g2 + 64, :, 1], reads=[hout], batch=True)
        else:
            for s in range(NSEQ_S):
                for g2 in range(2):
                    K.dma("pool", hoch, hrs[s].rearrange("(k two) n -> two n k", two=2)[g2], Ssb.ap[64 * g2:64 * g2 + 64, :, 0, 8 * s + 7], reads=[Ssb], batch=(s + g2 > 0))
                    K.dma("pool", hoch, his[s].rearrange("(k two) n -> two n k", two=2)[g2], Ssb.ap[64 * g2:64 * g2 + 64, :, 1, 8 * s + 7], reads=[Ssb], batch=True)
        (wg_t, wg_v), = R.get("wglu")
        for kk in range(4):
            b = 6 + kk % 2
            yv = PB[b].ap[:, 0:TT].rearrange("p (c i) -> p i c", i=8)
            fns = [lambda kk=kk, b=b: nc.tensor.matmul(PB[b].ap[:, 0:TT], Dd.ap[:, kk, :], uT.ap[:, kk, 0:TT], start=True, stop=True)]
            for j in range(8):
                for i in range(j + 1):
                    fns.append(lambda kk=kk, j=j, i=i, yv=yv: nc.tensor.matmul(yv[:, j, :], BD.ap[:, kk, j - i, :], uTc[:, kk, i, :], start=False, stop=True, skip_group_check=True))
            for r in range(4):
                kp = 4 * kk + r
                for j in range(8):
                    for reim in range(2):
                        lastone = (r == 3 and j == 7 and reim == 1)
                        fns.append(lambda kp=kp, r=r, j=j, reim=reim, yv=yv, lastone=lastone: nc.tensor.matmul(
                            yv[32 * r:32 * r + 32, j, :], W2.ap[:, kp, reim, j, :], Hbf.ap[:, kp, reim, 0:NC], start=False, stop=True, skip_group_check=True, tile_position=(0, 32 * r)))
            K.pe(fns, [Dd, uT, BD, W2, Hbf], [PB[b]])
            yp_ = PB[b].ap[:, 0:TT]
            K.op("act", lambda yp_=yp_: nc.scalar.activation(out=tA.ap[:, 0:TT], in_=yp_, func=AF.Square), [PB[b]], [tA])
            K.op("dve", lambda: nc.vector.tensor_scalar(out=tA.ap[:, 0:TT], in0=tA.ap[:, 0:TT], scalar1=0.044715, scalar2=1.0, op0=ALU.mult, op1=ALU.add), [tA], [tA])
            K.op("dve", lambda yp_=yp_: nc.vector.tensor_tensor(out=tA.ap[:, 0:TT], in0=yp_, in1=tA.ap[:, 0:TT], op=ALU.mult), [PB[b], tA], [tA])
            K.op("act", lambda: nc.scalar.activation(out=tA.ap[:, 0:TT], in_=tA.ap[:, 0:TT], func=AF.Sigmoid, scale=1.5957691216), [tA], [tA])
            K.op("dve", lambda kk=kk, yp_=yp_: nc.vector.tensor_tensor(out=y2.ap[:, kk, 0:TT], in0=yp_, in1=tA.ap[:, 0:TT], op=ALU.mult), [PB[b], tA], [y2])
            K.op("pool", lambda kk=kk: nc.gpsimd.tensor_copy(out=y2b.ap[:, kk, 0:TT], in_=y2.ap[:, kk, 0:TT]), [y2], [y2b])
        for m in range(4):
            b = 4 + m % 2
            fns = [lambda k=k, m=m, b=b: nc.tensor.matmul(PB[b].ap[:, 0:TT], wg_v[:, k, 128 * m:128 * m + 128], y2b.ap[:, k, 0:TT], start=(k == 0), stop=(k == 3)) for k in range(4)]
            K.pe(fns, [wg_t, y2b], [PB[b]])
            K.op("act", lambda b=b: nc.scalar.activation(out=tB.ap[:, 0:TT], in_=PB[b].ap[:, 0:TT], func=AF.Sigmoid), [PB[b]], [tB])
            K.op("dve", lambda m=m: nc.vector.tensor_tensor(out=y2.ap[:, m, 0:TT], in0=y2.ap[:, m, 0:TT], in1=tB.ap[:, 0:TT], op=ALU.mult), [y2, tB], [y2])
            K.op("pool", lambda m=m: nc.gpsimd.tensor_tensor(out=sqb.ap[:, m, 0:TT], in0=y2.ap[:, m, 0:TT], in1=y2.ap[:, m, 0:TT], op=ALU.mult), [y2], [sqb])
        fns = [lambda m=m: nc.tensor.matmul(PB[6].ap[:, 0:TT], ones_b.ap, sqb.ap[:, m, 0:TT], start=(m == 0), stop=(m == 3)) for m in range(4)]
        K.pe(fns, [ones_b, sqb], [PB[6]])
        K.op("act", lambda: nc.scalar.activation(out=rbc.ap[:, 0:TT], in_=PB[6].ap[:, 0:TT], func=AF.Sqrt, scale=1.0 / 512, bias=EPS), [PB[6]], [rbc])
        K.op("dve", lambda: nc.vector.reciprocal(out=rbc.ap[:, 0:TT], in_=rbc.ap[:, 0:TT]), [rbc], [rbc])
        for m in range(4):
            K.op("dve", lambda m=m: nc.vector.scalar_tensor_tensor(out=actT.ap[:, 4 + m, 0:TT], in0=y2.ap[:, m, 0:TT], scalar=gcols.ap[:, 26 + m:27 + m], in1=rbc.ap[:, 0:TT], op0=ALU.mult, op1=ALU.mult), [y2, gcols, rbc], [actT])

    kvbufs = list(kvtok.items)
    for t_ in cqn.items:
        kvbufs.append(T(t_.ap[:, 0:352], share=t_))
    for t_ in qtok.items:
        kvbufs.append(T(t_.ap.rearrange("p h c -> p (h c)")[:, 0:352], share=t_))
    kvl_items = [(s, pt_i, sub) for s in range(NSEQ_S) for pt_i in range(NPT_S) for sub in range(4)]
    kvl_state = {"loaded": 0}
    kvl_ch = [K.chan("kvl%d" % i) for i in range(len(kvbufs))]

    def kvl_prefetch(upto):
        while kvl_state["loaded"] < min(len(kvl_items), upto):
            i_ = kvl_state["loaded"]
            s, pt_i, sub = kvl_items[i_]
            kvt = kvbufs[i_ % len(kvbufs)]
            ch = kvl_ch[i_ % len(kvbufs)]
            r0 = 512 * pt_i + 128 * sub
            K.op("pool", lambda kvt=kvt: nc.gpsimd.memset(kvt.ap[:, 256:320], 0.0), [], [kvt])
            K.dma("pool", ch, kvt.ap[:, 0:256], ckl[s, r0:r0 + 128, :], writes=[kvt])
            K.dma("pool", ch, kvt.ap[:, 320:352], ckr[s, r0:r0 + 128, :], writes=[kvt], batch=True)
            kvl_state["loaded"] += 1

    def kv_from_cache(s, pt_i):
        base = (s * NPT_S + pt_i) * 4
        for sub in range(4):
            kvl_prefetch(base + sub + 5)
            kvt = kvbufs[(base + sub) % len(kvbufs)]
            kv_from_tok(kvt, sub, 512)
        (wkv_t, wkv_v), = R.get("wkv")
        kv_build(wkv_t, wkv_v, 4)
        reg = kvreg[s + 1][pt_i]
        K.dma("pool", ktsch, ktc[s + 1][:, :, 512 * pt_i:512 * pt_i + 512].rearrange("h r n -> r h n"), KTs.ap[0:96, :, :], reads=[KTs], writes=[reg])
        K.dma("pool", vsch, vc[s + 1][:, :, 4 * pt_i:4 * pt_i + 4, :].rearrange("h p s c -> p h s c"), Vs.ap[:, :, :, :], reads=[Vs], writes=[reg])

    K.op("pool", lambda: nc.gpsimd.memset(Hc.ap, 0.0), [], [Hc])
    for ti in range(NT_P):
        R.add(plan_tok)
    for s in range(NSEQ_S * NPT_S):
        R.add(plan_kv)
    R.add(plan_tok)
    try:
        for ti in range(NT_P):
            token_tile("p", ti)
        chk(9)
        for s in range(NSEQ_S):
            for pt_i in range(NPT_S):
                kv_from_cache(s, pt_i)
        chk(10)
        token_tile("s", 0)
    except Stop:
        pass
    K.finish("pool")
    return nc, K


_CACHE = {}


def _rope_table(pos):
    half = 16
    inv_freq = (10000.0 ** (-(np.arange(half, dtype=np.float32) * 2.0) / 32)).astype(np.float32)
    ang = pos.astype(np.float32)[:, None] * inv_freq[None, :]
    return np.concatenate([np.cos(ang), np.sin(ang)], axis=1).astype(np.float32)


def run(inputs, SEQ, PAST, ncores=8):
    key = (SEQ, PAST)
    if key not in _CACHE:
        _CACHE[key] = build(SEQ, PAST)
    nc, K = _CACHE[key]
    f = lambda a: np.ascontiguousarray(np.asarray(a, dtype=np.float32))
    x_prompt = f(inputs["x_prompt"]); x_sample = f(inputs["x_sample"])
    ckl = f(inputs["cache_kv_latent"])[0]; ckr = f(inputs["cache_k_rope"])[0]
    sre = f(inputs["state_ssm_re"])[0]; sim = f(inputs["state_ssm_im"])[0]
    cs_p = _rope_table(np.arange(SEQ))
    cs_s = np.tile(_rope_table(PAST + np.arange(DSEQ)), (NSEQ_S, 1))
    ident = np.eye(128, dtype=np.float32)
    shared = {
        "g_mix": f(inputs["g_mix"]), "w_in": f(inputs["w_in"])[0], "g_q": f(inputs["g_q_a"]), "w_q": f(inputs["w_q_up"])[0],
        "g_kv": f(inputs["g_kv_a"]), "w_kv": f(inputs["w_kv_up"])[0], "a_re": f(inputs["a_re"])[0], "a_im": f(inputs["a_im"])[0],
        "lstep": f(inputs["log_step"]), "b_re": f(inputs["b_re"])[0], "b_im": f(inputs["b_im"])[0],
        "c_re": f(inputs["c_re"])[0], "c_im": f(inputs["c_im"])[0], "d_skip": f(inputs["d_skip"]), "w_glu": f(inputs["w_glu"])[0],
        "g_attn": f(inputs["g_attn_out"]), "g_ssm": f(inputs["g_ssm_out"]), "w_out": f(inputs["w_out"])[0],
        "g_mlp": f(inputs["g_mlp"]), "w_up": f(inputs["w_up"])[0], "w_down": f(inputs["w_down"])[0],
        "g_fin": f(inputs["g_final"]).reshape(1, D), "cs_p": cs_p, "cs_s": cs_s, "ident": ident,
    }
    in_maps = []
    for c in range(ncores):
        m = dict(shared)
        m["xp"] = x_prompt[c]
        sl = slice(NSEQ_S * c, NSEQ_S * (c + 1))
        m["xs"] = np.ascontiguousarray(x_sample[sl].reshape(NSEQ_S * DSEQ, D))
        m["ckl"] = np.ascontiguousarray(ckl[sl]); m["ckr"] = np.ascontiguousarray(ckr[sl])
        m["sre"] = np.ascontiguousarray(sre[sl]); m["sim"] = np.ascontiguousarray(sim[sl])
        in_maps.append(m)
    res = run_bass_kernel_spmd(nc, in_maps, core_ids=list(range(ncores)))
    rs = res.results
    cat = lambda k: np.stack([np.asarray(r[k], dtype=np.float32) for r in rs])
    y_prompt = cat("yp")
    y_sample = cat("ys").reshape(ncores * NSEQ_S, DSEQ, D)
    lat_p = cat("latp")[None]
    kr_p = cat("krp")[None]
    hr_p = cat("hrp")[None]
    hi_p = cat("hip")[None]
    lat_s = cat("lats").reshape(ncores * NSEQ_S, DSEQ, KVL)[None]
    kr_s = cat("krs").reshape(ncores * NSEQ_S, DSEQ, 32)[None]
    hr_s = cat("hrs").reshape(ncores * NSEQ_S, 32, 64)[None]
    hi_s = cat("his").reshape(ncores * NSEQ_S, 32, 64)[None]
    return (y_prompt, y_sample, lat_p, kr_p, hr_p, hi_p, lat_s, kr_s, hr_s, hi_s)


def kernel(**inputs):
    return run(inputs, 8192, 4096, 8)
```

```python
import math
import numpy as np
import ml_dtypes
import concourse.bass as bass
import concourse.mybir as mybir
from concourse.bass_utils import run_bass_kernel_spmd

F32 = mybir.dt.float32
BF16 = mybir.dt.bfloat16
I32 = mybir.dt.int32
AF = mybir.ActivationFunctionType
ALU = mybir.AluOpType
AX = mybir.AxisListType

D = 1024
DIN = 1568
QL = 768
KVL = 256
NH = 8
DFF = 4096
EPS = 1e-6
SCALE = 96 ** -0.5
NSEQ_S = 4
DSEQ = 64
TWO_PI = 2.0 * math.pi


class T:
    def __init__(self, ap, name="", share=None):
        self.ap = ap
        self.name = name
        self.d = share.d if share is not None else {"w": None, "r": {}}

    @property
    def w(self):
        return self.d["w"]

    @w.setter
    def w(self, v):
        self.d["w"] = v

    @property
    def r(self):
        return self.d["r"]

    @r.setter
    def r(self, v):
        self.d["r"] = v


class Chan:
    def __init__(self, nc, name):
        self.sem = nc.alloc_semaphore(name)
        self.total = 0
        self.key = name
        self.holder = [0]


class Trk:
    def __init__(self, nc):
        self.nc = nc
        self.E = {"pe": nc.tensor, "act": nc.scalar, "dve": nc.vector, "pool": nc.gpsimd, "sp": nc.sync}
        self.sem = {}
        self.cnt = {}
        self.gen = {}
        for e in ("pe", "act", "dve", "pool"):
            self.gen[e] = 0
            self._newsem(e)
        self.seen = {}
        self.chans = []
        self.ninst = {e: 0 for e in self.E}

    def _newsem(self, e):
        self.sem[e] = self.nc.alloc_semaphore("c_%s_%d" % (e, self.gen[e]))
        self.cnt[e] = 0
        self.gen[e] += 1

    def chan(self, name):
        c = Chan(self.nc, name)
        self.chans.append(c)
        return c

    def _wait(self, eng, ev):
        if ev is None:
            return
        sem, key, val = ev
        if isinstance(val, list):
            val = val[0]
        k = (eng, key)
        if self.seen.get(k, 0) >= val:
            return
        self.E[eng].wait_ge(sem, val)
        self.ninst[eng] += 1
        self.seen[k] = val

    def _deps(self, eng, reads, writes, skip_same=False):
        for t in reads:
            if t.w is not None and not (skip_same and t.w[1][0] == eng):
                self._wait(eng, t.w)
        for t in writes:
            if t.w is not None and not (skip_same and t.w[1][0] == eng):
                self._wait(eng, t.w)
            for ev in t.r.values():
                if not (skip_same and ev[1][0] == eng):
                    self._wait(eng, ev)

    def _done(self, ev, reads, writes):
        for t in reads:
            t.r[ev[1]] = ev
        for t in writes:
            t.w = ev
            t.r = {}

    def op(self, eng, fn, reads=(), writes=()):
        self._deps(eng, reads, writes)
        inst = fn()
        if self.cnt[eng] >= 60000:
            self._newsem(eng)
        self.cnt[eng] += 1
        inst.then_inc(self.sem[eng], 1)
        self.ninst[eng] += 1
        ev = (self.sem[eng], (eng, self.gen[eng]), self.cnt[eng])
        self._done(ev, reads, writes)

    def pe(self, fns, reads=(), writes=()):
        self._deps("pe", reads, writes, skip_same=True)
        inst = None
        for f in fns:
            inst = f()
            self.ninst["pe"] += 1
        if self.cnt["pe"] >= 60000:
            self._newsem("pe")
        self.cnt["pe"] += 1
        inst.then_inc(self.sem["pe"], 1)
        ev = (self.sem["pe"], ("pe", self.gen["pe"]), self.cnt["pe"])
        self._done(ev, reads, writes)

    def dma(self, q, ch, out_ap, in_ap, reads=(), writes=(), batch=False, **kw):
        self._deps(q, reads, writes)
        if not batch:
            if ch.total > 0:
                self._wait(q, (ch.sem, ch.key, ch.total))
            ch.holder = [ch.total]
        inst = self.E[q].dma_start(out=out_ap, in_=in_ap, allow_slow_non_contiguous=True, **kw)
        ch.total += 16
        ch.holder[0] = ch.total
        inst.then_inc(ch.sem, 16)
        self.ninst[q] += 1
        ev = (ch.sem, ch.key, ch.holder)
        self._done(ev, reads, writes)

    def barrier(self):
        for eng in ("pe", "act", "dve", "pool", "sp"):
            self.finish(eng)

    def finish(self, eng="pool"):
        for c in self.chans:
            if c.total > 0:
                self._wait(eng, (c.sem, c.key, c.total))
        for e in ("pe", "act", "dve", "pool"):
            if self.cnt[e] > 0 and e != eng:
                self._wait(eng, (self.sem[e], (e, self.gen[e]), self.cnt[e]))


class Rot:
    def __init__(self, items):
        self.items = items
        self.i = 0

    def next(self):
        t = self.items[self.i % len(self.items)]
        self.i += 1
        return t


def bc(ap, shape):
    return ap.to_broadcast(shape)


def build(SEQ, PAST, stage=99):
    nc = bass.Bass("TRN2", target_bir_lowering=False)
    K = Trk(nc)
    NT_P = SEQ // 512
    NKT_P = SEQ // 128
    NK_S = PAST + 128
    NKT_S = NK_S // 128
    NPT_S = PAST // 512

    def din(name, shape, dt=F32):
        return nc.dram_tensor(name, list(shape), dt, kind="ExternalInput").ap()

    def dout(name, shape):
        return nc.dram_tensor(name, list(shape), F32, kind="ExternalOutput").ap()

    def dscr(name, shape, dt=BF16):
        return nc.dram_tensor(name, list(shape), dt, kind="Internal").ap()

    def sb(name, shape, dt=F32):
        return nc.alloc_sbuf_tensor(name, list(shape), dt).ap()

    xp = din("xp", [SEQ, D]); xs = din("xs", [NSEQ_S * DSEQ, D])
    ckl = din("ckl", [NSEQ_S, PAST, KVL]); ckr = din("ckr", [NSEQ_S, PAST, 32])
    sre = din("sre", [NSEQ_S, 32, 64]); sim = din("sim", [NSEQ_S, 32, 64])
    g_mix = din("g_mix", [1, D]); w_in = din("w_in", [D, DIN]); g_q = din("g_q", [1, QL])
    w_q = din("w_q", [QL, QL]); g_kv = din("g_kv", [1, KVL]); w_kv = din("w_kv", [KVL, 1024])
    a_re = din("a_re", [32, 64]); a_im = din("a_im", [32, 64]); lstep = din("lstep", [1, 32])
    b_re = din("b_re", [32, 64, 16]); b_im = din("b_im", [32, 64, 16])
    c_re = din("c_re", [32, 16, 64]); c_im = din("c_im", [32, 16, 64])
    d_skip = din("d_skip", [1, 512]); w_glu = din("w_glu", [512, 512])
    g_attn = din("g_attn", [1, 512]); g_ssm = din("g_ssm", [1, 512]); w_out = din("w_out", [D, D])
    g_mlp = din("g_mlp", [1, D]); w_up = din("w_up", [D, DFF]); w_down = din("w_down", [DFF, D])
    g_fin = din("g_fin", [1, D])
    cs_p = din("cs_p", [SEQ, 32]); cs_s = din("cs_s", [NSEQ_S * DSEQ, 32])
    ident_in = din("ident", [128, 128])

    yp = dout("yp", [SEQ, D]); ys = dout("ys", [NSEQ_S * DSEQ, D])
    latp = dout("latp", [SEQ, KVL]); krp = dout("krp", [SEQ, 32])
    hrp = dout("hrp", [32, 64]); hip = dout("hip", [32, 64])
    lats = dout("lats", [NSEQ_S * DSEQ, KVL]); krs = dout("krs", [NSEQ_S * DSEQ, 32])
    hrs = dout("hrs", [NSEQ_S, 32, 64]); his = dout("his", [NSEQ_S, 32, 64])

    s_winc = dscr("s_winc", [128, 8, 1056]); s_winu = dscr("s_winu", [128, 8, 512])
    s_wq = dscr("s_wq", [128, 6, 768]); s_wkv = dscr("s_wkv", [128, 2, 1024])
    s_wglu = dscr("s_wglu", [128, 4, 512]); s_wout = dscr("s_wout", [128, 8, 1024])
    s_wup = dscr("s_wup", [8, 128, 8, 512]); s_wdn = dscr("s_wdn", [8, 128, 4, 1024])
    ktc = [dscr("ktc0", [NH, 96, SEQ])] + [dscr("ktc%d" % (s + 1), [NH, 96, NK_S]) for s in range(NSEQ_S)]
    vc = [dscr("vc0", [NH, 128, NKT_P, 65])] + [dscr("vc%d" % (s + 1), [NH, 128, NKT_S, 65]) for s in range(NSEQ_S)]
    kvreg = [[T(None, "kvreg0_%d" % i) for i in range(NT_P)]] + \
            [[T(None, "kvreg%d_%d" % (s + 1, i)) for i in range(NPT_S + 1)] for s in range(NSEQ_S)]
    wscr = T(None, "wscr")

    ps = nc.alloc_psum_tensor("ps", [128, 4096], F32).ap()
    PB = [T(ps[:, 512 * b:512 * (b + 1)], "pb%d" % b) for b in range(8)]

    def pbf(b):
        return ps[:, 512 * b:512 * (b + 1)].bitcast(BF16)

    from contextlib import ExitStack

    def prod(sh):
        n = 1
        for v_ in sh:
            n *= v_
        return n

    def vw(t, dt, shape, off=0):
        a_ = t.ap if dt == F32 else t.ap.bitcast(dt)
        esz = 4 if dt in (F32, I32) else 2
        n = prod(shape)
        a_ = a_[:, off // esz:off // esz + n]
        if len(shape) == 2:
            a_ = a_.rearrange("p (a b) -> p a b", a=shape[0])
        elif len(shape) == 3:
            a_ = a_.rearrange("p (a b c) -> p a b c", a=shape[0], b=shape[1])
        return a_

    def raw(name, nbytes):
        return T(sb(name, [128, nbytes // 4]), name)

    ident_f = T(sb("ident_f", [128, 128]))
    ident_b = T(sb("ident_b", [128, 128], BF16))
    ones_b = T(sb("ones_b", [128, 128], BF16))
    gkv_bc = T(sb("gkv_bc", [128, KVL])); gfin_bc = T(sb("gfin_bc", [128, D]))
    gcols = T(sb("gcols", [128, 40]))
    W1 = T(sb("W1", [128, 4, 2, 8, 128], BF16))
    W2 = T(sb("W2", [128, 16, 2, 8, 32], BF16))
    BD = T(sb("BD", [128, 4, 8, 128], BF16))
    Dd = T(sb("Dd", [128, 4, 128], BF16))
    A8 = T(sb("A8", [128, 16, 2])); B8 = T(sb("B8", [128, 16, 2]))
    Hc = T(sb("Hc", [128, 16, 2])); H0s = T(sb("H0s", [128, NSEQ_S, 16, 2])); h0ch = K.chan("h0")
    cch = K.chan("const")
    cchA = K.chan("ssmA")
    cchB = K.chan("ssmB")
    castch = K.chan("cast")

    def alloc_runtime():
        g = {}
        NSLOT = 5
        g["NSLOT"] = NSLOT
        g["ring"] = [T(sb("ring%d" % i, [128, 4096], BF16)) for i in range(NSLOT)]
        g["ringch"] = [K.chan("ring%d" % i) for i in range(NSLOT)]
        g["xt"] = T(sb("xt", [128, 4, D])); g["xch"] = [K.chan("xt%d" % i) for i in range(4)]; g["ych"] = [K.chan("yst%d" % i) for i in range(4)]
        g["cst"] = T(sb("cst", [128, 4, 32])); g["csch"] = K.chan("cst"); g["kvlch"] = K.chan("kvl")
        g["actT"] = T(sb("actT", [128, 8, 512], BF16))
        rx = [raw("rx%d" % i, 2048) for i in range(2)]
        g["xnb"] = Rot([T(vw(r_, BF16, (D,)), share=r_) for r_ in rx])
        g["trl"] = Rot([T(vw(r_, F32, (512,)), share=r_) for r_ in rx])
        g["junk"] = T(sb("junk", [128, D], BF16))
        g["cqn"] = Rot([T(sb("cqn%d" % i, [128, QL], BF16)) for i in range(2)])
        rq = [raw("rq%d" % i, 4096) for i in range(2)]
        g["aT"] = [T(vw(r_, BF16, (4, 512)), share=r_) for r_ in rq]
        g["qTh"] = [T(vw(r_, BF16, (4, 512)), share=r_) for r_ in rq]
        rD = raw("rD", 8192)
        g["cqnT"] = T(sb("cqnT", [128, 6, 512], BF16))
        g["Ssb"] = T(vw(rD, F32, (16, 2, 64)), share=rD)
        g["kvtok"] = Rot([T(sb("kvtok%d" % i, [128, 352], BF16)) for i in range(2)])
        g["kvout"] = [T(sb("kvout%d" % i, [128, 288])) for i in range(2)]
        g["kvoch"] = [K.chan("kvout%d" % i) for i in range(2)]
        g["latT"] = T(sb("latT", [128, 2, 512], BF16))
        g["qtok"] = Rot([T(sb("qtok%d" % i, [128, 8, 96], BF16)) for i in range(2)])
        g["uT"] = T(sb("uT", [128, 4, 512], BF16))
        g["KTs"] = T(sb("KTs", [128, 8, 512], BF16)); g["ktsch"] = K.chan("kts")
        g["Vs"] = T(sb("Vs", [128, 8, 4, 65], BF16)); g["vsch"] = K.chan("vs")
        rA = raw("rA", 8192)
        g["attn"] = T(vw(rA, F32, (4, 512)), share=rA)
        g["y2"] = T(vw(rA, F32, (4, 512)), share=rA)
        rB = [raw("rB%d" % i, 2048) for i in range(2)]
        g["OTs"] = Rot([T(vw(r_, F32, (512,)), share=r_) for r_ in rB])
        g["tA"] = T(vw(rB[0], F32, (512,)), share=rB[0]); g["tB"] = T(vw(rB[1], F32, (512,)), share=rB[1])
        rC = [raw("rC%d" % i, 4096) for i in range(2)]
        g["Kblk"] = [T(vw(r_, BF16, (2048,)), share=r_) for r_ in rC]
        g["y2b"] = T(vw(rC[0], BF16, (4, 512)), share=rC[0]); g["sqb"] = T(vw(rC[1], BF16, (4, 512)), share=rC[1])
        g["Vblk"] = [T(sb("Vblk%d" % i, [128, 16, 65], BF16)) for i in range(2)]
        g["kbch"] = [K.chan("kb%d" % i) for i in range(2)]
        g["vbch"] = [K.chan("vb%d" % i) for i in range(2)]
        g["PT"] = Rot([T(sb("PT%d" % i, [128, 512], BF16)) for i in range(3)])
        g["stat"] = T(sb("stat", [128, 64]))
        g["rtmp"] = T(sb("rtmp", [128, 8, 64]))
        g["Hbf"] = T(sb("Hbf", [128, 16, 2, 64], BF16))
        g["st1"] = T(sb("st1", [128, 16, 2])); g["st2"] = T(sb("st2", [128, 16, 2]))
        g["hout"] = T(sb("hout", [128, 16, 2])); g["hoch"] = K.chan("hout")
        g["rbc"] = T(sb("rbc", [128, 512]))
        g["qsb"] = T(sb("qsb", [128, 768]))
        return g


    K.dma("sp", cch, ident_f.ap, ident_in, writes=[ident_f], batch=True)
    K.dma("sp", cch, gkv_bc.ap, g_kv.partition_broadcast(128), writes=[gkv_bc], batch=True)
    K.dma("sp", cch, gfin_bc.ap, g_fin.partition_broadcast(128), writes=[gfin_bc], batch=True)
    for (src, off, n) in ((g_mix, 0, 8), (g_mlp, 8, 8), (g_q, 16, 6), (g_attn, 22, 4), (g_ssm, 26, 4), (d_skip, 30, 4)):
        K.dma("sp", cch, gcols.ap[:, off:off + n], src.rearrange("o (k c) -> c (o k)", c=128), writes=[gcols], batch=True)
    K.op("dve", lambda: nc.vector.tensor_copy(out=ident_b.ap, in_=ident_f.ap), [ident_f], [ident_b])
    K.op("pool", lambda: nc.gpsimd.memset(ones_b.ap, 1.0), [], [ones_b])

    wv = w_in.rearrange("(k p) n -> p k n", p=128)
    casts = [
        (s_winc, wv[:, :, 0:1056]), (s_winu, wv[:, :, 1056:1568]),
        (s_wq, w_q.rearrange("(k p) n -> p k n", p=128)),
    ]
    wkvv = w_kv.rearrange("(k p) (h c) -> p k h c", p=128, c=128)
    for k in range(2):
        casts.append((s_wkv[:, k, 0:512].rearrange("p (h c) -> p h c", c=64), wkvv[:, k, :, 0:64]))
        casts.append((s_wkv[:, k, 512:1024].rearrange("p (h c) -> p h c", c=64), wkvv[:, k, :, 64:128]))
    casts.append((s_wglu, w_glu.rearrange("(k p) n -> p k n", p=128)))
    casts.append((s_wout, w_out.rearrange("(k p) n -> p k n", p=128)))
    wupv = w_up.rearrange("(k p) (fc n) -> fc p k n", p=128, n=512)
    wdnv = w_down.rearrange("(fc ft p) n -> fc p ft n", ft=4, p=128)
    for fc in range(8):
        for k in range(0, 8, 4):
            casts.append((s_wup[fc, :, k:k + 4, :], wupv[fc, :, k:k + 4, :]))
        for ft in range(0, 4, 2):
            casts.append((s_wdn[fc, :, ft:ft + 2, :], wdnv[fc, :, ft:ft + 2, :]))
    for (o, i) in casts:
        K.dma("pool", castch, o, i, writes=[wscr], batch=True)

    if stage == 0:
        K.finish("pool")
        return nc, K
    def ssm_setup(es):
        def sb(name, shape, dt=F32):
            return es.enter_context(nc.sbuf_tensor(name, list(shape), dt)).ap()
        Are = T(sb("Are", [128, 16])); Aim = T(sb("Aim", [128, 16])); LS = T(sb("LS", [128, 16]))
        for g2 in range(2):
            K.dma("sp", cchA, Are.ap[64 * g2:64 * g2 + 64, :], a_re.rearrange("(k two) n -> two n k", two=2)[g2], writes=[Are], batch=True)
            K.dma("sp", cchA, Aim.ap[64 * g2:64 * g2 + 64, :], a_im.rearrange("(k two) n -> two n k", two=2)[g2], writes=[Aim], batch=True)
            K.dma("sp", cchA, LS.ap[64 * g2:64 * g2 + 64, :], lstep.rearrange("o (k two) -> two o k", two=2)[g2].partition_broadcast(64), writes=[LS], batch=True)
        BreD = T(sb("BreD", [128, 16, 32])); BimD = T(sb("BimD", [128, 16, 32]))
        CreD = T(sb("CreD", [32, 16, 128])); CimD = T(sb("CimD", [32, 16, 128]))
        for t in (BreD, BimD, CreD, CimD):
            K.op("pool", lambda t=t: nc.gpsimd.memset(t.ap, 0.0), [], [t])
        for g2 in range(2):
            K.dma("sp", cchB, BreD.ap[64 * g2:64 * g2 + 64, :, 16 * g2:16 * g2 + 16], b_re.rearrange("(k two) n q -> two n k q", two=2)[g2], writes=[BreD], batch=True)
            K.dma("sp", cchB, BimD.ap[64 * g2:64 * g2 + 64, :, 16 * g2:16 * g2 + 16], b_im.rearrange("(k two) n q -> two n k q", two=2)[g2], writes=[BimD], batch=True)
            K.dma("sp", cchB, CreD.ap[16 * g2:16 * g2 + 16, :, 64 * g2:64 * g2 + 64], c_re.rearrange("(k two) p n -> two p k n", two=2)[g2], writes=[CreD], batch=True)
            K.dma("sp", cchB, CimD.ap[16 * g2:16 * g2 + 16, :, 64 * g2:64 * g2 + 64], c_im.rearrange("(k two) p n -> two p k n", two=2)[g2], writes=[CimD], batch=True)
        for s in range(NSEQ_S):
            for g2 in range(2):
                K.dma("sp", h0ch, H0s.ap[64 * g2:64 * g2 + 64, s, :, 0], sre[s].rearrange("(k two) n -> two n k", two=2)[g2], writes=[H0s], batch=True)
                K.dma("sp", h0ch, H0s.ap[64 * g2:64 * g2 + 64, s, :, 1], sim[s].rearrange("(k two) n -> two n k", two=2)[g2], writes=[H0s], batch=True)

        def V(name):
            return T(sb(name, [128, 16]))
        dt_ = V("dt_"); mag = V("mag"); th = V("th"); cs = V("cs"); sn = V("sn")
        w1 = V("w1"); w2 = V("w2"); w3 = V("w3"); wi = T(sb("wi", [128, 16], I32))
        K.op("act", lambda: nc.scalar.activation(out=dt_.ap, in_=LS.ap, func=AF.Exp), [LS], [dt_])
        K.op("dve", lambda: nc.vector.tensor_tensor(out=w1.ap, in0=Are.ap, in1=dt_.ap, op=ALU.mult), [Are, dt_], [w1])
        K.op("act", lambda: nc.scalar.activation(out=mag.ap, in_=w1.ap, func=AF.Exp), [w1], [mag])
        K.op("dve", lambda: nc.vector.tensor_tensor(out=th.ap, in0=Aim.ap, in1=dt_.ap, op=ALU.mult), [Aim, dt_], [th])

        def sin_of(dst, shift):
            K.op("dve", lambda: nc.vector.tensor_scalar(out=w1.ap, in0=th.ap, scalar1=shift, scalar2=1.0 / TWO_PI, op0=ALU.add, op1=ALU.mult), [th], [w1])
            K.op("dve", lambda: nc.vector.tensor_copy(out=wi.ap, in_=w1.ap), [w1], [wi])
            K.op("dve", lambda: nc.vector.tensor_copy(out=w2.ap, in_=wi.ap), [wi], [w2])
            K.op("dve", lambda: nc.vector.tensor_scalar(out=w1.ap, in0=th.ap, scalar1=shift, scalar2=None, op0=ALU.add), [th], [w1])
            K.op("dve", lambda: nc.vector.scalar_tensor_tensor(out=w1.ap, in0=w2.ap, scalar=-TWO_PI, in1=w1.ap, op0=ALU.mult, op1=ALU.add), [w2, w1], [w1])
            K.op("dve", lambda: nc.vector.tensor_scalar(out=w2.ap, in0=w1.ap, scalar1=math.pi, scalar2=-TWO_PI, op0=ALU.is_gt, op1=ALU.mult), [w1], [w2])
            K.op("dve", lambda: nc.vector.tensor_tensor(out=w1.ap, in0=w1.ap, in1=w2.ap, op=ALU.add), [w1, w2], [w1])
            K.op("dve", lambda: nc.vector.tensor_scalar(out=w2.ap, in0=w1.ap, scalar1=-math.pi, scalar2=TWO_PI, op0=ALU.is_lt, op1=ALU.mult), [w1], [w2])
            K.op("dve", lambda: nc.vector.tensor_tensor(out=w1.ap, in0=w1.ap, in1=w2.ap, op=ALU.add), [w1, w2], [w1])
            K.op("dve", lambda: nc.vector.tensor_scalar(out=w1.ap, in0=w1.ap, scalar1=3.1415925, scalar2=-3.1415925, op0=ALU.min, op1=ALU.max), [w1], [w1])
            K.op("act", lambda: nc.scalar.activation(out=dst.ap, in_=w1.ap, func=AF.Sin), [w1], [dst])
        sin_of(sn, 0.0)
        sin_of(cs, math.pi / 2)
        LP = T(sb("LP", [128, 9, 2, 16]))
        K.op("pool", lambda: nc.gpsimd.memset(LP.ap[:, 0, 0, :], 1.0), [], [LP])
        K.op("pool", lambda: nc.gpsimd.memset(LP.ap[:, 0, 1, :], 0.0), [], [LP])
        K.op("dve", lambda: nc.vector.tensor_tensor(out=LP.ap[:, 1, 0, :], in0=mag.ap, in1=cs.ap, op=ALU.mult), [mag, cs], [LP])
        K.op("dve", lambda: nc.vector.tensor_tensor(out=LP.ap[:, 1, 1, :], in0=mag.ap, in1=sn.ap, op=ALU.mult), [mag, sn], [LP])
        for k in range(2, 9):
            pr, pi_ = LP.ap[:, k - 1, 0, :], LP.ap[:, k - 1, 1, :]
            lr, li = LP.ap[:, 1, 0, :], LP.ap[:, 1, 1, :]
            K.op("dve", lambda: nc.vector.tensor_tensor(out=w1.ap, in0=pr, in1=lr, op=ALU.mult), [LP], [w1])
            K.op("dve", lambda: nc.vector.tensor_tensor(out=w2.ap, in0=pi_, in1=li, op=ALU.mult), [LP], [w2])
            K.op("dve", lambda: nc.vector.tensor_tensor(out=LP.ap[:, k, 0, :], in0=w1.ap, in1=w2.ap, op=ALU.subtract), [w1, w2], [LP])
            K.op("dve", lambda: nc.vector.tensor_tensor(out=w1.ap, in0=pr, in1=li, op=ALU.mult), [LP], [w1])
            K.op("dve", lambda: nc.vector.tensor_tensor(out=w2.ap, in0=pi_, in1=lr, op=ALU.mult), [LP], [w2])
            K.op("dve", lambda: nc.vector.tensor_tensor(out=LP.ap[:, k, 1, :], in0=w1.ap, in1=w2.ap, op=ALU.add), [w1, w2], [LP])
        K.op("dve", lambda: nc.vector.tensor_copy(out=A8.ap[:, :, 0], in_=LP.ap[:, 8, 0, :]), [LP], [A8])
        K.op("dve", lambda: nc.vector.tensor_copy(out=A8.ap[:, :, 1], in_=LP.ap[:, 8, 1, :]), [LP], [A8])
        K.op("dve", lambda: nc.vector.tensor_scalar(out=B8.ap[:, :, 0], in0=LP.ap[:, 8, 1, :], scalar1=-1.0, scalar2=None, op0=ALU.mult), [LP], [B8])
        K.op("dve", lambda: nc.vector.tensor_copy(out=B8.ap[:, :, 1], in_=LP.ap[:, 8, 0, :]), [LP], [B8])
        cre = V("cre"); cim = V("cim"); den = V("den"); nr = V("nr")
        K.op("dve", lambda: nc.vector.tensor_scalar(out=nr.ap, in0=LP.ap[:, 1, 0, :], scalar1=-1.0, scalar2=None, op0=ALU.add), [LP], [nr])
        K.op("dve", lambda: nc.vector.tensor_tensor(out=w1.ap, in0=Are.ap, in1=Are.ap, op=ALU.mult), [Are], [w1])
        K.op("dve", lambda: nc.vector.tensor_tensor(out=w2.ap, in0=Aim.ap, in1=Aim.ap, op=ALU.mult), [Aim], [w2])
        K.op("dve", lambda: nc.vector.tensor_tensor(out=den.ap, in0=w1.ap, in1=w2.ap, op=ALU.add), [w1, w2], [den])
        K.op("dve", lambda: nc.vector.reciprocal(out=den.ap, in_=den.ap), [den], [den])
        K.op("dve", lambda: nc.vector.tensor_tensor(out=w1.ap, in0=nr.ap, in1=Are.ap, op=ALU.mult), [nr, Are], [w1])
        K.op("dve", lambda: nc.vector.tensor_tensor(out=w2.ap, in0=LP.ap[:, 1, 1, :], in1=Aim.ap, op=ALU.mult), [LP, Aim], [w2])
        K.op("dve", lambda: nc.vector.tensor_tensor(out=w1.ap, in0=w1.ap, in1=w2.ap, op=ALU.add), [w1, w2], [w1])
        K.op("dve", lambda: nc.vector.tensor_tensor(out=cre.ap, in0=w1.ap, in1=den.ap, op=ALU.mult), [w1, den], [cre])
        K.op("dve", lambda: nc.vector.tensor_tensor(out=w1.ap, in0=LP.ap[:, 1, 1, :], in1=Are.ap, op=ALU.mult), [LP, Are], [w1])
        K.op("dve", lambda: nc.vector.tensor_tensor(out=w2.ap, in0=nr.ap, in1=Aim.ap, op=ALU.mult), [nr, Aim], [w2])
        K.op("dve", lambda: nc.vector.tensor_tensor(out=w1.ap, in0=w1.ap, in1=w2.ap, op=ALU.subtract), [w1, w2], [w1])
        K.op("dve", lambda: nc.vector.tensor_tensor(out=cim.ap, in0=w1.ap, in1=den.ap, op=ALU.mult), [w1, den], [cim])

        def B3(t):
            return t.unsqueeze(2).to_broadcast([128, 16, 32])
        m1 = T(sb("m1", [128, 16, 32])); m2 = T(sb("m2", [128, 16, 32]))
        BbR = T(sb("BbR", [128, 16, 32])); BbI = T(sb("BbI", [128, 16, 32]))

        def cmul(dre, dim, are, aim, bre, bim, rd, wr):
            if dre is not None:
                K.op("dve", lambda: nc.vector.tensor_tensor(out=m1.ap, in0=bre, in1=B3(are), op=ALU.mult), rd, [m1])
                K.op("dve", lambda: nc.vector.tensor_tensor(out=m2.ap, in0=bim, in1=B3(aim), op=ALU.mult), rd, [m2])
                K.op("dve", lambda: nc.vector.tensor_tensor(out=dre, in0=m1.ap, in1=m2.ap, op=ALU.subtract), [m1, m2], wr)
            if dim is not None:
                K.op("dve", lambda: nc.vector.tensor_tensor(out=m1.ap, in0=bim, in1=B3(are), op=ALU.mult), rd, [m1])
                K.op("dve", lambda: nc.vector.tensor_tensor(out=m2.ap, in0=bre, in1=B3(aim), op=ALU.mult), rd, [m2])
                K.op("dve", lambda: nc.vector.tensor_tensor(out=dim, in0=m1.ap, in1=m2.ap, op=ALU.add), [m1, m2], wr)
        cmul(BbR.ap, BbI.ap, cre.ap, cim.ap, BreD.ap, BimD.ap, [cre, cim, BreD, BimD], [BbR, BbI])
        GR = T(sb("GR", [128, 16, 32])); GI = T(sb("GI", [128, 16, 32]))
        for i in range(8):
            cmul(GR.ap, GI.ap, LP.ap[:, 7 - i, 0, :], LP.ap[:, 7 - i, 1, :], BbR.ap, BbI.ap, [LP, BbR, BbI], [GR, GI])
            for reim, G in ((0, GR), (1, GI)):
                for r in range(4):
                    bank = (reim * 4 + r)
                    fns = []
                    for kk in range(4):
                        kp = 4 * kk + r
                        fns.append(lambda kk=kk, kp=kp, G=G, bank=bank: nc.tensor.transpose(
                            out=PB[bank].ap[0:32, 128 * kk:128 * kk + 128], in_=G.ap[:, kp, :], identity=ident_f.ap))
                    K.pe(fns, [G, ident_f], [PB[bank]])
                    K.op("act", lambda r=r, bank=bank, reim=reim, i=i: nc.scalar.activation(
                        out=W1.ap[32 * r:32 * r + 32, :, reim, i, :],
                        in_=PB[bank].ap[0:32, :].rearrange("p (a b) -> p a b", a=4), func=AF.Copy), [PB[bank]], [W1])
        CTR = T(sb("CTR", [128, 16, 32])); CTI = T(sb("CTI", [128, 16, 32]))
        for (src, dst, bank) in ((CreD, CTR, 0), (CimD, CTI, 1)):
            fns = [lambda kp=kp, src=src, bank=bank: nc.tensor.transpose(out=PB[bank].ap[:, 32 * kp:32 * kp + 32], in_=src.ap[:, kp, :], identity=ident_f.ap[0:32, 0:32]) for kp in range(16)]
            K.pe(fns, [src, ident_f], [PB[bank]])
            K.op("act", lambda dst=dst, bank=bank: nc.scalar.activation(out=dst.ap, in_=PB[bank].ap.rearrange("p (a b) -> p a b", a=16), func=AF.Copy), [PB[bank]], [dst])
        K.op("pool", lambda: nc.gpsimd.memset(BD.ap, 0.0), [], [BD])
        CLR = T(sb("CLR", [128, 16, 32])); CLI = T(sb("CLI", [128, 16, 32])); NCLI = T(sb("NCLI", [128, 16, 32]))
        for kpow in range(9):
            cmul(CLR.ap, CLI.ap, LP.ap[:, kpow, 0, :], LP.ap[:, kpow, 1, :], CTR.ap, CTI.ap, [LP, CTR, CTI], [CLR, CLI])
            K.op("dve", lambda: nc.vector.tensor_scalar(out=NCLI.ap, in0=CLI.ap, scalar1=-1.0, scalar2=None, op0=ALU.mult), [CLI], [NCLI])
            if kpow >= 1:
                j = kpow - 1
                K.op("act", lambda j=j: nc.scalar.activation(out=W2.ap[:, :, 0, j, :], in_=CLR.ap, func=AF.Copy), [CLR], [W2])
                K.op("act", lambda j=j: nc.scalar.activation(out=W2.ap[:, :, 1, j, :], in_=NCLI.ap, func=AF.Copy), [NCLI], [W2])
            if kpow <= 7:
                tau = kpow
                for r in range(4):
                    for kk in range(4):
                        kp = 4 * kk + r
                        col = (kk * 8 + tau) * 32
                        bank = 2 * r + col // 512
                        c0 = col % 512
                        fns = [
                            lambda kp=kp, bank=bank, c0=c0: nc.tensor.matmul(PB[bank].ap[0:32, c0:c0 + 32], BbR.ap[:, kp, :], CLR.ap[:, kp, :], start=True, stop=False),
                            lambda kp=kp, bank=bank, c0=c0: nc.tensor.matmul(PB[bank].ap[0:32, c0:c0 + 32], BbI.ap[:, kp, :], NCLI.ap[:, kp, :], start=False, stop=True),
                        ]
                        K.pe(fns, [BbR, BbI, CLR, NCLI], [PB[bank]])
        for r in range(4):
            for hb in range(2):
                bank = 2 * r + hb
                K.op("act", lambda r=r, hb=hb, bank=bank: nc.scalar.activation(
                    out=BD.ap[32 * r:32 * r + 32, 2 * hb:2 * hb + 2, :, 32 * r:32 * r + 32],
                    in_=PB[bank].ap[0:32, :].rearrange("p (a t c) -> p a t c", a=2, t=8), func=AF.Copy), [PB[bank]], [BD])
        for kk in range(4):
            K.op("dve", lambda kk=kk: nc.vector.tensor_scalar(out=Dd.ap[:, kk, :], in0=ident_f.ap, scalar1=gcols.ap[:, 30 + kk:31 + kk], scalar2=None, op0=ALU.mult), [ident_f, gcols], [Dd])

    with ExitStack() as es_:
        ssm_setup(es_)
        K.barrier()
    if stage == 1:
        K.finish("pool")
        return nc, K
    G = alloc_runtime()
    NSLOT = G["NSLOT"]; ring = G["ring"]; ringch = G["ringch"]; xt = G["xt"]; xch = G["xch"]; ych = G["ych"]; cst = G["cst"]; csch = G["csch"]; kvlch = G["kvlch"]
    actT = G["actT"]; xnb = G["xnb"]; trl = G["trl"]; junk = G["junk"]; cqn = G["cqn"]; aT = G["aT"]; qTh = G["qTh"]; cqnT = G["cqnT"]; Ssb = G["Ssb"]
    kvtok = G["kvtok"]; kvout = G["kvout"]; kvoch = G["kvoch"]; latT = G["latT"]; qtok = G["qtok"]; uT = G["uT"]; KTs = G["KTs"]; ktsch = G["ktsch"]
    Vs = G["Vs"]; vsch = G["vsch"]; attn = G["attn"]; y2 = G["y2"]; OTs = G["OTs"]; tA = G["tA"]; tB = G["tB"]; Kblk = G["Kblk"]; y2b = G["y2b"]; sqb = G["sqb"]
    Vblk = G["Vblk"]; kbch = G["kbch"]; vbch = G["vbch"]; PT = G["PT"]; stat = G["stat"]; rtmp = G["rtmp"]; Hbf = G["Hbf"]; st1 = G["st1"]; st2 = G["st2"]
    hout = G["hout"]; hoch = G["hoch"]; rbc = G["rbc"]; qsb = G["qsb"]
    NKB = 2
    statc = [T(stat.ap[:, i:i + 1], "stat%d" % i) for i in range(64)]
    xts = [T(xt.ap[:, i, :], "xt%d" % i) for i in range(4)]
    actTs = [T(actT.ap[:, :, 128 * i:128 * i + 128], "actT%d" % i) for i in range(4)]
    cqnTs = [T(cqnT.ap[:, :, 128 * i:128 * i + 128], "cqnT%d" % i) for i in range(4)]
    aTr = Rot(aT)
    kvo_i = [0]
    K.op("pool", lambda: nc.gpsimd.memset(Vs.ap, 1.0), [], [Vs])

    class Ring:
        def __init__(self):
            self.plan = []
            self.issued = 0
            self.got = 0

        def add(self, items):
            self.plan.extend(items)

        def _issue(self, n):
            name, src, shape = self.plan[n]
            slot = n % NSLOT
            dst = ring[slot].ap
            ne = 1
            for s_ in shape:
                ne *= s_
            d = dst[:, 0:ne]
            if len(shape) == 2:
                d = d.rearrange("p (a b) -> p a b", a=shape[0])
            K.dma("sp", ringch[slot], d, src, reads=[wscr], writes=[ring[slot]])

        def get(self, *names, hold=0):
            n0 = self.got
            assert len(names) + hold <= NSLOT
            for i_, nm in enumerate(names):
                assert self.plan[n0 + i_][0] == nm, (self.plan[n0 + i_][0], nm)
            while self.issued < min(len(self.plan), n0 + NSLOT - hold):
                self._issue(self.issued)
                self.issued += 1
            self.got += len(names)
            outs = []
            for i_ in range(len(names)):
                n = n0 + i_
                shape = self.plan[n][2]
                ne = prod(shape)
                v = ring[n % NSLOT].ap[:, 0:ne]
                if len(shape) == 2:
                    v = v.rearrange("p (a b) -> p a b", a=shape[0])
                outs.append((ring[n % NSLOT], v))
            return outs

    R = Ring()
    plan_tok = [("winc0", s_winc[:, 0:3, :], (3, 1056)), ("winc1", s_winc[:, 3:6, :], (3, 1056)), ("winc2", s_winc[:, 6:8, :], (2, 1056)),
                ("wq0", s_wq[:, 0:3, :], (3, 768)), ("wq1", s_wq[:, 3:6, :], (3, 768)),
                ("wkv", s_wkv, (2, 1024)), ("winu", s_winu, (8, 512)), ("wglu", s_wglu, (4, 512)),
                ("wout0", s_wout[:, 0:4, :], (4, 1024)), ("wout1", s_wout[:, 4:8, :], (4, 1024))]
    for fc in range(8):
        plan_tok.append(("wup%d" % fc, s_wup[fc], (8, 512)))
        plan_tok.append(("wdn%d" % fc, s_wdn[fc], (4, 1024)))
    plan_kv = [("wkv", s_wkv, (2, 1024))]

    def rms_stats(src_ap, reads, n, col):
        st = statc[col]
        ss = st.ap
        K.op("act", lambda: nc.scalar.activation(out=junk.ap[:, 0:n], in_=src_ap, func=AF.Square, accum_out=ss), reads, [junk, st])
        K.op("act", lambda: nc.scalar.activation(out=ss, in_=ss, func=AF.Sqrt, scale=1.0 / n, bias=EPS), [st], [st])
        K.op("dve", lambda: nc.vector.reciprocal(out=ss, in_=ss), [st], [st])
        return st

    def transposes_to(dstT, dst_kslice, src_tile, nk, sub, gcol0, trk=None):
        fns = [lambda k=k: nc.tensor.transpose(out=pbf(3)[:, 128 * k:128 * k + 128], in_=src_tile.ap[:, 128 * k:128 * k + 128], identity=ident_b.ap) for k in range(nk)]
        K.pe(fns, [src_tile, ident_b], [PB[3]])
        for k in range(nk):
            K.op("dve", lambda k=k: nc.vector.tensor_scalar(out=dstT.ap[:, dst_kslice + k, 128 * sub:128 * sub + 128], in0=pbf(3)[:, 128 * k:128 * k + 128],
                                                           scalar1=gcols.ap[:, gcol0 + k:gcol0 + k + 1], scalar2=None, op0=ALU.mult), [PB[3], gcols], [trk if trk is not None else dstT])

    def kv_from_tok(kvt, sub, ncols_total):
        fns = [lambda k=k: nc.tensor.transpose(out=pbf(3)[:, 128 * k:128 * k + 128], in_=kvt.ap[:, 128 * k:128 * k + 128], identity=ident_b.ap) for k in range(2)]
        fns.append(lambda: nc.tensor.transpose(out=pbf(3)[0:96, 256:384], in_=kvt.ap[:, 256:352], identity=ident_b.ap))
        K.pe(fns, [kvt, ident_b], [PB[3]])
        K.op("dve", lambda: nc.vector.tensor_copy(out=latT.ap[:, :, 128 * sub:128 * sub + 128], in_=pbf(3)[:, 0:256].rearrange("p (a b) -> p a b", a=2)), [PB[3]], [latT])
        K.op("dve", lambda: nc.vector.tensor_copy(out=KTs.ap[64:96, :, 128 * sub:128 * sub + 128],
                                                 in_=pbf(3)[64:96, 256:384].unsqueeze(1).to_broadcast([32, 8, 128])), [PB[3]], [KTs])

    def kv_build(wkv_t, wkv_v, nsub):
        N = 128 * nsub
        for hp in range(4):
            b = 4 + hp % 2
            fns = [lambda k=k, hp=hp, b=b: nc.tensor.matmul(PB[b].ap[:, 0:N], wkv_v[:, k, 128 * hp:128 * hp + 128], latT.ap[:, k, 0:N], start=(k == 0), stop=(k == 1)) for k in range(2)]
            K.pe(fns, [wkv_t, latT], [PB[b]])
            K.op("dve", lambda hp=hp, b=b: nc.vector.tensor_copy(out=KTs.ap[0:64, 2 * hp, 0:N], in_=PB[b].ap[0:64, 0:N]), [PB[b]], [KTs])
            K.op("act", lambda hp=hp, b=b: nc.scalar.activation(out=KTs.ap[0:64, 2 * hp + 1, 0:N], in_=PB[b].ap[64:128, 0:N], func=AF.Copy), [PB[b]], [KTs])
        for sub in range(nsub):
            b = 6 + sub % 2
            fns = [lambda k=k, sub=sub, b=b: nc.tensor.matmul(PB[b].ap[:, :], latT.ap[:, k, 128 * sub:128 * sub + 128], wkv_v[:, k, 512:1024], start=(k == 0), stop=(k == 1)) for k in range(2)]
            K.pe(fns, [wkv_t, latT], [PB[b]])
            K.op("dve", lambda sub=sub, b=b: nc.vector.tensor_copy(out=Vs.ap[:, :, sub, 0:64], in_=PB[b].ap.rearrange("p (h c) -> p h c", c=64)), [PB[b]], [Vs])

    def attention(ci, n_full_kt, qc0, nq, diag_i, half_last, dsts, between=None):
        nkt = n_full_kt + (1 if half_last else 0)
        nblk = (nkt + 15) // 16
        tot_keys = n_full_kt * 128 + (64 if half_last else 0)
        for h in range(NH):
            ob = 6 + h % 2
            qh = qTh[h // 4]
            hl = h % 4
            steps = []
            for blk in range(nblk):
                kt0 = blk * 16
                nkb = min(16, nkt - kt0)
                for kl in range(nkb):
                    kt = kt0 + kl
                    kp = 64 if (half_last and kt == nkt - 1) else 128
                    isdiag = diag_i is not None and kt >= 4 * diag_i
                    n0 = 128 * (kt - 4 * diag_i) if isdiag else 0
                    steps.append(dict(blk=blk, kt0=kt0, nkb=nkb, kl=kl, kt=kt, kp=kp, isdiag=isdiag, n0=n0, N=nq - n0))
            blkslot = {}

            def emit_S(st):
                blk = st["blk"]
                if blk not in blkslot:
                    slot = actr[0] % NKB
                    actr[0] += 1
                    blkslot[blk] = slot
                    kt0, nkb = st["kt0"], st["nkb"]
                    nkeys = min(128 * nkb, tot_keys - 128 * kt0)
                    regs = kvreg[ci][(kt0 * 128) // 512:(kt0 * 128 + nkeys + 511) // 512]
                    K.dma("sp", kbch[slot], Kblk[slot].ap[0:96, 0:nkeys], ktc[ci][h, :, 128 * kt0:128 * kt0 + nkeys], reads=regs, writes=[Kblk[slot]])
                    nvp = 128 if nkeys >= 128 else 64
                    K.dma("sp", vbch[slot], Vblk[slot].ap[0:nvp, 0:nkb, :], vc[ci][h, 0:nvp, kt0:kt0 + nkb, :], reads=regs, writes=[Vblk[slot]])
                slot = blkslot[blk]
                st["slot"] = slot
                sbk = 4 + actr[1] % 2
                actr[1] += 1
                st["sbk"] = sbk
                kl, kp, n0, N = st["kl"], st["kp"], st["n0"], st["N"]
                K.pe([lambda: nc.tensor.matmul(PB[sbk].ap[0:kp, 0:N], Kblk[slot].ap[0:96, 128 * kl:128 * kl + kp], qh.ap[0:96, hl, qc0 + n0:qc0 + n0 + N], start=True, stop=True)],
                     [Kblk[slot], qh], [PB[sbk]])

            def emit_PV(st, first):
                slot, sbk, kl, kp, n0, N = st["slot"], st["sbk"], st["kl"], st["kp"], st["n0"], st["N"]
                pt = PT.next()
                K.op("act", lambda: nc.scalar.activation(out=pt.ap[0:kp, 0:N], in_=PB[sbk].ap[0:kp, 0:N], func=AF.Exp, scale=SCALE), [PB[sbk]], [pt])
                if st["isdiag"]:
                    K.op("dve", lambda: nc.vector.memset(pt.ap[64:128, 0:64], 0.0), [], [pt])
                K.pe([lambda: nc.tensor.matmul(PB[ob].ap[0:65, n0:n0 + N], Vblk[slot].ap[0:kp, kl, :], pt.ap[0:kp, 0:N], start=first, stop=True, skip_group_check=(not first))],
                     [Vblk[slot], pt], [PB[ob]])

            emit_S(steps[0])
            for j in range(len(steps)):
                if j + 1 < len(steps):
                    emit_S(steps[j + 1])
                emit_PV(steps[j], j == 0)
            if between is not None:
                between(h)
            ot = OTs.next()
            K.op("dve", lambda ot=ot, ob=ob: nc.vector.tensor_copy(out=ot.ap[0:64, 0:nq], in_=PB[ob].ap[0:64, 0:nq]), [PB[ob]], [ot])
            K.op("dve", lambda ot=ot, ob=ob: nc.vector.tensor_copy(out=ot.ap[64:65, 0:nq], in_=PB[ob].ap[64:65, 0:nq]), [PB[ob]], [ot])
            c0 = 0
            for (po, nqq, sub) in dsts:
                K.pe([lambda ot=ot, c0=c0, nqq=nqq: nc.tensor.transpose(out=PB[3].ap[0:nqq, 0:65], in_=ot.ap[0:65, c0:c0 + nqq], identity=ident_f.ap[0:65, 0:65])], [ot, ident_f], [PB[3]])
                rct = statc[32 + (actr[2] % 16)]
                rc = rct.ap[0:nqq, :]
                actr[2] += 1
                K.op("dve", lambda rc=rc, nqq=nqq: nc.vector.reciprocal(out=rc, in_=PB[3].ap[0:nqq, 64:65]), [PB[3]], [rct])
                K.op("dve", lambda rc=rc, po=po, nqq=nqq, sub=sub, h=h: nc.vector.tensor_scalar(out=attn.ap[po:po + nqq, sub, 64 * h:64 * h + 64], in0=PB[3].ap[0:nqq, 0:64], scalar1=rc, scalar2=None, op0=ALU.mult), [PB[3], rct], [attn])
                c0 += nqq
    actr = [0, 0, 0]

    class Stop(Exception):
        pass

    def chk(n):
        if stage == n:
            raise Stop()

    def token_tile(kind, ti):
        if kind == "p":
            nsub, x_src, cs_src = 4, xp[512 * ti:512 * ti + 512, :], cs_p[512 * ti:512 * ti + 512, :]
            y_dst, lat_dst, kr_dst = yp[512 * ti:512 * ti + 512, :], latp[512 * ti:512 * ti + 512, :], krp[512 * ti:512 * ti + 512, :]
        else:
            nsub, x_src, cs_src = 2, xs, cs_s
            y_dst, lat_dst, kr_dst = ys, lats, krs
        TT = 128 * nsub
        NC = TT // 8
        for sub in range(nsub):
            K.dma("sp", xch[sub], xt.ap[:, sub, :], x_src[128 * sub:128 * sub + 128, :], writes=[xts[sub]])
        K.dma("sp", csch, cst.ap[:, 0:nsub, :], cs_src.rearrange("(s p) d -> p s d", p=128), writes=[cst])
        rs = [rms_stats(xt.ap[:, sub, :], [xts[sub]], D, sub) for sub in range(nsub)]
        xbs = []

        def s1_scale(sub):
            xb = xnb.next()
            xbs.append(xb)
            K.op("dve", lambda: nc.vector.tensor_scalar(out=xb.ap, in0=xt.ap[:, sub, :], scalar1=rs[sub].ap, scalar2=None, op0=ALU.mult), [xts[sub], rs[sub]], [xb])
        s1_scale(0)
        for sub in range(nsub):
            if sub + 1 < nsub:
                s1_scale(sub + 1)
            transposes_to(actT, 0, xbs[sub], 8, sub, 0, actTs[sub])
        chk(2)
        wc = R.get("winc0", "winc1", "winc2")

        def c_mm(sub):
            pb0 = 0 if sub % 2 == 0 else 5
            base = 512 * pb0
            fns = []
            for k in range(8):
                wvw = wc[k // 3][1]
                kl = k % 3
                for (c0, c1) in ((0, 512), (512, 1024), (1024, 1056)):
                    fns.append(lambda k=k, kl=kl, wvw=wvw, c0=c0, c1=c1: nc.tensor.matmul(
                        ps[:, base + c0:base + c1], actT.ap[:, k, 128 * sub:128 * sub + 128], wvw[:, kl, c0:c1], start=(k == 0), stop=(k == 7)))
            K.pe(fns, [actTs[sub], wc[0][0], wc[1][0], wc[2][0]], [PB[pb0], PB[pb0 + 1], PB[pb0 + 2]])

        def c_post(sub):
            pb0 = 0 if sub % 2 == 0 else 5
            base = 512 * pb0
            P0, P1, P2 = PB[pb0], PB[pb0 + 1], PB[pb0 + 2]
            r = rms_stats(ps[:, base:base + QL], [P0, P1], QL, 8 + sub)
            cq = cqn.next()
            K.op("dve", lambda: nc.vector.tensor_scalar(out=cq.ap, in0=ps[:, base:base + QL], scalar1=r.ap, scalar2=None, op0=ALU.mult), [P0, P1, r], [cq])
            r2 = rms_stats(ps[:, base + QL:base + QL + KVL], [P1], KVL, 12 + sub)
            ko = kvout[kvo_i[0] % 2]; koc = kvoch[kvo_i[0] % 2]; kvo_i[0] += 1
            kvt = kvtok.next()
            K.op("dve", lambda: nc.vector.scalar_tensor_tensor(out=ko.ap[:, 0:KVL], in0=ps[:, base + QL:base + QL + KVL], scalar=r2.ap, in1=gkv_bc.ap, op0=ALU.mult, op1=ALU.mult), [P1, r2, gkv_bc], [ko])
            x1, x2 = ps[:, base + 1024:base + 1040], ps[:, base + 1040:base + 1056]
            cs_, sn_ = cst.ap[:, sub, 0:16], cst.ap[:, sub, 16:32]
            rt = rtmp.ap[:, 0, :]
            K.op("dve", lambda: nc.vector.tensor_tensor(out=rt[:, 0:16], in0=x1, in1=cs_, op=ALU.mult), [P2, cst], [rtmp])
            K.op("dve", lambda: nc.vector.tensor_tensor(out=rt[:, 16:32], in0=x2, in1=sn_, op=ALU.mult), [P2, cst], [rtmp])
            K.op("dve", lambda: nc.vector.tensor_tensor(out=rt[:, 32:48], in0=x1, in1=sn_, op=ALU.mult), [P2, cst], [rtmp])
            K.op("dve", lambda: nc.vector.tensor_tensor(out=rt[:, 48:64], in0=x2, in1=cs_, op=ALU.mult), [P2, cst], [rtmp])
            K.op("dve", lambda: nc.vector.tensor_tensor(out=ko.ap[:, 256:272], in0=rt[:, 0:16], in1=rt[:, 16:32], op=ALU.subtract), [rtmp], [ko])
            K.op("dve", lambda: nc.vector.tensor_tensor(out=ko.ap[:, 272:288], in0=rt[:, 32:48], in1=rt[:, 48:64], op=ALU.add), [rtmp], [ko])
            K.op("pool", lambda: nc.gpsimd.tensor_copy(out=kvt.ap[:, 0:256], in_=ko.ap[:, 0:256]), [ko], [kvt])
            K.op("pool", lambda: nc.gpsimd.tensor_copy(out=kvt.ap[:, 320:352], in_=ko.ap[:, 256:288]), [ko], [kvt])
            K.op("pool", lambda: nc.gpsimd.memset(kvt.ap[:, 256:320], 0.0), [], [kvt])
            K.dma("pool", koc, lat_dst[128 * sub:128 * sub + 128, :], ko.ap[:, 0:256], reads=[ko])
            K.dma("pool", koc, kr_dst[128 * sub:128 * sub + 128, :], ko.ap[:, 256:288], reads=[ko], batch=True)
            kv_from_tok(kvt, sub, TT)
            transposes_to(cqnT, 0, cq, 6, sub, 16, cqnTs[sub])
        c_mm(0)
        for sub in range(nsub):
            if sub + 1 < nsub:
                c_mm(sub + 1)
            c_post(sub)
        chk(3)
        wqs = R.get("wq0", "wq1")

        def q_mm(sub):
            pb0 = 0 if sub % 2 == 0 else 5
            base = 512 * pb0
            fns = []
            for k in range(6):
                wvw = wqs[k // 3][1]
                kl = k % 3
                for (c0, c1) in ((0, 512), (512, 768)):
                    fns.append(lambda k=k, kl=kl, wvw=wvw, c0=c0, c1=c1: nc.tensor.matmul(
                        ps[:, base + c0:base + c1], cqnT.ap[:, k, 128 * sub:128 * sub + 128], wvw[:, kl, c0:c1], start=(k == 0), stop=(k == 5)))
            K.pe(fns, [cqnTs[sub], wqs[0][0], wqs[1][0]], [PB[pb0], PB[pb0 + 1]])

        def q_post(sub):
            pb0 = 0 if sub % 2 == 0 else 5
            base = 512 * pb0
            K.op("act", lambda: nc.scalar.activation(out=qsb.ap[:, 0:512], in_=ps[:, base:base + 512], func=AF.Copy), [PB[pb0]], [qsb])
            K.op("dve", lambda: nc.vector.tensor_copy(out=qsb.ap[:, 512:768], in_=ps[:, base + 512:base + 768]), [PB[pb0 + 1]], [qsb])
            qv = qsb.ap.rearrange("p (h c) -> p h c", c=96)
            qk = qtok.next()
            K.op("pool", lambda: nc.gpsimd.tensor_copy(out=qk.ap[:, :, 0:64], in_=qv[:, :, 0:64]), [qsb], [qk])
            cs8 = cst.ap[:, sub, 0:16].unsqueeze(1).to_broadcast([128, 8, 16])
            sn8 = cst.ap[:, sub, 16:32].unsqueeze(1).to_broadcast([128, 8, 16])
            q1, q2 = qv[:, :, 64:80], qv[:, :, 80:96]
            rt4 = rtmp.ap.rearrange("p h (a c) -> p h a c", a=4)
            K.op("dve", lambda: nc.vector.tensor_tensor(out=rt4[:, :, 0, :], in0=q1, in1=cs8, op=ALU.mult), [qsb, cst], [rtmp])
            K.op("dve", lambda: nc.vector.tensor_tensor(out=rt4[:, :, 1, :], in0=q2, in1=sn8, op=ALU.mult), [qsb, cst], [rtmp])
            K.op("dve", lambda: nc.vector.tensor_tensor(out=rt4[:, :, 2, :], in0=q1, in1=sn8, op=ALU.mult), [qsb, cst], [rtmp])
            K.op("dve", lambda: nc.vector.tensor_tensor(out=rt4[:, :, 3, :], in0=q2, in1=cs8, op=ALU.mult), [qsb, cst], [rtmp])
            K.op("dve", lambda: nc.vector.tensor_tensor(out=qk.ap[:, :, 64:80], in0=rt4[:, :, 0, :], in1=rt4[:, :, 1, :], op=ALU.subtract), [rtmp], [qk])
            K.op("dve", lambda: nc.vector.tensor_tensor(out=qk.ap[:, :, 80:96], in0=rt4[:, :, 2, :], in1=rt4[:, :, 3, :], op=ALU.add), [rtmp], [qk])
            fns = [lambda h=h: nc.tensor.transpose(out=pbf(3)[0:96, 128 * h:128 * h + 128], in_=qk.ap[:, h, :], identity=ident_b.ap) for h in range(NH)]
            K.pe(fns, [qk, ident_b], [PB[3]])
            for hh in range(2):
                for (p0, p1) in ((0, 64), (64, 96)):
                    K.op("act", lambda hh=hh, p0=p0, p1=p1: nc.scalar.activation(out=qTh[hh].ap[p0:p1, :, 128 * sub:128 * sub + 128],
                                                                               in_=pbf(3)[p0:p1, 512 * hh:512 * hh + 512].rearrange("p (h c) -> p h c", h=4), func=AF.Copy), [PB[3]], [qTh[hh]])
        q_mm(0)
        for sub in range(nsub):
            if sub + 1 < nsub:
                q_mm(sub + 1)
            q_post(sub)
        chk(4)
        (wkv_t, wkv_v), = R.get("wkv")
        kv_build(wkv_t, wkv_v, nsub)
        if kind == "p":
            reg = kvreg[0][ti]
            K.dma("pool", ktsch, ktc[0][:, :, 512 * ti:512 * ti + 512].rearrange("h r n -> r h n"), KTs.ap[0:96, :, :], reads=[KTs], writes=[reg])
            K.dma("pool", vsch, vc[0][:, :, 4 * ti:4 * ti + 4, :].rearrange("h p s c -> p h s c"), Vs.ap[:, :, :, :], reads=[Vs], writes=[reg])
        else:
            for s in range(NSEQ_S):
                reg = kvreg[s + 1][NPT_S]
                sub, po = s // 2, 64 * (s % 2)
                K.dma("pool", ktsch, ktc[s + 1][:, :, PAST:PAST + 64].rearrange("h r n -> r h n"), KTs.ap[0:96, :, 64 * s:64 * s + 64], reads=[KTs], writes=[reg], batch=(s > 0))
                K.dma("pool", vsch, vc[s + 1][:, 0:64, NKT_S - 1, :].rearrange("h p c -> p h c"), Vs.ap[po:po + 64, :, sub, :], reads=[Vs], writes=[reg], batch=(s > 0))
                K.dma("pool", vsch, vc[s + 1][:, 64:128, NKT_S - 1, :].rearrange("h p c -> p h c"), Vs.ap[64 - po:128 - po, :, sub, :], reads=[Vs], writes=[reg], batch=True)
        (wu_t, wu_v), = R.get("winu")
        for m in range(4):
            b = 4 + m % 2
            fns = [lambda k=k, m=m, b=b: nc.tensor.matmul(PB[b].ap[:, 0:TT], wu_v[:, k, 128 * m:128 * m + 128], actT.ap[:, k, 0:TT], start=(k == 0), stop=(k == 7)) for k in range(8)]
            K.pe(fns, [wu_t] + actTs[0:nsub], [PB[b]])
            K.op("act", lambda m=m, b=b: nc.scalar.activation(out=uT.ap[:, m, 0:TT], in_=PB[b].ap[:, 0:TT], func=AF.Copy), [PB[b]], [uT])
        chk(5)
        ssm_a(kind, ti, nsub, TT, NC)
        scan_q = scan_step_fns(kind, NC)
        ncalls = [NH if kind == "p" else NH * NSEQ_S]

        def between(h):
            n_ = (len(scan_q) + ncalls[0] - 1) // ncalls[0]
            ncalls[0] -= 1
            for _ in range(n_):
                scan_q.pop(0)()
        if kind == "p":
            attention(0, 4 * ti + 4, 0, 512, ti, False, [(0, 128, s_) for s_ in range(4)], between)
        else:
            for s in range(NSEQ_S):
                attention(s + 1, PAST // 128, 64 * s, 64, None, True, [(64 * (s % 2), 64, s // 2)], between)
        while scan_q:
            scan_q.pop(0)()
        chk(6)
        for sub in range(nsub):
            r = rms_stats(attn.ap[:, sub, :], [attn], 512, 16 + sub)
            xb = xnb.next()
            K.op("dve", lambda sub=sub, xb=xb, r=r: nc.vector.tensor_scalar(out=xb.ap[:, 0:512], in0=attn.ap[:, sub, :], scalar1=r.ap, scalar2=None, op0=ALU.mult), [attn, r], [xb])
            transposes_to(actT, 0, xb, 4, sub, 22, actTs[sub])
        ssm_c(kind, ti, nsub, TT, NC)
        chk(7)
        wo = R.get("wout0", "wout1")
        for sub in range(nsub):
            fns = []
            for k in range(8):
                wvw = wo[k // 4][1]
                kl = k % 4
                for (c0, c1) in ((0, 512), (512, 1024)):
                    fns.append(lambda k=k, kl=kl, wvw=wvw, c0=c0, c1=c1, sub=sub: nc.tensor.matmul(
                        ps[:, c0:c1], actT.ap[:, k, 128 * sub:128 * sub + 128], wvw[:, kl, c0:c1], start=(k == 0), stop=(k == 7)))
            K.pe(fns, [actTs[sub], wo[0][0], wo[1][0]], [PB[0], PB[1]])
            K.op("dve", lambda sub=sub: nc.vector.tensor_tensor(out=xt.ap[:, sub, :], in0=ps[:, 0:1024], in1=xt.ap[:, sub, :], op=ALU.add), [PB[0], PB[1], xts[sub]], [xts[sub]])
        chk(8)
        for sub in range(nsub):
            r = rms_stats(xt.ap[:, sub, :], [xts[sub]], D, 20 + sub)
            xb = xnb.next()
            K.op("dve", lambda sub=sub, xb=xb, r=r: nc.vector.tensor_scalar(out=xb.ap, in0=xt.ap[:, sub, :], scalar1=r.ap, scalar2=None, op0=ALU.mult), [xts[sub], r], [xb])
            transposes_to(actT, 0, xb, 8, sub, 8, actTs[sub])
        def mlp_up(fc):
            (wu_t, wu_v), (wd_t, wd_v) = R.get("wup%d" % fc, "wdn%d" % fc, hold=(2 if fc > 0 else 0))
            a = aTr.next()
            for ft in range(4):
                b = 4 + ft % 2
                fns = [lambda k=k, ft=ft, b=b: nc.tensor.matmul(PB[b].ap[:, 0:TT], wu_v[:, k, 128 * ft:128 * ft + 128], actT.ap[:, k, 0:TT], start=(k == 0), stop=(k == 7)) for k in range(8)]
                K.pe(fns, [wu_t] + actTs[0:nsub], [PB[b]])
                tr = trl.next()
                K.op("dve", lambda b=b, tr=tr: nc.vector.tensor_scalar(out=tr.ap[:, 0:TT], in0=PB[b].ap[:, 0:TT], scalar1=0.0, scalar2=None, op0=ALU.max), [PB[b]], [tr])
                K.op("pool", lambda ft=ft, tr=tr, a=a: nc.gpsimd.tensor_tensor(out=a.ap[:, ft, 0:TT], in0=tr.ap[:, 0:TT], in1=tr.ap[:, 0:TT], op=ALU.mult), [tr], [a])
            return a, wd_t, wd_v

        def mlp_down(a, wd_t, wd_v):
            for sub in range(nsub):
                bb = (0, 1) if sub % 2 == 0 else (6, 7)
                base = 512 * bb[0]
                fns = []
                for ft in range(4):
                    for hh in range(2):
                        fns.append(lambda ft=ft, hh=hh, sub=sub, base=base: nc.tensor.matmul(
                            ps[:, base + 512 * hh:base + 512 * hh + 512], a.ap[:, ft, 128 * sub:128 * sub + 128], wd_v[:, ft, 512 * hh:512 * hh + 512], start=(ft == 0), stop=(ft == 3)))
                K.pe(fns, [a, wd_t], [PB[bb[0]], PB[bb[1]]])
                K.op("dve", lambda sub=sub, base=base: nc.vector.tensor_tensor(out=xt.ap[:, sub, :], in0=ps[:, base:base + 1024], in1=xt.ap[:, sub, :], op=ALU.add), [PB[bb[0]], PB[bb[1]], xts[sub]], [xts[sub]])

        cur = mlp_up(0)
        for fc in range(8):
            nxt = mlp_up(fc + 1) if fc + 1 < 8 else None
            mlp_down(*cur)
            cur = nxt
        for sub in range(nsub):
            r = rms_stats(xt.ap[:, sub, :], [xts[sub]], D, 24 + sub)
            K.op("dve", lambda sub=sub, r=r: nc.vector.scalar_tensor_tensor(out=xt.ap[:, sub, :], in0=xt.ap[:, sub, :], scalar=r.ap, in1=gfin_bc.ap, op0=ALU.mult, op1=ALU.mult), [xts[sub], r, gfin_bc], [xts[sub]])
            K.dma("pool", ych[sub], y_dst[128 * sub:128 * sub + 128, :], xt.ap[:, sub, :], reads=[xts[sub]])

    def ssm_a(kind, ti, nsub, TT, NC):
        uTc = uT.ap[:, :, 0:TT].rearrange("p m (c i) -> p m i c", i=8)
        Sv = Ssb.ap.rearrange("p (kk r) t c -> p r kk t c", r=4)
        for r in range(4):
            bank = 4 + r
            for kk in range(4):
                for reim in range(2):
                    c0 = (kk * 2 + reim) * NC
                    fns = [lambda i=i, kk=kk, r=r, reim=reim, bank=bank, c0=c0: nc.tensor.matmul(
                        PB[bank].ap[:, c0:c0 + NC], W1.ap[32 * r:32 * r + 32, kk, reim, i, :], uTc[32 * r:32 * r + 32, kk, i, :],
                        start=(i == 0), stop=(i == 7), tile_position=(32 * r, 0)) for i in range(8)]
                    K.pe(fns, [W1, uT], [PB[bank]])
            K.op("act", lambda r=r, bank=bank: nc.scalar.activation(
                out=Sv[:, r, :, :, 0:NC], in_=PB[bank].ap[:, 0:8 * NC].rearrange("p (a t c) -> p a t c", a=4, t=2), func=AF.Copy), [PB[bank]], [Ssb])
    def scan_step_fns(kind, NC):
        return [(lambda c=c: scan_step(kind, c)) for c in range(NC)]

    def scan_step(kind, c):
        if True:
            if kind == "p":
                prev_t, prev = (Hc, Hc.ap) if c == 0 else (Ssb, Ssb.ap[:, :, :, c - 1])
            else:
                if c % 8 == 0:
                    prev_t, prev = H0s, H0s.ap[:, c // 8, :, :]
                else:
                    prev_t, prev = Ssb, Ssb.ap[:, :, :, c - 1]
            pre = prev[:, :, 0:1].to_broadcast([128, 16, 2])
            pim = prev[:, :, 1:2].to_broadcast([128, 16, 2])
            if c % 8 == 0 or kind != "p" or True:
                pass
            K.op("pool", lambda c=c, prev=prev: nc.gpsimd.tensor_copy(out=Hbf.ap[:, :, :, c], in_=prev), [prev_t], [Hbf])
            K.op("pool", lambda pre=pre: nc.gpsimd.tensor_tensor(out=st1.ap, in0=A8.ap, in1=pre, op=ALU.mult), [A8, prev_t], [st1])
            K.op("pool", lambda pim=pim: nc.gpsimd.tensor_tensor(out=st2.ap, in0=B8.ap, in1=pim, op=ALU.mult), [B8, prev_t], [st2])
            K.op("pool", lambda: nc.gpsimd.tensor_tensor(out=st1.ap, in0=st1.ap, in1=st2.ap, op=ALU.add), [st1, st2], [st1])
            K.op("pool", lambda c=c: nc.gpsimd.tensor_tensor(out=Ssb.ap[:, :, :, c], in0=Ssb.ap[:, :, :, c], in1=st1.ap, op=ALU.add), [Ssb, st1], [Ssb])
    def ssm_c(kind, ti, nsub, TT, NC):
        uTc = uT.ap[:, :, 0:TT].rearrange("p m (c i) -> p m i c", i=8)
        if kind == "p":
            K.op("dve", lambda: nc.vector.tensor_copy(out=Hc.ap, in_=Ssb.ap[:, :, :, NC - 1]), [Ssb], [Hc])
            if ti == NT_P - 1:
                K.op("dve", lambda: nc.vector.tensor_copy(out=hout.ap, in_=Ssb.ap[:, :, :, NC - 1]), [Ssb], [hout])
                for g2 in range(2):
                    K.dma("pool", hoch, hrp.rearrange("(k two) n -> two n k", two=2)[g2], hout.ap[64 * g2:64 * g2 + 64, :, 0], reads=[hout], batch=(g2 > 0))
                    K.dma("pool", hoch, hip.rearrange("(k two) n -> two n k", two=2)[g2], hout.ap[64 * g2:64 * g2 + 64, :, 1], reads=[hout], batch=True)
        else:
            for s in range(NSEQ_S):
                for g2 in range(2):
                    K.dma("pool", hoch, hrs[s].rearrange("(k two) n -> two n k", two=2)[g2], Ssb.ap[64 * g2:64 * g2 + 64, :, 0, 8 * s + 7], reads=[Ssb], batch=(s + g2 > 0))
                    K.dma("pool", hoch, his[s].rearrange("(k two) n -> two n k", two=2)[g2], Ssb.ap[64 * g2:64 * g2 + 64, :, 1, 8 * s + 7], reads=[Ssb], batch=True)
        (wg_t, wg_v), = R.get("wglu")
        for kk in range(4):
            b = 6 + kk % 2
            yv = PB[b].ap[:, 0:TT].rearrange("p (c i) -> p i c", i=8)
            fns = [lambda kk=kk, b=b: nc.tensor.matmul(PB[b].ap[:, 0:TT], Dd.ap[:, kk, :], uT.ap[:, kk, 0:TT], start=True, stop=True)]
            for j in range(8):
                for i in range(j + 1):
                    fns.append(lambda kk=kk, j=j, i=i, yv=yv: nc.tensor.matmul(yv[:, j, :], BD.ap[:, kk, j - i, :], uTc[:, kk, i, :], start=False, stop=True, skip_group_check=True))
            for r in range(4):
                kp = 4 * kk + r
                for j in range(8):
                    for reim in range(2):
                        lastone = (r == 3 and j == 7 and reim == 1)
                        fns.append(lambda kp=kp, r=r, j=j, reim=reim, yv=yv, lastone=lastone: nc.tensor.matmul(
                            yv[32 * r:32 * r + 32, j, :], W2.ap[:, kp, reim, j, :], Hbf.ap[:, kp, reim, 0:NC], start=False, stop=True, skip_group_check=True, tile_position=(0, 32 * r)))
            K.pe(fns, [Dd, uT, BD, W2, Hbf], [PB[b]])
            yp_ = PB[b].ap[:, 0:TT]
            K.op("act", lambda yp_=yp_: nc.scalar.activation(out=tA.ap[:, 0:TT], in_=yp_, func=AF.Square), [PB[b]], [tA])
            K.op("dve", lambda: nc.vector.tensor_scalar(out=tA.ap[:, 0:TT], in0=tA.ap[:, 0:TT], scalar1=0.044715, scalar2=1.0, op0=ALU.mult, op1=ALU.add), [tA], [tA])
            K.op("dve", lambda yp_=yp_: nc.vector.tensor_tensor(out=tA.ap[:, 0:TT], in0=yp_, in1=tA.ap[:, 0:TT], op=ALU.mult), [PB[b], tA], [tA])
            K.op("act", lambda: nc.scalar.activation(out=tA.ap[:, 0:TT], in_=tA.ap[:, 0:TT], func=AF.Sigmoid, scale=1.5957691216), [tA], [tA])
            K.op("dve", lambda kk=kk, yp_=yp_: nc.vector.tensor_tensor(out=y2.ap[:, kk, 0:TT], in0=yp_, in1=tA.ap[:, 0:TT], op=ALU.mult), [PB[b], tA], [y2])
            K.op("pool", lambda kk=kk: nc.gpsimd.tensor_copy(out=y2b.ap[:, kk, 0:TT], in_=y2.ap[:, kk, 0:TT]), [y2], [y2b])
        for m in range(4):
            b = 4 + m % 2
            fns = [lambda k=k, m=m, b=b: nc.tensor.matmul(PB[b].ap[:, 0:TT], wg_v[:, k, 128 * m:128 * m + 128], y2b.ap[:, k, 0:TT], start=(k == 0), stop=(k == 3)) for k in range(4)]
            K.pe(fns, [wg_t, y2b], [PB[b]])
            K.op("act", lambda b=b: nc.scalar.activation(out=tB.ap[:, 0:TT], in_=PB[b].ap[:, 0:TT], func=AF.Sigmoid), [PB[b]], [tB])
            K.op("dve", lambda m=m: nc.vector.tensor_tensor(out=y2.ap[:, m, 0:TT], in0=y2.ap[:, m, 0:TT], in1=tB.ap[:, 0:TT], op=ALU.mult), [y2, tB], [y2])
            K.op("pool", lambda m=m: nc.gpsimd.tensor_tensor(out=sqb.ap[:, m, 0:TT], in0=y2.ap[:, m, 0:TT], in1=y2.ap[:, m, 0:TT], op=ALU.mult), [y2], [sqb])
        fns = [lambda m=m: nc.tensor.matmul(PB[6].ap[:, 0:TT], ones_b.ap, sqb.ap[:, m, 0:TT], start=(m == 0), stop=(m == 3)) for m in range(4)]
        K.pe(fns, [ones_b, sqb], [PB[6]])
        K.op("act", lambda: nc.scalar.activation(out=rbc.ap[:, 0:TT], in_=PB[6].ap[:, 0:TT], func=AF.Sqrt, scale=1.0 / 512, bias=EPS), [PB[6]], [rbc])
        K.op("dve", lambda: nc.vector.reciprocal(out=rbc.ap[:, 0:TT], in_=rbc.ap[:, 0:TT]), [rbc], [rbc])
        for m in range(4):
            K.op("dve", lambda m=m: nc.vector.scalar_tensor_tensor(out=actT.ap[:, 4 + m, 0:TT], in0=y2.ap[:, m, 0:TT], scalar=gcols.ap[:, 26 + m:27 + m], in1=rbc.ap[:, 0:TT], op0=ALU.mult, op1=ALU.mult), [y2, gcols, rbc], actTs[0:nsub])

    kvbufs = list(kvtok.items)
    for t_ in cqn.items:
        kvbufs.append(T(t_.ap[:, 0:352], share=t_))
    for t_ in qtok.items:
        kvbufs.append(T(t_.ap.rearrange("p h c -> p (h c)")[:, 0:352], share=t_))
    kvl_items = [(s, pt_i, sub) for s in range(NSEQ_S) for pt_i in range(NPT_S) for sub in range(4)]
    kvl_state = {"loaded": 0}
    kvl_ch = [K.chan("kvl%d" % i) for i in range(len(kvbufs))]

    def kvl_prefetch(upto):
        while kvl_state["loaded"] < min(len(kvl_items), upto):
            i_ = kvl_state["loaded"]
            s, pt_i, sub = kvl_items[i_]
            kvt = kvbufs[i_ % len(kvbufs)]
            ch = kvl_ch[i_ % len(kvbufs)]
            r0 = 512 * pt_i + 128 * sub
            K.op("pool", lambda kvt=kvt: nc.gpsimd.memset(kvt.ap[:, 256:320], 0.0), [], [kvt])
            K.dma("pool", ch, kvt.ap[:, 0:256], ckl[s, r0:r0 + 128, :], writes=[kvt])
            K.dma("pool", ch, kvt.ap[:, 320:352], ckr[s, r0:r0 + 128, :], writes=[kvt], batch=True)
            kvl_state["loaded"] += 1

    def kv_from_cache(s, pt_i):
        base = (s * NPT_S + pt_i) * 4
        for sub in range(4):
            kvl_prefetch(base + sub + 5)
            kvt = kvbufs[(base + sub) % len(kvbufs)]
            kv_from_tok(kvt, sub, 512)
        (wkv_t, wkv_v), = R.get("wkv")
        kv_build(wkv_t, wkv_v, 4)
        reg = kvreg[s + 1][pt_i]
        K.dma("pool", ktsch, ktc[s + 1][:, :, 512 * pt_i:512 * pt_i + 512].rearrange("h r n -> r h n"), KTs.ap[0:96, :, :], reads=[KTs], writes=[reg])
        K.dma("pool", vsch, vc[s + 1][:, :, 4 * pt_i:4 * pt_i + 4, :].rearrange("h p s c -> p h s c"), Vs.ap[:, :, :, :], reads=[Vs], writes=[reg])

    K.op("pool", lambda: nc.gpsimd.memset(Hc.ap, 0.0), [], [Hc])
    for ti in range(NT_P):
        R.add(plan_tok)
    for s in range(NSEQ_S * NPT_S):
        R.add(plan_kv)
    R.add(plan_tok)
    try:
        for ti in range(NT_P):
            token_tile("p", ti)
        chk(9)
        for s in range(NSEQ_S):
            for pt_i in range(NPT_S):
                kv_from_cache(s, pt_i)
        chk(10)
        token_tile("s", 0)
    except Stop:
        pass
    K.finish("pool")
    return nc, K


_CACHE = {}


def _rope_table(pos):
    half = 16
    inv_freq = (10000.0 ** (-(np.arange(half, dtype=np.float32) * 2.0) / 32)).astype(np.float32)
    ang = pos.astype(np.float32)[:, None] * inv_freq[None, :]
    return np.concatenate([np.cos(ang), np.sin(ang)], axis=1).astype(np.float32)


def run(inputs, SEQ, PAST, ncores=8):
    key = (SEQ, PAST)
    if key not in _CACHE:
        _CACHE[key] = build(SEQ, PAST)
    nc, K = _CACHE[key]
    f = lambda a: np.ascontiguousarray(np.asarray(a, dtype=np.float32))
    x_prompt = f(inputs["x_prompt"]); x_sample = f(inputs["x_sample"])
    ckl = f(inputs["cache_kv_latent"])[0]; ckr = f(inputs["cache_k_rope"])[0]
    sre = f(inputs["state_ssm_re"])[0]; sim = f(inputs["state_ssm_im"])[0]
    cs_p = _rope_table(np.arange(SEQ))
    cs_s = np.tile(_rope_table(PAST + np.arange(DSEQ)), (NSEQ_S, 1))
    ident = np.eye(128, dtype=np.float32)
    shared = {
        "g_mix": f(inputs["g_mix"]), "w_in": f(inputs["w_in"])[0], "g_q": f(inputs["g_q_a"]), "w_q": f(inputs["w_q_up"])[0],
        "g_kv": f(inputs["g_kv_a"]), "w_kv": f(inputs["w_kv_up"])[0], "a_re": f(inputs["a_re"])[0], "a_im": f(inputs["a_im"])[0],
        "lstep": f(inputs["log_step"]), "b_re": f(inputs["b_re"])[0], "b_im": f(inputs["b_im"])[0],
        "c_re": f(inputs["c_re"])[0], "c_im": f(inputs["c_im"])[0], "d_skip": f(inputs["d_skip"]), "w_glu": f(inputs["w_glu"])[0],
        "g_attn": f(inputs["g_attn_out"]), "g_ssm": f(inputs["g_ssm_out"]), "w_out": f(inputs["w_out"])[0],
        "g_mlp": f(inputs["g_mlp"]), "w_up": f(inputs["w_up"])[0], "w_down": f(inputs["w_down"])[0],
        "g_fin": f(inputs["g_final"]).reshape(1, D), "cs_p": cs_p, "cs_s": cs_s, "ident": ident,
    }
    in_maps = []
    for c in range(ncores):
        m = dict(shared)
        m["xp"] = x_prompt[c]
        sl = slice(NSEQ_S * c, NSEQ_S * (c + 1))
        m["xs"] = np.ascontiguousarray(x_sample[sl].reshape(NSEQ_S * DSEQ, D))
        m["ckl"] = np.ascontiguousarray(ckl[sl]); m["ckr"] = np.ascontiguousarray(ckr[sl])
        m["sre"] = np.ascontiguousarray(sre[sl]); m["sim"] = np.ascontiguousarray(sim[sl])
        in_maps.append(m)
    res = run_bass_kernel_spmd(nc, in_maps, core_ids=list(range(ncores)))
    rs = res.results
    cat = lambda k: np.stack([np.asarray(r[k], dtype=np.float32) for r in rs])
    y_prompt = cat("yp")
    y_sample = cat("ys").reshape(ncores * NSEQ_S, DSEQ, D)
    lat_p = cat("latp")[None]
    kr_p = cat("krp")[None]
    hr_p = cat("hrp")[None]
    hi_p = cat("hip")[None]
    lat_s = cat("lats").reshape(ncores * NSEQ_S, DSEQ, KVL)[None]
    kr_s = cat("krs").reshape(ncores * NSEQ_S, DSEQ, 32)[None]
    hr_s = cat("hrs").reshape(ncores * NSEQ_S, 32, 64)[None]
    hi_s = cat("his").reshape(ncores * NSEQ_S, 32, 64)[None]
    return (y_prompt, y_sample, lat_p, kr_p, hr_p, hi_p, lat_s, kr_s, hr_s, hi_s)


def kernel(**inputs):
    return run(inputs, 8192, 4096, 8)
```

```python
import math
import numpy as np
import ml_dtypes
import concourse.bass as bass
import concourse.mybir as mybir
from concourse.bass_utils import run_bass_kernel_spmd

F32 = mybir.dt.float32
BF16 = mybir.dt.bfloat16
I32 = mybir.dt.int32
AF = mybir.ActivationFunctionType
ALU = mybir.AluOpType
AX = mybir.AxisListType

D = 1024
DIN = 1568
QL = 768
KVL = 256
NH = 8
DFF = 4096
EPS = 1e-6
SCALE = 96 ** -0.5
NSEQ_S = 4
DSEQ = 64
TWO_PI = 2.0 * math.pi


class T:
    def __init__(self, ap, name="", share=None):
        self.ap = ap
        self.name = name
        self.d = share.d if share is not None else {"w": None, "r": {}}

    @property
    def w(self):
        return self.d["w"]

    @w.setter
    def w(self, v):
        self.d["w"] = v

    @property
    def r(self):
        return self.d["r"]

    @r.setter
    def r(self, v):
        self.d["r"] = v


class Chan:
    def __init__(self, nc, name):
        self.sem = nc.alloc_semaphore(name)
        self.total = 0
        self.key = name
        self.holder = [0]


class Trk:
    def __init__(self, nc):
        self.nc = nc
        self.E = {"pe": nc.tensor, "act": nc.scalar, "dve": nc.vector, "pool": nc.gpsimd, "sp": nc.sync}
        self.sem = {}
        self.cnt = {}
        self.gen = {}
        for e in ("pe", "act", "dve", "pool"):
            self.gen[e] = 0
            self._newsem(e)
        self.seen = {}
        self.chans = []
        self.ninst = {e: 0 for e in self.E}

    def _newsem(self, e):
        self.sem[e] = self.nc.alloc_semaphore("c_%s_%d" % (e, self.gen[e]))
        self.cnt[e] = 0
        self.gen[e] += 1

    def chan(self, name):
        c = Chan(self.nc, name)
        self.chans.append(c)
        return c

    def _wait(self, eng, ev):
        if ev is None:
            return
        sem, key, val = ev
        if isinstance(val, list):
            val = val[0]
        k = (eng, key)
        if self.seen.get(k, 0) >= val:
            return
        self.E[eng].wait_ge(sem, val)
        self.ninst[eng] += 1
        self.seen[k] = val

    def _deps(self, eng, reads, writes, skip_same=False):
        for t in reads:
            if t.w is not None and not (skip_same and t.w[1][0] == eng):
                self._wait(eng, t.w)
        for t in writes:
            if t.w is not None and not (skip_same and t.w[1][0] == eng):
                self._wait(eng, t.w)
            for ev in t.r.values():
                if not (skip_same and ev[1][0] == eng):
                    self._wait(eng, ev)

    def _done(self, ev, reads, writes):
        for t in reads:
            t.r[ev[1]] = ev
        for t in writes:
            t.w = ev
            t.r = {}

    def op(self, eng, fn, reads=(), writes=()):
        self._deps(eng, reads, writes)
        inst = fn()
        if self.cnt[eng] >= 60000:
            self._newsem(eng)
        self.cnt[eng] += 1
        inst.then_inc(self.sem[eng], 1)
        self.ninst[eng] += 1
        ev = (self.sem[eng], (eng, self.gen[eng]), self.cnt[eng])
        self._done(ev, reads, writes)

    def pe(self, fns, reads=(), writes=()):
        self._deps("pe", reads, writes, skip_same=True)
        inst = None
        for f in fns:
            inst = f()
            self.ninst["pe"] += 1
        if self.cnt["pe"] >= 60000:
            self._newsem("pe")
        self.cnt["pe"] += 1
        inst.then_inc(self.sem["pe"], 1)
        ev = (self.sem["pe"], ("pe", self.gen["pe"]), self.cnt["pe"])
        self._done(ev, reads, writes)

    def dma(self, q, ch, out_ap, in_ap, reads=(), writes=(), batch=False, **kw):
        self._deps(q, reads, writes)
        if not batch:
            if ch.total > 0:
                self._wait(q, (ch.sem, ch.key, ch.total))
            ch.holder = [ch.total]
        inst = self.E[q].dma_start(out=out_ap, in_=in_ap, allow_slow_non_contiguous=True, **kw)
        ch.total += 16
        ch.holder[0] = ch.total
        inst.then_inc(ch.sem, 16)
        self.ninst[q] += 1
        ev = (ch.sem, ch.key, ch.holder)
        self._done(ev, reads, writes)

    def barrier(self):
        for eng in ("pe", "act", "dve", "pool", "sp"):
            self.finish(eng, True)

    def finish(self, eng="pool", skip_nb=False):
        for c in self.chans:
            if skip_nb and getattr(c, "nobarrier", False):
                continue
            if c.total > 0:
                self._wait(eng, (c.sem, c.key, c.total))
        for e in ("pe", "act", "dve", "pool"):
            if self.cnt[e] > 0 and e != eng:
                self._wait(eng, (self.sem[e], (e, self.gen[e]), self.cnt[e]))


class Rot:
    def __init__(self, items):
        self.items = items
        self.i = 0

    def next(self):
        t = self.items[self.i % len(self.items)]
        self.i += 1
        return t


def bc(ap, shape):
    return ap.to_broadcast(shape)


def build(SEQ, PAST, stage=99):
    nc = bass.Bass("TRN2", target_bir_lowering=False)
    K = Trk(nc)
    NT_P = SEQ // 512
    NKT_P = SEQ // 128
    NK_S = PAST + 128
    NKT_S = NK_S // 128
    NPT_S = PAST // 512

    def din(name, shape, dt=F32):
        return nc.dram_tensor(name, list(shape), dt, kind="ExternalInput").ap()

    def dout(name, shape):
        return nc.dram_tensor(name, list(shape), F32, kind="ExternalOutput").ap()

    def dscr(name, shape, dt=BF16):
        return nc.dram_tensor(name, list(shape), dt, kind="Internal").ap()

    def sb(name, shape, dt=F32):
        return nc.alloc_sbuf_tensor(name, list(shape), dt).ap()

    xp = din("xp", [SEQ, D]); xs = din("xs", [NSEQ_S * DSEQ, D])
    ckl = din("ckl", [NSEQ_S, PAST, KVL]); ckr = din("ckr", [NSEQ_S, PAST, 32])
    sre = din("sre", [NSEQ_S, 32, 64]); sim = din("sim", [NSEQ_S, 32, 64])
    g_mix = din("g_mix", [1, D]); w_in = din("w_in", [D, DIN]); g_q = din("g_q", [1, QL])
    w_q = din("w_q", [QL, QL]); g_kv = din("g_kv", [1, KVL]); w_kv = din("w_kv", [KVL, 1024])
    a_re = din("a_re", [32, 64]); a_im = din("a_im", [32, 64]); lstep = din("lstep", [1, 32])
    b_re = din("b_re", [32, 64, 16]); b_im = din("b_im", [32, 64, 16])
    c_re = din("c_re", [32, 16, 64]); c_im = din("c_im", [32, 16, 64])
    d_skip = din("d_skip", [1, 512]); w_glu = din("w_glu", [512, 512])
    g_attn = din("g_attn", [1, 512]); g_ssm = din("g_ssm", [1, 512]); w_out = din("w_out", [D, D])
    g_mlp = din("g_mlp", [1, D]); w_up = din("w_up", [D, DFF]); w_down = din("w_down", [DFF, D])
    g_fin = din("g_fin", [1, D])
    cs_p = din("cs_p", [SEQ, 32]); cs_s = din("cs_s", [NSEQ_S * DSEQ, 32])
    ident_in = din("ident", [128, 128])

    yp = dout("yp", [SEQ, D]); ys = dout("ys", [NSEQ_S * DSEQ, D])
    latp = dout("latp", [SEQ, KVL]); krp = dout("krp", [SEQ, 32])
    hrp = dout("hrp", [32, 64]); hip = dout("hip", [32, 64])
    lats = dout("lats", [NSEQ_S * DSEQ, KVL]); krs = dout("krs", [NSEQ_S * DSEQ, 32])
    hrs = dout("hrs", [NSEQ_S, 32, 64]); his = dout("his", [NSEQ_S, 32, 64])

    s_winc = dscr("s_winc", [128, 8, 1056]); s_winu = dscr("s_winu", [128, 8, 512])
    s_wq = dscr("s_wq", [128, 6, 768]); s_wkv = dscr("s_wkv", [128, 2, 1024])
    s_wglu = dscr("s_wglu", [128, 4, 512]); s_wout = dscr("s_wout", [128, 8, 1024])
    s_wup = dscr("s_wup", [8, 128, 8, 512]); s_wdn = dscr("s_wdn", [8, 128, 4, 1024])
    ktc = [dscr("ktc0", [NH, 96, SEQ])] + [dscr("ktc%d" % (s + 1), [NH, 96, NK_S]) for s in range(NSEQ_S)]
    vc = [dscr("vc0", [NH, 128, NKT_P, 65])] + [dscr("vc%d" % (s + 1), [NH, 128, NKT_S, 65]) for s in range(NSEQ_S)]
    kvreg = [[T(None, "kvreg0_%d" % i) for i in range(NT_P)]] + \
            [[T(None, "kvreg%d_%d" % (s + 1, i)) for i in range(NPT_S + 1)] for s in range(NSEQ_S)]
    wscr = {}

    ps = nc.alloc_psum_tensor("ps", [128, 4096], F32).ap()
    PB = [T(ps[:, 512 * b:512 * (b + 1)], "pb%d" % b) for b in range(8)]

    def pbf(b):
        return ps[:, 512 * b:512 * (b + 1)].bitcast(BF16)

    from contextlib import ExitStack

    def prod(sh):
        n = 1
        for v_ in sh:
            n *= v_
        return n

    def vw(t, dt, shape, off=0):
        a_ = t.ap if dt == F32 else t.ap.bitcast(dt)
        esz = 4 if dt in (F32, I32) else 2
        n = prod(shape)
        a_ = a_[:, off // esz:off // esz + n]
        if len(shape) == 2:
            a_ = a_.rearrange("p (a b) -> p a b", a=shape[0])
        elif len(shape) == 3:
            a_ = a_.rearrange("p (a b c) -> p a b c", a=shape[0], b=shape[1])
        return a_

    def raw(name, nbytes):
        return T(sb(name, [128, nbytes // 4]), name)

    ident_f = T(sb("ident_f", [128, 128]))
    ident_b = T(sb("ident_b", [128, 128], BF16))
    ones_b = T(sb("ones_b", [128, 128], BF16))
    gkv_bc = T(sb("gkv_bc", [128, KVL])); gfin_bc = T(sb("gfin_bc", [128, D]))
    gcols = T(sb("gcols", [128, 40]))
    W1 = T(sb("W1", [128, 4, 2, 8, 128], BF16))
    W2 = T(sb("W2", [128, 16, 2, 8, 32], BF16))
    BD = T(sb("BD", [128, 4, 8, 128], BF16))
    Dd = T(sb("Dd", [128, 4, 128], BF16))
    A8 = T(sb("A8", [128, 16, 2])); B8 = T(sb("B8", [128, 16, 2]))
    Hc = T(sb("Hc", [128, 16, 2])); H0s = T(sb("H0s", [128, NSEQ_S, 16, 2])); h0ch = K.chan("h0")
    cch = K.chan("const")
    cchA = K.chan("ssmA")
    cchB = K.chan("ssmB")
    castch = K.chan("cast")

    def alloc_runtime():
        g = {}
        NSLOT = 5
        g["NSLOT"] = NSLOT
        g["ring"] = [T(sb("ring%d" % i, [128, 4096], BF16)) for i in range(NSLOT)]
        g["ringch"] = [K.chan("ring%d" % i) for i in range(NSLOT)]
        g["xt"] = T(sb("xt", [128, 4, D])); g["xch"] = [K.chan("xt%d" % i) for i in range(4)]; g["ych"] = [K.chan("yst%d" % i) for i in range(4)]
        g["cst"] = T(sb("cst", [128, 4, 32])); g["csch"] = K.chan("cst"); g["kvlch"] = K.chan("kvl")
        g["actT"] = T(sb("actT", [128, 8, 512], BF16))
        rx = [raw("rx%d" % i, 2048) for i in range(2)]
        g["xnb"] = Rot([T(vw(r_, BF16, (D,)), share=r_) for r_ in rx])
        g["trl"] = Rot([T(vw(r_, F32, (512,)), share=r_) for r_ in rx])
        g["junk"] = T(sb("junk", [128, D], BF16))
        g["cqn"] = Rot([T(sb("cqn%d" % i, [128, QL], BF16)) for i in range(2)])
        rq = [raw("rq%d" % i, 4096) for i in range(2)]
        g["aT"] = [T(vw(r_, BF16, (4, 512)), share=r_) for r_ in rq]
        g["qTh"] = [T(vw(r_, BF16, (4, 512)), share=r_) for r_ in rq]
        rD = raw("rD", 8192)
        g["cqnT"] = T(sb("cqnT", [128, 6, 512], BF16))
        g["Ssb"] = T(vw(rD, F32, (16, 2, 64)), share=rD)
        g["kvtok"] = Rot([T(sb("kvtok%d" % i, [128, 352], BF16)) for i in range(2)])
        g["kvout"] = [T(sb("kvout%d" % i, [128, 288])) for i in range(2)]
        g["kvoch"] = [K.chan("kvout%d" % i) for i in range(2)]
        g["latT"] = T(sb("latT", [128, 2, 512], BF16))
        g["qtok"] = Rot([T(sb("qtok%d" % i, [128, 8, 96], BF16)) for i in range(2)])
        g["uT"] = T(sb("uT", [128, 4, 512], BF16))
        g["KTs"] = T(sb("KTs", [128, 8, 512], BF16)); g["ktsch"] = K.chan("kts")
        g["Vs"] = T(sb("Vs", [128, 8, 4, 65], BF16)); g["vsch"] = K.chan("vs")
        rA = raw("rA", 8192)
        g["attn"] = T(vw(rA, F32, (4, 512)), share=rA)
        g["y2"] = T(vw(rA, F32, (4, 512)), share=rA)
        rB = [raw("rB%d" % i, 2048) for i in range(2)]
        g["OTs"] = Rot([T(vw(r_, F32, (512,)), share=r_) for r_ in rB])
        g["tA"] = T(vw(rB[0], F32, (512,)), share=rB[0]); g["tB"] = T(vw(rB[1], F32, (512,)), share=rB[1])
        rC = [raw("rC%d" % i, 4096) for i in range(2)]
        g["Kblk"] = [T(vw(r_, BF16, (2048,)), share=r_) for r_ in rC]
        g["y2b"] = T(vw(rC[0], BF16, (4, 512)), share=rC[0]); g["sqb"] = T(vw(rC[1], BF16, (4, 512)), share=rC[1])
        g["Vblk"] = [T(sb("Vblk%d" % i, [128, 16, 65], BF16)) for i in range(2)]
        g["kbch"] = [K.chan("kb%d" % i) for i in range(2)]
        g["vbch"] = [K.chan("vb%d" % i) for i in range(2)]
        g["PT"] = Rot([T(sb("PT%d" % i, [128, 512], BF16)) for i in range(3)])
        g["stat"] = T(sb("stat", [128, 64]))
        g["rtmp"] = T(sb("rtmp", [128, 8, 64]))
        g["Hbf"] = T(sb("Hbf", [128, 16, 2, 64], BF16))
        g["st1"] = T(sb("st1", [128, 16, 2])); g["st2"] = T(sb("st2", [128, 16, 2]))
        g["hout"] = T(sb("hout", [128, 16, 2])); g["hoch"] = K.chan("hout")
        g["rbc"] = T(sb("rbc", [128, 512]))
        g["qsb"] = T(sb("qsb", [128, 768]))
        return g


    K.dma("sp", cch, ident_f.ap, ident_in, writes=[ident_f], batch=True)
    K.dma("sp", cch, gkv_bc.ap, g_kv.partition_broadcast(128), writes=[gkv_bc], batch=True)
    K.dma("sp", cch, gfin_bc.ap, g_fin.partition_broadcast(128), writes=[gfin_bc], batch=True)
    for (src, off, n) in ((g_mix, 0, 8), (g_mlp, 8, 8), (g_q, 16, 6), (g_attn, 22, 4), (g_ssm, 26, 4), (d_skip, 30, 4)):
        K.dma("sp", cch, gcols.ap[:, off:off + n], src.rearrange("o (k c) -> c (o k)", c=128), writes=[gcols], batch=True)
    K.op("dve", lambda: nc.vector.tensor_copy(out=ident_b.ap, in_=ident_f.ap), [ident_f], [ident_b])
    K.op("pool", lambda: nc.gpsimd.memset(ones_b.ap, 1.0), [], [ones_b])

    wv = w_in.rearrange("(k p) n -> p k n", p=128)
    casts = [
        (s_winc, wv[:, :, 0:1056]), (s_winu, wv[:, :, 1056:1568]),
        (s_wq, w_q.rearrange("(k p) n -> p k n", p=128)),
    ]
    wkvv = w_kv.rearrange("(k p) (h c) -> p k h c", p=128, c=128)
    for k in range(2):
        casts.append((s_wkv[:, k, 0:512].rearrange("p (h c) -> p h c", c=64), wkvv[:, k, :, 0:64]))
        casts.append((s_wkv[:, k, 512:1024].rearrange("p (h c) -> p h c", c=64), wkvv[:, k, :, 64:128]))
    casts.append((s_wglu, w_glu.rearrange("(k p) n -> p k n", p=128)))
    casts.append((s_wout, w_out.rearrange("(k p) n -> p k n", p=128)))
    wupv = w_up.rearrange("(k p) (fc n) -> fc p k n", p=128, n=512)
    wdnv = w_down.rearrange("(fc ft p) n -> fc p ft n", ft=4, p=128)
    for fc in range(8):
        for k in range(0, 8, 4):
            casts.append((s_wup[fc, :, k:k + 4, :], wupv[fc, :, k:k + 4, :]))
        for ft in range(0, 4, 2):
            casts.append((s_wdn[fc, :, ft:ft + 2, :], wdnv[fc, :, ft:ft + 2, :]))
    cast_groups = {}
    for (o, i) in casts:
        nm = o.name if hasattr(o, "name") else str(id(o))
        nm = nm.split("[")[0]
        if nm not in cast_groups:
            cast_groups[nm] = K.chan("cast_" + nm)
            wscr[nm] = T(None, "wscr_" + nm)
            cast_groups[nm].nobarrier = True
        K.dma("pool", cast_groups[nm], o, i, writes=[wscr[nm]], batch=True)

    if stage == 0:
        K.finish("pool")
        return nc, K
    def ssm_setup(es):
        def sb(name, shape, dt=F32):
            return es.enter_context(nc.sbuf_tensor(name, list(shape), dt)).ap()
        Are = T(sb("Are", [128, 16])); Aim = T(sb("Aim", [128, 16])); LS = T(sb("LS", [128, 16]))
        for g2 in range(2):
            K.dma("sp", cchA, Are.ap[64 * g2:64 * g2 + 64, :], a_re.rearrange("(k two) n -> two n k", two=2)[g2], writes=[Are], batch=True)
            K.dma("sp", cchA, Aim.ap[64 * g2:64 * g2 + 64, :], a_im.rearrange("(k two) n -> two n k", two=2)[g2], writes=[Aim], batch=True)
            K.dma("sp", cchA, LS.ap[64 * g2:64 * g2 + 64, :], lstep.rearrange("o (k two) -> two o k", two=2)[g2].partition_broadcast(64), writes=[LS], batch=True)
        BreD = T(sb("BreD", [128, 16, 32])); BimD = T(sb("BimD", [128, 16, 32]))
        CreD = T(sb("CreD", [32, 16, 128])); CimD = T(sb("CimD", [32, 16, 128]))
        for t in (BreD, BimD, CreD, CimD):
            K.op("pool", lambda t=t: nc.gpsimd.memset(t.ap, 0.0), [], [t])
        for g2 in range(2):
            K.dma("sp", cchB, BreD.ap[64 * g2:64 * g2 + 64, :, 16 * g2:16 * g2 + 16], b_re.rearrange("(k two) n q -> two n k q", two=2)[g2], writes=[BreD], batch=True)
            K.dma("sp", cchB, BimD.ap[64 * g2:64 * g2 + 64, :, 16 * g2:16 * g2 + 16], b_im.rearrange("(k two) n q -> two n k q", two=2)[g2], writes=[BimD], batch=True)
            K.dma("sp", cchB, CreD.ap[16 * g2:16 * g2 + 16, :, 64 * g2:64 * g2 + 64], c_re.rearrange("(k two) p n -> two p k n", two=2)[g2], writes=[CreD], batch=True)
            K.dma("sp", cchB, CimD.ap[16 * g2:16 * g2 + 16, :, 64 * g2:64 * g2 + 64], c_im.rearrange("(k two) p n -> two p k n", two=2)[g2], writes=[CimD], batch=True)
        for s in range(NSEQ_S):
            for g2 in range(2):
                K.dma("sp", h0ch, H0s.ap[64 * g2:64 * g2 + 64, s, :, 0], sre[s].rearrange("(k two) n -> two n k", two=2)[g2], writes=[H0s], batch=True)
                K.dma("sp", h0ch, H0s.ap[64 * g2:64 * g2 + 64, s, :, 1], sim[s].rearrange("(k two) n -> two n k", two=2)[g2], writes=[H0s], batch=True)

        def V(name):
            return T(sb(name, [128, 16]))
        dt_ = V("dt_"); mag = V("mag"); th = V("th"); cs = V("cs"); sn = V("sn")
        w1 = V("w1"); w2 = V("w2"); w3 = V("w3"); wi = T(sb("wi", [128, 16], I32))
        K.op("act", lambda: nc.scalar.activation(out=dt_.ap, in_=LS.ap, func=AF.Exp), [LS], [dt_])
        K.op("dve", lambda: nc.vector.tensor_tensor(out=w1.ap, in0=Are.ap, in1=dt_.ap, op=ALU.mult), [Are, dt_], [w1])
        K.op("act", lambda: nc.scalar.activation(out=mag.ap, in_=w1.ap, func=AF.Exp), [w1], [mag])
        K.op("dve", lambda: nc.vector.tensor_tensor(out=th.ap, in0=Aim.ap, in1=dt_.ap, op=ALU.mult), [Aim, dt_], [th])

        def sin_of(dst, shift):
            K.op("dve", lambda: nc.vector.tensor_scalar(out=w1.ap, in0=th.ap, scalar1=shift, scalar2=1.0 / TWO_PI, op0=ALU.add, op1=ALU.mult), [th], [w1])
            K.op("dve", lambda: nc.vector.tensor_copy(out=wi.ap, in_=w1.ap), [w1], [wi])
            K.op("dve", lambda: nc.vector.tensor_copy(out=w2.ap, in_=wi.ap), [wi], [w2])
            K.op("dve", lambda: nc.vector.tensor_scalar(out=w1.ap, in0=th.ap, scalar1=shift, scalar2=None, op0=ALU.add), [th], [w1])
            K.op("dve", lambda: nc.vector.scalar_tensor_tensor(out=w1.ap, in0=w2.ap, scalar=-TWO_PI, in1=w1.ap, op0=ALU.mult, op1=ALU.add), [w2, w1], [w1])
            K.op("dve", lambda: nc.vector.tensor_scalar(out=w2.ap, in0=w1.ap, scalar1=math.pi, scalar2=-TWO_PI, op0=ALU.is_gt, op1=ALU.mult), [w1], [w2])
            K.op("dve", lambda: nc.vector.tensor_tensor(out=w1.ap, in0=w1.ap, in1=w2.ap, op=ALU.add), [w1, w2], [w1])
            K.op("dve", lambda: nc.vector.tensor_scalar(out=w2.ap, in0=w1.ap, scalar1=-math.pi, scalar2=TWO_PI, op0=ALU.is_lt, op1=ALU.mult), [w1], [w2])
            K.op("dve", lambda: nc.vector.tensor_tensor(out=w1.ap, in0=w1.ap, in1=w2.ap, op=ALU.add), [w1, w2], [w1])
            K.op("dve", lambda: nc.vector.tensor_scalar(out=w1.ap, in0=w1.ap, scalar1=3.1415925, scalar2=-3.1415925, op0=ALU.min, op1=ALU.max), [w1], [w1])
            K.op("act", lambda: nc.scalar.activation(out=dst.ap, in_=w1.ap, func=AF.Sin), [w1], [dst])
        sin_of(sn, 0.0)
        sin_of(cs, math.pi / 2)
        LP = T(sb("LP", [128, 9, 2, 16]))
        K.op("pool", lambda: nc.gpsimd.memset(LP.ap[:, 0, 0, :], 1.0), [], [LP])
        K.op("pool", lambda: nc.gpsimd.memset(LP.ap[:, 0, 1, :], 0.0), [], [LP])
        K.op("dve", lambda: nc.vector.tensor_tensor(out=LP.ap[:, 1, 0, :], in0=mag.ap, in1=cs.ap, op=ALU.mult), [mag, cs], [LP])
        K.op("dve", lambda: nc.vector.tensor_tensor(out=LP.ap[:, 1, 1, :], in0=mag.ap, in1=sn.ap, op=ALU.mult), [mag, sn], [LP])
        for k in range(2, 9):
            pr, pi_ = LP.ap[:, k - 1, 0, :], LP.ap[:, k - 1, 1, :]
            lr, li = LP.ap[:, 1, 0, :], LP.ap[:, 1, 1, :]
            K.op("dve", lambda: nc.vector.tensor_tensor(out=w1.ap, in0=pr, in1=lr, op=ALU.mult), [LP], [w1])
            K.op("dve", lambda: nc.vector.tensor_tensor(out=w2.ap, in0=pi_, in1=li, op=ALU.mult), [LP], [w2])
            K.op("dve", lambda: nc.vector.tensor_tensor(out=LP.ap[:, k, 0, :], in0=w1.ap, in1=w2.ap, op=ALU.subtract), [w1, w2], [LP])
            K.op("dve", lambda: nc.vector.tensor_tensor(out=w1.ap, in0=pr, in1=li, op=ALU.mult), [LP], [w1])
            K.op("dve", lambda: nc.vector.tensor_tensor(out=w2.ap, in0=pi_, in1=lr, op=ALU.mult), [LP], [w2])
            K.op("dve", lambda: nc.vector.tensor_tensor(out=LP.ap[:, k, 1, :], in0=w1.ap, in1=w2.ap, op=ALU.add), [w1, w2], [LP])
        K.op("dve", lambda: nc.vector.tensor_copy(out=A8.ap[:, :, 0], in_=LP.ap[:, 8, 0, :]), [LP], [A8])
        K.op("dve", lambda: nc.vector.tensor_copy(out=A8.ap[:, :, 1], in_=LP.ap[:, 8, 1, :]), [LP], [A8])
        K.op("dve", lambda: nc.vector.tensor_scalar(out=B8.ap[:, :, 0], in0=LP.ap[:, 8, 1, :], scalar1=-1.0, scalar2=None, op0=ALU.mult), [LP], [B8])
        K.op("dve", lambda: nc.vector.tensor_copy(out=B8.ap[:, :, 1], in_=LP.ap[:, 8, 0, :]), [LP], [B8])
        cre = V("cre"); cim = V("cim"); den = V("den"); nr = V("nr")
        K.op("dve", lambda: nc.vector.tensor_scalar(out=nr.ap, in0=LP.ap[:, 1, 0, :], scalar1=-1.0, scalar2=None, op0=ALU.add), [LP], [nr])
        K.op("dve", lambda: nc.vector.tensor_tensor(out=w1.ap, in0=Are.ap, in1=Are.ap, op=ALU.mult), [Are], [w1])
        K.op("dve", lambda: nc.vector.tensor_tensor(out=w2.ap, in0=Aim.ap, in1=Aim.ap, op=ALU.mult), [Aim], [w2])
        K.op("dve", lambda: nc.vector.tensor_tensor(out=den.ap, in0=w1.ap, in1=w2.ap, op=ALU.add), [w1, w2], [den])
        K.op("dve", lambda: nc.vector.reciprocal(out=den.ap, in_=den.ap), [den], [den])
        K.op("dve", lambda: nc.vector.tensor_tensor(out=w1.ap, in0=nr.ap, in1=Are.ap, op=ALU.mult), [nr, Are], [w1])
        K.op("dve", lambda: nc.vector.tensor_tensor(out=w2.ap, in0=LP.ap[:, 1, 1, :], in1=Aim.ap, op=ALU.mult), [LP, Aim], [w2])
        K.op("dve", lambda: nc.vector.tensor_tensor(out=w1.ap, in0=w1.ap, in1=w2.ap, op=ALU.add), [w1, w2], [w1])
        K.op("dve", lambda: nc.vector.tensor_tensor(out=cre.ap, in0=w1.ap, in1=den.ap, op=ALU.mult), [w1, den], [cre])
        K.op("dve", lambda: nc.vector.tensor_tensor(out=w1.ap, in0=LP.ap[:, 1, 1, :], in1=Are.ap, op=ALU.mult), [LP, Are], [w1])
        K.op("dve", lambda: nc.vector.tensor_tensor(out=w2.ap, in0=nr.ap, in1=Aim.ap, op=ALU.mult), [nr, Aim], [w2])
        K.op("dve", lambda: nc.vector.tensor_tensor(out=w1.ap, in0=w1.ap, in1=w2.ap, op=ALU.subtract), [w1, w2], [w1])
        K.op("dve", lambda: nc.vector.tensor_tensor(out=cim.ap, in0=w1.ap, in1=den.ap, op=ALU.mult), [w1, den], [cim])

        def B3(t):
            return t.unsqueeze(2).to_broadcast([128, 16, 32])
        m1 = T(sb("m1", [128, 16, 32])); m2 = T(sb("m2", [128, 16, 32]))
        BbR = T(sb("BbR", [128, 16, 32])); BbI = T(sb("BbI", [128, 16, 32]))

        def cmul(dre, dim, are, aim, bre, bim, rd, wr):
            if dre is not None:
                K.op("dve", lambda: nc.vector.tensor_tensor(out=m1.ap, in0=bre, in1=B3(are), op=ALU.mult), rd, [m1])
                K.op("dve", lambda: nc.vector.tensor_tensor(out=m2.ap, in0=bim, in1=B3(aim), op=ALU.mult), rd, [m2])
                K.op("dve", lambda: nc.vector.tensor_tensor(out=dre, in0=m1.ap, in1=m2.ap, op=ALU.subtract), [m1, m2], wr)
            if dim is not None:
                K.op("dve", lambda: nc.vector.tensor_tensor(out=m1.ap, in0=bim, in1=B3(are), op=ALU.mult), rd, [m1])
                K.op("dve", lambda: nc.vector.tensor_tensor(out=m2.ap, in0=bre, in1=B3(aim), op=ALU.mult), rd, [m2])
                K.op("dve", lambda: nc.vector.tensor_tensor(out=dim, in0=m1.ap, in1=m2.ap, op=ALU.add), [m1, m2], wr)
        cmul(BbR.ap, BbI.ap, cre.ap, cim.ap, BreD.ap, BimD.ap, [cre, cim, BreD, BimD], [BbR, BbI])
        GR = T(sb("GR", [128, 16, 32])); GI = T(sb("GI", [128, 16, 32]))
        for i in range(8):
            cmul(GR.ap, GI.ap, LP.ap[:, 7 - i, 0, :], LP.ap[:, 7 - i, 1, :], BbR.ap, BbI.ap, [LP, BbR, BbI], [GR, GI])
            for reim, G in ((0, GR), (1, GI)):
                for r in range(4):
                    bank = (reim * 4 + r)
                    fns = []
                    for kk in range(4):
                        kp = 4 * kk + r
                        fns.append(lambda kk=kk, kp=kp, G=G, bank=bank: nc.tensor.transpose(
                            out=PB[bank].ap[0:32, 128 * kk:128 * kk + 128], in_=G.ap[:, kp, :], identity=ident_f.ap))
                    K.pe(fns, [G, ident_f], [PB[bank]])
                    K.op("act", lambda r=r, bank=bank, reim=reim, i=i: nc.scalar.activation(
                        out=W1.ap[32 * r:32 * r + 32, :, reim, i, :],
                        in_=PB[bank].ap[0:32, :].rearrange("p (a b) -> p a b", a=4), func=AF.Copy), [PB[bank]], [W1])
        CTR = T(sb("CTR", [128, 16, 32])); CTI = T(sb("CTI", [128, 16, 32]))
        for (src, dst, bank) in ((CreD, CTR, 0), (CimD, CTI, 1)):
            fns = [lambda kp=kp, src=src, bank=bank: nc.tensor.transpose(out=PB[bank].ap[:, 32 * kp:32 * kp + 32], in_=src.ap[:, kp, :], identity=ident_f.ap[0:32, 0:32]) for kp in range(16)]
            K.pe(fns, [src, ident_f], [PB[bank]])
            K.op("act", lambda dst=dst, bank=bank: nc.scalar.activation(out=dst.ap, in_=PB[bank].ap.rearrange("p (a b) -> p a b", a=16), func=AF.Copy), [PB[bank]], [dst])
        K.op("pool", lambda: nc.gpsimd.memset(BD.ap, 0.0), [], [BD])
        CLR = T(sb("CLR", [128, 16, 32])); CLI = T(sb("CLI", [128, 16, 32])); NCLI = T(sb("NCLI", [128, 16, 32]))
        for kpow in range(9):
            cmul(CLR.ap, CLI.ap, LP.ap[:, kpow, 0, :], LP.ap[:, kpow, 1, :], CTR.ap, CTI.ap, [LP, CTR, CTI], [CLR, CLI])
            K.op("dve", lambda: nc.vector.tensor_scalar(out=NCLI.ap, in0=CLI.ap, scalar1=-1.0, scalar2=None, op0=ALU.mult), [CLI], [NCLI])
            if kpow >= 1:
                j = kpow - 1
                K.op("act", lambda j=j: nc.scalar.activation(out=W2.ap[:, :, 0, j, :], in_=CLR.ap, func=AF.Copy), [CLR], [W2])
                K.op("act", lambda j=j: nc.scalar.activation(out=W2.ap[:, :, 1, j, :], in_=NCLI.ap, func=AF.Copy), [NCLI], [W2])
            if kpow <= 7:
                tau = kpow
                for r in range(4):
                    for kk in range(4):
                        kp = 4 * kk + r
                        col = (kk * 8 + tau) * 32
                        bank = 2 * r + col // 512
                        c0 = col % 512
                        fns = [
                            lambda kp=kp, bank=bank, c0=c0: nc.tensor.matmul(PB[bank].ap[0:32, c0:c0 + 32], BbR.ap[:, kp, :], CLR.ap[:, kp, :], start=True, stop=False),
                            lambda kp=kp, bank=bank, c0=c0: nc.tensor.matmul(PB[bank].ap[0:32, c0:c0 + 32], BbI.ap[:, kp, :], NCLI.ap[:, kp, :], start=False, stop=True),
                        ]
                        K.pe(fns, [BbR, BbI, CLR, NCLI], [PB[bank]])
        for r in range(4):
            for hb in range(2):
                bank = 2 * r + hb
                K.op("act", lambda r=r, hb=hb, bank=bank: nc.scalar.activation(
                    out=BD.ap[32 * r:32 * r + 32, 2 * hb:2 * hb + 2, :, 32 * r:32 * r + 32],
                    in_=PB[bank].ap[0:32, :].rearrange("p (a t c) -> p a t c", a=2, t=8), func=AF.Copy), [PB[bank]], [BD])
        for kk in range(4):
            K.op("dve", lambda kk=kk: nc.vector.tensor_scalar(out=Dd.ap[:, kk, :], in0=ident_f.ap, scalar1=gcols.ap[:, 30 + kk:31 + kk], scalar2=None, op0=ALU.mult), [ident_f, gcols], [Dd])

    with ExitStack() as es_:
        ssm_setup(es_)
        K.barrier()
    if stage == 1:
        K.finish("pool")
        return nc, K
    G = alloc_runtime()
    NSLOT = G["NSLOT"]; ring = G["ring"]; ringch = G["ringch"]; xt = G["xt"]; xch = G["xch"]; ych = G["ych"]; cst = G["cst"]; csch = G["csch"]; kvlch = G["kvlch"]
    actT = G["actT"]; xnb = G["xnb"]; trl = G["trl"]; junk = G["junk"]; cqn = G["cqn"]; aT = G["aT"]; qTh = G["qTh"]; cqnT = G["cqnT"]; Ssb = G["Ssb"]
    kvtok = G["kvtok"]; kvout = G["kvout"]; kvoch = G["kvoch"]; latT = G["latT"]; qtok = G["qtok"]; uT = G["uT"]; KTs = G["KTs"]; ktsch = G["ktsch"]
    Vs = G["Vs"]; vsch = G["vsch"]; attn = G["attn"]; y2 = G["y2"]; OTs = G["OTs"]; tA = G["tA"]; tB = G["tB"]; Kblk = G["Kblk"]; y2b = G["y2b"]; sqb = G["sqb"]
    Vblk = G["Vblk"]; kbch = G["kbch"]; vbch = G["vbch"]; PT = G["PT"]; stat = G["stat"]; rtmp = G["rtmp"]; Hbf = G["Hbf"]; st1 = G["st1"]; st2 = G["st2"]
    hout = G["hout"]; hoch = G["hoch"]; rbc = G["rbc"]; qsb = G["qsb"]
    NKB = 2
    statc = [T(stat.ap[:, i:i + 1], "stat%d" % i) for i in range(64)]
    xts = [T(xt.ap[:, i, :], "xt%d" % i) for i in range(4)]
    actTs = [T(actT.ap[:, :, 128 * i:128 * i + 128], "actT%d" % i) for i in range(4)]
    cqnTs = [T(cqnT.ap[:, :, 128 * i:128 * i + 128], "cqnT%d" % i) for i in range(4)]
    aTr = Rot(aT)
    kvo_i = [0]
    K.op("pool", lambda: nc.gpsimd.memset(Vs.ap, 1.0), [], [Vs])

    class Ring:
        def __init__(self):
            self.plan = []
            self.issued = 0
            self.got = 0

        def add(self, items):
            self.plan.extend(items)

        def _issue(self, n):
            name, src, shape = self.plan[n]
            slot = n % NSLOT
            dst = ring[slot].ap
            ne = 1
            for s_ in shape:
                ne *= s_
            d = dst[:, 0:ne]
            if len(shape) == 2:
                d = d.rearrange("p (a b) -> p a b", a=shape[0])
            K.dma("sp", ringch[slot], d, src, reads=[wscr[src.name.split("[")[0]]], writes=[ring[slot]])

        def get(self, *names, hold=0):
            n0 = self.got
            assert len(names) + hold <= NSLOT
            for i_, nm in enumerate(names):
                assert self.plan[n0 + i_][0] == nm, (self.plan[n0 + i_][0], nm)
            while self.issued < min(len(self.plan), n0 + NSLOT - hold):
                self._issue(self.issued)
                self.issued += 1
            self.got += len(names)
            outs = []
            for i_ in range(len(names)):
                n = n0 + i_
                shape = self.plan[n][2]
                ne = prod(shape)
                v = ring[n % NSLOT].ap[:, 0:ne]
                if len(shape) == 2:
                    v = v.rearrange("p (a b) -> p a b", a=shape[0])
                outs.append((ring[n % NSLOT], v))
            return outs

    R = Ring()
    plan_tok = [("winc0", s_winc[:, 0:3, :], (3, 1056)), ("winc1", s_winc[:, 3:6, :], (3, 1056)), ("winc2", s_winc[:, 6:8, :], (2, 1056)),
                ("wq0", s_wq[:, 0:3, :], (3, 768)), ("wq1", s_wq[:, 3:6, :], (3, 768)),
                ("wkv", s_wkv, (2, 1024)), ("winu", s_winu, (8, 512)), ("wglu", s_wglu, (4, 512)),
                ("wout0", s_wout[:, 0:4, :], (4, 1024)), ("wout1", s_wout[:, 4:8, :], (4, 1024))]
    for fc in range(8):
        plan_tok.append(("wup%d" % fc, s_wup[fc], (8, 512)))
        plan_tok.append(("wdn%d" % fc, s_wdn[fc], (4, 1024)))
    plan_kv = [("wkv", s_wkv, (2, 1024))]

    def rms_stats(src_ap, reads, n, col):
        st = statc[col]
        ss = st.ap
        K.op("act", lambda: nc.scalar.activation(out=junk.ap[:, 0:n], in_=src_ap, func=AF.Square, accum_out=ss), reads, [junk, st])
        K.op("act", lambda: nc.scalar.activation(out=ss, in_=ss, func=AF.Sqrt, scale=1.0 / n, bias=EPS), [st], [st])
        K.op("dve", lambda: nc.vector.reciprocal(out=ss, in_=ss), [st], [st])
        return st

    def transposes_to(dstT, dst_kslice, src_tile, nk, sub, gcol0, trk=None):
        fns = [lambda k=k: nc.tensor.transpose(out=pbf(3)[:, 128 * k:128 * k + 128], in_=src_tile.ap[:, 128 * k:128 * k + 128], identity=ident_b.ap) for k in range(nk)]
        K.pe(fns, [src_tile, ident_b], [PB[3]])
        for k in range(nk):
            K.op("dve", lambda k=k: nc.vector.tensor_scalar(out=dstT.ap[:, dst_kslice + k, 128 * sub:128 * sub + 128], in0=pbf(3)[:, 128 * k:128 * k + 128],
                                                           scalar1=gcols.ap[:, gcol0 + k:gcol0 + k + 1], scalar2=None, op0=ALU.mult), [PB[3], gcols], [trk if trk is not None else dstT])

    def kv_from_tok(kvt, sub, ncols_total):
        fns = [lambda k=k: nc.tensor.transpose(out=pbf(3)[:, 128 * k:128 * k + 128], in_=kvt.ap[:, 128 * k:128 * k + 128], identity=ident_b.ap) for k in range(2)]
        fns.append(lambda: nc.tensor.transpose(out=pbf(3)[0:96, 256:384], in_=kvt.ap[:, 256:352], identity=ident_b.ap))
        K.pe(fns, [kvt, ident_b], [PB[3]])
        K.op("dve", lambda: nc.vector.tensor_copy(out=latT.ap[:, :, 128 * sub:128 * sub + 128], in_=pbf(3)[:, 0:256].rearrange("p (a b) -> p a b", a=2)), [PB[3]], [latT])
        K.op("dve", lambda: nc.vector.tensor_copy(out=KTs.ap[64:96, :, 128 * sub:128 * sub + 128],
                                                 in_=pbf(3)[64:96, 256:384].unsqueeze(1).to_broadcast([32, 8, 128])), [PB[3]], [KTs])

    def kv_build(wkv_t, wkv_v, nsub):
        N = 128 * nsub
        for hp in range(4):
            b = 4 + hp % 2
            fns = [lambda k=k, hp=hp, b=b: nc.tensor.matmul(PB[b].ap[:, 0:N], wkv_v[:, k, 128 * hp:128 * hp + 128], latT.ap[:, k, 0:N], start=(k == 0), stop=(k == 1)) for k in range(2)]
            K.pe(fns, [wkv_t, latT], [PB[b]])
            K.op("dve", lambda hp=hp, b=b: nc.vector.tensor_copy(out=KTs.ap[0:64, 2 * hp, 0:N], in_=PB[b].ap[0:64, 0:N]), [PB[b]], [KTs])
            K.op("act", lambda hp=hp, b=b: nc.scalar.activation(out=KTs.ap[0:64, 2 * hp + 1, 0:N], in_=PB[b].ap[64:128, 0:N], func=AF.Copy), [PB[b]], [KTs])
        for sub in range(nsub):
            b = 6 + sub % 2
            fns = [lambda k=k, sub=sub, b=b: nc.tensor.matmul(PB[b].ap[:, :], latT.ap[:, k, 128 * sub:128 * sub + 128], wkv_v[:, k, 512:1024], start=(k == 0), stop=(k == 1)) for k in range(2)]
            K.pe(fns, [wkv_t, latT], [PB[b]])
            K.op("dve", lambda sub=sub, b=b: nc.vector.tensor_copy(out=Vs.ap[:, :, sub, 0:64], in_=PB[b].ap.rearrange("p (h c) -> p h c", c=64)), [PB[b]], [Vs])

    def attention(ci, n_full_kt, qc0, nq, diag_i, half_last, dsts, between=None):
        nkt = n_full_kt + (1 if half_last else 0)
        nblk = (nkt + 15) // 16
        tot_keys = n_full_kt * 128 + (64 if half_last else 0)
        for h in range(NH):
            ob = 6 + h % 2
            qh = qTh[h // 4]
            hl = h % 4
            steps = []
            for blk in range(nblk):
                kt0 = blk * 16
                nkb = min(16, nkt - kt0)
                for kl in range(nkb):
                    kt = kt0 + kl
                    kp = 64 if (half_last and kt == nkt - 1) else 128
                    isdiag = diag_i is not None and kt >= 4 * diag_i
                    n0 = 128 * (kt - 4 * diag_i) if isdiag else 0
                    steps.append(dict(blk=blk, kt0=kt0, nkb=nkb, kl=kl, kt=kt, kp=kp, isdiag=isdiag, n0=n0, N=nq - n0))
            blkslot = {}

            def emit_S(st):
                blk = st["blk"]
                if blk not in blkslot:
                    slot = actr[0] % NKB
                    actr[0] += 1
                    blkslot[blk] = slot
                    kt0, nkb = st["kt0"], st["nkb"]
                    nkeys = min(128 * nkb, tot_keys - 128 * kt0)
                    regs = kvreg[ci][(kt0 * 128) // 512:(kt0 * 128 + nkeys + 511) // 512]
                    K.dma("sp", kbch[slot], Kblk[slot].ap[0:96, 0:nkeys], ktc[ci][h, :, 128 * kt0:128 * kt0 + nkeys], reads=regs, writes=[Kblk[slot]])
                    nvp = 128 if nkeys >= 128 else 64
                    K.dma("sp", vbch[slot], Vblk[slot].ap[0:nvp, 0:nkb, :], vc[ci][h, 0:nvp, kt0:kt0 + nkb, :], reads=regs, writes=[Vblk[slot]])
                slot = blkslot[blk]
                st["slot"] = slot
                sbk = SBANKS[actr[1] % len(SBANKS)]
                actr[1] += 1
                st["sbk"] = sbk
                kl, kp, n0, N = st["kl"], st["kp"], st["n0"], st["N"]
                K.pe([lambda: nc.tensor.matmul(PB[sbk].ap[0:kp, 0:N], Kblk[slot].ap[0:96, 128 * kl:128 * kl + kp], qh.ap[0:96, hl, qc0 + n0:qc0 + n0 + N], start=True, stop=True)],
                     [Kblk[slot], qh], [PB[sbk]])

            def emit_PV(st, first):
                slot, sbk, kl, kp, n0, N = st["slot"], st["sbk"], st["kl"], st["kp"], st["n0"], st["N"]
                pt = PT.next()
                K.op("act", lambda: nc.scalar.activation(out=pt.ap[0:kp, 0:N], in_=PB[sbk].ap[0:kp, 0:N], func=AF.Exp, scale=SCALE), [PB[sbk]], [pt])
                if st["isdiag"]:
                    K.op("dve", lambda: nc.vector.memset(pt.ap[64:128, 0:64], 0.0), [], [pt])
                K.pe([lambda: nc.tensor.matmul(PB[ob].ap[0:65, n0:n0 + N], Vblk[slot].ap[0:kp, kl, :], pt.ap[0:kp, 0:N], start=first, stop=True, skip_group_check=(not first))],
                     [Vblk[slot], pt], [PB[ob]])

            LA = 3
            for j in range(min(LA, len(steps))):
                emit_S(steps[j])
            for j in range(len(steps)):
                if j + LA < len(steps):
                    emit_S(steps[j + LA])
                emit_PV(steps[j], j == 0)
            if between is not None:
                between(h)
            ot = OTs.next()
            K.op("dve", lambda ot=ot, ob=ob: nc.vector.tensor_copy(out=ot.ap[0:64, 0:nq], in_=PB[ob].ap[0:64, 0:nq]), [PB[ob]], [ot])
            K.op("dve", lambda ot=ot, ob=ob: nc.vector.tensor_copy(out=ot.ap[64:65, 0:nq], in_=PB[ob].ap[64:65, 0:nq]), [PB[ob]], [ot])
            c0 = 0
            for (po, nqq, sub) in dsts:
                K.pe([lambda ot=ot, c0=c0, nqq=nqq: nc.tensor.transpose(out=PB[3].ap[0:nqq, 0:65], in_=ot.ap[0:65, c0:c0 + nqq], identity=ident_f.ap[0:65, 0:65])], [ot, ident_f], [PB[3]])
                rct = statc[32 + (actr[2] % 16)]
                rc = rct.ap[0:nqq, :]
                actr[2] += 1
                K.op("dve", lambda rc=rc, nqq=nqq: nc.vector.reciprocal(out=rc, in_=PB[3].ap[0:nqq, 64:65]), [PB[3]], [rct])
                K.op("dve", lambda rc=rc, po=po, nqq=nqq, sub=sub, h=h: nc.vector.tensor_scalar(out=attn.ap[po:po + nqq, sub, 64 * h:64 * h + 64], in0=PB[3].ap[0:nqq, 0:64], scalar1=rc, scalar2=None, op0=ALU.mult), [PB[3], rct], [attn])
                c0 += nqq
    actr = [0, 0, 0]
    SBANKS = [4, 5, 0, 1, 2]

    class Stop(Exception):
        pass

    def chk(n):
        if stage == n:
            raise Stop()

    def token_tile(kind, ti):
        if kind == "p":
            nsub, x_src, cs_src = 4, xp[512 * ti:512 * ti + 512, :], cs_p[512 * ti:512 * ti + 512, :]
            y_dst, lat_dst, kr_dst = yp[512 * ti:512 * ti + 512, :], latp[512 * ti:512 * ti + 512, :], krp[512 * ti:512 * ti + 512, :]
        else:
            nsub, x_src, cs_src = 2, xs, cs_s
            y_dst, lat_dst, kr_dst = ys, lats, krs
        TT = 128 * nsub
        NC = TT // 8
        for sub in range(nsub):
            K.dma("sp", xch[sub], xt.ap[:, sub, :], x_src[128 * sub:128 * sub + 128, :], writes=[xts[sub]])
        K.dma("sp", csch, cst.ap[:, 0:nsub, :], cs_src.rearrange("(s p) d -> p s d", p=128), writes=[cst])
        rs = [rms_stats(xt.ap[:, sub, :], [xts[sub]], D, sub) for sub in range(nsub)]
        xbs = []

        def s1_scale(sub):
            xb = xnb.next()
            xbs.append(xb)
            K.op("dve", lambda: nc.vector.tensor_scalar(out=xb.ap, in0=xt.ap[:, sub, :], scalar1=rs[sub].ap, scalar2=None, op0=ALU.mult), [xts[sub], rs[sub]], [xb])
        s1_scale(0)
        for sub in range(nsub):
            if sub + 1 < nsub:
                s1_scale(sub + 1)
            transposes_to(actT, 0, xbs[sub], 8, sub, 0, actTs[sub])
        chk(2)
        wc = R.get("winc0", "winc1", "winc2")

        def c_mm(sub):
            pb0 = 0 if sub % 2 == 0 else 5
            base = 512 * pb0
            fns = []
            for k in range(8):
                wvw = wc[k // 3][1]
                kl = k % 3
                for (c0, c1) in ((0, 512), (512, 1024), (1024, 1056)):
                    fns.append(lambda k=k, kl=kl, wvw=wvw, c0=c0, c1=c1: nc.tensor.matmul(
                        ps[:, base + c0:base + c1], actT.ap[:, k, 128 * sub:128 * sub + 128], wvw[:, kl, c0:c1], start=(k == 0), stop=(k == 7)))
            K.pe(fns, [actTs[sub], wc[0][0], wc[1][0], wc[2][0]], [PB[pb0], PB[pb0 + 1], PB[pb0 + 2]])

        def c_post(sub):
            pb0 = 0 if sub % 2 == 0 else 5
            base = 512 * pb0
            P0, P1, P2 = PB[pb0], PB[pb0 + 1], PB[pb0 + 2]
            r = rms_stats(ps[:, base:base + QL], [P0, P1], QL, 8 + sub)
            cq = cqn.next()
            K.op("dve", lambda: nc.vector.tensor_scalar(out=cq.ap, in0=ps[:, base:base + QL], scalar1=r.ap, scalar2=None, op0=ALU.mult), [P0, P1, r], [cq])
            r2 = rms_stats(ps[:, base + QL:base + QL + KVL], [P1], KVL, 12 + sub)
            ko = kvout[kvo_i[0] % 2]; koc = kvoch[kvo_i[0] % 2]; kvo_i[0] += 1
            kvt = kvtok.next()
            K.op("dve", lambda: nc.vector.scalar_tensor_tensor(out=ko.ap[:, 0:KVL], in0=ps[:, base + QL:base + QL + KVL], scalar=r2.ap, in1=gkv_bc.ap, op0=ALU.mult, op1=ALU.mult), [P1, r2, gkv_bc], [ko])
            x1, x2 = ps[:, base + 1024:base + 1040], ps[:, base + 1040:base + 1056]
            cs_, sn_ = cst.ap[:, sub, 0:16], cst.ap[:, sub, 16:32]
            rt = rtmp.ap[:, 0, :]
            K.op("dve", lambda: nc.vector.tensor_tensor(out=rt[:, 0:16], in0=x1, in1=cs_, op=ALU.mult), [P2, cst], [rtmp])
            K.op("dve", lambda: nc.vector.tensor_tensor(out=rt[:, 16:32], in0=x2, in1=sn_, op=ALU.mult), [P2, cst], [rtmp])
            K.op("dve", lambda: nc.vector.tensor_tensor(out=rt[:, 32:48], in0=x1, in1=sn_, op=ALU.mult), [P2, cst], [rtmp])
            K.op("dve", lambda: nc.vector.tensor_tensor(out=rt[:, 48:64], in0=x2, in1=cs_, op=ALU.mult), [P2, cst], [rtmp])
            K.op("dve", lambda: nc.vector.tensor_tensor(out=ko.ap[:, 256:272], in0=rt[:, 0:16], in1=rt[:, 16:32], op=ALU.subtract), [rtmp], [ko])
            K.op("dve", lambda: nc.vector.tensor_tensor(out=ko.ap[:, 272:288], in0=rt[:, 32:48], in1=rt[:, 48:64], op=ALU.add), [rtmp], [ko])
            K.op("pool", lambda: nc.gpsimd.tensor_copy(out=kvt.ap[:, 0:256], in_=ko.ap[:, 0:256]), [ko], [kvt])
            K.op("pool", lambda: nc.gpsimd.tensor_copy(out=kvt.ap[:, 320:352], in_=ko.ap[:, 256:288]), [ko], [kvt])
            K.op("pool", lambda: nc.gpsimd.memset(kvt.ap[:, 256:320], 0.0), [], [kvt])
            K.dma("pool", koc, lat_dst[128 * sub:128 * sub + 128, :], ko.ap[:, 0:256], reads=[ko])
            K.dma("pool", koc, kr_dst[128 * sub:128 * sub + 128, :], ko.ap[:, 256:288], reads=[ko], batch=True)
            kv_from_tok(kvt, sub, TT)
            transposes_to(cqnT, 0, cq, 6, sub, 16, cqnTs[sub])
        c_mm(0)
        for sub in range(nsub):
            if sub + 1 < nsub:
                c_mm(sub + 1)
            c_post(sub)
        chk(3)
        wqs = R.get("wq0", "wq1")

        def q_mm(sub):
            pb0 = 0 if sub % 2 == 0 else 5
            base = 512 * pb0
            fns = []
            for k in range(6):
                wvw = wqs[k // 3][1]
                kl = k % 3
                for (c0, c1) in ((0, 512), (512, 768)):
                    fns.append(lambda k=k, kl=kl, wvw=wvw, c0=c0, c1=c1: nc.tensor.matmul(
                        ps[:, base + c0:base + c1], cqnT.ap[:, k, 128 * sub:128 * sub + 128], wvw[:, kl, c0:c1], start=(k == 0), stop=(k == 5)))
            K.pe(fns, [cqnTs[sub], wqs[0][0], wqs[1][0]], [PB[pb0], PB[pb0 + 1]])

        def q_post(sub):
            pb0 = 0 if sub % 2 == 0 else 5
            base = 512 * pb0
            K.op("act", lambda: nc.scalar.activation(out=qsb.ap[:, 0:512], in_=ps[:, base:base + 512], func=AF.Copy), [PB[pb0]], [qsb])
            K.op("dve", lambda: nc.vector.tensor_copy(out=qsb.ap[:, 512:768], in_=ps[:, base + 512:base + 768]), [PB[pb0 + 1]], [qsb])
            qv = qsb.ap.rearrange("p (h c) -> p h c", c=96)
            qk = qtok.next()
            K.op("pool", lambda: nc.gpsimd.tensor_copy(out=qk.ap[:, :, 0:64], in_=qv[:, :, 0:64]), [qsb], [qk])
            cs8 = cst.ap[:, sub, 0:16].unsqueeze(1).to_broadcast([128, 8, 16])
            sn8 = cst.ap[:, sub, 16:32].unsqueeze(1).to_broadcast([128, 8, 16])
            q1, q2 = qv[:, :, 64:80], qv[:, :, 80:96]
            rt4 = rtmp.ap.rearrange("p h (a c) -> p h a c", a=4)
            K.op("dve", lambda: nc.vector.tensor_tensor(out=rt4[:, :, 0, :], in0=q1, in1=cs8, op=ALU.mult), [qsb, cst], [rtmp])
            K.op("dve", lambda: nc.vector.tensor_tensor(out=rt4[:, :, 1, :], in0=q2, in1=sn8, op=ALU.mult), [qsb, cst], [rtmp])
            K.op("dve", lambda: nc.vector.tensor_tensor(out=rt4[:, :, 2, :], in0=q1, in1=sn8, op=ALU.mult), [qsb, cst], [rtmp])
            K.op("dve", lambda: nc.vector.tensor_tensor(out=rt4[:, :, 3, :], in0=q2, in1=cs8, op=ALU.mult), [qsb, cst], [rtmp])
            K.op("dve", lambda: nc.vector.tensor_tensor(out=qk.ap[:, :, 64:80], in0=rt4[:, :, 0, :], in1=rt4[:, :, 1, :], op=ALU.subtract), [rtmp], [qk])
            K.op("dve", lambda: nc.vector.tensor_tensor(out=qk.ap[:, :, 80:96], in0=rt4[:, :, 2, :], in1=rt4[:, :, 3, :], op=ALU.add), [rtmp], [qk])
            fns = [lambda h=h: nc.tensor.transpose(out=pbf(3)[0:96, 128 * h:128 * h + 128], in_=qk.ap[:, h, :], identity=ident_b.ap) for h in range(NH)]
            K.pe(fns, [qk, ident_b], [PB[3]])
            for hh in range(2):
                for (p0, p1) in ((0, 64), (64, 96)):
                    K.op("act", lambda hh=hh, p0=p0, p1=p1: nc.scalar.activation(out=qTh[hh].ap[p0:p1, :, 128 * sub:128 * sub + 128],
                                                                               in_=pbf(3)[p0:p1, 512 * hh:512 * hh + 512].rearrange("p (h c) -> p h c", h=4), func=AF.Copy), [PB[3]], [qTh[hh]])
        q_mm(0)
        for sub in range(nsub):
            if sub + 1 < nsub:
                q_mm(sub + 1)
            q_post(sub)
        chk(4)
        (wkv_t, wkv_v), = R.get("wkv")
        kv_build(wkv_t, wkv_v, nsub)
        if kind == "p":
            reg = kvreg[0][ti]
            K.dma("pool", ktsch, ktc[0][:, :, 512 * ti:512 * ti + 512].rearrange("h r n -> r h n"), KTs.ap[0:96, :, :], reads=[KTs], writes=[reg])
            K.dma("pool", vsch, vc[0][:, :, 4 * ti:4 * ti + 4, :].rearrange("h p s c -> p h s c"), Vs.ap[:, :, :, :], reads=[Vs], writes=[reg])
        else:
            for s in range(NSEQ_S):
                reg = kvreg[s + 1][NPT_S]
                sub, po = s // 2, 64 * (s % 2)
                K.dma("pool", ktsch, ktc[s + 1][:, :, PAST:PAST + 64].rearrange("h r n -> r h n"), KTs.ap[0:96, :, 64 * s:64 * s + 64], reads=[KTs], writes=[reg], batch=(s > 0))
                K.dma("pool", vsch, vc[s + 1][:, 0:64, NKT_S - 1, :].rearrange("h p c -> p h c"), Vs.ap[po:po + 64, :, sub, :], reads=[Vs], writes=[reg], batch=(s > 0))
                K.dma("pool", vsch, vc[s + 1][:, 64:128, NKT_S - 1, :].rearrange("h p c -> p h c"), Vs.ap[64 - po:128 - po, :, sub, :], reads=[Vs], writes=[reg], batch=True)
        (wu_t, wu_v), = R.get("winu")
        for m in range(4):
            b = 4 + m % 2
            fns = [lambda k=k, m=m, b=b: nc.tensor.matmul(PB[b].ap[:, 0:TT], wu_v[:, k, 128 * m:128 * m + 128], actT.ap[:, k, 0:TT], start=(k == 0), stop=(k == 7)) for k in range(8)]
            K.pe(fns, [wu_t] + actTs[0:nsub], [PB[b]])
            K.op("act", lambda m=m, b=b: nc.scalar.activation(out=uT.ap[:, m, 0:TT], in_=PB[b].ap[:, 0:TT], func=AF.Copy), [PB[b]], [uT])
        chk(5)
        ssm_a(kind, ti, nsub, TT, NC)
        scan_q = scan_step_fns(kind, NC)
        ncalls = [NH if kind == "p" else NH * NSEQ_S]

        def between(h):
            n_ = (len(scan_q) + ncalls[0] - 1) // ncalls[0]
            ncalls[0] -= 1
            for _ in range(n_):
                scan_q.pop(0)()
        if kind == "p":
            attention(0, 4 * ti + 4, 0, 512, ti, False, [(0, 128, s_) for s_ in range(4)], between)
        else:
            for s in range(NSEQ_S):
                attention(s + 1, PAST // 128, 64 * s, 64, None, True, [(64 * (s % 2), 64, s // 2)], between)
        while scan_q:
            scan_q.pop(0)()
        chk(6)
        for sub in range(nsub):
            r = rms_stats(attn.ap[:, sub, :], [attn], 512, 16 + sub)
            xb = xnb.next()
            K.op("dve", lambda sub=sub, xb=xb, r=r: nc.vector.tensor_scalar(out=xb.ap[:, 0:512], in0=attn.ap[:, sub, :], scalar1=r.ap, scalar2=None, op0=ALU.mult), [attn, r], [xb])
            transposes_to(actT, 0, xb, 4, sub, 22, actTs[sub])
        ssm_c(kind, ti, nsub, TT, NC)
        chk(7)
        wo = R.get("wout0", "wout1")
        for sub in range(nsub):
            fns = []
            for k in range(8):
                wvw = wo[k // 4][1]
                kl = k % 4
                for (c0, c1) in ((0, 512), (512, 1024)):
                    fns.append(lambda k=k, kl=kl, wvw=wvw, c0=c0, c1=c1, sub=sub: nc.tensor.matmul(
                        ps[:, c0:c1], actT.ap[:, k, 128 * sub:128 * sub + 128], wvw[:, kl, c0:c1], start=(k == 0), stop=(k == 7)))
            K.pe(fns, [actTs[sub], wo[0][0], wo[1][0]], [PB[0], PB[1]])
            K.op("dve", lambda sub=sub: nc.vector.tensor_tensor(out=xt.ap[:, sub, :], in0=ps[:, 0:1024], in1=xt.ap[:, sub, :], op=ALU.add), [PB[0], PB[1], xts[sub]], [xts[sub]])
        chk(8)
        for sub in range(nsub):
            r = rms_stats(xt.ap[:, sub, :], [xts[sub]], D, 20 + sub)
            xb = xnb.next()
            K.op("dve", lambda sub=sub, xb=xb, r=r: nc.vector.tensor_scalar(out=xb.ap, in0=xt.ap[:, sub, :], scalar1=r.ap, scalar2=None, op0=ALU.mult), [xts[sub], r], [xb])
            transposes_to(actT, 0, xb, 8, sub, 8, actTs[sub])
        def mlp_up(fc):
            (wu_t, wu_v), (wd_t, wd_v) = R.get("wup%d" % fc, "wdn%d" % fc, hold=(2 if fc > 0 else 0))
            a = aTr.next()
            for ft in range(4):
                b = 4 + ft % 2
                fns = [lambda k=k, ft=ft, b=b: nc.tensor.matmul(PB[b].ap[:, 0:TT], wu_v[:, k, 128 * ft:128 * ft + 128], actT.ap[:, k, 0:TT], start=(k == 0), stop=(k == 7)) for k in range(8)]
                K.pe(fns, [wu_t] + actTs[0:nsub], [PB[b]])
                tr = trl.next()
                K.op("dve", lambda b=b, tr=tr: nc.vector.tensor_scalar(out=tr.ap[:, 0:TT], in0=PB[b].ap[:, 0:TT], scalar1=0.0, scalar2=None, op0=ALU.max), [PB[b]], [tr])
                K.op("pool", lambda ft=ft, tr=tr, a=a: nc.gpsimd.tensor_tensor(out=a.ap[:, ft, 0:TT], in0=tr.ap[:, 0:TT], in1=tr.ap[:, 0:TT], op=ALU.mult), [tr], [a])
            return a, wd_t, wd_v

        def mlp_down(a, wd_t, wd_v):
            for sub in range(nsub):
                bb = (0, 1) if sub % 2 == 0 else (6, 7)
                base = 512 * bb[0]
                fns = []
                for ft in range(4):
                    for hh in range(2):
                        fns.append(lambda ft=ft, hh=hh, sub=sub, base=base: nc.tensor.matmul(
                            ps[:, base + 512 * hh:base + 512 * hh + 512], a.ap[:, ft, 128 * sub:128 * sub + 128], wd_v[:, ft, 512 * hh:512 * hh + 512], start=(ft == 0), stop=(ft == 3)))
                K.pe(fns, [a, wd_t], [PB[bb[0]], PB[bb[1]]])
                K.op("dve", lambda sub=sub, base=base: nc.vector.tensor_tensor(out=xt.ap[:, sub, :], in0=ps[:, base:base + 1024], in1=xt.ap[:, sub, :], op=ALU.add), [PB[bb[0]], PB[bb[1]], xts[sub]], [xts[sub]])

        cur = mlp_up(0)
        for fc in range(8):
            nxt = mlp_up(fc + 1) if fc + 1 < 8 else None
            mlp_down(*cur)
            cur = nxt
        for sub in range(nsub):
            r = rms_stats(xt.ap[:, sub, :], [xts[sub]], D, 24 + sub)
            K.op("dve", lambda sub=sub, r=r: nc.vector.scalar_tensor_tensor(out=xt.ap[:, sub, :], in0=xt.ap[:, sub, :], scalar=r.ap, in1=gfin_bc.ap, op0=ALU.mult, op1=ALU.mult), [xts[sub], r, gfin_bc], [xts[sub]])
            K.dma("pool", ych[sub], y_dst[128 * sub:128 * sub + 128, :], xt.ap[:, sub, :], reads=[xts[sub]])

    def ssm_a(kind, ti, nsub, TT, NC):
        uTc = uT.ap[:, :, 0:TT].rearrange("p m (c i) -> p m i c", i=8)
        Sv = Ssb.ap.rearrange("p (kk r) t c -> p r kk t c", r=4)
        for r in range(4):
            bank = 4 + r
            for kk in range(4):
                for reim in range(2):
                    c0 = (kk * 2 + reim) * NC
                    fns = [lambda i=i, kk=kk, r=r, reim=reim, bank=bank, c0=c0: nc.tensor.matmul(
                        PB[bank].ap[:, c0:c0 + NC], W1.ap[32 * r:32 * r + 32, kk, reim, i, :], uTc[32 * r:32 * r + 32, kk, i, :],
                        start=(i == 0), stop=(i == 7), tile_position=(32 * r, 0)) for i in range(8)]
                    K.pe(fns, [W1, uT], [PB[bank]])
            K.op("act", lambda r=r, bank=bank: nc.scalar.activation(
                out=Sv[:, r, :, :, 0:NC], in_=PB[bank].ap[:, 0:8 * NC].rearrange("p (a t c) -> p a t c", a=4, t=2), func=AF.Copy), [PB[bank]], [Ssb])
    def scan_step_fns(kind, NC):
        return [(lambda c=c: scan_step(kind, c)) for c in range(NC)]

    def scan_step(kind, c):
        if True:
            if kind == "p":
                prev_t, prev = (Hc, Hc.ap) if c == 0 else (Ssb, Ssb.ap[:, :, :, c - 1])
            else:
                if c % 8 == 0:
                    prev_t, prev = H0s, H0s.ap[:, c // 8, :, :]
                else:
                    prev_t, prev = Ssb, Ssb.ap[:, :, :, c - 1]
            pre = prev[:, :, 0:1].to_broadcast([128, 16, 2])
            pim = prev[:, :, 1:2].to_broadcast([128, 16, 2])
            if c % 8 == 0 or kind != "p" or True:
                pass
            K.op("pool", lambda c=c, prev=prev: nc.gpsimd.tensor_copy(out=Hbf.ap[:, :, :, c], in_=prev), [prev_t], [Hbf])
            K.op("pool", lambda pre=pre: nc.gpsimd.tensor_tensor(out=st1.ap, in0=A8.ap, in1=pre, op=ALU.mult), [A8, prev_t], [st1])
            K.op("pool", lambda pim=pim: nc.gpsimd.tensor_tensor(out=st2.ap, in0=B8.ap, in1=pim, op=ALU.mult), [B8, prev_t], [st2])
            K.op("pool", lambda: nc.gpsimd.tensor_tensor(out=st1.ap, in0=st1.ap, in1=st2.ap, op=ALU.add), [st1, st2], [st1])
            K.op("pool", lambda c=c: nc.gpsimd.tensor_tensor(out=Ssb.ap[:, :, :, c], in0=Ssb.ap[:, :, :, c], in1=st1.ap, op=ALU.add), [Ssb, st1], [Ssb])
    def ssm_c(kind, ti, nsub, TT, NC):
        uTc = uT.ap[:, :, 0:TT].rearrange("p m (c i) -> p m i c", i=8)
        if kind == "p":
            K.op("dve", lambda: nc.vector.tensor_copy(out=Hc.ap, in_=Ssb.ap[:, :, :, NC - 1]), [Ssb], [Hc])
            if ti == NT_P - 1:
                K.op("dve", lambda: nc.vector.tensor_copy(out=hout.ap, in_=Ssb.ap[:, :, :, NC - 1]), [Ssb], [hout])
                for g2 in range(2):
                    K.dma("pool", hoch, hrp.rearrange("(k two) n -> two n k", two=2)[g2], hout.ap[64 * g2:64 * g2 + 64, :, 0], reads=[hout], batch=(g2 > 0))
                    K.dma("pool", hoch, hip.rearrange("(k two) n -> two n k", two=2)[g2], hout.ap[64 * g2:64 * g2 + 64, :, 1], reads=[hout], batch=True)
        else:
            for s in range(NSEQ_S):
                for g2 in range(2):
                    K.dma("pool", hoch, hrs[s].rearrange("(k two) n -> two n k", two=2)[g2], Ssb.ap[64 * g2:64 * g2 + 64, :, 0, 8 * s + 7], reads=[Ssb], batch=(s + g2 > 0))
                    K.dma("pool", hoch, his[s].rearrange("(k two) n -> two n k", two=2)[g2], Ssb.ap[64 * g2:64 * g2 + 64, :, 1, 8 * s + 7], reads=[Ssb], batch=True)
        (wg_t, wg_v), = R.get("wglu")
        for kk in range(4):
            b = 6 + kk % 2
            yv = PB[b].ap[:, 0:TT].rearrange("p (c i) -> p i c", i=8)
            fns = [lambda kk=kk, b=b: nc.tensor.matmul(PB[b].ap[:, 0:TT], Dd.ap[:, kk, :], uT.ap[:, kk, 0:TT], start=True, stop=True)]
            for j in range(8):
                for i in range(j + 1):
                    fns.append(lambda kk=kk, j=j, i=i, yv=yv: nc.tensor.matmul(yv[:, j, :], BD.ap[:, kk, j - i, :], uTc[:, kk, i, :], start=False, stop=True, skip_group_check=True))
            for r in range(4):
                kp = 4 * kk + r
                for j in range(8):
                    for reim in range(2):
                        lastone = (r == 3 and j == 7 and reim == 1)
                        fns.append(lambda kp=kp, r=r, j=j, reim=reim, yv=yv, lastone=lastone: nc.tensor.matmul(
                            yv[32 * r:32 * r + 32, j, :], W2.ap[:, kp, reim, j, :], Hbf.ap[:, kp, reim, 0:NC], start=False, stop=True, skip_group_check=True, tile_position=(0, 32 * r)))
            K.pe(fns, [Dd, uT, BD, W2, Hbf], [PB[b]])
            yp_ = PB[b].ap[:, 0:TT]
            K.op("act", lambda yp_=yp_: nc.scalar.activation(out=tA.ap[:, 0:TT], in_=yp_, func=AF.Square), [PB[b]], [tA])
            K.op("dve", lambda: nc.vector.tensor_scalar(out=tA.ap[:, 0:TT], in0=tA.ap[:, 0:TT], scalar1=0.044715, scalar2=1.0, op0=ALU.mult, op1=ALU.add), [tA], [tA])
            K.op("dve", lambda yp_=yp_: nc.vector.tensor_tensor(out=tA.ap[:, 0:TT], in0=yp_, in1=tA.ap[:, 0:TT], op=ALU.mult), [PB[b], tA], [tA])
            K.op("act", lambda: nc.scalar.activation(out=tA.ap[:, 0:TT], in_=tA.ap[:, 0:TT], func=AF.Sigmoid, scale=1.5957691216), [tA], [tA])
            K.op("dve", lambda kk=kk, yp_=yp_: nc.vector.tensor_tensor(out=y2.ap[:, kk, 0:TT], in0=yp_, in1=tA.ap[:, 0:TT], op=ALU.mult), [PB[b], tA], [y2])
            K.op("pool", lambda kk=kk: nc.gpsimd.tensor_copy(out=y2b.ap[:, kk, 0:TT], in_=y2.ap[:, kk, 0:TT]), [y2], [y2b])
        for m in range(4):
            b = 4 + m % 2
            fns = [lambda k=k, m=m, b=b: nc.tensor.matmul(PB[b].ap[:, 0:TT], wg_v[:, k, 128 * m:128 * m + 128], y2b.ap[:, k, 0:TT], start=(k == 0), stop=(k == 3)) for k in range(4)]
            K.pe(fns, [wg_t, y2b], [PB[b]])
            K.op("act", lambda b=b: nc.scalar.activation(out=tB.ap[:, 0:TT], in_=PB[b].ap[:, 0:TT], func=AF.Sigmoid), [PB[b]], [tB])
            K.op("dve", lambda m=m: nc.vector.tensor_tensor(out=y2.ap[:, m, 0:TT], in0=y2.ap[:, m, 0:TT], in1=tB.ap[:, 0:TT], op=ALU.mult), [y2, tB], [y2])
            K.op("pool", lambda m=m: nc.gpsimd.tensor_tensor(out=sqb.ap[:, m, 0:TT], in0=y2.ap[:, m, 0:TT], in1=y2.ap[:, m, 0:TT], op=ALU.mult), [y2], [sqb])
        fns = [lambda m=m: nc.tensor.matmul(PB[6].ap[:, 0:TT], ones_b.ap, sqb.ap[:, m, 0:TT], start=(m == 0), stop=(m == 3)) for m in range(4)]
        K.pe(fns, [ones_b, sqb], [PB[6]])
        K.op("act", lambda: nc.scalar.activation(out=rbc.ap[:, 0:TT], in_=PB[6].ap[:, 0:TT], func=AF.Sqrt, scale=1.0 / 512, bias=EPS), [PB[6]], [rbc])
        K.op("dve", lambda: nc.vector.reciprocal(out=rbc.ap[:, 0:TT], in_=rbc.ap[:, 0:TT]), [rbc], [rbc])
        for m in range(4):
            K.op("dve", lambda m=m: nc.vector.scalar_tensor_tensor(out=actT.ap[:, 4 + m, 0:TT], in0=y2.ap[:, m, 0:TT], scalar=gcols.ap[:, 26 + m:27 + m], in1=rbc.ap[:, 0:TT], op0=ALU.mult, op1=ALU.mult), [y2, gcols, rbc], actTs[0:nsub])

    kvbufs = list(kvtok.items)
    for t_ in cqn.items:
        kvbufs.append(T(t_.ap[:, 0:352], share=t_))
    for t_ in qtok.items:
        kvbufs.append(T(t_.ap.rearrange("p h c -> p (h c)")[:, 0:352], share=t_))
    kvl_items = [(s, pt_i, sub) for s in range(NSEQ_S) for pt_i in range(NPT_S) for sub in range(4)]
    kvl_state = {"loaded": 0}
    kvl_ch = [K.chan("kvl%d" % i) for i in range(len(kvbufs))]

    def kvl_prefetch(upto):
        while kvl_state["loaded"] < min(len(kvl_items), upto):
            i_ = kvl_state["loaded"]
            s, pt_i, sub = kvl_items[i_]
            kvt = kvbufs[i_ % len(kvbufs)]
            ch = kvl_ch[i_ % len(kvbufs)]
            r0 = 512 * pt_i + 128 * sub
            K.op("pool", lambda kvt=kvt: nc.gpsimd.memset(kvt.ap[:, 256:320], 0.0), [], [kvt])
            K.dma("pool", ch, kvt.ap[:, 0:256], ckl[s, r0:r0 + 128, :], writes=[kvt])
            K.dma("pool", ch, kvt.ap[:, 320:352], ckr[s, r0:r0 + 128, :], writes=[kvt], batch=True)
            kvl_state["loaded"] += 1

    def kv_from_cache(s, pt_i):
        base = (s * NPT_S + pt_i) * 4
        for sub in range(4):
            kvl_prefetch(base + sub + 5)
            kvt = kvbufs[(base + sub) % len(kvbufs)]
            kv_from_tok(kvt, sub, 512)
        (wkv_t, wkv_v), = R.get("wkv")
        kv_build(wkv_t, wkv_v, 4)
        reg = kvreg[s + 1][pt_i]
        K.dma("pool", ktsch, ktc[s + 1][:, :, 512 * pt_i:512 * pt_i + 512].rearrange("h r n -> r h n"), KTs.ap[0:96, :, :], reads=[KTs], writes=[reg])
        K.dma("pool", vsch, vc[s + 1][:, :, 4 * pt_i:4 * pt_i + 4, :].rearrange("h p s c -> p h s c"), Vs.ap[:, :, :, :], reads=[Vs], writes=[reg])

    K.op("pool", lambda: nc.gpsimd.memset(Hc.ap, 0.0), [], [Hc])
    for ti in range(NT_P):
        R.add(plan_tok)
    for s in range(NSEQ_S * NPT_S):
        R.add(plan_kv)
    R.add(plan_tok)
    try:
        for ti in range(NT_P):
            token_tile("p", ti)
        chk(9)
        for s in range(NSEQ_S):
            for pt_i in range(NPT_S):
                kv_from_cache(s, pt_i)
        chk(10)
        token_tile("s", 0)
    except Stop:
        pass
    K.finish("pool")
    return nc, K


_CACHE = {}


def _rope_table(pos):
    half = 16
    inv_freq = (10000.0 ** (-(np.arange(half, dtype=np.float32) * 2.0) / 32)).astype(np.float32)
    ang = pos.astype(np.float32)[:, None] * inv_freq[None, :]
    return np.concatenate([np.cos(ang), np.sin(ang)], axis=1).astype(np.float32)


def run(inputs, SEQ, PAST, ncores=8):
    key = (SEQ, PAST)
    if key not in _CACHE:
        _CACHE[key] = build(SEQ, PAST)
    nc, K = _CACHE[key]
    f = lambda a: np.ascontiguousarray(np.asarray(a, dtype=np.float32))
    x_prompt = f(inputs["x_prompt"]); x_sample = f(inputs["x_sample"])
    ckl = f(inputs["cache_kv_latent"])[0]; ckr = f(inputs["cache_k_rope"])[0]
    sre = f(inputs["state_ssm_re"])[0]; sim = f(inputs["state_ssm_im"])[0]
    cs_p = _rope_table(np.arange(SEQ))
    cs_s = np.tile(_rope_table(PAST + np.arange(DSEQ)), (NSEQ_S, 1))
    ident = np.eye(128, dtype=np.float32)
    shared = {
        "g_mix": f(inputs["g_mix"]), "w_in": f(inputs["w_in"])[0], "g_q": f(inputs["g_q_a"]), "w_q": f(inputs["w_q_up"])[0],
        "g_kv": f(inputs["g_kv_a"]), "w_kv": f(inputs["w_kv_up"])[0], "a_re": f(inputs["a_re"])[0], "a_im": f(inputs["a_im"])[0],
        "lstep": f(inputs["log_step"]), "b_re": f(inputs["b_re"])[0], "b_im": f(inputs["b_im"])[0],
        "c_re": f(inputs["c_re"])[0], "c_im": f(inputs["c_im"])[0], "d_skip": f(inputs["d_skip"]), "w_glu": f(inputs["w_glu"])[0],
        "g_attn": f(inputs["g_attn_out"]), "g_ssm": f(inputs["g_ssm_out"]), "w_out": f(inputs["w_out"])[0],
        "g_mlp": f(inputs["g_mlp"]), "w_up": f(inputs["w_up"])[0], "w_down": f(inputs["w_down"])[0],
        "g_fin": f(inputs["g_final"]).reshape(1, D), "cs_p": cs_p, "cs_s": cs_s, "ident": ident,
    }
    in_maps = []
    for c in range(ncores):
        m = dict(shared)
        m["xp"] = x_prompt[c]
        sl = slice(NSEQ_S * c, NSEQ_S * (c + 1))
        m["xs"] = np.ascontiguousarray(x_sample[sl].reshape(NSEQ_S * DSEQ, D))
        m["ckl"] = np.ascontiguousarray(ckl[sl]); m["ckr"] = np.ascontiguousarray(ckr[sl])
        m["sre"] = np.ascontiguousarray(sre[sl]); m["sim"] = np.ascontiguousarray(sim[sl])
        in_maps.append(m)
    res = run_bass_kernel_spmd(nc, in_maps, core_ids=list(range(ncores)))
    rs = res.results
    cat = lambda k: np.stack([np.asarray(r[k], dtype=np.float32) for r in rs])
    y_prompt = cat("yp")
    y_sample = cat("ys").reshape(ncores * NSEQ_S, DSEQ, D)
    lat_p = cat("latp")[None]
    kr_p = cat("krp")[None]
    hr_p = cat("hrp")[None]
    hi_p = cat("hip")[None]
    lat_s = cat("lats").reshape(ncores * NSEQ_S, DSEQ, KVL)[None]
    kr_s = cat("krs").reshape(ncores * NSEQ_S, DSEQ, 32)[None]
    hr_s = cat("hrs").reshape(ncores * NSEQ_S, 32, 64)[None]
    hi_s = cat("his").reshape(ncores * NSEQ_S, 32, 64)[None]
    return (y_prompt, y_sample, lat_p, kr_p, hr_p, hi_p, lat_s, kr_s, hr_s, hi_s)


def kernel(**inputs):
    return run(inputs, 8192, 4096, 8)
```

```python
import math
import numpy as np
import ml_dtypes
import concourse.bass as bass
import concourse.mybir as mybir
from concourse.bass_utils import run_bass_kernel_spmd

F32 = mybir.dt.float32
BF16 = mybir.dt.bfloat16
I32 = mybir.dt.int32
AF = mybir.ActivationFunctionType
ALU = mybir.AluOpType
AX = mybir.AxisListType

D = 1024
DIN = 1568
QL = 768
KVL = 256
NH = 8
DFF = 4096
EPS = 1e-6
SCALE = 96 ** -0.5
NSEQ_S = 4
DSEQ = 64
TWO_PI = 2.0 * math.pi


class T:
    def __init__(self, ap, name="", share=None):
        self.ap = ap
        self.name = name
        self.d = share.d if share is not None else {"w": None, "r": {}}

    @property
    def w(self):
        return self.d["w"]

    @w.setter
    def w(self, v):
        self.d["w"] = v

    @property
    def r(self):
        return self.d["r"]

    @r.setter
    def r(self, v):
        self.d["r"] = v


class Chan:
    def __init__(self, nc, name):
        self.sem = nc.alloc_semaphore(name)
        self.total = 0
        self.key = name
        self.holder = [0]


class Trk:
    def __init__(self, nc):
        self.nc = nc
        self.E = {"pe": nc.tensor, "act": nc.scalar, "dve": nc.vector, "pool": nc.gpsimd, "sp": nc.sync}
        self.sem = {}
        self.cnt = {}
        self.gen = {}
        for e in ("pe", "act", "dve", "pool"):
            self.gen[e] = 0
            self._newsem(e)
        self.seen = {}
        self.chans = []
        self.ninst = {e: 0 for e in self.E}

    def _newsem(self, e):
        self.sem[e] = self.nc.alloc_semaphore("c_%s_%d" % (e, self.gen[e]))
        self.cnt[e] = 0
        self.gen[e] += 1

    def chan(self, name):
        c = Chan(self.nc, name)
        self.chans.append(c)
        return c

    def _wait(self, eng, ev):
        if ev is None:
            return
        sem, key, val = ev
        if isinstance(val, list):
            val = val[0]
        k = (eng, key)
        if self.seen.get(k, 0) >= val:
            return
        self.E[eng].wait_ge(sem, val)
        self.ninst[eng] += 1
        self.seen[k] = val

    def _deps(self, eng, reads, writes, skip_same=False):
        for t in reads:
            if t.w is not None and not (skip_same and t.w[1][0] == eng):
                self._wait(eng, t.w)
        for t in writes:
            if t.w is not None and not (skip_same and t.w[1][0] == eng):
                self._wait(eng, t.w)
            for ev in t.r.values():
                if not (skip_same and ev[1][0] == eng):
                    self._wait(eng, ev)

    def _done(self, ev, reads, writes):
        for t in reads:
            t.r[ev[1]] = ev
        for t in writes:
            t.w = ev
            t.r = {}

    def op(self, eng, fn, reads=(), writes=()):
        self._deps(eng, reads, writes)
        inst = fn()
        if self.cnt[eng] >= 60000:
            self._newsem(eng)
        self.cnt[eng] += 1
        inst.then_inc(self.sem[eng], 1)
        self.ninst[eng] += 1
        ev = (self.sem[eng], (eng, self.gen[eng]), self.cnt[eng])
        self._done(ev, reads, writes)

    def pe(self, fns, reads=(), writes=()):
        self._deps("pe", reads, writes, skip_same=True)
        inst = None
        for f in fns:
            inst = f()
            self.ninst["pe"] += 1
        if self.cnt["pe"] >= 60000:
            self._newsem("pe")
        self.cnt["pe"] += 1
        inst.then_inc(self.sem["pe"], 1)
        ev = (self.sem["pe"], ("pe", self.gen["pe"]), self.cnt["pe"])
        self._done(ev, reads, writes)

    def dma(self, q, ch, out_ap, in_ap, reads=(), writes=(), batch=False, **kw):
        self._deps(q, reads, writes)
        if not batch:
            if ch.total > 0:
                self._wait(q, (ch.sem, ch.key, ch.total))
            ch.holder = [ch.total]
        inst = self.E[q].dma_start(out=out_ap, in_=in_ap, allow_slow_non_contiguous=True, **kw)
        ch.total += 16
        ch.holder[0] = ch.total
        inst.then_inc(ch.sem, 16)
        self.ninst[q] += 1
        ev = (ch.sem, ch.key, ch.holder)
        self._done(ev, reads, writes)

    def barrier(self):
        for eng in ("pe", "act", "dve", "pool", "sp"):
            self.finish(eng, True)

    def finish(self, eng="pool", skip_nb=False):
        for c in self.chans:
            if skip_nb and getattr(c, "nobarrier", False):
                continue
            if c.total > 0:
                self._wait(eng, (c.sem, c.key, c.total))
        for e in ("pe", "act", "dve", "pool"):
            if self.cnt[e] > 0 and e != eng:
                self._wait(eng, (self.sem[e], (e, self.gen[e]), self.cnt[e]))


class Rot:
    def __init__(self, items):
        self.items = items
        self.i = 0

    def next(self):
        t = self.items[self.i % len(self.items)]
        self.i += 1
        return t


def bc(ap, shape):
    return ap.to_broadcast(shape)


def build(SEQ, PAST, stage=99):
    nc = bass.Bass("TRN2", target_bir_lowering=False)
    K = Trk(nc)
    NT_P = SEQ // 512
    NKT_P = SEQ // 128
    NK_S = PAST + 128
    NKT_S = NK_S // 128
    NPT_S = PAST // 512

    def din(name, shape, dt=F32):
        return nc.dram_tensor(name, list(shape), dt, kind="ExternalInput").ap()

    def dout(name, shape):
        return nc.dram_tensor(name, list(shape), F32, kind="ExternalOutput").ap()

    def dscr(name, shape, dt=BF16):
        return nc.dram_tensor(name, list(shape), dt, kind="Internal").ap()

    def sb(name, shape, dt=F32):
        return nc.alloc_sbuf_tensor(name, list(shape), dt).ap()

    xp = din("xp", [SEQ, D]); xs = din("xs", [NSEQ_S * DSEQ, D])
    ckl = din("ckl", [NSEQ_S, PAST, KVL]); ckr = din("ckr", [NSEQ_S, PAST, 32])
    sre = din("sre", [NSEQ_S, 32, 64]); sim = din("sim", [NSEQ_S, 32, 64])
    g_mix = din("g_mix", [1, D]); w_in = din("w_in", [D, DIN]); g_q = din("g_q", [1, QL])
    w_q = din("w_q", [QL, QL]); g_kv = din("g_kv", [1, KVL]); w_kv = din("w_kv", [KVL, 1024])
    a_re = din("a_re", [32, 64]); a_im = din("a_im", [32, 64]); lstep = din("lstep", [1, 32])
    b_re = din("b_re", [32, 64, 16]); b_im = din("b_im", [32, 64, 16])
    c_re = din("c_re", [32, 16, 64]); c_im = din("c_im", [32, 16, 64])
    d_skip = din("d_skip", [1, 512]); w_glu = din("w_glu", [512, 512])
    g_attn = din("g_attn", [1, 512]); g_ssm = din("g_ssm", [1, 512]); w_out = din("w_out", [D, D])
    g_mlp = din("g_mlp", [1, D]); w_up = din("w_up", [D, DFF]); w_down = din("w_down", [DFF, D])
    g_fin = din("g_fin", [1, D])
    cs_p = din("cs_p", [SEQ, 32]); cs_s = din("cs_s", [NSEQ_S * DSEQ, 32])
    ident_in = din("ident", [128, 128])

    yp = dout("yp", [SEQ, D]); ys = dout("ys", [NSEQ_S * DSEQ, D])
    latp = dout("latp", [SEQ, KVL]); krp = dout("krp", [SEQ, 32])
    hrp = dout("hrp", [32, 64]); hip = dout("hip", [32, 64])
    lats = dout("lats", [NSEQ_S * DSEQ, KVL]); krs = dout("krs", [NSEQ_S * DSEQ, 32])
    hrs = dout("hrs", [NSEQ_S, 32, 64]); his = dout("his", [NSEQ_S, 32, 64])

    s_winc = dscr("s_winc", [128, 8, 1056]); s_winu = dscr("s_winu", [128, 8, 512])
    s_wq = dscr("s_wq", [128, 6, 768]); s_wkv = dscr("s_wkv", [128, 2, 1024])
    s_wglu = dscr("s_wglu", [128, 4, 512]); s_wout = dscr("s_wout", [128, 8, 1024])
    s_wup = dscr("s_wup", [8, 128, 8, 512]); s_wdn = dscr("s_wdn", [8, 128, 4, 1024])
    ktc = [dscr("ktc0", [NH, 96, SEQ])] + [dscr("ktc%d" % (s + 1), [NH, 96, NK_S]) for s in range(NSEQ_S)]
    vc = [dscr("vc0", [NH, 128, NKT_P, 65])] + [dscr("vc%d" % (s + 1), [NH, 128, NKT_S, 65]) for s in range(NSEQ_S)]
    kvreg = [[T(None, "kvreg0_%d" % i) for i in range(NT_P)]] + \
            [[T(None, "kvreg%d_%d" % (s + 1, i)) for i in range(NPT_S + 1)] for s in range(NSEQ_S)]
    wscr = {}

    ps = nc.alloc_psum_tensor("ps", [128, 4096], F32).ap()
    PB = [T(ps[:, 512 * b:512 * (b + 1)], "pb%d" % b) for b in range(8)]

    def pbf(b):
        return ps[:, 512 * b:512 * (b + 1)].bitcast(BF16)

    from contextlib import ExitStack

    def prod(sh):
        n = 1
        for v_ in sh:
            n *= v_
        return n

    def vw(t, dt, shape, off=0):
        a_ = t.ap if dt == F32 else t.ap.bitcast(dt)
        esz = 4 if dt in (F32, I32) else 2
        n = prod(shape)
        a_ = a_[:, off // esz:off // esz + n]
        if len(shape) == 2:
            a_ = a_.rearrange("p (a b) -> p a b", a=shape[0])
        elif len(shape) == 3:
            a_ = a_.rearrange("p (a b c) -> p a b c", a=shape[0], b=shape[1])
        return a_

    def raw(name, nbytes):
        return T(sb(name, [128, nbytes // 4]), name)

    ident_f = T(sb("ident_f", [128, 128]))
    ident_b = T(sb("ident_b", [128, 128], BF16))
    ones_b = T(sb("ones_b", [128, 128], BF16))
    gkv_bc = T(sb("gkv_bc", [128, KVL])); gfin_bc = T(sb("gfin_bc", [128, D]))
    gcols = T(sb("gcols", [128, 40]))
    W1 = T(sb("W1", [128, 4, 2, 8, 128], BF16))
    W2 = T(sb("W2", [128, 16, 2, 8, 32], BF16))
    BD = T(sb("BD", [128, 4, 8, 128], BF16))
    Dd = T(sb("Dd", [128, 4, 128], BF16))
    A8 = T(sb("A8", [128, 16, 2])); B8 = T(sb("B8", [128, 16, 2]))
    Hc = T(sb("Hc", [128, 16, 2])); H0s = T(sb("H0s", [128, NSEQ_S, 16, 2])); h0ch = K.chan("h0")
    cch = K.chan("const")
    cchA = K.chan("ssmA")
    cchB = K.chan("ssmB")
    castch = K.chan("cast")

    def alloc_runtime():
        g = {}
        NSLOT = 5
        g["NSLOT"] = NSLOT
        g["ring"] = [T(sb("ring%d" % i, [128, 4096], BF16)) for i in range(NSLOT)]
        g["ringch"] = [K.chan("ring%d" % i) for i in range(NSLOT)]
        g["xt"] = T(sb("xt", [128, 4, D])); g["xch"] = [K.chan("xt%d" % i) for i in range(4)]; g["ych"] = [K.chan("yst%d" % i) for i in range(4)]
        g["cst"] = T(sb("cst", [128, 4, 32])); g["csch"] = K.chan("cst"); g["kvlch"] = K.chan("kvl")
        g["actT"] = T(sb("actT", [128, 8, 512], BF16))
        rx = [raw("rx%d" % i, 2048) for i in range(2)]
        g["xnb"] = Rot([T(vw(r_, BF16, (D,)), share=r_) for r_ in rx])
        g["trl"] = Rot([T(vw(r_, F32, (512,)), share=r_) for r_ in rx])
        g["junk"] = T(sb("junk", [128, D], BF16))
        g["cqn"] = Rot([T(sb("cqn%d" % i, [128, QL], BF16)) for i in range(2)])
        rq = [raw("rq%d" % i, 4096) for i in range(2)]
        g["aT"] = [T(vw(r_, BF16, (4, 512)), share=r_) for r_ in rq]
        g["qTh"] = [T(vw(r_, BF16, (4, 512)), share=r_) for r_ in rq]
        rD = raw("rD", 8192)
        g["cqnT"] = T(sb("cqnT", [128, 6, 512], BF16))
        g["Ssb"] = T(vw(rD, F32, (16, 2, 64)), share=rD)
        g["kvtok"] = Rot([T(sb("kvtok%d" % i, [128, 352], BF16)) for i in range(2)])
        g["kvout"] = [T(sb("kvout%d" % i, [128, 288])) for i in range(2)]
        g["kvoch"] = [K.chan("kvout%d" % i) for i in range(2)]
        g["latT"] = T(sb("latT", [128, 2, 512], BF16))
        g["qtok"] = Rot([T(sb("qtok%d" % i, [128, 8, 96], BF16)) for i in range(2)])
        g["uT"] = T(sb("uT", [128, 4, 512], BF16))
        g["KTs"] = T(sb("KTs", [128, 8, 512], BF16)); g["ktsch"] = K.chan("kts")
        g["Vs"] = T(sb("Vs", [128, 8, 4, 65], BF16)); g["vsch"] = K.chan("vs")
        rA = raw("rA", 8192)
        g["attn"] = T(vw(rA, F32, (4, 512)), share=rA)
        g["y2"] = T(vw(rA, F32, (4, 512)), share=rA)
        rB = [raw("rB%d" % i, 2048) for i in range(2)]
        g["OTs"] = Rot([T(vw(r_, F32, (512,)), share=r_) for r_ in rB])
        g["tA"] = T(vw(rB[0], F32, (512,)), share=rB[0]); g["tB"] = T(vw(rB[1], F32, (512,)), share=rB[1])
        rC = [raw("rC%d" % i, 4096) for i in range(2)]
        g["Kblk"] = [T(vw(r_, BF16, (2048,)), share=r_) for r_ in rC]
        g["y2b"] = T(vw(rC[0], BF16, (4, 512)), share=rC[0]); g["sqb"] = T(vw(rC[1], BF16, (4, 512)), share=rC[1])
        g["Vblk"] = [T(sb("Vblk%d" % i, [128, 16, 65], BF16)) for i in range(2)]
        g["kbch"] = [K.chan("kb%d" % i) for i in range(2)]
        g["vbch"] = [K.chan("vb%d" % i) for i in range(2)]
        g["PT"] = Rot([T(sb("PT%d" % i, [128, 512], BF16)) for i in range(3)])
        g["stat"] = T(sb("stat", [128, 64]))
        g["rtmp"] = T(sb("rtmp", [128, 8, 64]))
        g["Hbf"] = T(sb("Hbf", [128, 16, 2, 64], BF16))
        g["st1"] = T(sb("st1", [128, 16, 2])); g["st2"] = T(sb("st2", [128, 16, 2]))
        g["hout"] = T(sb("hout", [128, 16, 2])); g["hoch"] = K.chan("hout")
        g["rbc"] = T(sb("rbc", [128, 512]))
        g["qsb"] = T(sb("qsb", [128, 768]))
        return g


    K.dma("sp", cch, ident_f.ap, ident_in, writes=[ident_f], batch=True)
    K.dma("sp", cch, gkv_bc.ap, g_kv.partition_broadcast(128), writes=[gkv_bc], batch=True)
    K.dma("sp", cch, gfin_bc.ap, g_fin.partition_broadcast(128), writes=[gfin_bc], batch=True)
    for (src, off, n) in ((g_mix, 0, 8), (g_mlp, 8, 8), (g_q, 16, 6), (g_attn, 22, 4), (g_ssm, 26, 4), (d_skip, 30, 4)):
        K.dma("sp", cch, gcols.ap[:, off:off + n], src.rearrange("o (k c) -> c (o k)", c=128), writes=[gcols], batch=True)
    K.op("dve", lambda: nc.vector.tensor_copy(out=ident_b.ap, in_=ident_f.ap), [ident_f], [ident_b])
    K.op("dve", lambda: nc.vector.memset(ones_b.ap, 1.0), [], [ones_b])

    wv = w_in.rearrange("(k p) n -> p k n", p=128)
    casts = [
        (s_winc, wv[:, :, 0:1056]), (s_winu, wv[:, :, 1056:1568]),
        (s_wq, w_q.rearrange("(k p) n -> p k n", p=128)),
    ]
    wkvv = w_kv.rearrange("(k p) (h c) -> p k h c", p=128, c=128)
    for k in range(2):
        casts.append((s_wkv[:, k, 0:512].rearrange("p (h c) -> p h c", c=64), wkvv[:, k, :, 0:64]))
        casts.append((s_wkv[:, k, 512:1024].rearrange("p (h c) -> p h c", c=64), wkvv[:, k, :, 64:128]))
    casts.append((s_wglu, w_glu.rearrange("(k p) n -> p k n", p=128)))
    casts.append((s_wout, w_out.rearrange("(k p) n -> p k n", p=128)))
    wupv = w_up.rearrange("(k p) (fc n) -> fc p k n", p=128, n=512)
    wdnv = w_down.rearrange("(fc ft p) n -> fc p ft n", ft=4, p=128)
    for fc in range(8):
        for k in range(0, 8, 4):
            casts.append((s_wup[fc, :, k:k + 4, :], wupv[fc, :, k:k + 4, :]))
        for ft in range(0, 4, 2):
            casts.append((s_wdn[fc, :, ft:ft + 2, :], wdnv[fc, :, ft:ft + 2, :]))
    cast_groups = {}
    for (o, i) in casts:
        nm = o.name if hasattr(o, "name") else str(id(o))
        nm = nm.split("[")[0]
        if nm not in cast_groups:
            cast_groups[nm] = K.chan("cast_" + nm)
            wscr[nm] = T(None, "wscr_" + nm)
            cast_groups[nm].nobarrier = True
        K.dma("pool", cast_groups[nm], o, i, writes=[wscr[nm]], batch=True)

    if stage == 0:
        K.finish("pool")
        return nc, K
    def ssm_setup(es):
        def sb(name, shape, dt=F32):
            return es.enter_context(nc.sbuf_tensor(name, list(shape), dt)).ap()
        Are = T(sb("Are", [128, 16])); Aim = T(sb("Aim", [128, 16])); LS = T(sb("LS", [128, 16]))
        for g2 in range(2):
            K.dma("sp", cchA, Are.ap[64 * g2:64 * g2 + 64, :], a_re.rearrange("(k two) n -> two n k", two=2)[g2], writes=[Are], batch=True)
            K.dma("sp", cchA, Aim.ap[64 * g2:64 * g2 + 64, :], a_im.rearrange("(k two) n -> two n k", two=2)[g2], writes=[Aim], batch=True)
            K.dma("sp", cchA, LS.ap[64 * g2:64 * g2 + 64, :], lstep.rearrange("o (k two) -> two o k", two=2)[g2].partition_broadcast(64), writes=[LS], batch=True)
        BreD = T(sb("BreD", [128, 16, 32])); BimD = T(sb("BimD", [128, 16, 32]))
        CreD = T(sb("CreD", [32, 16, 128])); CimD = T(sb("CimD", [32, 16, 128]))
        for t in (BreD, BimD, CreD, CimD):
            K.op("dve", lambda t=t: nc.vector.memset(t.ap, 0.0), [], [t])
        for g2 in range(2):
            K.dma("sp", cchB, BreD.ap[64 * g2:64 * g2 + 64, :, 16 * g2:16 * g2 + 16], b_re.rearrange("(k two) n q -> two n k q", two=2)[g2], writes=[BreD], batch=True)
            K.dma("sp", cchB, BimD.ap[64 * g2:64 * g2 + 64, :, 16 * g2:16 * g2 + 16], b_im.rearrange("(k two) n q -> two n k q", two=2)[g2], writes=[BimD], batch=True)
            K.dma("sp", cchB, CreD.ap[16 * g2:16 * g2 + 16, :, 64 * g2:64 * g2 + 64], c_re.rearrange("(k two) p n -> two p k n", two=2)[g2], writes=[CreD], batch=True)
            K.dma("sp", cchB, CimD.ap[16 * g2:16 * g2 + 16, :, 64 * g2:64 * g2 + 64], c_im.rearrange("(k two) p n -> two p k n", two=2)[g2], writes=[CimD], batch=True)
        for s in range(NSEQ_S):
            for g2 in range(2):
                K.dma("sp", h0ch, H0s.ap[64 * g2:64 * g2 + 64, s, :, 0], sre[s].rearrange("(k two) n -> two n k", two=2)[g2], writes=[H0s], batch=True)
                K.dma("sp", h0ch, H0s.ap[64 * g2:64 * g2 + 64, s, :, 1], sim[s].rearrange("(k two) n -> two n k", two=2)[g2], writes=[H0s], batch=True)

        def V(name):
            return T(sb(name, [128, 16]))
        dt_ = V("dt_"); mag = V("mag"); th = V("th"); cs = V("cs"); sn = V("sn")
        w1 = V("w1"); w2 = V("w2"); w3 = V("w3"); wi = T(sb("wi", [128, 16], I32))
        K.op("act", lambda: nc.scalar.activation(out=dt_.ap, in_=LS.ap, func=AF.Exp), [LS], [dt_])
        K.op("dve", lambda: nc.vector.tensor_tensor(out=w1.ap, in0=Are.ap, in1=dt_.ap, op=ALU.mult), [Are, dt_], [w1])
        K.op("act", lambda: nc.scalar.activation(out=mag.ap, in_=w1.ap, func=AF.Exp), [w1], [mag])
        K.op("dve", lambda: nc.vector.tensor_tensor(out=th.ap, in0=Aim.ap, in1=dt_.ap, op=ALU.mult), [Aim, dt_], [th])

        def sin_of(dst, shift):
            K.op("dve", lambda: nc.vector.tensor_scalar(out=w1.ap, in0=th.ap, scalar1=shift, scalar2=1.0 / TWO_PI, op0=ALU.add, op1=ALU.mult), [th], [w1])
            K.op("dve", lambda: nc.vector.tensor_copy(out=wi.ap, in_=w1.ap), [w1], [wi])
            K.op("dve", lambda: nc.vector.tensor_copy(out=w2.ap, in_=wi.ap), [wi], [w2])
            K.op("dve", lambda: nc.vector.tensor_scalar(out=w1.ap, in0=th.ap, scalar1=shift, scalar2=None, op0=ALU.add), [th], [w1])
            K.op("dve", lambda: nc.vector.scalar_tensor_tensor(out=w1.ap, in0=w2.ap, scalar=-TWO_PI, in1=w1.ap, op0=ALU.mult, op1=ALU.add), [w2, w1], [w1])
            K.op("dve", lambda: nc.vector.tensor_scalar(out=w2.ap, in0=w1.ap, scalar1=math.pi, scalar2=-TWO_PI, op0=ALU.is_gt, op1=ALU.mult), [w1], [w2])
            K.op("dve", lambda: nc.vector.tensor_tensor(out=w1.ap, in0=w1.ap, in1=w2.ap, op=ALU.add), [w1, w2], [w1])
            K.op("dve", lambda: nc.vector.tensor_scalar(out=w2.ap, in0=w1.ap, scalar1=-math.pi, scalar2=TWO_PI, op0=ALU.is_lt, op1=ALU.mult), [w1], [w2])
            K.op("dve", lambda: nc.vector.tensor_tensor(out=w1.ap, in0=w1.ap, in1=w2.ap, op=ALU.add), [w1, w2], [w1])
            K.op("dve", lambda: nc.vector.tensor_scalar(out=w1.ap, in0=w1.ap, scalar1=3.1415925, scalar2=-3.1415925, op0=ALU.min, op1=ALU.max), [w1], [w1])
            K.op("act", lambda: nc.scalar.activation(out=dst.ap, in_=w1.ap, func=AF.Sin), [w1], [dst])
        sin_of(sn, 0.0)
        sin_of(cs, math.pi / 2)
        LP = T(sb("LP", [128, 9, 2, 16]))
        K.op("dve", lambda: nc.vector.memset(LP.ap[:, 0, 0, :], 1.0), [], [LP])
        K.op("dve", lambda: nc.vector.memset(LP.ap[:, 0, 1, :], 0.0), [], [LP])
        K.op("dve", lambda: nc.vector.tensor_tensor(out=LP.ap[:, 1, 0, :], in0=mag.ap, in1=cs.ap, op=ALU.mult), [mag, cs], [LP])
        K.op("dve", lambda: nc.vector.tensor_tensor(out=LP.ap[:, 1, 1, :], in0=mag.ap, in1=sn.ap, op=ALU.mult), [mag, sn], [LP])
        for k in range(2, 9):
            pr, pi_ = LP.ap[:, k - 1, 0, :], LP.ap[:, k - 1, 1, :]
            lr, li = LP.ap[:, 1, 0, :], LP.ap[:, 1, 1, :]
            K.op("dve", lambda: nc.vector.tensor_tensor(out=w1.ap, in0=pr, in1=lr, op=ALU.mult), [LP], [w1])
            K.op("dve", lambda: nc.vector.tensor_tensor(out=w2.ap, in0=pi_, in1=li, op=ALU.mult), [LP], [w2])
            K.op("dve", lambda: nc.vector.tensor_tensor(out=LP.ap[:, k, 0, :], in0=w1.ap, in1=w2.ap, op=ALU.subtract), [w1, w2], [LP])
            K.op("dve", lambda: nc.vector.tensor_tensor(out=w1.ap, in0=pr, in1=li, op=ALU.mult), [LP], [w1])
            K.op("dve", lambda: nc.vector.tensor_tensor(out=w2.ap, in0=pi_, in1=lr, op=ALU.mult), [LP], [w2])
            K.op("dve", lambda: nc.vector.tensor_tensor(out=LP.ap[:, k, 1, :], in0=w1.ap, in1=w2.ap, op=ALU.add), [w1, w2], [LP])
        K.op("dve", lambda: nc.vector.tensor_copy(out=A8.ap[:, :, 0], in_=LP.ap[:, 8, 0, :]), [LP], [A8])
        K.op("dve", lambda: nc.vector.tensor_copy(out=A8.ap[:, :, 1], in_=LP.ap[:, 8, 1, :]), [LP], [A8])
        K.op("dve", lambda: nc.vector.tensor_scalar(out=B8.ap[:, :, 0], in0=LP.ap[:, 8, 1, :], scalar1=-1.0, scalar2=None, op0=ALU.mult), [LP], [B8])
        K.op("dve", lambda: nc.vector.tensor_copy(out=B8.ap[:, :, 1], in_=LP.ap[:, 8, 0, :]), [LP], [B8])
        cre = V("cre"); cim = V("cim"); den = V("den"); nr = V("nr")
        K.op("dve", lambda: nc.vector.tensor_scalar(out=nr.ap, in0=LP.ap[:, 1, 0, :], scalar1=-1.0, scalar2=None, op0=ALU.add), [LP], [nr])
        K.op("dve", lambda: nc.vector.tensor_tensor(out=w1.ap, in0=Are.ap, in1=Are.ap, op=ALU.mult), [Are], [w1])
        K.op("dve", lambda: nc.vector.tensor_tensor(out=w2.ap, in0=Aim.ap, in1=Aim.ap, op=ALU.mult), [Aim], [w2])
        K.op("dve", lambda: nc.vector.tensor_tensor(out=den.ap, in0=w1.ap, in1=w2.ap, op=ALU.add), [w1, w2], [den])
        K.op("dve", lambda: nc.vector.reciprocal(out=den.ap, in_=den.ap), [den], [den])
        K.op("dve", lambda: nc.vector.tensor_tensor(out=w1.ap, in0=nr.ap, in1=Are.ap, op=ALU.mult), [nr, Are], [w1])
        K.op("dve", lambda: nc.vector.tensor_tensor(out=w2.ap, in0=LP.ap[:, 1, 1, :], in1=Aim.ap, op=ALU.mult), [LP, Aim], [w2])
        K.op("dve", lambda: nc.vector.tensor_tensor(out=w1.ap, in0=w1.ap, in1=w2.ap, op=ALU.add), [w1, w2], [w1])
        K.op("dve", lambda: nc.vector.tensor_tensor(out=cre.ap, in0=w1.ap, in1=den.ap, op=ALU.mult), [w1, den], [cre])
        K.op("dve", lambda: nc.vector.tensor_tensor(out=w1.ap, in0=LP.ap[:, 1, 1, :], in1=Are.ap, op=ALU.mult), [LP, Are], [w1])
        K.op("dve", lambda: nc.vector.tensor_tensor(out=w2.ap, in0=nr.ap, in1=Aim.ap, op=ALU.mult), [nr, Aim], [w2])
        K.op("dve", lambda: nc.vector.tensor_tensor(out=w1.ap, in0=w1.ap, in1=w2.ap, op=ALU.subtract), [w1, w2], [w1])
        K.op("dve", lambda: nc.vector.tensor_tensor(out=cim.ap, in0=w1.ap, in1=den.ap, op=ALU.mult), [w1, den], [cim])

        def B3(t):
            return t.unsqueeze(2).to_broadcast([128, 16, 32])
        m1 = T(sb("m1", [128, 16, 32])); m2 = T(sb("m2", [128, 16, 32]))
        BbR = T(sb("BbR", [128, 16, 32])); BbI = T(sb("BbI", [128, 16, 32]))

        def cmul(dre, dim, are, aim, bre, bim, rd, wr):
            if dre is not None:
                K.op("dve", lambda: nc.vector.tensor_tensor(out=m1.ap, in0=bre, in1=B3(are), op=ALU.mult), rd, [m1])
                K.op("dve", lambda: nc.vector.tensor_tensor(out=m2.ap, in0=bim, in1=B3(aim), op=ALU.mult), rd, [m2])
                K.op("dve", lambda: nc.vector.tensor_tensor(out=dre, in0=m1.ap, in1=m2.ap, op=ALU.subtract), [m1, m2], wr)
            if dim is not None:
                K.op("dve", lambda: nc.vector.tensor_tensor(out=m1.ap, in0=bim, in1=B3(are), op=ALU.mult), rd, [m1])
                K.op("dve", lambda: nc.vector.tensor_tensor(out=m2.ap, in0=bre, in1=B3(aim), op=ALU.mult), rd, [m2])
                K.op("dve", lambda: nc.vector.tensor_tensor(out=dim, in0=m1.ap, in1=m2.ap, op=ALU.add), [m1, m2], wr)
        cmul(BbR.ap, BbI.ap, cre.ap, cim.ap, BreD.ap, BimD.ap, [cre, cim, BreD, BimD], [BbR, BbI])
        GR = T(sb("GR", [128, 16, 32])); GI = T(sb("GI", [128, 16, 32]))
        for i in range(8):
            cmul(GR.ap, GI.ap, LP.ap[:, 7 - i, 0, :], LP.ap[:, 7 - i, 1, :], BbR.ap, BbI.ap, [LP, BbR, BbI], [GR, GI])
            for reim, G in ((0, GR), (1, GI)):
                for r in range(4):
                    bank = (reim * 4 + r)
                    fns = []
                    for kk in range(4):
                        kp = 4 * kk + r
                        fns.append(lambda kk=kk, kp=kp, G=G, bank=bank: nc.tensor.transpose(
                            out=PB[bank].ap[0:32, 128 * kk:128 * kk + 128], in_=G.ap[:, kp, :], identity=ident_f.ap))
                    K.pe(fns, [G, ident_f], [PB[bank]])
                    K.op("act", lambda r=r, bank=bank, reim=reim, i=i: nc.scalar.activation(
                        out=W1.ap[32 * r:32 * r + 32, :, reim, i, :],
                        in_=PB[bank].ap[0:32, :].rearrange("p (a b) -> p a b", a=4), func=AF.Copy), [PB[bank]], [W1])
        CTR = T(sb("CTR", [128, 16, 32])); CTI = T(sb("CTI", [128, 16, 32]))
        for (src, dst, bank) in ((CreD, CTR, 0), (CimD, CTI, 1)):
            fns = [lambda kp=kp, src=src, bank=bank: nc.tensor.transpose(out=PB[bank].ap[:, 32 * kp:32 * kp + 32], in_=src.ap[:, kp, :], identity=ident_f.ap[0:32, 0:32]) for kp in range(16)]
            K.pe(fns, [src, ident_f], [PB[bank]])
            K.op("act", lambda dst=dst, bank=bank: nc.scalar.activation(out=dst.ap, in_=PB[bank].ap.rearrange("p (a b) -> p a b", a=16), func=AF.Copy), [PB[bank]], [dst])
        K.op("dve", lambda: nc.vector.memset(BD.ap, 0.0), [], [BD])
        CLR = T(sb("CLR", [128, 16, 32])); CLI = T(sb("CLI", [128, 16, 32])); NCLI = T(sb("NCLI", [128, 16, 32]))
        for kpow in range(9):
            cmul(CLR.ap, CLI.ap, LP.ap[:, kpow, 0, :], LP.ap[:, kpow, 1, :], CTR.ap, CTI.ap, [LP, CTR, CTI], [CLR, CLI])
            K.op("dve", lambda: nc.vector.tensor_scalar(out=NCLI.ap, in0=CLI.ap, scalar1=-1.0, scalar2=None, op0=ALU.mult), [CLI], [NCLI])
            if kpow >= 1:
                j = kpow - 1
                K.op("act", lambda j=j: nc.scalar.activation(out=W2.ap[:, :, 0, j, :], in_=CLR.ap, func=AF.Copy), [CLR], [W2])
                K.op("act", lambda j=j: nc.scalar.activation(out=W2.ap[:, :, 1, j, :], in_=NCLI.ap, func=AF.Copy), [NCLI], [W2])
            if kpow <= 7:
                tau = kpow
                for r in range(4):
                    for kk in range(4):
                        kp = 4 * kk + r
                        col = (kk * 8 + tau) * 32
                        bank = 2 * r + col // 512
                        c0 = col % 512
                        fns = [
                            lambda kp=kp, bank=bank, c0=c0: nc.tensor.matmul(PB[bank].ap[0:32, c0:c0 + 32], BbR.ap[:, kp, :], CLR.ap[:, kp, :], start=True, stop=False),
                            lambda kp=kp, bank=bank, c0=c0: nc.tensor.matmul(PB[bank].ap[0:32, c0:c0 + 32], BbI.ap[:, kp, :], NCLI.ap[:, kp, :], start=False, stop=True),
                        ]
                        K.pe(fns, [BbR, BbI, CLR, NCLI], [PB[bank]])
        for r in range(4):
            for hb in range(2):
                bank = 2 * r + hb
                K.op("act", lambda r=r, hb=hb, bank=bank: nc.scalar.activation(
                    out=BD.ap[32 * r:32 * r + 32, 2 * hb:2 * hb + 2, :, 32 * r:32 * r + 32],
                    in_=PB[bank].ap[0:32, :].rearrange("p (a t c) -> p a t c", a=2, t=8), func=AF.Copy), [PB[bank]], [BD])
        for kk in range(4):
            K.op("dve", lambda kk=kk: nc.vector.tensor_scalar(out=Dd.ap[:, kk, :], in0=ident_f.ap, scalar1=gcols.ap[:, 30 + kk:31 + kk], scalar2=None, op0=ALU.mult), [ident_f, gcols], [Dd])

    with ExitStack() as es_:
        ssm_setup(es_)
        K.barrier()
    if stage == 1:
        K.finish("pool")
        return nc, K
    G = alloc_runtime()
    NSLOT = G["NSLOT"]; ring = G["ring"]; ringch = G["ringch"]; xt = G["xt"]; xch = G["xch"]; ych = G["ych"]; cst = G["cst"]; csch = G["csch"]; kvlch = G["kvlch"]
    actT = G["actT"]; xnb = G["xnb"]; trl = G["trl"]; junk = G["junk"]; cqn = G["cqn"]; aT = G["aT"]; qTh = G["qTh"]; cqnT = G["cqnT"]; Ssb = G["Ssb"]
    kvtok = G["kvtok"]; kvout = G["kvout"]; kvoch = G["kvoch"]; latT = G["latT"]; qtok = G["qtok"]; uT = G["uT"]; KTs = G["KTs"]; ktsch = G["ktsch"]
    Vs = G["Vs"]; vsch = G["vsch"]; attn = G["attn"]; y2 = G["y2"]; OTs = G["OTs"]; tA = G["tA"]; tB = G["tB"]; Kblk = G["Kblk"]; y2b = G["y2b"]; sqb = G["sqb"]
    Vblk = G["Vblk"]; kbch = G["kbch"]; vbch = G["vbch"]; PT = G["PT"]; stat = G["stat"]; rtmp = G["rtmp"]; Hbf = G["Hbf"]; st1 = G["st1"]; st2 = G["st2"]
    hout = G["hout"]; hoch = G["hoch"]; rbc = G["rbc"]; qsb = G["qsb"]
    NKB = 2
    statc = [T(stat.ap[:, i:i + 1], "stat%d" % i) for i in range(64)]
    xts = [T(xt.ap[:, i, :], "xt%d" % i) for i in range(4)]
    actTs = [T(actT.ap[:, :, 128 * i:128 * i + 128], "actT%d" % i) for i in range(4)]
    cqnTs = [T(cqnT.ap[:, :, 128 * i:128 * i + 128], "cqnT%d" % i) for i in range(4)]
    aTr = Rot(aT)
    kvo_i = [0]
    K.op("pool", lambda: nc.gpsimd.memset(Vs.ap, 1.0), [], [Vs])

    class Ring:
        def __init__(self):
            self.plan = []
            self.issued = 0
            self.got = 0

        def add(self, items):
            self.plan.extend(items)

        def _issue(self, n):
            name, src, shape = self.plan[n]
            slot = n % NSLOT
            dst = ring[slot].ap
            ne = 1
            for s_ in shape:
                ne *= s_
            d = dst[:, 0:ne]
            if len(shape) == 2:
                d = d.rearrange("p (a b) -> p a b", a=shape[0])
            K.dma("sp", ringch[slot], d, src, reads=[wscr[src.name.split("[")[0]]], writes=[ring[slot]])

        def get(self, *names, hold=0):
            n0 = self.got
            assert len(names) + hold <= NSLOT
            for i_, nm in enumerate(names):
                assert self.plan[n0 + i_][0] == nm, (self.plan[n0 + i_][0], nm)
            while self.issued < min(len(self.plan), n0 + NSLOT - hold):
                self._issue(self.issued)
                self.issued += 1
            self.got += len(names)
            outs = []
            for i_ in range(len(names)):
                n = n0 + i_
                shape = self.plan[n][2]
                ne = prod(shape)
                v = ring[n % NSLOT].ap[:, 0:ne]
                if len(shape) == 2:
                    v = v.rearrange("p (a b) -> p a b", a=shape[0])
                outs.append((ring[n % NSLOT], v))
            return outs

    R = Ring()
    plan_tok = [("winc0", s_winc[:, 0:3, :], (3, 1056)), ("winc1", s_winc[:, 3:6, :], (3, 1056)), ("winc2", s_winc[:, 6:8, :], (2, 1056)),
                ("wq0", s_wq[:, 0:3, :], (3, 768)), ("wq1", s_wq[:, 3:6, :], (3, 768)),
                ("wkv", s_wkv, (2, 1024)), ("winu", s_winu, (8, 512)), ("wglu", s_wglu, (4, 512)),
                ("wout0", s_wout[:, 0:4, :], (4, 1024)), ("wout1", s_wout[:, 4:8, :], (4, 1024))]
    for fc in range(8):
        plan_tok.append(("wup%d" % fc, s_wup[fc], (8, 512)))
        plan_tok.append(("wdn%d" % fc, s_wdn[fc], (4, 1024)))
    plan_kv = [("wkv", s_wkv, (2, 1024))]

    def rms_stats(src_ap, reads, n, col):
        st = statc[col]
        ss = st.ap
        K.op("act", lambda: nc.scalar.activation(out=junk.ap[:, 0:n], in_=src_ap, func=AF.Square, accum_out=ss), reads, [junk, st])
        K.op("act", lambda: nc.scalar.activation(out=ss, in_=ss, func=AF.Sqrt, scale=1.0 / n, bias=EPS), [st], [st])
        K.op("dve", lambda: nc.vector.reciprocal(out=ss, in_=ss), [st], [st])
        return st

    def transposes_to(dstT, dst_kslice, src_tile, nk, sub, gcol0, trk=None):
        fns = [lambda k=k: nc.tensor.transpose(out=pbf(3)[:, 128 * k:128 * k + 128], in_=src_tile.ap[:, 128 * k:128 * k + 128], identity=ident_b.ap) for k in range(nk)]
        K.pe(fns, [src_tile, ident_b], [PB[3]])
        for k in range(nk):
            K.op("dve", lambda k=k: nc.vector.tensor_scalar(out=dstT.ap[:, dst_kslice + k, 128 * sub:128 * sub + 128], in0=pbf(3)[:, 128 * k:128 * k + 128],
                                                           scalar1=gcols.ap[:, gcol0 + k:gcol0 + k + 1], scalar2=None, op0=ALU.mult), [PB[3], gcols], [trk if trk is not None else dstT])

    def kv_from_tok(kvt, sub, ncols_total):
        fns = [lambda k=k: nc.tensor.transpose(out=pbf(3)[:, 128 * k:128 * k + 128], in_=kvt.ap[:, 128 * k:128 * k + 128], identity=ident_b.ap) for k in range(2)]
        fns.append(lambda: nc.tensor.transpose(out=pbf(3)[0:96, 256:384], in_=kvt.ap[:, 256:352], identity=ident_b.ap))
        K.pe(fns, [kvt, ident_b], [PB[3]])
        K.op("dve", lambda: nc.vector.tensor_copy(out=latT.ap[:, :, 128 * sub:128 * sub + 128], in_=pbf(3)[:, 0:256].rearrange("p (a b) -> p a b", a=2)), [PB[3]], [latT])
        K.op("dve", lambda: nc.vector.tensor_copy(out=KTs.ap[64:96, :, 128 * sub:128 * sub + 128],
                                                 in_=pbf(3)[64:96, 256:384].unsqueeze(1).to_broadcast([32, 8, 128])), [PB[3]], [KTs])

    def kv_build(wkv_t, wkv_v, nsub):
        N = 128 * nsub
        for hp in range(4):
            b = 4 + hp % 2
            fns = [lambda k=k, hp=hp, b=b: nc.tensor.matmul(PB[b].ap[:, 0:N], wkv_v[:, k, 128 * hp:128 * hp + 128], latT.ap[:, k, 0:N], start=(k == 0), stop=(k == 1)) for k in range(2)]
            K.pe(fns, [wkv_t, latT], [PB[b]])
            K.op("dve", lambda hp=hp, b=b: nc.vector.tensor_copy(out=KTs.ap[0:64, 2 * hp, 0:N], in_=PB[b].ap[0:64, 0:N]), [PB[b]], [KTs])
            K.op("act", lambda hp=hp, b=b: nc.scalar.activation(out=KTs.ap[0:64, 2 * hp + 1, 0:N], in_=PB[b].ap[64:128, 0:N], func=AF.Copy), [PB[b]], [KTs])
        for sub in range(nsub):
            b = 6 + sub % 2
            fns = [lambda k=k, sub=sub, b=b: nc.tensor.matmul(PB[b].ap[:, :], latT.ap[:, k, 128 * sub:128 * sub + 128], wkv_v[:, k, 512:1024], start=(k == 0), stop=(k == 1)) for k in range(2)]
            K.pe(fns, [wkv_t, latT], [PB[b]])
            K.op("dve", lambda sub=sub, b=b: nc.vector.tensor_copy(out=Vs.ap[:, :, sub, 0:64], in_=PB[b].ap.rearrange("p (h c) -> p h c", c=64)), [PB[b]], [Vs])

    def attention(ci, n_full_kt, qc0, nq, diag_i, half_last, dsts, between=None):
        nkt = n_full_kt + (1 if half_last else 0)
        nblk = (nkt + 15) // 16
        tot_keys = n_full_kt * 128 + (64 if half_last else 0)
        for h in range(NH):
            ob = 6 + h % 2
            qh = qTh[h // 4]
            hl = h % 4
            steps = []
            for blk in range(nblk):
                kt0 = blk * 16
                nkb = min(16, nkt - kt0)
                for kl in range(nkb):
                    kt = kt0 + kl
                    kp = 64 if (half_last and kt == nkt - 1) else 128
                    isdiag = diag_i is not None and kt >= 4 * diag_i
                    n0 = 128 * (kt - 4 * diag_i) if isdiag else 0
                    steps.append(dict(blk=blk, kt0=kt0, nkb=nkb, kl=kl, kt=kt, kp=kp, isdiag=isdiag, n0=n0, N=nq - n0))
            blkslot = {}

            def emit_S(st):
                blk = st["blk"]
                if blk not in blkslot:
                    slot = actr[0] % NKB
                    actr[0] += 1
                    blkslot[blk] = slot
                    kt0, nkb = st["kt0"], st["nkb"]
                    nkeys = min(128 * nkb, tot_keys - 128 * kt0)
                    regs = kvreg[ci][(kt0 * 128) // 512:(kt0 * 128 + nkeys + 511) // 512]
                    K.dma("sp", kbch[slot], Kblk[slot].ap[0:96, 0:nkeys], ktc[ci][h, :, 128 * kt0:128 * kt0 + nkeys], reads=regs, writes=[Kblk[slot]])
                    nvp = 128 if nkeys >= 128 else 64
                    K.dma("sp", vbch[slot], Vblk[slot].ap[0:nvp, 0:nkb, :], vc[ci][h, 0:nvp, kt0:kt0 + nkb, :], reads=regs, writes=[Vblk[slot]])
                slot = blkslot[blk]
                st["slot"] = slot
                sbk = SBANKS[actr[1] % len(SBANKS)]
                actr[1] += 1
                st["sbk"] = sbk
                kl, kp, n0, N = st["kl"], st["kp"], st["n0"], st["N"]
                K.pe([lambda: nc.tensor.matmul(PB[sbk].ap[0:kp, 0:N], Kblk[slot].ap[0:96, 128 * kl:128 * kl + kp], qh.ap[0:96, hl, qc0 + n0:qc0 + n0 + N], start=True, stop=True)],
                     [Kblk[slot], qh], [PB[sbk]])

            def emit_PV(st, first):
                slot, sbk, kl, kp, n0, N = st["slot"], st["sbk"], st["kl"], st["kp"], st["n0"], st["N"]
                pt = PT.next()
                K.op("act", lambda: nc.scalar.activation(out=pt.ap[0:kp, 0:N], in_=PB[sbk].ap[0:kp, 0:N], func=AF.Exp, scale=SCALE), [PB[sbk]], [pt])
                if st["isdiag"]:
                    K.op("dve", lambda: nc.vector.memset(pt.ap[64:128, 0:64], 0.0), [], [pt])
                K.pe([lambda: nc.tensor.matmul(PB[ob].ap[0:65, n0:n0 + N], Vblk[slot].ap[0:kp, kl, :], pt.ap[0:kp, 0:N], start=first, stop=True, skip_group_check=(not first))],
                     [Vblk[slot], pt], [PB[ob]])

            LA = 3
            for j in range(min(LA, len(steps))):
                emit_S(steps[j])
            for j in range(len(steps)):
                if j + LA < len(steps):
                    emit_S(steps[j + LA])
                emit_PV(steps[j], j == 0)
            if between is not None:
                between(h)
            ot = OTs.next()
            K.op("dve", lambda ot=ot, ob=ob: nc.vector.tensor_copy(out=ot.ap[0:64, 0:nq], in_=PB[ob].ap[0:64, 0:nq]), [PB[ob]], [ot])
            K.op("dve", lambda ot=ot, ob=ob: nc.vector.tensor_copy(out=ot.ap[64:65, 0:nq], in_=PB[ob].ap[64:65, 0:nq]), [PB[ob]], [ot])
            c0 = 0
            for (po, nqq, sub) in dsts:
                K.pe([lambda ot=ot, c0=c0, nqq=nqq: nc.tensor.transpose(out=PB[3].ap[0:nqq, 0:65], in_=ot.ap[0:65, c0:c0 + nqq], identity=ident_f.ap[0:65, 0:65])], [ot, ident_f], [PB[3]])
                rct = statc[32 + (actr[2] % 16)]
                rc = rct.ap[0:nqq, :]
                actr[2] += 1
                K.op("dve", lambda rc=rc, nqq=nqq: nc.vector.reciprocal(out=rc, in_=PB[3].ap[0:nqq, 64:65]), [PB[3]], [rct])
                K.op("dve", lambda rc=rc, po=po, nqq=nqq, sub=sub, h=h: nc.vector.tensor_scalar(out=attn.ap[po:po + nqq, sub, 64 * h:64 * h + 64], in0=PB[3].ap[0:nqq, 0:64], scalar1=rc, scalar2=None, op0=ALU.mult), [PB[3], rct], [attn])
                c0 += nqq
    actr = [0, 0, 0]
    SBANKS = [4, 5, 0, 1, 2]

    class Stop(Exception):
        pass

    def chk(n):
        if stage == n:
            raise Stop()

    def token_tile(kind, ti):
        if kind == "p":
            nsub, x_src, cs_src = 4, xp[512 * ti:512 * ti + 512, :], cs_p[512 * ti:512 * ti + 512, :]
            y_dst, lat_dst, kr_dst = yp[512 * ti:512 * ti + 512, :], latp[512 * ti:512 * ti + 512, :], krp[512 * ti:512 * ti + 512, :]
        else:
            nsub, x_src, cs_src = 2, xs, cs_s
            y_dst, lat_dst, kr_dst = ys, lats, krs
        TT = 128 * nsub
        NC = TT // 8
        for sub in range(nsub):
            K.dma("sp", xch[sub], xt.ap[:, sub, :], x_src[128 * sub:128 * sub + 128, :], writes=[xts[sub]])
        K.dma("sp", csch, cst.ap[:, 0:nsub, :], cs_src.rearrange("(s p) d -> p s d", p=128), writes=[cst])
        rs = [rms_stats(xt.ap[:, sub, :], [xts[sub]], D, sub) for sub in range(nsub)]
        xbs = []

        def s1_scale(sub):
            xb = xnb.next()
            xbs.append(xb)
            K.op("dve", lambda: nc.vector.tensor_scalar(out=xb.ap, in0=xt.ap[:, sub, :], scalar1=rs[sub].ap, scalar2=None, op0=ALU.mult), [xts[sub], rs[sub]], [xb])
        s1_scale(0)
        for sub in range(nsub):
            if sub + 1 < nsub:
                s1_scale(sub + 1)
            transposes_to(actT, 0, xbs[sub], 8, sub, 0, actTs[sub])
        chk(2)
        wc = R.get("winc0", "winc1", "winc2")

        def c_mm(sub):
            pb0 = 0 if sub % 2 == 0 else 5
            base = 512 * pb0
            fns = []
            for k in range(8):
                wvw = wc[k // 3][1]
                kl = k % 3
                for (c0, c1) in ((0, 512), (512, 1024), (1024, 1056)):
                    fns.append(lambda k=k, kl=kl, wvw=wvw, c0=c0, c1=c1: nc.tensor.matmul(
                        ps[:, base + c0:base + c1], actT.ap[:, k, 128 * sub:128 * sub + 128], wvw[:, kl, c0:c1], start=(k == 0), stop=(k == 7)))
            K.pe(fns, [actTs[sub], wc[0][0], wc[1][0], wc[2][0]], [PB[pb0], PB[pb0 + 1], PB[pb0 + 2]])

        def c_post(sub):
            pb0 = 0 if sub % 2 == 0 else 5
            base = 512 * pb0
            P0, P1, P2 = PB[pb0], PB[pb0 + 1], PB[pb0 + 2]
            r = rms_stats(ps[:, base:base + QL], [P0, P1], QL, 8 + sub)
            cq = cqn.next()
            K.op("dve", lambda: nc.vector.tensor_scalar(out=cq.ap, in0=ps[:, base:base + QL], scalar1=r.ap, scalar2=None, op0=ALU.mult), [P0, P1, r], [cq])
            r2 = rms_stats(ps[:, base + QL:base + QL + KVL], [P1], KVL, 12 + sub)
            ko = kvout[kvo_i[0] % 2]; koc = kvoch[kvo_i[0] % 2]; kvo_i[0] += 1
            kvt = kvtok.next()
            K.op("dve", lambda: nc.vector.scalar_tensor_tensor(out=ko.ap[:, 0:KVL], in0=ps[:, base + QL:base + QL + KVL], scalar=r2.ap, in1=gkv_bc.ap, op0=ALU.mult, op1=ALU.mult), [P1, r2, gkv_bc], [ko])
            x1, x2 = ps[:, base + 1024:base + 1040], ps[:, base + 1040:base + 1056]
            cs_, sn_ = cst.ap[:, sub, 0:16], cst.ap[:, sub, 16:32]
            rt = rtmp.ap[:, 0, :]
            K.op("dve", lambda: nc.vector.tensor_tensor(out=rt[:, 0:16], in0=x1, in1=cs_, op=ALU.mult), [P2, cst], [rtmp])
            K.op("dve", lambda: nc.vector.tensor_tensor(out=rt[:, 16:32], in0=x2, in1=sn_, op=ALU.mult), [P2, cst], [rtmp])
            K.op("dve", lambda: nc.vector.tensor_tensor(out=rt[:, 32:48], in0=x1, in1=sn_, op=ALU.mult), [P2, cst], [rtmp])
            K.op("dve", lambda: nc.vector.tensor_tensor(out=rt[:, 48:64], in0=x2, in1=cs_, op=ALU.mult), [P2, cst], [rtmp])
            K.op("dve", lambda: nc.vector.tensor_tensor(out=ko.ap[:, 256:272], in0=rt[:, 0:16], in1=rt[:, 16:32], op=ALU.subtract), [rtmp], [ko])
            K.op("dve", lambda: nc.vector.tensor_tensor(out=ko.ap[:, 272:288], in0=rt[:, 32:48], in1=rt[:, 48:64], op=ALU.add), [rtmp], [ko])
            K.op("pool", lambda: nc.gpsimd.tensor_copy(out=kvt.ap[:, 0:256], in_=ko.ap[:, 0:256]), [ko], [kvt])
            K.op("pool", lambda: nc.gpsimd.tensor_copy(out=kvt.ap[:, 320:352], in_=ko.ap[:, 256:288]), [ko], [kvt])
            K.op("pool", lambda: nc.gpsimd.memset(kvt.ap[:, 256:320], 0.0), [], [kvt])
            K.dma("pool", koc, lat_dst[128 * sub:128 * sub + 128, :], ko.ap[:, 0:256], reads=[ko])
            K.dma("pool", koc, kr_dst[128 * sub:128 * sub + 128, :], ko.ap[:, 256:288], reads=[ko], batch=True)
            kv_from_tok(kvt, sub, TT)
            transposes_to(cqnT, 0, cq, 6, sub, 16, cqnTs[sub])
        c_mm(0)
        for sub in range(nsub):
            if sub + 1 < nsub:
                c_mm(sub + 1)
            c_post(sub)
        chk(3)
        wqs = R.get("wq0", "wq1")

        def q_mm(sub):
            pb0 = 0 if sub % 2 == 0 else 5
            base = 512 * pb0
            fns = []
            for k in range(6):
                wvw = wqs[k // 3][1]
                kl = k % 3
                for (c0, c1) in ((0, 512), (512, 768)):
                    fns.append(lambda k=k, kl=kl, wvw=wvw, c0=c0, c1=c1: nc.tensor.matmul(
                        ps[:, base + c0:base + c1], cqnT.ap[:, k, 128 * sub:128 * sub + 128], wvw[:, kl, c0:c1], start=(k == 0), stop=(k == 5)))
            K.pe(fns, [cqnTs[sub], wqs[0][0], wqs[1][0]], [PB[pb0], PB[pb0 + 1]])

        def q_post(sub):
            pb0 = 0 if sub % 2 == 0 else 5
            base = 512 * pb0
            K.op("act", lambda: nc.scalar.activation(out=qsb.ap[:, 0:512], in_=ps[:, base:base + 512], func=AF.Copy), [PB[pb0]], [qsb])
            K.op("dve", lambda: nc.vector.tensor_copy(out=qsb.ap[:, 512:768], in_=ps[:, base + 512:base + 768]), [PB[pb0 + 1]], [qsb])
            qv = qsb.ap.rearrange("p (h c) -> p h c", c=96)
            qk = qtok.next()
            K.op("pool", lambda: nc.gpsimd.tensor_copy(out=qk.ap[:, :, 0:64], in_=qv[:, :, 0:64]), [qsb], [qk])
            cs8 = cst.ap[:, sub, 0:16].unsqueeze(1).to_broadcast([128, 8, 16])
            sn8 = cst.ap[:, sub, 16:32].unsqueeze(1).to_broadcast([128, 8, 16])
            q1, q2 = qv[:, :, 64:80], qv[:, :, 80:96]
            rt4 = rtmp.ap.rearrange("p h (a c) -> p h a c", a=4)
            K.op("dve", lambda: nc.vector.tensor_tensor(out=rt4[:, :, 0, :], in0=q1, in1=cs8, op=ALU.mult), [qsb, cst], [rtmp])
            K.op("dve", lambda: nc.vector.tensor_tensor(out=rt4[:, :, 1, :], in0=q2, in1=sn8, op=ALU.mult), [qsb, cst], [rtmp])
            K.op("dve", lambda: nc.vector.tensor_tensor(out=rt4[:, :, 2, :], in0=q1, in1=sn8, op=ALU.mult), [qsb, cst], [rtmp])
            K.op("dve", lambda: nc.vector.tensor_tensor(out=rt4[:, :, 3, :], in0=q2, in1=cs8, op=ALU.mult), [qsb, cst], [rtmp])
            K.op("dve", lambda: nc.vector.tensor_tensor(out=qk.ap[:, :, 64:80], in0=rt4[:, :, 0, :], in1=rt4[:, :, 1, :], op=ALU.subtract), [rtmp], [qk])
            K.op("dve", lambda: nc.vector.tensor_tensor(out=qk.ap[:, :, 80:96], in0=rt4[:, :, 2, :], in1=rt4[:, :, 3, :], op=ALU.add), [rtmp], [qk])
            fns = [lambda h=h: nc.tensor.transpose(out=pbf(3)[0:96, 128 * h:128 * h + 128], in_=qk.ap[:, h, :], identity=ident_b.ap) for h in range(NH)]
            K.pe(fns, [qk, ident_b], [PB[3]])
            for hh in range(2):
                for (p0, p1) in ((0, 64), (64, 96)):
                    K.op("act", lambda hh=hh, p0=p0, p1=p1: nc.scalar.activation(out=qTh[hh].ap[p0:p1, :, 128 * sub:128 * sub + 128],
                                                                               in_=pbf(3)[p0:p1, 512 * hh:512 * hh + 512].rearrange("p (h c) -> p h c", h=4), func=AF.Copy), [PB[3]], [qTh[hh]])
        q_mm(0)
        for sub in range(nsub):
            if sub + 1 < nsub:
                q_mm(sub + 1)
            q_post(sub)
        chk(4)
        (wkv_t, wkv_v), = R.get("wkv")
        kv_build(wkv_t, wkv_v, nsub)
        if kind == "p":
            reg = kvreg[0][ti]
            K.dma("pool", ktsch, ktc[0][:, :, 512 * ti:512 * ti + 512].rearrange("h r n -> r h n"), KTs.ap[0:96, :, :], reads=[KTs], writes=[reg])
            K.dma("pool", vsch, vc[0][:, :, 4 * ti:4 * ti + 4, :].rearrange("h p s c -> p h s c"), Vs.ap[:, :, :, :], reads=[Vs], writes=[reg])
        else:
            for s in range(NSEQ_S):
                reg = kvreg[s + 1][NPT_S]
                sub, po = s // 2, 64 * (s % 2)
                K.dma("pool", ktsch, ktc[s + 1][:, :, PAST:PAST + 64].rearrange("h r n -> r h n"), KTs.ap[0:96, :, 64 * s:64 * s + 64], reads=[KTs], writes=[reg], batch=(s > 0))
                K.dma("pool", vsch, vc[s + 1][:, 0:64, NKT_S - 1, :].rearrange("h p c -> p h c"), Vs.ap[po:po + 64, :, sub, :], reads=[Vs], writes=[reg], batch=(s > 0))
                K.dma("pool", vsch, vc[s + 1][:, 64:128, NKT_S - 1, :].rearrange("h p c -> p h c"), Vs.ap[64 - po:128 - po, :, sub, :], reads=[Vs], writes=[reg], batch=True)
        (wu_t, wu_v), = R.get("winu")
        for m in range(4):
            b = 4 + m % 2
            fns = [lambda k=k, m=m, b=b: nc.tensor.matmul(PB[b].ap[:, 0:TT], wu_v[:, k, 128 * m:128 * m + 128], actT.ap[:, k, 0:TT], start=(k == 0), stop=(k == 7)) for k in range(8)]
            K.pe(fns, [wu_t] + actTs[0:nsub], [PB[b]])
            K.op("act", lambda m=m, b=b: nc.scalar.activation(out=uT.ap[:, m, 0:TT], in_=PB[b].ap[:, 0:TT], func=AF.Copy), [PB[b]], [uT])
        chk(5)
        ssm_a(kind, ti, nsub, TT, NC)
        scan_q = scan_step_fns(kind, NC)
        ncalls = [NH if kind == "p" else NH * NSEQ_S]

        def between(h):
            n_ = (len(scan_q) + ncalls[0] - 1) // ncalls[0]
            ncalls[0] -= 1
            for _ in range(n_):
                scan_q.pop(0)()
        if kind == "p":
            attention(0, 4 * ti + 4, 0, 512, ti, False, [(0, 128, s_) for s_ in range(4)], between)
        else:
            for s in range(NSEQ_S):
                attention(s + 1, PAST // 128, 64 * s, 64, None, True, [(64 * (s % 2), 64, s // 2)], between)
        while scan_q:
            scan_q.pop(0)()
        chk(6)
        for sub in range(nsub):
            r = rms_stats(attn.ap[:, sub, :], [attn], 512, 16 + sub)
            xb = xnb.next()
            K.op("dve", lambda sub=sub, xb=xb, r=r: nc.vector.tensor_scalar(out=xb.ap[:, 0:512], in0=attn.ap[:, sub, :], scalar1=r.ap, scalar2=None, op0=ALU.mult), [attn, r], [xb])
            transposes_to(actT, 0, xb, 4, sub, 22, actTs[sub])
        ssm_c(kind, ti, nsub, TT, NC)
        chk(7)
        wo = R.get("wout0", "wout1")
        for sub in range(nsub):
            fns = []
            for k in range(8):
                wvw = wo[k // 4][1]
                kl = k % 4
                for (c0, c1) in ((0, 512), (512, 1024)):
                    fns.append(lambda k=k, kl=kl, wvw=wvw, c0=c0, c1=c1, sub=sub: nc.tensor.matmul(
                        ps[:, c0:c1], actT.ap[:, k, 128 * sub:128 * sub + 128], wvw[:, kl, c0:c1], start=(k == 0), stop=(k == 7)))
            K.pe(fns, [actTs[sub], wo[0][0], wo[1][0]], [PB[0], PB[1]])
            K.op("dve", lambda sub=sub: nc.vector.tensor_tensor(out=xt.ap[:, sub, :], in0=ps[:, 0:1024], in1=xt.ap[:, sub, :], op=ALU.add), [PB[0], PB[1], xts[sub]], [xts[sub]])
        chk(8)
        for sub in range(nsub):
            r = rms_stats(xt.ap[:, sub, :], [xts[sub]], D, 20 + sub)
            xb = xnb.next()
            K.op("dve", lambda sub=sub, xb=xb, r=r: nc.vector.tensor_scalar(out=xb.ap, in0=xt.ap[:, sub, :], scalar1=r.ap, scalar2=None, op0=ALU.mult), [xts[sub], r], [xb])
            transposes_to(actT, 0, xb, 8, sub, 8, actTs[sub])
        def mlp_up(fc):
            (wu_t, wu_v), (wd_t, wd_v) = R.get("wup%d" % fc, "wdn%d" % fc, hold=(2 if fc > 0 else 0))
            a = aTr.next()
            for ft in range(4):
                b = 4 + ft % 2
                fns = [lambda k=k, ft=ft, b=b: nc.tensor.matmul(PB[b].ap[:, 0:TT], wu_v[:, k, 128 * ft:128 * ft + 128], actT.ap[:, k, 0:TT], start=(k == 0), stop=(k == 7)) for k in range(8)]
                K.pe(fns, [wu_t] + actTs[0:nsub], [PB[b]])
                tr = trl.next()
                K.op("dve", lambda b=b, tr=tr: nc.vector.tensor_scalar(out=tr.ap[:, 0:TT], in0=PB[b].ap[:, 0:TT], scalar1=0.0, scalar2=None, op0=ALU.max), [PB[b]], [tr])
                K.op("pool", lambda ft=ft, tr=tr, a=a: nc.gpsimd.tensor_tensor(out=a.ap[:, ft, 0:TT], in0=tr.ap[:, 0:TT], in1=tr.ap[:, 0:TT], op=ALU.mult), [tr], [a])
            return a, wd_t, wd_v

        def mlp_down(a, wd_t, wd_v):
            for sub in range(nsub):
                bb = (0, 1) if sub % 2 == 0 else (6, 7)
                base = 512 * bb[0]
                fns = []
                for ft in range(4):
                    for hh in range(2):
                        fns.append(lambda ft=ft, hh=hh, sub=sub, base=base: nc.tensor.matmul(
                            ps[:, base + 512 * hh:base + 512 * hh + 512], a.ap[:, ft, 128 * sub:128 * sub + 128], wd_v[:, ft, 512 * hh:512 * hh + 512], start=(ft == 0), stop=(ft == 3)))
                K.pe(fns, [a, wd_t], [PB[bb[0]], PB[bb[1]]])
                K.op("dve", lambda sub=sub, base=base: nc.vector.tensor_tensor(out=xt.ap[:, sub, :], in0=ps[:, base:base + 1024], in1=xt.ap[:, sub, :], op=ALU.add), [PB[bb[0]], PB[bb[1]], xts[sub]], [xts[sub]])

        cur = mlp_up(0)
        for fc in range(8):
            nxt = mlp_up(fc + 1) if fc + 1 < 8 else None
            mlp_down(*cur)
            cur = nxt
        for sub in range(nsub):
            r = rms_stats(xt.ap[:, sub, :], [xts[sub]], D, 24 + sub)
            K.op("dve", lambda sub=sub, r=r: nc.vector.scalar_tensor_tensor(out=xt.ap[:, sub, :], in0=xt.ap[:, sub, :], scalar=r.ap, in1=gfin_bc.ap, op0=ALU.mult, op1=ALU.mult), [xts[sub], r, gfin_bc], [xts[sub]])
            K.dma("pool", ych[sub], y_dst[128 * sub:128 * sub + 128, :], xt.ap[:, sub, :], reads=[xts[sub]])

    def ssm_a(kind, ti, nsub, TT, NC):
        uTc = uT.ap[:, :, 0:TT].rearrange("p m (c i) -> p m i c", i=8)
        Sv = Ssb.ap.rearrange("p (kk r) t c -> p r kk t c", r=4)
        fns = []
        for kk in range(4):
            for reim in range(2):
                c0 = (kk * 2 + reim) * NC
                for i in range(8):
                    for r in range(4):
                        fns.append(lambda i=i, kk=kk, r=r, reim=reim, c0=c0: nc.tensor.matmul(
                            PB[4 + r].ap[:, c0:c0 + NC], W1.ap[32 * r:32 * r + 32, kk, reim, i, :], uTc[32 * r:32 * r + 32, kk, i, :],
                            start=(i == 0), stop=(i == 7), tile_position=(32 * r, 0)))
        K.pe(fns, [W1, uT], [PB[4], PB[5], PB[6], PB[7]])
        for r in range(4):
            bank = 4 + r
            K.op("act", lambda r=r, bank=bank: nc.scalar.activation(
                out=Sv[:, r, :, :, 0:NC], in_=PB[bank].ap[:, 0:8 * NC].rearrange("p (a t c) -> p a t c", a=4, t=2), func=AF.Copy), [PB[bank]], [Ssb])

    def scan_step_fns(kind, NC):
        return [(lambda c=c: scan_step(kind, c)) for c in range(NC)]

    def scan_step(kind, c):
        if True:
            if kind == "p":
                prev_t, prev = (Hc, Hc.ap) if c == 0 else (Ssb, Ssb.ap[:, :, :, c - 1])
            else:
                if c % 8 == 0:
                    prev_t, prev = H0s, H0s.ap[:, c // 8, :, :]
                else:
                    prev_t, prev = Ssb, Ssb.ap[:, :, :, c - 1]
            pre = prev[:, :, 0:1].to_broadcast([128, 16, 2])
            pim = prev[:, :, 1:2].to_broadcast([128, 16, 2])
            if c % 8 == 0 or kind != "p" or True:
                pass
            K.op("pool", lambda c=c, prev=prev: nc.gpsimd.tensor_copy(out=Hbf.ap[:, :, :, c], in_=prev), [prev_t], [Hbf])
            K.op("pool", lambda pre=pre: nc.gpsimd.tensor_tensor(out=st1.ap, in0=A8.ap, in1=pre, op=ALU.mult), [A8, prev_t], [st1])
            K.op("pool", lambda pim=pim: nc.gpsimd.tensor_tensor(out=st2.ap, in0=B8.ap, in1=pim, op=ALU.mult), [B8, prev_t], [st2])
            K.op("pool", lambda: nc.gpsimd.tensor_tensor(out=st1.ap, in0=st1.ap, in1=st2.ap, op=ALU.add), [st1, st2], [st1])
            K.op("pool", lambda c=c: nc.gpsimd.tensor_tensor(out=Ssb.ap[:, :, :, c], in0=Ssb.ap[:, :, :, c], in1=st1.ap, op=ALU.add), [Ssb, st1], [Ssb])
    def ssm_c(kind, ti, nsub, TT, NC):
        uTc = uT.ap[:, :, 0:TT].rearrange("p m (c i) -> p m i c", i=8)
        if kind == "p":
            K.op("dve", lambda: nc.vector.tensor_copy(out=Hc.ap, in_=Ssb.ap[:, :, :, NC - 1]), [Ssb], [Hc])
            if ti == NT_P - 1:
                K.op("dve", lambda: nc.vector.tensor_copy(out=hout.ap, in_=Ssb.ap[:, :, :, NC - 1]), [Ssb], [hout])
                for g2 in range(2):
                    K.dma("pool", hoch, hrp.rearrange("(k two) n -> two n k", two=2)[g2], hout.ap[64 * g2:64 * g2 + 64, :, 0], reads=[hout], batch=(g2 > 0))
                    K.dma("pool", hoch, hip.rearrange("(k two) n -> two n k", two=2)[g2], hout.ap[64 * g2:64 * g2 + 64, :, 1], reads=[hout], batch=True)
        else:
            for s in range(NSEQ_S):
                for g2 in range(2):
                    K.dma("pool", hoch, hrs[s].rearrange("(k two) n -> two n k", two=2)[g2], Ssb.ap[64 * g2:64 * g2 + 64, :, 0, 8 * s + 7], reads=[Ssb], batch=(s + g2 > 0))
                    K.dma("pool", hoch, his[s].rearrange("(k two) n -> two n k", two=2)[g2], Ssb.ap[64 * g2:64 * g2 + 64, :, 1, 8 * s + 7], reads=[Ssb], batch=True)
        (wg_t, wg_v), = R.get("wglu")
        for kk in range(4):
            b = 6 + kk % 2
            yv = PB[b].ap[:, 0:TT].rearrange("p (c i) -> p i c", i=8)
            fns = [lambda kk=kk, b=b: nc.tensor.matmul(PB[b].ap[:, 0:TT], Dd.ap[:, kk, :], uT.ap[:, kk, 0:TT], start=True, stop=True)]
            for j in range(8):
                for i in range(j + 1):
                    fns.append(lambda kk=kk, j=j, i=i, yv=yv: nc.tensor.matmul(yv[:, j, :], BD.ap[:, kk, j - i, :], uTc[:, kk, i, :], start=False, stop=True, skip_group_check=True))
            for j in range(8):
                for reim in range(2):
                    for r in range(4):
                        kp = 4 * kk + r
                        lastone = (r == 3 and j == 7 and reim == 1)
                        fns.append(lambda kp=kp, r=r, j=j, reim=reim, yv=yv, lastone=lastone: nc.tensor.matmul(
                            yv[32 * r:32 * r + 32, j, :], W2.ap[:, kp, reim, j, :], Hbf.ap[:, kp, reim, 0:NC], start=False, stop=True, skip_group_check=True, tile_position=(0, 32 * r)))
            K.pe(fns, [Dd, uT, BD, W2, Hbf], [PB[b]])
            yp_ = PB[b].ap[:, 0:TT]
            K.op("act", lambda yp_=yp_: nc.scalar.activation(out=tA.ap[:, 0:TT], in_=yp_, func=AF.Square), [PB[b]], [tA])
            K.op("dve", lambda: nc.vector.tensor_scalar(out=tA.ap[:, 0:TT], in0=tA.ap[:, 0:TT], scalar1=0.044715, scalar2=1.0, op0=ALU.mult, op1=ALU.add), [tA], [tA])
            K.op("dve", lambda yp_=yp_: nc.vector.tensor_tensor(out=tA.ap[:, 0:TT], in0=yp_, in1=tA.ap[:, 0:TT], op=ALU.mult), [PB[b], tA], [tA])
            K.op("act", lambda: nc.scalar.activation(out=tA.ap[:, 0:TT], in_=tA.ap[:, 0:TT], func=AF.Sigmoid, scale=1.5957691216), [tA], [tA])
            K.op("dve", lambda kk=kk, yp_=yp_: nc.vector.tensor_tensor(out=y2.ap[:, kk, 0:TT], in0=yp_, in1=tA.ap[:, 0:TT], op=ALU.mult), [PB[b], tA], [y2])
            K.op("pool", lambda kk=kk: nc.gpsimd.tensor_copy(out=y2b.ap[:, kk, 0:TT], in_=y2.ap[:, kk, 0:TT]), [y2], [y2b])
        for m in range(4):
            b = 4 + m % 2
            fns = [lambda k=k, m=m, b=b: nc.tensor.matmul(PB[b].ap[:, 0:TT], wg_v[:, k, 128 * m:128 * m + 128], y2b.ap[:, k, 0:TT], start=(k == 0), stop=(k == 3)) for k in range(4)]
            K.pe(fns, [wg_t, y2b], [PB[b]])
            K.op("act", lambda b=b: nc.scalar.activation(out=tB.ap[:, 0:TT], in_=PB[b].ap[:, 0:TT], func=AF.Sigmoid), [PB[b]], [tB])
            K.op("dve", lambda m=m: nc.vector.tensor_tensor(out=y2.ap[:, m, 0:TT], in0=y2.ap[:, m, 0:TT], in1=tB.ap[:, 0:TT], op=ALU.mult), [y2, tB], [y2])
            K.op("pool", lambda m=m: nc.gpsimd.tensor_tensor(out=sqb.ap[:, m, 0:TT], in0=y2.ap[:, m, 0:TT], in1=y2.ap[:, m, 0:TT], op=ALU.mult), [y2], [sqb])
        fns = [lambda m=m: nc.tensor.matmul(PB[6].ap[:, 0:TT], ones_b.ap, sqb.ap[:, m, 0:TT], start=(m == 0), stop=(m == 3)) for m in range(4)]
        K.pe(fns, [ones_b, sqb], [PB[6]])
        K.op("act", lambda: nc.scalar.activation(out=rbc.ap[:, 0:TT], in_=PB[6].ap[:, 0:TT], func=AF.Sqrt, scale=1.0 / 512, bias=EPS), [PB[6]], [rbc])
        K.op("dve", lambda: nc.vector.reciprocal(out=rbc.ap[:, 0:TT], in_=rbc.ap[:, 0:TT]), [rbc], [rbc])
        for m in range(4):
            K.op("dve", lambda m=m: nc.vector.scalar_tensor_tensor(out=actT.ap[:, 4 + m, 0:TT], in0=y2.ap[:, m, 0:TT], scalar=gcols.ap[:, 26 + m:27 + m], in1=rbc.ap[:, 0:TT], op0=ALU.mult, op1=ALU.mult), [y2, gcols, rbc], actTs[0:nsub])

    kvbufs = list(kvtok.items)
    for t_ in cqn.items:
        kvbufs.append(T(t_.ap[:, 0:352], share=t_))
    for t_ in qtok.items:
        kvbufs.append(T(t_.ap.rearrange("p h c -> p (h c)")[:, 0:352], share=t_))
    kvl_items = [(s, pt_i, sub) for s in range(NSEQ_S) for pt_i in range(NPT_S) for sub in range(4)]
    kvl_state = {"loaded": 0}
    stg = [T(kvout[0].ap, share=kvout[0]), T(kvout[1].ap, share=kvout[1]), T(qsb.ap[:, 0:288], share=qsb), T(rbc.ap[:, 0:288], share=rbc)]
    stg_ch = [K.chan("stg%d" % i) for i in range(len(stg))]

    def kvl_prefetch(upto):
        while kvl_state["loaded"] < min(len(kvl_items), upto):
            i_ = kvl_state["loaded"]
            s, pt_i, sub = kvl_items[i_]
            sg = stg[i_ % len(stg)]
            ch = stg_ch[i_ % len(stg)]
            r0 = 512 * pt_i + 128 * sub
            K.dma("sp", ch, sg.ap[:, 0:256], ckl[s, r0:r0 + 128, :], writes=[sg])
            K.dma("sp", ch, sg.ap[:, 256:288], ckr[s, r0:r0 + 128, :], writes=[sg], batch=True)
            kvl_state["loaded"] += 1

    def kv_from_cache(s, pt_i):
        base = (s * NPT_S + pt_i) * 4
        for sub in range(4):
            kvl_prefetch(base + sub + len(stg))
            sg = stg[(base + sub) % len(stg)]
            kvt = kvbufs[(base + sub) % len(kvbufs)]
            K.op("act", lambda: nc.scalar.activation(out=kvt.ap[:, 0:256], in_=sg.ap[:, 0:256], func=AF.Copy), [sg], [kvt])
            K.op("act", lambda: nc.scalar.activation(out=kvt.ap[:, 320:352], in_=sg.ap[:, 256:288], func=AF.Copy), [sg], [kvt])
            kv_from_tok(kvt, sub, 512)
        (wkv_t, wkv_v), = R.get("wkv")
        kv_build(wkv_t, wkv_v, 4)
        reg = kvreg[s + 1][pt_i]
        K.dma("pool", ktsch, ktc[s + 1][:, :, 512 * pt_i:512 * pt_i + 512].rearrange("h r n -> r h n"), KTs.ap[0:96, :, :], reads=[KTs], writes=[reg])
        K.dma("pool", vsch, vc[s + 1][:, :, 4 * pt_i:4 * pt_i + 4, :].rearrange("h p s c -> p h s c"), Vs.ap[:, :, :, :], reads=[Vs], writes=[reg])

    K.op("pool", lambda: nc.gpsimd.memset(Hc.ap, 0.0), [], [Hc])
    for ti in range(NT_P):
        R.add(plan_tok)
    for s in range(NSEQ_S * NPT_S):
        R.add(plan_kv)
    R.add(plan_tok)
    try:
        for ti in range(NT_P):
            token_tile("p", ti)
        chk(9)
        for s in range(NSEQ_S):
            for pt_i in range(NPT_S):
                kv_from_cache(s, pt_i)
        chk(10)
        token_tile("s", 0)
    except Stop:
        pass
    K.finish("pool")
    return nc, K


_CACHE = {}


def _rope_table(pos):
    half = 16
    inv_freq = (10000.0 ** (-(np.arange(half, dtype=np.float32) * 2.0) / 32)).astype(np.float32)
    ang = pos.astype(np.float32)[:, None] * inv_freq[None, :]
    return np.concatenate([np.cos(ang), np.sin(ang)], axis=1).astype(np.float32)


def run(inputs, SEQ, PAST, ncores=8):
    key = (SEQ, PAST)
    if key not in _CACHE:
        _CACHE[key] = build(SEQ, PAST)
    nc, K = _CACHE[key]
    f = lambda a: np.ascontiguousarray(np.asarray(a, dtype=np.float32))
    x_prompt = f(inputs["x_prompt"]); x_sample = f(inputs["x_sample"])
    ckl = f(inputs["cache_kv_latent"])[0]; ckr = f(inputs["cache_k_rope"])[0]
    sre = f(inputs["state_ssm_re"])[0]; sim = f(inputs["state_ssm_im"])[0]
    cs_p = _rope_table(np.arange(SEQ))
    cs_s = np.tile(_rope_table(PAST + np.arange(DSEQ)), (NSEQ_S, 1))
    ident = np.eye(128, dtype=np.float32)
    shared = {
        "g_mix": f(inputs["g_mix"]), "w_in": f(inputs["w_in"])[0], "g_q": f(inputs["g_q_a"]), "w_q": f(inputs["w_q_up"])[0],
        "g_kv": f(inputs["g_kv_a"]), "w_kv": f(inputs["w_kv_up"])[0], "a_re": f(inputs["a_re"])[0], "a_im": f(inputs["a_im"])[0],
        "lstep": f(inputs["log_step"]), "b_re": f(inputs["b_re"])[0], "b_im": f(inputs["b_im"])[0],
        "c_re": f(inputs["c_re"])[0], "c_im": f(inputs["c_im"])[0], "d_skip": f(inputs["d_skip"]), "w_glu": f(inputs["w_glu"])[0],
        "g_attn": f(inputs["g_attn_out"]), "g_ssm": f(inputs["g_ssm_out"]), "w_out": f(inputs["w_out"])[0],
        "g_mlp": f(inputs["g_mlp"]), "w_up": f(inputs["w_up"])[0], "w_down": f(inputs["w_down"])[0],
        "g_fin": f(inputs["g_final"]).reshape(1, D), "cs_p": cs_p, "cs_s": cs_s, "ident": ident,
    }
    in_maps = []
    for c in range(ncores):
        m = dict(shared)
        m["xp"] = x_prompt[c]
        sl = slice(NSEQ_S * c, NSEQ_S * (c + 1))
        m["xs"] = np.ascontiguousarray(x_sample[sl].reshape(NSEQ_S * DSEQ, D))
        m["ckl"] = np.ascontiguousarray(ckl[sl]); m["ckr"] = np.ascontiguousarray(ckr[sl])
        m["sre"] = np.ascontiguousarray(sre[sl]); m["sim"] = np.ascontiguousarray(sim[sl])
        in_maps.append(m)
    res = run_bass_kernel_spmd(nc, in_maps, core_ids=list(range(ncores)))
    rs = res.results
    cat = lambda k: np.stack([np.asarray(r[k], dtype=np.float32) for r in rs])
    y_prompt = cat("yp")
    y_sample = cat("ys").reshape(ncores * NSEQ_S, DSEQ, D)
    lat_p = cat("latp")[None]
    kr_p = cat("krp")[None]
    hr_p = cat("hrp")[None]
    hi_p = cat("hip")[None]
    lat_s = cat("lats").reshape(ncores * NSEQ_S, DSEQ, KVL)[None]
    kr_s = cat("krs").reshape(ncores * NSEQ_S, DSEQ, 32)[None]
    hr_s = cat("hrs").reshape(ncores * NSEQ_S, 32, 64)[None]
    hi_s = cat("his").reshape(ncores * NSEQ_S, 32, 64)[None]
    return (y_prompt, y_sample, lat_p, kr_p, hr_p, hi_p, lat_s, kr_s, hr_s, hi_s)


def kernel(**inputs):
    return run(inputs, 8192, 4096, 8)
```
